# Optimizing a Trainium2 kernel written in Bass

```python
import math
import jax, jax.numpy as jnp
from jax import lax
import numpy as np

D_MODEL = 1024
BATCH = 8
SEQ = 2048
DEPTH = 2

N_MIXERS = 2
HEAD_DIM = 64
MIX_WIDTH = D_MODEL
MEM_HEADS = 4
MEM_WIDTH = MEM_HEADS * HEAD_DIM
MEM_TOKENS = 256
TOK_WIDTH = MIX_WIDTH - MEM_WIDTH
HYENA_CH = TOK_WIDTH
HYENA_IN = 3 * HYENA_CH
HYENA_EMB = 33
HYENA_FILTER_W = 64
DECAY_TARGET = 1e-2
SHORTEST_DECAY_FRAC = 0.3
LONGEST_DECAY_FRAC = 1.5
SHORT_CONV = 3
SWA_HEADS = TOK_WIDTH // HEAD_DIM
SWA_KV_HEADS = SWA_HEADS // 3
SWA_IN = (SWA_HEADS + 2 * SWA_KV_HEADS) * HEAD_DIM
WINDOW = 128
BLOCK = 128
ROPE_THETA = 10000.0
D_FF = 128 * int(math.ceil(8 * D_MODEL / 3 / 128))
FFN_CONV = 3
LN_EPS = 1e-5
DEEPNORM_ALPHA = (2 * DEPTH) ** 0.25
DEEPNORM_BETA = (8 * DEPTH) ** -0.25
NEG_INF = -1e30

kernel_name = "hybrid_hyena_swa_memory_encoder"


def layer_norm(x, g, b):
    xf = x.astype(jnp.float32)
    mu = jnp.mean(xf, axis=-1, keepdims=True)
    var = jnp.mean(jnp.square(xf - mu), axis=-1, keepdims=True)
    y = (xf - mu) * lax.rsqrt(var + LN_EPS) * g.astype(jnp.float32) + b.astype(jnp.float32)
    return y.astype(x.dtype)


def dwconv3(x, w, b):
    xp = jnp.pad(x, ((0, 0), (1, 1), (0, 0)))
    return xp[:, :-2] * w[0] + xp[:, 1:-1] * w[1] + xp[:, 2:] * w[2] + b


def rope_tables(L):
    inv = ROPE_THETA ** (-jnp.arange(0, HEAD_DIM, 2, dtype=jnp.float32) / HEAD_DIM)
    ang = jnp.arange(L, dtype=jnp.float32)[:, None] * inv[None, :]
    ang = jnp.concatenate([ang, ang], axis=-1)
    return jnp.cos(ang), jnp.sin(ang)


def apply_rope(x, cos, sin):
    xf = x.astype(jnp.float32)
    x1, x2 = jnp.split(xf, 2, axis=-1)
    rot = jnp.concatenate([-x2, x1], axis=-1)
    return (xf * cos[None, :, None, :] + rot * sin[None, :, None, :]).astype(x.dtype)


def hyena_filters(L, w1, b1, f1, w2, b2, f2, w3, b3, f3, w_out):
    f32 = jnp.float32
    t = jnp.linspace(0.0, 1.0, L, dtype=f32)[:, None]
    bands = (HYENA_EMB - 1) // 2
    w = 2.0 * math.pi * jnp.arange(L, dtype=f32)[:, None] / L
    fr = jnp.linspace(1e-4, bands - 1, bands, dtype=f32)[None, :]
    z = jnp.concatenate([t, jnp.cos(fr * w), -jnp.sin(fr * w)], axis=-1)
    h = jnp.sin(f1.astype(f32) * (z @ w1.astype(f32) + b1.astype(f32)))
    h = jnp.sin(f2.astype(f32) * (h @ w2.astype(f32) + b2.astype(f32)))
    h = jnp.sin(f3.astype(f32) * (h @ w3.astype(f32) + b3.astype(f32)))
    h = h @ w_out.astype(f32)
    min_decay = math.log(DECAY_TARGET) / LONGEST_DECAY_FRAC
    max_decay = math.log(DECAY_TARGET) / SHORTEST_DECAY_FRAC
    deltas = jnp.abs(jnp.linspace(min_decay, max_decay, HYENA_CH, dtype=f32))
    decay = jnp.exp(-t * deltas[None, :])
    return h[:, :HYENA_CH] * decay, h[:, HYENA_CH:] * decay


def bidirectional_long_conv(u, h_fwd, h_bwd):
    L, C = h_fwd.shape
    k0 = h_fwd.at[0].add(h_bwd[0])
    k_circ = jnp.concatenate([k0, jnp.zeros((1, C), jnp.float32), h_bwd[1:][::-1]], axis=0)
    U = jnp.fft.rfft(u, n=2 * L, axis=1)
    K = jnp.fft.rfft(k_circ, n=2 * L, axis=0)
    return jnp.fft.irfft(U * K[None], n=2 * L, axis=1)[:, :L]


def hyena_mixer(u, conv_w, conv_b, w1, b1, f1, w2, b2, f2, w3, b3, f3, filt_w_out, d_bias):
    L = u.shape[1]
    uc = dwconv3(u, conv_w, conv_b)
    x0, x1, v = jnp.split(uc, 3, axis=-1)
    h_fwd, h_bwd = hyena_filters(L, w1, b1, f1, w2, b2, f2, w3, b3, f3, filt_w_out)
    z = (v * x1).astype(jnp.float32)
    z = bidirectional_long_conv(z, h_fwd, h_bwd) + z * d_bias.astype(jnp.float32)
    return z.astype(u.dtype) * x0


def windowed_gqa_sink(q, k, v, sink):
    B, L, H, hd = q.shape
    kvh = k.shape[2]
    g = H // kvh
    nb = L // BLOCK
    qb = q.reshape(B, nb, BLOCK, kvh, g, hd)
    pad = ((0, 0), (BLOCK, BLOCK), (0, 0), (0, 0))
    kp = jnp.pad(k, pad).reshape(B, nb + 2, BLOCK, kvh, hd)
    vp = jnp.pad(v, pad).reshape(B, nb + 2, BLOCK, kvh, hd)
    kw = jnp.concatenate([kp[:, :-2], kp[:, 1:-1], kp[:, 2:]], axis=2)
    vw = jnp.concatenate([vp[:, :-2], vp[:, 1:-1], vp[:, 2:]], axis=2)
    s = jnp.einsum('bnqkgd,bnskd->bnkgqs', qb, kw).astype(jnp.float32) * (hd ** -0.5)
    blk = jnp.arange(nb)[:, None]
    qpos = blk * BLOCK + jnp.arange(BLOCK)[None, :]
    kpos = (blk - 1) * BLOCK + jnp.arange(3 * BLOCK)[None, :]
    rel = kpos[:, None, :] - qpos[:, :, None]
    valid = (jnp.abs(rel) <= WINDOW) & (kpos[:, None, :] >= 0) & (kpos[:, None, :] < L)
    s = jnp.where(valid[None, :, None, None], s, NEG_INF)
    sink_logit = jnp.broadcast_to(
        sink.astype(jnp.float32).reshape(kvh, g)[None, None, :, :, None, None],
        s.shape[:-1] + (1,))
    p = jax.nn.softmax(jnp.concatenate([s, sink_logit], axis=-1), axis=-1)[..., :-1]
    o = jnp.einsum('bnkgqs,bnskd->bnqkgd', p.astype(v.dtype), vw)
    return o.reshape(B, L, H * hd)


def swa_mixer(u, sink, cos, sin):
    B, L, _ = u.shape
    q, k, v = jnp.split(u, [SWA_HEADS * HEAD_DIM, (SWA_HEADS + SWA_KV_HEADS) * HEAD_DIM], axis=-1)
    q = apply_rope(q.reshape(B, L, SWA_HEADS, HEAD_DIM), cos, sin)
    k = apply_rope(k.reshape(B, L, SWA_KV_HEADS, HEAD_DIM), cos, sin)
    v = v.reshape(B, L, SWA_KV_HEADS, HEAD_DIM)
    return windowed_gqa_sink(q, k, v, sink)


def memory_attention(mq, mem_k, mem_v):
    B, L, _ = mq.shape
    q = mq.reshape(B, L, MEM_HEADS, HEAD_DIM)
    s = jnp.einsum('blhd,bmhd->bhlm', q, mem_k).astype(jnp.float32) * (HEAD_DIM ** -0.5)
    p = jax.nn.softmax(s, axis=-1).astype(mem_v.dtype)
    o = jnp.einsum('bhlm,bmhd->blhd', p, mem_v)
    return o.reshape(B, L, MEM_WIDTH)


def conv_glu_ffn(x, w_up, conv_w, conv_b, w_down):
    h = dwconv3(x @ w_up, conv_w, conv_b)
    a, gate = jnp.split(h, 2, axis=-1)
    return (jax.nn.silu(gate) * a) @ w_down


def _normal(k, shape, scale):
    return jax.random.normal(k, shape, jnp.float32) * scale


def setup_inputs(seed: int = 0) -> dict:
    key = jax.random.key(seed)
    ks = iter(jax.random.split(key, 64))
    d = D_MODEL
    W = HYENA_FILTER_W
    beta = DEEPNORM_BETA
    inp = {}
    inp['x'] = _normal(next(ks), (BATCH, SEQ, d), 1.0)
    inp['mem'] = _normal(next(ks), (BATCH, MEM_TOKENS, d), 1.0)
    inp['w_mem_kv'] = _normal(next(ks), (d, 2 * MEM_WIDTH), d ** -0.5)
    inp['l0_w_in'] = _normal(next(ks), (d, HYENA_IN + MEM_WIDTH), d ** -0.5)
    inp['l0_conv_w'] = _normal(next(ks), (SHORT_CONV, HYENA_IN), SHORT_CONV ** -0.5)
    inp['l0_conv_b'] = _normal(next(ks), (HYENA_IN,), 0.02)
    inp['l0_filt_w1'] = _normal(next(ks), (HYENA_EMB, W), HYENA_EMB ** -0.5)
    inp['l0_filt_b1'] = _normal(next(ks), (W,), 0.1)
    inp['l0_filt_f1'] = 1.0 + _normal(next(ks), (W,), 0.01)
    inp['l0_filt_w2'] = _normal(next(ks), (W, W), W ** -0.5)
    inp['l0_filt_b2'] = _normal(next(ks), (W,), 0.1)
    inp['l0_filt_f2'] = 1.0 + _normal(next(ks), (W,), 0.01)
    inp['l0_filt_w3'] = _normal(next(ks), (W, W), W ** -0.5)
    inp['l0_filt_b3'] = _normal(next(ks), (W,), 0.1)
    inp['l0_filt_f3'] = 1.0 + _normal(next(ks), (W,), 0.01)
    inp['l0_filt_w_out'] = _normal(next(ks), (W, 2 * HYENA_CH), 0.1 * W ** -0.5)
    inp['l0_hyena_d'] = _normal(next(ks), (HYENA_CH,), 0.5)
    inp['l0_w_out'] = _normal(next(ks), (MIX_WIDTH, d), beta * MIX_WIDTH ** -0.5)
    inp['l0_ln1_g'] = 1.0 + _normal(next(ks), (d,), 0.02)
    inp['l0_ln1_b'] = _normal(next(ks), (d,), 0.02)
    inp['l0_ffn_w_up'] = _normal(next(ks), (d, 2 * D_FF), d ** -0.5)
    inp['l0_ffn_conv_w'] = _normal(next(ks), (FFN_CONV, 2 * D_FF), FFN_CONV ** -0.5)
    inp['l0_ffn_conv_b'] = _normal(next(ks), (2 * D_FF,), 0.02)
    inp['l0_ffn_w_down'] = _normal(next(ks), (D_FF, d), beta * D_FF ** -0.5)
    inp['l0_ln2_g'] = 1.0 + _normal(next(ks), (d,), 0.02)
    inp['l0_ln2_b'] = _normal(next(ks), (d,), 0.02)
    inp['l1_w_in'] = _normal(next(ks), (d, SWA_IN + MEM_WIDTH), d ** -0.5)
    inp['l1_sink'] = _normal(next(ks), (SWA_HEADS,), 0.5)
    inp['l1_w_out'] = _normal(next(ks), (MIX_WIDTH, d), beta * MIX_WIDTH ** -0.5)
    inp['l1_ln1_g'] = 1.0 + _normal(next(ks), (d,), 0.02)
    inp['l1_ln1_b'] = _normal(next(ks), (d,), 0.02)
    inp['l1_ffn_w_up'] = _normal(next(ks), (d, 2 * D_FF), d ** -0.5)
    inp['l1_ffn_conv_w'] = _normal(next(ks), (FFN_CONV, 2 * D_FF), FFN_CONV ** -0.5)
    inp['l1_ffn_conv_b'] = _normal(next(ks), (2 * D_FF,), 0.02)
    inp['l1_ffn_w_down'] = _normal(next(ks), (D_FF, d), beta * D_FF ** -0.5)
    inp['l1_ln2_g'] = 1.0 + _normal(next(ks), (d,), 0.02)
    inp['l1_ln2_b'] = _normal(next(ks), (d,), 0.02)
    return inp


def reference(x, mem, w_mem_kv,
              l0_w_in, l0_conv_w, l0_conv_b,
              l0_filt_w1, l0_filt_b1, l0_filt_f1, l0_filt_w2, l0_filt_b2, l0_filt_f2,
              l0_filt_w3, l0_filt_b3, l0_filt_f3, l0_filt_w_out, l0_hyena_d,
              l0_w_out, l0_ln1_g, l0_ln1_b,
              l0_ffn_w_up, l0_ffn_conv_w, l0_ffn_conv_b, l0_ffn_w_down, l0_ln2_g, l0_ln2_b,
              l1_w_in, l1_sink, l1_w_out, l1_ln1_g, l1_ln1_b,
              l1_ffn_w_up, l1_ffn_conv_w, l1_ffn_conv_b, l1_ffn_w_down, l1_ln2_g, l1_ln2_b):
    B, L, _ = x.shape
    mem_kv = mem @ w_mem_kv
    mem_k, mem_v = jnp.split(mem_kv, 2, axis=-1)
    mem_k = mem_k.reshape(B, mem.shape[1], MEM_HEADS, HEAD_DIM)
    mem_v = mem_v.reshape(B, mem.shape[1], MEM_HEADS, HEAD_DIM)
    cos, sin = rope_tables(L)

    layers = [
        dict(w_in=l0_w_in,
             mix=(l0_conv_w, l0_conv_b, l0_filt_w1, l0_filt_b1, l0_filt_f1,
                  l0_filt_w2, l0_filt_b2, l0_filt_f2, l0_filt_w3, l0_filt_b3, l0_filt_f3,
                  l0_filt_w_out, l0_hyena_d),
             w_out=l0_w_out, ln1=(l0_ln1_g, l0_ln1_b),
             ffn=(l0_ffn_w_up, l0_ffn_conv_w, l0_ffn_conv_b, l0_ffn_w_down),
             ln2=(l0_ln2_g, l0_ln2_b)),
        dict(w_in=l1_w_in, mix=(l1_sink,),
             w_out=l1_w_out, ln1=(l1_ln1_g, l1_ln1_b),
             ffn=(l1_ffn_w_up, l1_ffn_conv_w, l1_ffn_conv_b, l1_ffn_w_down),
             ln2=(l1_ln2_g, l1_ln2_b)),
    ]

    for i in range(DEPTH):
        p = layers[i]
        h = x @ p['w_in']
        tok, mq = h[..., :-MEM_WIDTH], h[..., -MEM_WIDTH:]
        if i % N_MIXERS == 0:
            y_tok = hyena_mixer(tok, *p['mix'])
        else:
            y_tok = swa_mixer(tok, *p['mix'], cos, sin)
        y_mem = memory_attention(mq, mem_k, mem_v)
        y = jnp.concatenate([y_tok, y_mem], axis=-1) @ p['w_out']
        x = layer_norm(DEEPNORM_ALPHA * x + y, *p['ln1'])
        x = layer_norm(DEEPNORM_ALPHA * x + conv_glu_ffn(x, *p['ffn']), *p['ln2'])
    return x
```

```python
import math
from contextlib import ExitStack

import numpy as np
import ml_dtypes
import concourse.bass as bass
import concourse.mybir as mybir
from concourse.bass_utils import run_bass_kernel_spmd

F32 = mybir.dt.float32
BF16 = mybir.dt.bfloat16
AF = mybir.ActivationFunctionType
ALU = mybir.AluOpType
AX = mybir.AxisListType

L = 2048
D = 1024
NT = 16
KC = 8
DFF = 2816
NFF = 22
ALPHA = 4.0 ** 0.25
EPS = 1e-5
PI = float(np.pi)


class Res:
    __slots__ = ("name", "w", "r", "excl")

    def __init__(self, name, excl=False):
        self.name = name
        self.w = None
        self.r = []
        self.excl = excl


class Sched:
    def __init__(self, nc, es):
        self.nc = nc
        self.es = es
        self.engs = {}
        for n in ("pe", "act", "dve", "pool", "sp"):
            sem = es.enter_context(nc.semaphore("s_" + n))
            self.engs[n] = dict(sem=sem, count=0, known={}, ops=[])
        self.dma_sems = {}
        self.n_dma_sems = 0

    def dma_sem(self, key):
        if key not in self.dma_sems:
            sem = self.es.enter_context(self.nc.semaphore("d%d" % self.n_dma_sems))
            self.n_dma_sems += 1
            self.dma_sems[key] = [sem, 0]
        return self.dma_sems[key]

    def op(self, eng, fns, reads=(), writes=(), dma_key=None):
        E = self.engs[eng]
        deps = {}

        def add(ev):
            if ev is None:
                return
            sem, val = ev
            if deps.get(sem, 0) < val:
                deps[sem] = val

        excl_reads = [r for r in reads if r.excl]
        writes = list(writes) + [r for r in excl_reads if r not in writes]
        reads = [r for r in reads if not r.excl]
        for r in reads:
            add(r.w)
        for w in writes:
            add(w.w)
            for ev in w.r:
                add(ev)
        waits = []
        for sem, val in deps.items():
            if sem is E["sem"] and eng == "pe" and dma_key is None:
                continue
            if E["known"].get(sem, 0) >= val:
                continue
            E["known"][sem] = val
            waits.append((sem, val))
        if dma_key is not None:
            ds = self.dma_sem(dma_key)
            ds[1] += 16
            ev = (ds[0], ds[1])
            inc = (ds[0], 16)
        else:
            E["count"] += 1
            ev = (E["sem"], E["count"])
            inc = (E["sem"], 1)
        for r in reads:
            r.r.append(ev)
        for w in writes:
            w.w = ev
            w.r = []
        if not isinstance(fns, (list, tuple)):
            fns = [fns]
        E["ops"].append((waits, list(fns), inc))
        return ev

    def dma(self, queue, out, in_, reads=(), writes=(), key=None):
        assert key is not None
        return self.op(queue, [lambda e: e.dma_start(out=out, in_=in_)], reads, writes, dma_key=key)

    def barrier(self):
        evs = [(E["sem"], E["count"]) for E in self.engs.values() if E["count"] > 0]
        evs += [(s, c) for (s, c) in self.dma_sems.values() if c > 0]
        for n, E in self.engs.items():
            waits = []
            for sem, val in evs:
                if E["known"].get(sem, 0) >= val:
                    continue
                if sem is E["sem"] and n == "pe":
                    continue
                E["known"][sem] = val
                waits.append((sem, val))
            if waits:
                E["ops"].append((waits, [], None))

    def emit(self):
        nc = self.nc
        import sys
        print("SCHED counts", {n: E["count"] for n, E in self.engs.items()}, "dma", {k: v[1] for k, v in self.dma_sems.items()}, file=sys.stderr)

        def run(name):
            def f(e):
                for waits, fns, inc in self.engs[name]["ops"]:
                    for sem, val in waits:
                        e.wait_ge(sem, val)
                    n = len(fns)
                    for i, fn in enumerate(fns):
                        ins = fn(e)
                        if i == n - 1:
                            ins.then_inc(inc[0], inc[1])
            return f

        with nc.Block() as block:
            block.tensor(run("pe"))
            block.scalar(run("act"))
            block.vector(run("dve"))
            block.gpsimd(run("pool"))
            block.sync(run("sp"))


def _bf(a):
    return np.ascontiguousarray(a.astype(ml_dtypes.bfloat16))


_CONST_CACHE = {}


def const_tables():
    if _CONST_CACHE:
        return _CONST_CACHE
    N = 4096
    f = np.arange(2048, dtype=np.float64)
    t = np.arange(2048, dtype=np.float64)
    m = np.mod(np.outer(2 * f + 1, t), 2 * N)
    ang = np.pi * m / N
    C = np.cos(ang)
    Sn = np.sin(ang)
    CT = C.T.reshape(16, 128, 16, 128)
    ST = (-Sn).T.reshape(16, 128, 16, 128)
    fwd = np.stack([CT, ST], axis=0)
    fwd = fwd.transpose(3, 2, 0, 1, 4)
    _CONST_CACHE["fwd_tab"] = _bf(fwd.reshape(16, 128, 2 * 16 * 128))
    Ci = (C / 2048.0).reshape(4, 4, 128, 4, 512)
    Si = (-Sn / 2048.0).reshape(4, 4, 128, 4, 512)
    inv = np.stack([Ci, Si], axis=0)
    inv = inv.transpose(4, 1, 3, 2, 0, 5)
    _CONST_CACHE["inv_tab"] = _bf(inv.reshape(4, 4, 128, 4 * 2 * 512))
    f32 = np.float32
    tl = np.linspace(0.0, 1.0, L, dtype=f32)[:, None]
    w = (f32(2.0 * math.pi) * np.arange(L, dtype=f32)[:, None] / f32(L)).astype(f32)
    fr = np.linspace(1e-4, 15, 16, dtype=f32)[None, :]
    z = np.concatenate([tl, np.cos(fr * w), -np.sin(fr * w)], axis=-1).astype(f32)
    _CONST_CACHE["zfT"] = np.ascontiguousarray(z.T)
    _CONST_CACHE["ntn"] = np.ascontiguousarray((-tl[:, 0]).reshape(16, 128).T.astype(f32))
    min_decay = math.log(1e-2) / 1.5
    max_decay = math.log(1e-2) / 0.3
    _CONST_CACHE["deltas"] = np.abs(np.linspace(min_decay, max_decay, 768, dtype=f32)).astype(f32)
    inv_f = (10000.0 ** (-np.arange(0, 64, 2, dtype=f32) / f32(64))).astype(f32)
    angr = (np.arange(L, dtype=f32)[:, None] * inv_f[None, :]).astype(f32)
    angr = np.concatenate([angr, angr], axis=-1)
    cosT = np.cos(angr).T.astype(f32)
    sinT = np.sin(angr).T.astype(f32)
    _CONST_CACHE["ropec"] = np.ascontiguousarray(np.concatenate([cosT, cosT], axis=0))
    _CONST_CACHE["ropes"] = np.ascontiguousarray(np.concatenate([sinT, sinT], axis=0))
    Pm = np.zeros((128, 128), np.float32)
    for po in range(128):
        d = po % 64
        if d < 32:
            Pm[po + 32, po] = -1.0
        else:
            Pm[po - 32, po] = 1.0
    _CONST_CACHE["pm"] = _bf(Pm)
    _CONST_CACHE["ident"] = _bf(np.eye(128, dtype=np.float32))
    k = np.arange(128)[:, None]
    q = np.arange(128)[None, :]
    NEG = -30000.0
    m_next = np.where(k <= q, 0.0, NEG)
    m_prev = np.where(k >= q, 0.0, NEG)
    _CONST_CACHE["maskb"] = _bf(np.concatenate([m_next, np.zeros((128, 128)), m_prev], axis=1))
    return _CONST_CACHE


def build_program(debug=None):
    nc = bass.Bass("TRN2", target_bir_lowering=False)
    dbg = {}

    def din(name, shape, dt=F32):
        return nc.dram_tensor(name, list(shape), dt, kind="ExternalInput").ap()

    x_d = din("x", [L, D])
    mem_d = din("mem", [256, D])
    wkv_d = din("w_mem_kv", [D, 512])
    w_in_d = [din("l0_w_in", [D, 2560]), din("l1_w_in", [D, 1536])]
    w_out_d = [din("l0_w_out", [D, D]), din("l1_w_out", [D, D])]
    w_up_d = [din("l%d_ffn_w_up" % i, [D, 2 * DFF]) for i in range(2)]
    w_dn_d = [din("l%d_ffn_w_down" % i, [DFF, D]) for i in range(2)]
    ln_d = [[din("l%d_%s" % (i, n), [D]) for n in ("ln1_g", "ln1_b", "ln2_g", "ln2_b")] for i in range(2)]
    fcwb_d = [din("l%d_fcwb" % i, [128, 44, 4]) for i in range(2)]
    cwb_d = din("l0_cwb", [128, 18, 4])
    fw1_d = din("l0_filt_w1", [33, 64])
    fw2_d = din("l0_filt_w2", [64, 64])
    fw3_d = din("l0_filt_w3", [64, 64])
    fwo_d = din("l0_filt_w_out", [64, 1536])
    fbf_d = din("l0_fbf", [64, 6])
    hd_d = din("l0_hyena_d", [768])
    sink_d = din("l1_sink", [12])
    fwd_tab_d = din("fwd_tab", [16, 128, 4096], BF16)
    inv_tab_d = din("inv_tab", [4, 4, 128, 4096], BF16)
    zfT_d = din("zfT", [33, L])
    ntn_d = din("ntn", [128, 16])
    deltas_d = din("deltas", [768])
    ropec_d = din("ropec", [128, L])
    ropes_d = din("ropes", [128, L])
    pm_d = din("pm", [128, 128], BF16)
    ident_d = din("ident", [128, 128], BF16)
    maskb_d = din("maskb", [128, 384], BF16)
    out_d = nc.dram_tensor("out", [L, D], F32, kind="ExternalOutput").ap()
    if debug:
        dbg_d = nc.dram_tensor("dbg", [L, D], F32, kind="ExternalOutput").ap()

    es = ExitStack()
    with es:
        S = Sched(nc, es)
        AR_BYTES = 194 * 1024
        arena = es.enter_context(nc.sbuf_tensor("arena", [128, AR_BYTES // 2], BF16))
        ps = es.enter_context(nc.psum_tensor("ps", [128, 4096], F32))
        PB = [Res("pb%d" % i, excl=True) for i in range(8)]

        def bank(b, n=512, p0=0, p1=128):
            return ps[p0:p1, b * 512:b * 512 + n]

        def bank_bf(b):
            return ps[:, b * 512:(b + 1) * 512].bitcast(BF16)

        def AV(off, shape, dt):
            n = int(np.prod(shape))
            if dt == F32:
                v = arena[:, off // 2: off // 2 + 2 * n].bitcast(F32)
            else:
                v = arena[:, off // 2: off // 2 + n]
            if len(shape) == 2:
                v = v.rearrange("p (a b) -> p a b", a=shape[0])
            elif len(shape) == 3:
                v = v.rearrange("p (a b c) -> p a b c", a=shape[0], b=shape[1])
            elif len(shape) == 4:
                v = v.rearrange("p (a b c d) -> p a b c d", a=shape[0], b=shape[1], c=shape[2])
            return v

        K_ = 1024
        R_X, R_XT, R_Y, R_YM, R_MQ, R_T = 0, 64 * K_, 96 * K_, 120 * K_, 128 * K_, 136 * K_

        def small(name, shape, dt):
            return es.enter_context(nc.sbuf_tensor("sb_" + name, list(shape), dt))

        ident = small("ident", [128, 128], BF16)
        pm = small("pm", [128, 128], BF16)
        maskb = small("maskb", [128, 384], BF16)
        ones64 = small("ones64", [128, 64], BF16)
        memKT = small("memKT", [128, 2, 256], BF16)
        memV = small("memV", [128, 2, 256], BF16)
        cwb = small("cwb", [128, 18, 4], F32)
        fcwb = small("fcwb", [128, 44, 4], F32)
        stat = small("stat", [128, 32], F32)
        GB = small("GB", [128, 2, 1024], F32)
        r_consts = Res("consts")
        r_memKV = Res("memKV")
        r_cwb = Res("cwb")
        r_fcwb = Res("fcwb")
        r_GB = Res("GB")

        X = AV(R_X, [16, 1024], F32)
        XT = AV(R_XT, [8, 2048], BF16)
        YT = AV(R_Y, [6, 2048], BF16)
        YM = AV(R_YM, [2, 2048], BF16)
        MQ = AV(R_MQ, [2, 2048], BF16)
        r_X = [Res("X%d" % i) for i in range(16)]
        r_XT = Res("XT")
        r_YT = Res("YT")
        r_YM = Res("YM")
        r_MQ = Res("MQ")

        _bk = [0]

        def nxt(lst):
            b = lst[_bk[0] % len(lst)]
            _bk[0] += 1
            return b

        def mm_group(out, pairs, reads, writes):
            n = len(pairs)
            fns = []
            for i, (l, r) in enumerate(pairs):
                fns.append(lambda e, l=l, r=r, i=i: e.matmul(out, l, r, start=(i == 0), stop=(i == n - 1)))
            return S.op("pe", fns, reads, writes)

        def act_copy(out, in_, reads, writes):
            return S.op("act", [lambda e: e.activation(out=out, in_=in_, func=AF.Identity)], reads, writes)

        S.dma("sp", ident[:], ident_d[:, :], writes=[r_consts], key="c0")
        S.dma("sp", pm[:], pm_d[:, :], writes=[r_consts], key="c1")
        S.dma("sp", maskb[:], maskb_d[:, :], writes=[r_consts], key="c2")
        S.dma("sp", cwb[:], cwb_d[:, :, :], writes=[r_cwb], key="c3")
        S.op("dve", [lambda e: e.memset(ones64[:], 1.0)], writes=[r_consts])
        epst = small("epst", [128, 1], F32)
        S.op("dve", [lambda e: e.memset(epst[:], EPS)], writes=[r_consts])


        def transpose_to_XT(src_bf, tile, r_src):
            b = nxt([6, 7])
            bb = bank_bf(b)
            fns = [lambda e, kc=kc: e.transpose(bb[:, kc * 128:(kc + 1) * 128], src_bf[:, kc * 128:(kc + 1) * 128], ident[:])
                   for kc in range(8)]
            S.op("pe", fns, reads=[r_src, r_consts], writes=[PB[b]])
            S.op("act", [lambda e: e.activation(out=XT[:, :, tile * 128:(tile + 1) * 128],
                                                in_=bb.rearrange("p (k m) -> p k m", k=8), func=AF.Identity)],
                 reads=[PB[b]], writes=[r_XT])

        def load_ln(layer, which):
            g_d, b_d = ln_d[layer][2 * which], ln_d[layer][2 * which + 1]
            S.dma("sp", GB[:, 0, :], g_d.partition_broadcast(128), writes=[r_GB], key="gb0")
            S.dma("sp", GB[:, 1, :], b_d.partition_broadcast(128), writes=[r_GB], key="gb1")

        rbuf = [AV(R_T + 46 * K_ + i * 4 * K_, [1024], F32) for i in range(2)]
        xbuf = [AV(R_T + 54 * K_ + i * 2 * K_, [1024], BF16) for i in range(2)]
        r_rbuf = [Res("rbuf%d" % i) for i in range(2)]
        r_xbuf = [Res("xbuf%d" % i) for i in range(2)]
        r_stat = Res("stat")
        _ln = [0]

        def ln_epilogue(tile, r_in, r_r, final_out):
            i = _ln[0] % 2
            _ln[0] += 1
            so = (tile % 2) * 16
            st6 = stat[:, so:so + 12]
            mv = stat[:, so + 12:so + 14]
            rstd = stat[:, so + 14:so + 15]
            nmr = stat[:, so + 15:so + 16]
            S.op("dve", [lambda e: e.bn_stats(out=st6[:, 0:6], in_=r_in[:, 0:512])], reads=[r_r], writes=[r_stat])
            S.op("dve", [lambda e: e.bn_stats(out=st6[:, 6:12], in_=r_in[:, 512:1024])], reads=[r_r], writes=[r_stat])
            S.op("dve", [lambda e: e.bn_aggr(out=mv, in_=st6)], reads=[r_stat], writes=[r_stat])
            S.op("act", [lambda e: e.activation(out=rstd, in_=mv[:, 1:2], func=AF.Sqrt, bias=epst[:, 0:1])],
                 reads=[r_stat, r_consts], writes=[r_stat])
            S.op("dve", [lambda e: e.reciprocal(out=rstd, in_=rstd)], reads=[r_stat], writes=[r_stat])
            S.op("dve", [lambda e: e.scalar_tensor_tensor(out=nmr, in0=mv[:, 0:1], scalar=-1.0, in1=rstd,
                                                          op0=ALU.mult, op1=ALU.mult)], reads=[r_stat], writes=[r_stat])
            S.op("act", [lambda e: e.activation(out=r_in, in_=r_in, func=AF.Identity, scale=rstd, bias=nmr)],
                 reads=[r_r, r_stat], writes=[r_r])
            S.op("dve", [lambda e: e.tensor_tensor(out=r_in, in0=r_in, in1=GB[:, 0, :], op=ALU.mult)],
                 reads=[r_r, r_GB], writes=[r_r])
            S.op("pool", [lambda e: e.tensor_tensor(out=X[:, tile, :], in0=r_in, in1=GB[:, 1, :], op=ALU.add)],
                 reads=[r_r, r_GB], writes=[r_X[tile]])
            if final_out:
                S.dma("sp", out_d[tile * 128:(tile + 1) * 128, :], X[:, tile, :], reads=[r_X[tile]], key="out%d" % (tile % 4))
            else:
                xb = xbuf[i]
                S.op("act", [lambda e: e.activation(out=xb, in_=X[:, tile, :], func=AF.Identity)],
                     reads=[r_X[tile]], writes=[r_xbuf[i]])
                transpose_to_XT(xb, tile, r_xbuf[i])

        def mem_attention(toff):
            PT = [AV(toff + i * K_, [512], BF16) for i in range(4)]
            r_PT = [Res("mpt%d" % i) for i in range(4)]
            rden = AV(toff + 4 * K_, [512], F32)
            r_rden = Res("mrden")
            for hp in range(2):
                for qt in range(4):
                    qs = slice(qt * 512, (qt + 1) * 512)
                    for hh in range(2):
                        h = 2 * hp + hh
                        prow = slice(hh * 64, (hh + 1) * 64)
                        pts = []
                        for mt in range(2):
                            b = nxt([0, 1, 2, 3])
                            mm_group(bank(b), [(memKT[prow, hp, mt * 128:(mt + 1) * 128], MQ[prow, hp, qs])],
                                     reads=[r_memKV, r_MQ], writes=[PB[b]])
                            k = (hh * 2 + mt)
                            S.op("act", [lambda e, k=k, b=b: e.activation(out=PT[k], in_=bank(b), func=AF.Exp, scale=0.125)],
                                 reads=[PB[b]], writes=[r_PT[k]])
                            pts.append(k)
                        bo, bd = 4, 5
                        mm_group(bank(bo, 512, 0, 64), [(memV[:, mt, h * 64:(h + 1) * 64], PT[pts[mt]]) for mt in range(2)],
                                 reads=[r_memKV, r_PT[pts[0]], r_PT[pts[1]]], writes=[PB[bo]])
                        mm_group(bank(bd, 512, 0, 64), [(ones64[:], PT[pts[mt]]) for mt in range(2)],
                                 reads=[r_consts, r_PT[pts[0]], r_PT[pts[1]]], writes=[PB[bd]])
                        S.op("dve", [lambda e: e.reciprocal(out=rden[0:64, :], in_=bank(bd, 512, 0, 64))],
                             reads=[PB[bd]], writes=[r_rden])
                        S.op("dve", [lambda e, prow=prow, hp=hp, qs=qs: e.tensor_tensor(
                            out=YM[prow, hp, qs], in0=bank(bo, 512, 0, 64), in1=rden[0:64, :], op=ALU.mult)],
                             reads=[PB[bo], r_rden], writes=[r_YM])

        def out_proj_ln1(layer):
            wout = AV(R_T, [8, 1024], BF16)
            r_wout = Res("wout")
            xs = [AV(R_T + 16 * K_ + i * 4 * K_, [1024], F32) for i in range(2)]
            r_xs = [Res("xs%d" % i) for i in range(2)]
            S.dma("pool", wout, w_out_d[layer].rearrange("(kc p) n -> p kc n", p=128), writes=[r_wout], key="wout")
            load_ln(layer, 0)
            for tile in range(16):
                ts_ = slice(tile * 128, (tile + 1) * 128)
                bp = nxt([0, 2, 4]) if False else [0, 2, 4][tile % 3]
                for half in range(2):
                    b = bp + half
                    pairs = []
                    for kc in range(8):
                        lhs = YT[:, kc, ts_] if kc < 6 else YM[:, kc - 6, ts_]
                        pairs.append((lhs, wout[:, kc, half * 512:(half + 1) * 512]))
                    mm_group(bank(b), pairs, reads=[r_YT, r_YM, r_wout], writes=[PB[b]])
                i = tile % 2
                if layer == 0:
                    S.dma("sp", xs[i], x_d[ts_, :], writes=[r_xs[i]], key="xs%d" % i)
                    xin, rxin = xs[i], r_xs[i]
                else:
                    xin, rxin = X[:, tile, :], r_X[tile]
                rb = rbuf[i]
                S.op("dve", [lambda e, xin=xin, rb=rb, bp=bp: e.scalar_tensor_tensor(
                    out=rb, in0=xin, scalar=ALPHA, in1=ps[:, bp * 512:bp * 512 + 1024], op0=ALU.mult, op1=ALU.add)],
                     reads=[rxin, PB[bp], PB[bp + 1]], writes=[r_rbuf[i]])
                ln_epilogue(tile, rb, r_rbuf[i], False)

        def ffn_ln2(layer, final):
            blocks = [4, 4, 4, 4, 3, 3]
            S.dma("sp", fcwb[:], fcwb_d[layer][:, :, :], writes=[r_fcwb], key="fcwb")
            load_ln(layer, 1)
            GT = AV(R_Y, [4, 2048], BF16)
            r_GT = Res("GT")
            wdn = [AV(R_T + i * 8 * K_, [4, 1024], BF16) for i in range(2)]
            r_wdn = [Res("wdn%d" % i) for i in range(2)]
            wup = [AV(R_T + 16 * K_ + i * 4 * K_, [8, 2, 128], BF16) for i in range(3)]
            r_wup = [Res("wup%d" % i) for i in range(3)]
            hsb = [AV(R_T + 28 * K_ + i * 8208, [2052], F32) for i in range(2)]
            r_hsb = [Res("hsb%d" % i) for i in range(2)]
            for i in range(2):
                S.op("pool", [lambda e, i=i: e.memset(hsb[i][:, 0:1], 0.0)], writes=[r_hsb[i]])
                S.op("pool", [lambda e, i=i: e.memset(hsb[i][:, 2049:2050], 0.0)], writes=[r_hsb[i]])
            acc_a = AV(R_YM, [2048], F32)
            acc_g = AV(R_MQ, [2048], F32)
            r_acca, r_accg = Res("acca"), Res("accg")
            wd_v = w_dn_d[layer]
            wu_v = w_up_d[layer].rearrange("(kc p) n -> p kc n", p=128)
            j0 = 0
            for bi, nb in enumerate(blocks):
                wd = wdn[bi % 2]
                S.dma("pool", wd[:, 0:nb, :], wd_v[j0 * 128:(j0 + nb) * 128, :].rearrange("(j p) n -> p j n", p=128),
                      writes=[r_wdn[bi % 2]], key="wdn%d" % (bi % 2))
                for jj in range(nb):
                    j = j0 + jj
                    wi = j % 3
                    wu = wup[wi]
                    S.dma("pool", wu[:, :, 0, :], wu_v[:, :, j * 128:(j + 1) * 128], writes=[r_wup[wi]], key="wup%da" % wi)
                    S.dma("pool", wu[:, :, 1, :], wu_v[:, :, DFF + j * 128:DFF + (j + 1) * 128], writes=[r_wup[wi]], key="wup%db" % wi)
                    for ag in range(2):
                        hs_ = hsb[ag]
                        cj = ag * NFF + j
                        for tq in range(4):
                            b = nxt([0, 1, 2, 3])
                            mm_group(bank(b), [(wu[:, kc, ag, :], XT[:, kc, tq * 512:(tq + 1) * 512]) for kc in range(8)],
                                     reads=[r_wup[wi], r_XT], writes=[PB[b]])
                            S.op("act", [lambda e, hs_=hs_, tq=tq, b=b: e.activation(
                                out=hs_[:, 1 + tq * 512:1 + (tq + 1) * 512], in_=bank(b), func=AF.Identity)],
                                 reads=[PB[b]], writes=[r_hsb[ag]])
                        acc, r_acc = (acc_a, r_acca) if ag == 0 else (acc_g, r_accg)
                        S.op("act", [lambda e, acc=acc, hs_=hs_, cj=cj: e.activation(
                            out=acc, in_=hs_[:, 1:2049], func=AF.Identity, scale=fcwb[:, cj, 1:2], bias=fcwb[:, cj, 3:4])],
                             reads=[r_hsb[ag], r_fcwb], writes=[r_acc])
                        S.op("dve", [lambda e, acc=acc, hs_=hs_, cj=cj: e.scalar_tensor_tensor(
                            out=acc, in0=hs_[:, 0:2048], scalar=fcwb[:, cj, 0:1], in1=acc, op0=ALU.mult, op1=ALU.add)],
                             reads=[r_hsb[ag], r_fcwb, r_acc], writes=[r_acc])
                        S.op("dve", [lambda e, acc=acc, hs_=hs_, cj=cj: e.scalar_tensor_tensor(
                            out=acc, in0=hs_[:, 2:2050], scalar=fcwb[:, cj, 2:3], in1=acc, op0=ALU.mult, op1=ALU.add)],
                             reads=[r_hsb[ag], r_fcwb, r_acc], writes=[r_acc])
                    S.op("act", [lambda e: e.activation(out=acc_g, in_=acc_g, func=AF.Silu)], reads=[r_accg], writes=[r_accg])
                    S.op("dve", [lambda e, jj=jj: e.tensor_tensor(out=GT[:, jj, :], in0=acc_g, in1=acc_a, op=ALU.mult)],
                         reads=[r_accg, r_acca], writes=[r_GT])
                last = bi == len(blocks) - 1
                for tile in range(16):
                    ts_ = slice(tile * 128, (tile + 1) * 128)
                    bp = [4, 6][tile % 2]
                    for half in range(2):
                        mm_group(bank(bp + half), [(GT[:, jj, ts_], wd[:, jj, half * 512:(half + 1) * 512]) for jj in range(nb)],
                                 reads=[r_GT, r_wdn[bi % 2]], writes=[PB[bp + half]])
                    pin = ps[:, bp * 512:bp * 512 + 1024]
                    if not last:
                        if bi == 0:
                            S.op("dve", [lambda e, tile=tile, pin=pin: e.scalar_tensor_tensor(
                                out=X[:, tile, :], in0=X[:, tile, :], scalar=ALPHA, in1=pin, op0=ALU.mult, op1=ALU.add)],
                                 reads=[PB[bp], PB[bp + 1]], writes=[r_X[tile]])
                        else:
                            S.op("dve", [lambda e, tile=tile, pin=pin: e.tensor_tensor(
                                out=X[:, tile, :], in0=X[:, tile, :], in1=pin, op=ALU.add)],
                                 reads=[PB[bp], PB[bp + 1]], writes=[r_X[tile]])
                    else:
                        i = tile % 2
                        rb = rbuf[i]
                        S.op("dve", [lambda e, tile=tile, pin=pin, rb=rb: e.tensor_tensor(
                            out=rb, in0=X[:, tile, :], in1=pin, op=ALU.add)],
                             reads=[PB[bp], PB[bp + 1], r_X[tile]], writes=[r_rbuf[i]])
                        ln_epilogue(tile, rb, r_rbuf[i], final)
                j0 += nb

        memf = AV(R_X, [2, 1024], F32)
        memb = AV(R_X + 8 * K_, [2, 1024], BF16)
        memT = AV(R_X + 12 * K_, [8, 256], BF16)
        wkv = AV(R_X + 16 * K_, [8, 512], BF16)
        r_memf, r_memb, r_memT, r_wkv = Res("memf"), Res("memb"), Res("memT"), Res("wkv")
        S.dma("sp", memf, mem_d.rearrange("(mt p) d -> p mt d", p=128), writes=[r_memf], key="memf")
        S.dma("pool", wkv, wkv_d.rearrange("(kc p) n -> p kc n", p=128), writes=[r_wkv], key="wkv")
        act_copy(memb, memf, [r_memf], [r_memb])
        for mt in range(2):
            b = nxt([6, 7])
            bb = bank_bf(b)
            fns = [lambda e, kc=kc, mt=mt, bb=bb: e.transpose(bb[:, kc * 128:(kc + 1) * 128], memb[:, mt, kc * 128:(kc + 1) * 128], ident[:])
                   for kc in range(8)]
            S.op("pe", fns, reads=[r_memb, r_consts], writes=[PB[b]])
            S.op("act", [lambda e, mt=mt, bb=bb: e.activation(out=memT[:, :, mt * 128:(mt + 1) * 128],
                                                             in_=bb.rearrange("p (k m) -> p k m", k=8), func=AF.Identity)],
                 reads=[PB[b]], writes=[r_memT])
        for hp in range(2):
            b = nxt([0, 1])
            mm_group(bank(b, 256), [(wkv[:, kc, hp * 128:(hp + 1) * 128], memT[:, kc, :]) for kc in range(8)],
                     reads=[r_wkv, r_memT], writes=[PB[b]])
            act_copy(memKT[:, hp, :], bank(b, 256), [PB[b]], [r_memKV])
        for mt in range(2):
            b = nxt([0, 1])
            mm_group(bank(b, 256), [(memT[:, kc, mt * 128:(mt + 1) * 128], wkv[:, kc, 256:512]) for kc in range(8)],
                     reads=[r_wkv, r_memT], writes=[PB[b]])
            act_copy(memV[:, mt, :], bank(b, 256), [PB[b]], [r_memKV])
        S.barrier()

        if debug == "s_m":
            S.barrier()
            S.emit()
            return nc
        xs0 = [AV(R_T + 16 * K_ + i * 4 * K_, [1024], F32) for i in range(2)]
        r_xs0 = [Res("xs0_%d" % i) for i in range(2)]
        for tile in range(16):
            i = tile % 2
            S.dma("sp", xs0[i], x_d[tile * 128:(tile + 1) * 128, :], writes=[r_xs0[i]], key="xs%d" % i)
            S.op("act", [lambda e, i=i: e.activation(out=xbuf[i], in_=xs0[i], func=AF.Identity)],
                 reads=[r_xs0[i]], writes=[r_xbuf[i]])
            transpose_to_XT(xbuf[i], tile, r_xbuf[i])

        if debug == "s_x":
            S.barrier()
            S.emit()
            return nc
        HS = AV(R_X, [2, 16, 768], BF16)
        r_HS = Res("HS")
        hA = AV(R_X + 48 * K_, [2048], F32)
        hB = AV(R_X + 56 * K_, [2048], F32)
        r_hA, r_hB = Res("hA"), Res("hB")
        zf = AV(R_Y, [2048], F32)
        fw = AV(R_Y + 8 * K_, [3, 64], F32)
        fwo = AV(R_Y + 9 * K_, [1536], F32)
        fbf = AV(R_Y + 15 * K_, [8], F32)
        dbc = AV(R_Y + 16 * K_, [768], F32)
        dlt = AV(R_YM, [768], F32)
        dec = AV(R_YM + 3 * K_, [768], F32)
        ntn = AV(R_YM + 6 * K_, [16], F32)
        fsb = AV(R_MQ, [768], F32)
        wtmp = AV(R_MQ + 3 * K_, [512], F32)
        wtm2 = AV(R_MQ + 5 * K_, [512], F32)
        r_f = Res("filt_in")
        r_dec, r_fsb, r_wtmp, r_wtm2, r_dbc = Res("dec"), Res("fsb"), Res("wtmp"), Res("wtm2"), Res("dbc")
        S.dma("sp", zf[0:33, :], zfT_d[:, :], writes=[r_f], key="f0")
        S.dma("sp", fw[0:33, 0, :], fw1_d[:, :], writes=[r_f], key="f1")
        S.dma("sp", fw[0:64, 1, :], fw2_d[:, :], writes=[r_f], key="f2")
        S.dma("sp", fw[0:64, 2, :], fw3_d[:, :], writes=[r_f], key="f3")
        S.dma("sp", fwo[0:64, :], fwo_d[:, :], writes=[r_f], key="f4")
        S.dma("sp", fbf[0:64, 0:6], fbf_d[:, :], writes=[r_f], key="f5")
        S.dma("sp", dbc, hd_d.partition_broadcast(128), writes=[r_dbc], key="f6")
        S.dma("sp", dlt, deltas_d.partition_broadcast(128), writes=[r_f], key="f7")
        S.dma("sp", ntn, ntn_d[:, :], writes=[r_f], key="f8")
        fbs = stat[0:64, 20:23]
        for l in range(3):
            S.op("dve", [lambda e, l=l: e.tensor_tensor(out=fbs[:, l:l + 1], in0=fbf[0:64, 2 * l:2 * l + 1],
                                                        in1=fbf[0:64, 2 * l + 1:2 * l + 2], op=ALU.mult)],
                 reads=[r_f], writes=[r_stat])
        srcs = [(zf, 33, r_f), (hA, 64, r_hA), (hB, 64, r_hB)]
        dsts = [(hA, r_hA), (hB, r_hB), (hA, r_hA)]
        for l in range(3):
            src, kk, r_src = srcs[l]
            dst, r_dst = dsts[l]
            for tq in range(4):
                b = nxt([0, 1, 2, 3])
                cs = slice(tq * 512, (tq + 1) * 512)
                mm_group(bank(b, 512, 0, 64), [(fw[0:kk, l, :], src[0:kk, cs])], reads=[r_f, r_src], writes=[PB[b]])
                S.op("dve", [lambda e, b=b, l=l: e.tensor_scalar(out=wtmp[0:64, :], in0=bank(b, 512, 0, 64),
                                                                scalar1=fbf[0:64, 2 * l + 1:2 * l + 2], scalar2=fbs[:, l:l + 1],
                                                                op0=ALU.mult, op1=ALU.add)],
                     reads=[PB[b], r_f, r_stat], writes=[r_wtmp])
                S.op("dve", [lambda e: e.tensor_scalar(out=wtm2[0:64, :], in0=wtmp[0:64, :], scalar1=-PI, scalar2=2 * PI,
                                                       op0=ALU.is_lt, op1=ALU.mult)], reads=[r_wtmp], writes=[r_wtm2])
                S.op("dve", [lambda e: e.tensor_tensor(out=wtmp[0:64, :], in0=wtmp[0:64, :], in1=wtm2[0:64, :], op=ALU.add)],
                     reads=[r_wtmp, r_wtm2], writes=[r_wtmp])
                S.op("dve", [lambda e: e.tensor_scalar(out=wtm2[0:64, :], in0=wtmp[0:64, :], scalar1=PI, scalar2=-2 * PI,
                                                       op0=ALU.is_gt, op1=ALU.mult)], reads=[r_wtmp], writes=[r_wtm2])
                S.op("dve", [lambda e: e.tensor_tensor(out=wtmp[0:64, :], in0=wtmp[0:64, :], in1=wtm2[0:64, :], op=ALU.add)],
                     reads=[r_wtmp, r_wtm2], writes=[r_wtmp])
                S.op("act", [lambda e, dst=dst, cs=cs: e.activation(out=dst[0:64, cs], in_=wtmp[0:64, :], func=AF.Sin)],
                     reads=[r_wtmp], writes=[r_dst])
        for tile in range(16):
            ts_ = slice(tile * 128, (tile + 1) * 128)
            for q3 in range(3):
                mm_group(bank(q3), [(hA[0:64, ts_], fwo[0:64, q3 * 512:(q3 + 1) * 512])], reads=[r_hA, r_f], writes=[PB[q3]])
            S.op("act", [lambda e, tile=tile: e.activation(out=dec, in_=dlt, func=AF.Exp, scale=ntn[:, tile:tile + 1])],
                 reads=[r_f], writes=[r_dec])
            act_copy(fsb, ps[:, 0:768], [PB[0], PB[1]], [r_fsb])
            S.op("dve", [lambda e: e.tensor_tensor(out=hB[:, 0:768], in0=fsb, in1=ps[:, 768:1536], op=ALU.add)],
                 reads=[r_fsb, PB[1], PB[2]], writes=[r_hB])
            S.op("dve", [lambda e: e.tensor_tensor(out=hB[:, 768:1536], in0=fsb, in1=ps[:, 768:1536], op=ALU.subtract)],
                 reads=[r_fsb, PB[1], PB[2]], writes=[r_hB])
            S.op("dve", [lambda e, tile=tile: e.tensor_tensor(out=HS[:, 0, tile, :], in0=hB[:, 0:768], in1=dec, op=ALU.mult)],
                 reads=[r_hB, r_dec], writes=[r_HS])
            S.op("dve", [lambda e, tile=tile: e.tensor_tensor(out=HS[:, 1, tile, :], in0=hB[:, 768:1536], in1=dec, op=ALU.mult)],
                 reads=[r_hB, r_dec], writes=[r_HS])
        S.barrier()

        if debug == "s_f":
            S.barrier()
            S.emit()
            return nc
        KS = AV(R_T, [2, 16, 768], BF16)
        r_KS = Res("KS")

        def fwd_pass(rhs_re, rhs_im, r_rhs, ftab, r_ftab, epilogue):
            for fc in range(16):
                si = fc % 2
                ft = ftab[si]
                S.dma("sp", ft.rearrange("p a b c -> p (a b c)"), fwd_tab_d[fc], writes=[r_ftab[si]], key="ftab%d" % si)
                bs = [0, 1, 2, 3] if fc % 2 == 0 else [4, 5, 6, 7]
                for ri in range(2):
                    rhs = rhs_re if ri == 0 else rhs_im
                    o0 = bs[0] * 512 + ri * 1024
                    fns = []
                    for tc in range(16):
                        lhs = ft[:, ri, tc, :]
                        fns.append(lambda e, lhs=lhs, tc=tc, rhs=rhs, o0=o0: e.matmul(
                            ps[:, o0:o0 + 512], lhs, rhs[:, tc, 0:512], start=(tc == 0), stop=(tc == 15)))
                        fns.append(lambda e, lhs=lhs, tc=tc, rhs=rhs, o0=o0: e.matmul(
                            ps[:, o0 + 512:o0 + 768], lhs, rhs[:, tc, 512:768], start=(tc == 0), stop=(tc == 15)))
                    S.op("pe", fns, reads=[r_ftab[si], r_rhs], writes=[PB[bs[2 * ri]], PB[bs[2 * ri + 1]]])
                pre = ps[:, bs[0] * 512:bs[0] * 512 + 768]
                pim = ps[:, bs[2] * 512:bs[2] * 512 + 768]
                epilogue(fc, pre, pim, [PB[b] for b in bs])

        ftab = [AV(R_YM, [2, 16, 128], BF16), AV(R_MQ, [2, 16, 128], BF16)]
        r_ftab = [Res("ftab0"), Res("ftab1")]

        def k_epilogue(fc, pre, pim, rbs):
            S.op("dve", [lambda e: e.tensor_tensor(out=KS[:, 0, fc, :], in0=pre, in1=dbc, op=ALU.add)],
                 reads=rbs[0:2] + [r_dbc], writes=[r_KS])
            act_copy(KS[:, 1, fc, :], pim, rbs[2:4], [r_KS])

        fwd_pass(HS[:, 0], HS[:, 1], r_HS, ftab, r_ftab, k_epilogue)
        S.barrier()

        if debug == "s_k":
            S.barrier()
            S.emit()
            return nc
        Z = AV(R_X, [16, 768], BF16)
        r_Z = Res("Z")
        hsb0 = [AV(R_X + 24 * K_ + i * 8208, [2052], F32) for i in range(2)]
        r_hsb0 = [Res("hsb0_%d" % i) for i in range(2)]
        accx = AV(R_X + 24 * K_ + 16416, [2048], F32)
        accv = AV(R_X + 24 * K_ + 16416 + 8192, [2048], F32)
        zT = AV(R_X + 24 * K_ + 16416 + 16384, [2048], BF16)
        r_accx, r_accv, r_zT = Res("accx"), Res("accv"), Res("zT")
        wch = [AV(R_T + 48 * K_ + i * 2 * K_, [8, 128], BF16) for i in range(3)]
        r_wch = [Res("wch%d" % i) for i in range(3)]
        for i in range(2):
            S.op("pool", [lambda e, i=i: e.memset(hsb0[i][:, 0:1], 0.0)], writes=[r_hsb0[i]])
            S.op("pool", [lambda e, i=i: e.memset(hsb0[i][:, 2049:2050], 0.0)], writes=[r_hsb0[i]])
        w0v = w_in_d[0].rearrange("(kc p) n -> p kc n", p=128)
        order = [18, 19] + [0, 1, 2, 3, 4, 5]
        for i in range(6):
            order += [6 + i, 12 + i]
        _wc = [0]

        def proj_chunk(wv, col0, sink_fn, extra_reads=()):
            wi = _wc[0] % 3
            _wc[0] += 1
            S.dma("pool", wch[wi], wv[:, :, col0:col0 + 128], writes=[r_wch[wi]], key="wch%d" % wi)
            for tq in range(4):
                b = nxt([0, 1, 2, 3])
                mm_group(bank(b), [(wch[wi][:, kc, :], XT[:, kc, tq * 512:(tq + 1) * 512]) for kc in range(8)],
                         reads=[r_wch[wi], r_XT], writes=[PB[b]])
                sink_fn(tq, b)

        def conv_chunk(c, hs_, r_hs, acc_out, r_acc_list, out_final):
            S.op("act", [lambda e: e.activation(out=acc_out, in_=hs_[:, 1:2049], func=AF.Identity,
                                                scale=cwb[:, c, 1:2], bias=cwb[:, c, 3:4])],
                 reads=[r_hs, r_cwb], writes=r_acc_list)
            S.op("dve", [lambda e: e.scalar_tensor_tensor(out=acc_out, in0=hs_[:, 0:2048], scalar=cwb[:, c, 0:1],
                                                          in1=acc_out, op0=ALU.mult, op1=ALU.add)],
                 reads=[r_hs, r_cwb] + r_acc_list, writes=r_acc_list)
            S.op("dve", [lambda e: e.scalar_tensor_tensor(out=out_final[0], in0=hs_[:, 2:2050], scalar=cwb[:, c, 2:3],
                                                          in1=acc_out, op0=ALU.mult, op1=ALU.add)],
                 reads=[r_hs, r_cwb] + r_acc_list, writes=out_final[1])

        for ci_, c in enumerate(order):
            if debug and debug.startswith('s_p') and ci_ == int(debug[3:]):
                S.barrier()
                S.emit()
                return nc
            if c >= 18:
                hp = c - 18

                def sink_mq(tq, b, hp=hp):
                    act_copy(MQ[:, hp, tq * 512:(tq + 1) * 512], bank(b), [PB[b]], [r_MQ])
                proj_chunk(w0v, c * 128, sink_mq)
                continue
            si = c % 2
            hs_ = hsb0[si]

            def sink_h(tq, b, hs_=hs_, si=si):
                act_copy(hs_[:, 1 + tq * 512:1 + (tq + 1) * 512], bank(b), [PB[b]], [r_hsb0[si]])
            proj_chunk(w0v, c * 128, sink_h)
            if c < 6:
                conv_chunk(c, hs_, r_hsb0[si], accv, [r_accv], (YT[:, c, :], [r_YT]))
            elif c < 12:
                conv_chunk(c, hs_, r_hsb0[si], accx, [r_accx], (accx, [r_accx]))
            else:
                i6 = c - 12
                conv_chunk(c, hs_, r_hsb0[si], accv, [r_accv], (accv, [r_accv]))
                S.op("dve", [lambda e: e.tensor_tensor(out=zT, in0=accv, in1=accx, op=ALU.mult)],
                     reads=[r_accv, r_accx], writes=[r_zT])
                for g8 in range(2):
                    b = nxt([6, 7])
                    bb = bank_bf(b)
                    fns = [lambda e, t8=t8, bb=bb, g8=g8: e.transpose(bb[:, t8 * 128:(t8 + 1) * 128],
                                                                     zT[:, (g8 * 8 + t8) * 128:(g8 * 8 + t8 + 1) * 128], ident[:])
                           for t8 in range(8)]
                    S.op("pe", fns, reads=[r_zT, r_consts], writes=[PB[b]])
                    S.op("act", [lambda e, g8=g8, bb=bb, i6=i6: e.activation(
                        out=Z[:, g8 * 8:(g8 + 1) * 8, i6 * 128:(i6 + 1) * 128],
                        in_=bb.rearrange("p (k m) -> p k m", k=8), func=AF.Identity)],
                         reads=[PB[b]], writes=[r_Z])
        S.barrier()
        if debug == "z":
            for tile in range(16):
                S.op("act", [lambda e, tile=tile: e.activation(out=X[:, tile, 0:768] if False else hsb0[0][:, 0:768], in_=Z[:, tile, :], func=AF.Identity)],
                     reads=[r_Z], writes=[r_hsb0[0]])
                S.dma("sp", dbg_d[tile * 128:(tile + 1) * 128, 0:768], hsb0[0][:, 0:768], reads=[r_hsb0[0]], key="dbg")
            S.barrier()
            S.emit()
            return nc
        mem_attention(R_X + 24 * K_)
        S.barrier()

        YRE = AV(R_XT, [16, 768], BF16)
        YIM = AV(R_X + 24 * K_, [16, 768], BF16)
        r_Y = Res("Yspec")
        ct = [AV(R_X + 48 * K_ + i * 3 * K_, [768], F32) for i in range(4)]
        r_ct = [Res("ct%d" % i) for i in range(4)]
        ftabU = [AV(R_T + 48 * K_, [2, 16, 128], BF16), AV(R_MQ, [2, 16, 128], BF16)]
        r_ftabU = [Res("ftabU0"), Res("ftabU1")]

        def u_epilogue(fc, pre, pim, rbs):
            kre, kim = KS[:, 0, fc, :], KS[:, 1, fc, :]
            S.op("dve", [lambda e: e.tensor_tensor(out=ct[0], in0=pre, in1=kre, op=ALU.mult)], reads=rbs[0:2] + [r_KS], writes=[r_ct[0]])
            S.op("dve", [lambda e: e.tensor_tensor(out=ct[1], in0=pim, in1=kim, op=ALU.mult)], reads=rbs[2:4] + [r_KS], writes=[r_ct[1]])
            S.op("dve", [lambda e: e.tensor_tensor(out=ct[2], in0=pre, in1=kim, op=ALU.mult)], reads=rbs[0:2] + [r_KS], writes=[r_ct[2]])
            S.op("dve", [lambda e: e.tensor_tensor(out=ct[3], in0=pim, in1=kre, op=ALU.mult)], reads=rbs[2:4] + [r_KS], writes=[r_ct[3]])
            S.op("pool", [lambda e: e.tensor_tensor(out=YRE[:, fc, :], in0=ct[0], in1=ct[1], op=ALU.subtract)],
                 reads=[r_ct[0], r_ct[1]], writes=[r_Y])
            S.op("pool", [lambda e: e.tensor_tensor(out=YIM[:, fc, :], in0=ct[2], in1=ct[3], op=ALU.add)],
                 reads=[r_ct[2], r_ct[3]], writes=[r_Y])

        fwd_pass(Z, Z, r_Z, ftabU, r_ftabU, u_epilogue)
        S.barrier()

        itab = [AV(R_T + 48 * K_, [4, 2, 512], BF16), AV(R_MQ, [4, 2, 512], BF16)]
        r_itab = [Res("itab0"), Res("itab1")]
        _it = 0
        for tt in range(4):
            fn_all = []
            for fg in range(4):
                si = _it % 2
                _it += 1
                S.dma("sp", itab[si].rearrange("p a b c -> p (a b c)"), inv_tab_d[tt, fg], writes=[r_itab[si]], key="itab%d" % si)
                fns = []
                for fi in range(4):
                    fc = fg * 4 + fi
                    for ri in range(2):
                        Ysrc = YRE if ri == 0 else YIM
                        for cc in range(6):
                            first = (fc == 0 and ri == 0)
                            lastm = (fc == 15 and ri == 1)
                            fns.append(lambda e, cc=cc, Ysrc=Ysrc, fc=fc, si=si, fi=fi, ri=ri, first=first, lastm=lastm: e.matmul(
                                bank(cc), Ysrc[:, fc, cc * 128:(cc + 1) * 128], itab[si][:, fi, ri, :], start=first, stop=lastm))
                S.op("pe", fns, reads=[r_itab[si], r_Y], writes=[PB[cc] for cc in range(6)])
            for cc in range(6):
                S.op("dve", [lambda e, cc=cc, tt=tt: e.tensor_tensor(out=YT[:, cc, tt * 512:(tt + 1) * 512], in0=bank(cc),
                                                                    in1=YT[:, cc, tt * 512:(tt + 1) * 512], op=ALU.mult)],
                     reads=[PB[cc], r_YT], writes=[r_YT])
        S.barrier()

        if debug == "mix0":
            for c in range(8):
                src = YT[:, c, 0:1024] if c < 6 else YM[:, c - 6, 0:1024]
                S.op("act", [lambda e, src=src: e.activation(out=rbuf[0], in_=src, func=AF.Identity)], reads=[r_YT, r_YM], writes=[r_rbuf[0]])
                S.dma("sp", dbg_d[c * 128:(c + 1) * 128, :], rbuf[0], reads=[r_rbuf[0]], key="dbg")
            S.barrier()
            S.emit()
            return nc

        out_proj_ln1(0)
        S.barrier()
        if debug == "ln1_0":
            for tile in range(16):
                S.dma("sp", dbg_d[tile * 128:(tile + 1) * 128, :], X[:, tile, :], reads=[r_X[tile]], key="dbg")
            S.barrier()
            S.emit()
            return nc
        ffn_ln2(0, final=(debug == "l0"))
        S.barrier()

        if debug is None or debug.startswith("l1"):
            if debug == "l1_s":
                S.barrier()
                S.emit()
                return nc
            w1v = w_in_d[1].rearrange("(kc p) n -> p kc n", p=128)
            QT = YT
            r_QT = [Res("QT%d" % i) for i in range(12)]
            KT = AV(R_T, [4, 2048], BF16)
            r_KT = Res("KT")
            VT = AV(R_T + 16 * K_, [16, 256], BF16)
            r_VT = Res("VT")
            ropec = AV(R_T + 24 * K_, [2048], F32)
            ropes = AV(R_T + 32 * K_, [2048], F32)
            r_rope = Res("rope")
            wch1 = [AV(R_T + 40 * K_ + i * 2 * K_, [8, 128], BF16) for i in range(3)]
            r_wch1 = [Res("wch1_%d" % i) for i in range(3)]
            qsb = [AV(R_T + 46 * K_ + i * K_, [512], BF16) for i in range(2)]
            r_qsb = [Res("qsb%d" % i) for i in range(2)]
            rt1 = AV(R_T + 48 * K_, [512], F32)
            rt2 = AV(R_T + 50 * K_, [512], F32)
            r_rt1, r_rt2 = Res("rt1"), Res("rt2")
            wvt = AV(R_T + 52 * K_, [8, 256], BF16)
            r_wvt = Res("wvt")
            esk = small("esk", [64, 12], F32)
            r_esk = Res("esk")
            import os
            SK = os.environ.get("SKIP", "")
            if "r" not in SK:
                S.dma("sp", ropec, ropec_d[:, :], writes=[r_rope], key="rope0")
                S.dma("sp", ropes, ropes_d[:, :], writes=[r_rope], key="rope1")
            if "e" not in SK:
                S.dma("sp", esk[:], sink_d.partition_broadcast(64), writes=[r_esk], key="esk")
            if "x" not in SK:
                S.op("act", [lambda e: e.activation(out=esk[:], in_=esk[:], func=(AF.Identity if "I" in SK else AF.Exp))], reads=[r_esk], writes=[r_esk])
            if "w" not in SK:
                S.dma("pool", wvt, w1v[:, :, 1024:1280], writes=[r_wvt], key="wvt")
            if debug == "l1_p0":
                S.barrier()
                S.emit()
                return nc
            _w1 = [0]
            _rq = [0]

            def proj1(loads, sink_fn):
                wi = _w1[0] % 3
                _w1[0] += 1
                for k_, (dst0, dst1, c0, c1) in enumerate(loads):
                    S.dma("pool", wch1[wi][:, :, dst0:dst1], w1v[:, :, c0:c1], writes=[r_wch1[wi]], key="wch1_%d_%d" % (wi, k_))
                for tq in range(4):
                    b = nxt([0, 1, 2, 3])
                    mm_group(bank(b), [(wch1[wi][:, kc, :], XT[:, kc, tq * 512:(tq + 1) * 512]) for kc in range(8)],
                             reads=[r_wch1[wi], r_XT], writes=[PB[b]])
                    sink_fn(tq, b)

            def rope_sink(dest, r_dest):
                def sink(tq, b):
                    cs = slice(tq * 512, (tq + 1) * 512)
                    i = _rq[0] % 2
                    _rq[0] += 1
                    S.op("act", [lambda e: e.activation(out=qsb[i], in_=bank(b), func=AF.Identity)],
                         reads=[PB[b]], writes=[r_qsb[i]])
                    b2 = [4, 5][i]
                    mm_group(bank(b2), [(pm[:], qsb[i])], reads=[r_consts, r_qsb[i]], writes=[PB[b2]])
                    S.op("dve", [lambda e: e.tensor_tensor(out=rt1, in0=bank(b), in1=ropec[:, cs], op=ALU.mult)],
                         reads=[PB[b], r_rope], writes=[r_rt1])
                    S.op("dve", [lambda e: e.tensor_tensor(out=rt2, in0=bank(b2), in1=ropes[:, cs], op=ALU.mult)],
                         reads=[PB[b2], r_rope], writes=[r_rt2])
                    S.op("dve", [lambda e: e.tensor_tensor(out=dest[:, cs], in0=rt1, in1=rt2, op=ALU.add)],
                         reads=[r_rt1, r_rt2], writes=r_dest)
                return sink

            for c in range(6):
                proj1([(0, 128, c * 128, (c + 1) * 128)], rope_sink(QT[:, c, :], [r_QT[2 * c], r_QT[2 * c + 1]]))
            if debug == "l1_p1":
                S.barrier()
                S.emit()
                return nc
            for g in range(4):
                c0 = 768 + g * 64
                proj1([(0, 64, c0, c0 + 64), (64, 128, c0, c0 + 64)], rope_sink(KT[:, g, :], [r_KT]))
            if debug == "l1_p2":
                S.barrier()
                S.emit()
                return nc
            for hp in range(2):
                def sink_mq1(tq, b, hp=hp):
                    act_copy(MQ[:, hp, tq * 512:(tq + 1) * 512], bank(b), [PB[b]], [r_MQ])
                proj1([(0, 128, 1280 + hp * 128, 1280 + (hp + 1) * 128)], sink_mq1)
            for tile in range(16):
                b = nxt([0, 1, 2, 3])
                mm_group(bank(b, 256), [(XT[:, kc, tile * 128:(tile + 1) * 128], wvt[:, kc, :]) for kc in range(8)],
                         reads=[r_XT, r_wvt], writes=[PB[b]])
                act_copy(VT[:, tile, :], bank(b, 256), [PB[b]], [r_VT])
            S.barrier()

            if debug == "l1_p":
                S.barrier()
                S.emit()
                return nc
            PTs = [AV(R_XT + i * 12 * K_, [16, 384], BF16) for i in range(2)]
            r_PTs = [Res("PTs%d" % i) for i in range(2)]
            rden1 = AV(R_XT + 24 * K_, [512], F32)
            r_rden1 = Res("rden1")
            for h in range(12):
                g = h // 3
                hh = h % 2
                c = h // 2
                prow = slice(hh * 64, (hh + 1) * 64)
                pt = PTs[h % 2]
                r_pt = r_PTs[h % 2]
                for j in range(16):
                    qlo = max(0, j - 1) * 128
                    qhi = min(16, j + 2) * 128
                    n = qhi - qlo
                    moff = qlo - (j - 1) * 128
                    b = nxt([0, 1, 2, 3])
                    fns = [
                        lambda e, b=b, n=n, j=j, qlo=qlo, qhi=qhi, prow=prow, g=g, c=c: e.matmul(
                            bank(b, n), KT[prow, g, j * 128:(j + 1) * 128], QT[prow, c, qlo:qhi], start=True, stop=False),
                        lambda e, b=b, n=n, moff=moff: e.matmul(
                            bank(b, n), ident[:], maskb[:, moff:moff + n], start=False, stop=True),
                    ]
                    S.op("pe", fns, reads=[r_KT, r_QT[h], r_consts], writes=[PB[b]])
                    S.op("act", [lambda e, b=b, n=n, j=j, pt=pt: e.activation(out=pt[:, j, 0:n], in_=bank(b, n), func=AF.Exp, scale=0.125)],
                         reads=[PB[b]], writes=[r_pt])
                for qt in range(4):
                    bo, bd = [4, 6][qt % 2], [5, 7][qt % 2]
                    fo, fd = [], []
                    for i4 in range(4):
                        qb = 4 * qt + i4
                        js = [j for j in (qb - 1, qb, qb + 1) if 0 <= j < 16]
                        for k_, j in enumerate(js):
                            lc = (qb - max(0, j - 1)) * 128
                            st_, sp_ = (k_ == 0), (k_ == len(js) - 1)
                            fo.append(lambda e, i4=i4, j=j, lc=lc, st_=st_, sp_=sp_, bo=bo, pt=pt, g=g: e.matmul(
                                ps[0:64, bo * 512 + i4 * 128:bo * 512 + (i4 + 1) * 128], VT[:, j, g * 64:(g + 1) * 64],
                                pt[:, j, lc:lc + 128], start=st_, stop=sp_))
                            fd.append(lambda e, i4=i4, j=j, lc=lc, st_=st_, sp_=sp_, bd=bd, pt=pt: e.matmul(
                                ps[0:64, bd * 512 + i4 * 128:bd * 512 + (i4 + 1) * 128], ones64[:],
                                pt[:, j, lc:lc + 128], start=st_, stop=sp_))
                    S.op("pe", fo, reads=[r_pt, r_VT], writes=[PB[bo]])
                    S.op("pe", fd, reads=[r_pt, r_consts], writes=[PB[bd]])
                    S.op("dve", [lambda e, bd=bd, h=h: e.tensor_scalar(out=rden1[0:64, :], in0=bank(bd, 512, 0, 64),
                                                                     scalar1=esk[:, h:h + 1], scalar2=None, op0=ALU.add)],
                         reads=[PB[bd], r_esk], writes=[r_rden1])
                    S.op("dve", [lambda e: e.reciprocal(out=rden1[0:64, :], in_=rden1[0:64, :])], reads=[r_rden1], writes=[r_rden1])
                    S.op("dve", [lambda e, bo=bo, qt=qt, prow=prow, c=c: e.tensor_tensor(
                        out=QT[prow, c, qt * 512:(qt + 1) * 512], in0=bank(bo, 512, 0, 64), in1=rden1[0:64, :], op=ALU.mult)],
                         reads=[PB[bo], r_rden1], writes=[r_QT[h]])
            S.barrier()
            if debug == "l1_a":
                S.barrier()
                S.emit()
                return nc
            mem_attention(R_XT + 26 * K_ - 2 * K_)
            S.barrier()
            if debug == "l1mix":
                for c in range(8):
                    src = YT[:, c, 0:1024] if c < 6 else YM[:, c - 6, 0:1024]
                    S.op("act", [lambda e, src=src: e.activation(out=rbuf[0], in_=src, func=AF.Identity)], reads=[r_YM], writes=[r_rbuf[0]])
                    S.dma("sp", dbg_d[c * 128:(c + 1) * 128, :], rbuf[0], reads=[r_rbuf[0]], key="dbg")
                S.barrier()
                S.emit()
                return nc
            out_proj_ln1(1)
            S.barrier()
            ffn_ln2(1, final=True)
            S.barrier()

        S.barrier()
        S.emit()
    return nc


def prep_shared(inputs):
    f32 = np.float32
    sh = {}
    for k in ("w_mem_kv", "l0_w_in", "l1_w_in", "l0_w_out", "l1_w_out", "l0_ffn_w_up", "l1_ffn_w_up",
              "l0_ffn_w_down", "l1_ffn_w_down", "l0_filt_w1", "l0_filt_w2", "l0_filt_w3", "l0_filt_w_out",
              "l0_hyena_d", "l1_sink"):
        sh[k] = np.ascontiguousarray(np.asarray(inputs[k], dtype=f32))
    for i in range(2):
        for n in ("ln1_g", "ln1_b", "ln2_g", "ln2_b"):
            sh["l%d_%s" % (i, n)] = np.ascontiguousarray(np.asarray(inputs["l%d_%s" % (i, n)], dtype=f32))
        cw = np.asarray(inputs["l%d_ffn_conv_w" % i], f32)
        cb = np.asarray(inputs["l%d_ffn_conv_b" % i], f32)
        a = np.concatenate([cw, cb[None, :]], axis=0)
        sh["l%d_fcwb" % i] = np.ascontiguousarray(a.reshape(4, 44, 128).transpose(2, 1, 0))
    cw = np.asarray(inputs["l0_conv_w"], f32)
    cb = np.asarray(inputs["l0_conv_b"], f32)
    a = np.concatenate([cw, cb[None, :]], axis=0)
    sh["l0_cwb"] = np.ascontiguousarray(a.reshape(4, 18, 128).transpose(2, 1, 0))
    sh["l0_fbf"] = np.ascontiguousarray(np.stack(
        [np.asarray(inputs["l0_filt_%s%d" % (n, l)], f32) for l in (1, 2, 3) for n in ("b", "f")], axis=1))
    sh.update(const_tables())
    return sh


_NC_CACHE = {}


def kernel(**inputs):
    sh = prep_shared(inputs)
    x = np.asarray(inputs["x"], np.float32)
    mem = np.asarray(inputs["mem"], np.float32)
    if "nc" not in _NC_CACHE:
        _NC_CACHE["nc"] = build_program()
    nc = _NC_CACHE["nc"]
    in_maps = []
    for b in range(8):
        m = dict(sh)
        m["x"] = np.ascontiguousarray(x[b])
        m["mem"] = np.ascontiguousarray(mem[b])
        in_maps.append(m)
    res = run_bass_kernel_spmd(nc, in_maps, core_ids=list(range(8)))
    return np.stack([np.asarray(r["out"], np.float32) for r in res.results], axis=0)
```

```python
import math
from contextlib import ExitStack

import numpy as np
import ml_dtypes
import concourse.bass as bass
import concourse.mybir as mybir
from concourse.bass_utils import run_bass_kernel_spmd

F32 = mybir.dt.float32
BF16 = mybir.dt.bfloat16
AF = mybir.ActivationFunctionType
ALU = mybir.AluOpType
AX = mybir.AxisListType

L = 2048
D = 1024
NT = 16
KC = 8
DFF = 2816
NFF = 22
ALPHA = 4.0 ** 0.25
EPS = 1e-5
PI = float(np.pi)


class Res:
    __slots__ = ("name", "w", "r", "excl")

    def __init__(self, name, excl=False):
        self.name = name
        self.w = None
        self.r = []
        self.excl = excl


class Sched:
    def __init__(self, nc, es):
        self.nc = nc
        self.es = es
        self.engs = {}
        for n in ("pe", "act", "dve", "pool", "sp"):
            sem = es.enter_context(nc.semaphore("s_" + n))
            self.engs[n] = dict(sem=sem, count=0, known={}, ops=[])
        self.dma_sems = {}
        self.n_dma_sems = 0

    def dma_sem(self, key):
        if key not in self.dma_sems:
            sem = self.es.enter_context(self.nc.semaphore("d%d" % self.n_dma_sems))
            self.n_dma_sems += 1
            self.dma_sems[key] = [sem, 0]
        return self.dma_sems[key]

    def op(self, eng, fns, reads=(), writes=(), dma_key=None):
        E = self.engs[eng]
        deps = {}

        def add(ev):
            if ev is None:
                return
            sem, val = ev
            if deps.get(sem, 0) < val:
                deps[sem] = val

        excl_reads = [r for r in reads if r.excl]
        writes = list(writes) + [r for r in excl_reads if r not in writes]
        reads = [r for r in reads if not r.excl]
        for r in reads:
            add(r.w)
        for w in writes:
            add(w.w)
            for ev in w.r:
                add(ev)
        waits = []
        for sem, val in deps.items():
            if sem is E["sem"] and eng == "pe" and dma_key is None:
                continue
            if E["known"].get(sem, 0) >= val:
                continue
            E["known"][sem] = val
            waits.append((sem, val))
        if dma_key is not None:
            ds = self.dma_sem(dma_key)
            ds[1] += 16
            ev = (ds[0], ds[1])
            inc = (ds[0], 16)
        else:
            E["count"] += 1
            ev = (E["sem"], E["count"])
            inc = (E["sem"], 1)
        for r in reads:
            r.r.append(ev)
        for w in writes:
            w.w = ev
            w.r = []
        if not isinstance(fns, (list, tuple)):
            fns = [fns]
        E["ops"].append((waits, list(fns), inc))
        return ev

    def dma(self, queue, out, in_, reads=(), writes=(), key=None):
        assert key is not None
        return self.op(queue, [lambda e: e.dma_start(out=out, in_=in_)], reads, writes, dma_key=key)

    def barrier(self):
        evs = [(E["sem"], E["count"]) for E in self.engs.values() if E["count"] > 0]
        evs += [(s, c) for (s, c) in self.dma_sems.values() if c > 0]
        for n, E in self.engs.items():
            waits = []
            for sem, val in evs:
                if E["known"].get(sem, 0) >= val:
                    continue
                if sem is E["sem"] and n == "pe":
                    continue
                E["known"][sem] = val
                waits.append((sem, val))
            if waits:
                E["ops"].append((waits, [], None))

    def emit(self):
        nc = self.nc
        import sys
        print("SCHED counts", {n: E["count"] for n, E in self.engs.items()}, "dma", {k: v[1] for k, v in self.dma_sems.items()}, file=sys.stderr)

        def run(name):
            def f(e):
                for waits, fns, inc in self.engs[name]["ops"]:
                    for sem, val in waits:
                        e.wait_ge(sem, val)
                    n = len(fns)
                    for i, fn in enumerate(fns):
                        ins = fn(e)
                        if i == n - 1:
                            ins.then_inc(inc[0], inc[1])
            return f

        with nc.Block() as block:
            block.tensor(run("pe"))
            block.scalar(run("act"))
            block.vector(run("dve"))
            block.gpsimd(run("pool"))
            block.sync(run("sp"))


def _bf(a):
    return np.ascontiguousarray(a.astype(ml_dtypes.bfloat16))


_CONST_CACHE = {}


def const_tables():
    if _CONST_CACHE:
        return _CONST_CACHE
    N = 4096
    f = np.arange(2048, dtype=np.float64)
    t = np.arange(2048, dtype=np.float64)
    m = np.mod(np.outer(2 * f + 1, t), 2 * N)
    ang = np.pi * m / N
    C = np.cos(ang)
    Sn = np.sin(ang)
    CT = C.T.reshape(16, 128, 16, 128)
    ST = (-Sn).T.reshape(16, 128, 16, 128)
    fwd = np.stack([CT, ST], axis=0)
    fwd = fwd.transpose(3, 2, 0, 1, 4)
    _CONST_CACHE["fwd_tab"] = _bf(fwd.reshape(16, 128, 2 * 16 * 128))
    Ci = (C / 2048.0).reshape(4, 4, 128, 4, 512)
    Si = (-Sn / 2048.0).reshape(4, 4, 128, 4, 512)
    inv = np.stack([Ci, Si], axis=0)
    inv = inv.transpose(4, 1, 3, 2, 0, 5)
    _CONST_CACHE["inv_tab"] = _bf(inv.reshape(4, 4, 128, 4 * 2 * 512))
    f32 = np.float32
    tl = np.linspace(0.0, 1.0, L, dtype=f32)[:, None]
    w = (f32(2.0 * math.pi) * np.arange(L, dtype=f32)[:, None] / f32(L)).astype(f32)
    fr = np.linspace(1e-4, 15, 16, dtype=f32)[None, :]
    z = np.concatenate([tl, np.cos(fr * w), -np.sin(fr * w)], axis=-1).astype(f32)
    _CONST_CACHE["zfT"] = np.ascontiguousarray(z.T)
    _CONST_CACHE["ntn"] = np.ascontiguousarray((-tl[:, 0]).reshape(16, 128).T.astype(f32))
    min_decay = math.log(1e-2) / 1.5
    max_decay = math.log(1e-2) / 0.3
    _CONST_CACHE["deltas"] = np.abs(np.linspace(min_decay, max_decay, 768, dtype=f32)).astype(f32)
    inv_f = (10000.0 ** (-np.arange(0, 64, 2, dtype=f32) / f32(64))).astype(f32)
    angr = (np.arange(L, dtype=f32)[:, None] * inv_f[None, :]).astype(f32)
    angr = np.concatenate([angr, angr], axis=-1)
    cosT = np.cos(angr).T.astype(f32)
    sinT = np.sin(angr).T.astype(f32)
    _CONST_CACHE["ropec"] = np.ascontiguousarray(np.concatenate([cosT, cosT], axis=0))
    _CONST_CACHE["ropes"] = np.ascontiguousarray(np.concatenate([sinT, sinT], axis=0))
    Pm = np.zeros((128, 128), np.float32)
    for po in range(128):
        d = po % 64
        if d < 32:
            Pm[po + 32, po] = -1.0
        else:
            Pm[po - 32, po] = 1.0
    _CONST_CACHE["pm"] = _bf(Pm)
    _CONST_CACHE["ident"] = _bf(np.eye(128, dtype=np.float32))
    k = np.arange(128)[:, None]
    q = np.arange(128)[None, :]
    NEG = -30000.0
    m_next = np.where(k <= q, 0.0, NEG)
    m_prev = np.where(k >= q, 0.0, NEG)
    _CONST_CACHE["maskb"] = _bf(np.concatenate([m_next, np.zeros((128, 128)), m_prev], axis=1))
    return _CONST_CACHE


def build_program(debug=None):
    nc = bass.Bass("TRN2", target_bir_lowering=False)
    dbg = {}

    def din(name, shape, dt=F32):
        return nc.dram_tensor(name, list(shape), dt, kind="ExternalInput").ap()

    x_d = din("x", [L, D])
    mem_d = din("mem", [256, D])
    wkv_d = din("w_mem_kv", [D, 512])
    w_in_d = [din("l0_w_in", [D, 2560]), din("l1_w_in", [D, 1536])]
    w_out_d = [din("l0_w_out", [D, D]), din("l1_w_out", [D, D])]
    w_up_d = [din("l%d_ffn_w_up" % i, [D, 2 * DFF]) for i in range(2)]
    w_dn_d = [din("l%d_ffn_w_down" % i, [DFF, D]) for i in range(2)]
    ln_d = [[din("l%d_%s" % (i, n), [D]) for n in ("ln1_g", "ln1_b", "ln2_g", "ln2_b")] for i in range(2)]
    fcwb_d = [din("l%d_fcwb" % i, [128, 44, 4]) for i in range(2)]
    cwb_d = din("l0_cwb", [128, 18, 4])
    fw1_d = din("l0_filt_w1", [33, 64])
    fw2_d = din("l0_filt_w2", [64, 64])
    fw3_d = din("l0_filt_w3", [64, 64])
    fwo_d = din("l0_filt_w_out", [64, 1536])
    fbf_d = din("l0_fbf", [64, 6])
    hd_d = din("l0_hyena_d", [768])
    sink_d = din("l1_sink", [12])
    fwd_tab_d = din("fwd_tab", [16, 128, 4096], BF16)
    inv_tab_d = din("inv_tab", [4, 4, 128, 4096], BF16)
    zfT_d = din("zfT", [33, L])
    ntn_d = din("ntn", [128, 16])
    deltas_d = din("deltas", [768])
    ropec_d = din("ropec", [128, L])
    ropes_d = din("ropes", [128, L])
    pm_d = din("pm", [128, 128], BF16)
    ident_d = din("ident", [128, 128], BF16)
    maskb_d = din("maskb", [128, 384], BF16)
    out_d = nc.dram_tensor("out", [L, D], F32, kind="ExternalOutput").ap()
    if debug:
        dbg_d = nc.dram_tensor("dbg", [L, D], F32, kind="ExternalOutput").ap()

    es = ExitStack()
    with es:
        S = Sched(nc, es)
        AR_BYTES = 194 * 1024
        arena = es.enter_context(nc.sbuf_tensor("arena", [128, AR_BYTES // 2], BF16))
        ps = es.enter_context(nc.psum_tensor("ps", [128, 4096], F32))
        PB = [Res("pb%d" % i, excl=True) for i in range(8)]

        def bank(b, n=512, p0=0, p1=128):
            return ps[p0:p1, b * 512:b * 512 + n]

        def bank_bf(b):
            return ps[:, b * 512:(b + 1) * 512].bitcast(BF16)

        def AV(off, shape, dt):
            n = int(np.prod(shape))
            if dt == F32:
                v = arena[:, off // 2: off // 2 + 2 * n].bitcast(F32)
            else:
                v = arena[:, off // 2: off // 2 + n]
            if len(shape) == 2:
                v = v.rearrange("p (a b) -> p a b", a=shape[0])
            elif len(shape) == 3:
                v = v.rearrange("p (a b c) -> p a b c", a=shape[0], b=shape[1])
            elif len(shape) == 4:
                v = v.rearrange("p (a b c d) -> p a b c d", a=shape[0], b=shape[1], c=shape[2])
            return v

        K_ = 1024
        R_X, R_XT, R_Y, R_YM, R_MQ, R_T = 0, 64 * K_, 96 * K_, 120 * K_, 128 * K_, 136 * K_

        def small(name, shape, dt):
            return es.enter_context(nc.sbuf_tensor("sb_" + name, list(shape), dt))

        ident = small("ident", [128, 128], BF16)
        pm = small("pm", [128, 128], BF16)
        maskb = small("maskb", [128, 384], BF16)
        ones64 = small("ones64", [128, 64], BF16)
        memKT = small("memKT", [128, 2, 256], BF16)
        memV = small("memV", [128, 2, 256], BF16)
        cwb = small("cwb", [128, 18, 4], F32)
        fcwb = small("fcwb", [128, 44, 4], F32)
        stat = small("stat", [128, 128], F32)
        GB = small("GB", [128, 2, 1024], F32)
        r_consts = Res("consts")
        r_memKV = Res("memKV")
        r_cwb = Res("cwb")
        r_fcwb = Res("fcwb")
        r_GB = Res("GB")

        X = AV(R_X, [16, 1024], F32)
        XT = AV(R_XT, [8, 2048], BF16)
        YT = AV(R_Y, [6, 2048], BF16)
        YM = AV(R_YM, [2, 2048], BF16)
        MQ = AV(R_MQ, [2, 2048], BF16)
        r_X = [Res("X%d" % i) for i in range(16)]
        r_XT = Res("XT")
        r_YT = Res("YT")
        r_YM = Res("YM")
        r_MQ = Res("MQ")

        _bk = [0]

        def nxt(lst):
            b = lst[_bk[0] % len(lst)]
            _bk[0] += 1
            return b

        def mm_group(out, pairs, reads, writes):
            n = len(pairs)
            fns = []
            for i, (l, r) in enumerate(pairs):
                fns.append(lambda e, l=l, r=r, i=i: e.matmul(out, l, r, start=(i == 0), stop=(i == n - 1)))
            return S.op("pe", fns, reads, writes)

        def act_copy(out, in_, reads, writes):
            return S.op("act", [lambda e: e.activation(out=out, in_=in_, func=AF.Identity)], reads, writes)

        S.dma("sp", ident[:], ident_d[:, :], writes=[r_consts], key="c0")
        S.dma("sp", pm[:], pm_d[:, :], writes=[r_consts], key="c1")
        S.dma("sp", maskb[:], maskb_d[:, :], writes=[r_consts], key="c2")
        S.dma("sp", cwb[:], cwb_d[:, :, :], writes=[r_cwb], key="c3")
        S.op("dve", [lambda e: e.memset(ones64[:], 1.0)], writes=[r_consts])
        epst = small("epst", [128, 1], F32)
        S.op("dve", [lambda e: e.memset(epst[:], EPS)], writes=[r_consts])


        def transpose_to_XT(src_bf, tile, r_src):
            b = nxt([6, 7])
            bb = bank_bf(b)
            fns = [lambda e, kc=kc: e.transpose(bb[:, kc * 128:(kc + 1) * 128], src_bf[:, kc * 128:(kc + 1) * 128], ident[:])
                   for kc in range(8)]
            S.op("pe", fns, reads=[r_src, r_consts], writes=[PB[b]])
            S.op("act", [lambda e: e.activation(out=XT[:, :, tile * 128:(tile + 1) * 128],
                                                in_=bb.rearrange("p (k m) -> p k m", k=8), func=AF.Identity)],
                 reads=[PB[b]], writes=[r_XT])

        def load_ln(layer, which):
            g_d, b_d = ln_d[layer][2 * which], ln_d[layer][2 * which + 1]
            S.dma("sp", GB[:, 0, :], g_d.partition_broadcast(128), writes=[r_GB], key="gb0")
            S.dma("sp", GB[:, 1, :], b_d.partition_broadcast(128), writes=[r_GB], key="gb1")

        NRB = 6
        LN_LAG = 3
        _rboff = [46 * K_, 50 * K_, 28672, 28672 + 4096, 36880, 36880 + 4096]
        rbuf = [AV(R_T + _rboff[i], [1024], F32) for i in range(NRB)]
        _xboff = [54 * K_, 56 * K_, 24 * K_, 26 * K_]
        xbuf = [AV(R_T + _xboff[i], [1024], BF16) for i in range(4)]
        r_rbuf = [Res("rbuf%d" % i) for i in range(NRB)]
        r_xbuf = [Res("xbuf%d" % i) for i in range(4)]
        r_stat = Res("stat")
        r_stats = [Res("stat%d" % i) for i in range(8)]
        _ln = [0]

        def ln_epilogue(tile, r_in, r_r, final_out):
            i = _ln[0] % 2
            _ln[0] += 1
            so = (tile % NRB) * 16
            r_stat = r_stats[tile % NRB]
            st6 = stat[:, so:so + 12]
            mv = stat[:, so + 12:so + 14]
            rstd = stat[:, so + 14:so + 15]
            nmr = stat[:, so + 15:so + 16]
            S.op("dve", [lambda e: e.bn_stats(out=st6[:, 0:6], in_=r_in[:, 0:512])], reads=[r_r], writes=[r_stat])
            S.op("dve", [lambda e: e.bn_stats(out=st6[:, 6:12], in_=r_in[:, 512:1024])], reads=[r_r], writes=[r_stat])
            S.op("dve", [lambda e: e.bn_aggr(out=mv, in_=st6)], reads=[r_stat], writes=[r_stat])
            S.op("act", [lambda e: e.activation(out=rstd, in_=mv[:, 1:2], func=AF.Sqrt, bias=epst[:, 0:1])],
                 reads=[r_stat, r_consts], writes=[r_stat])
            S.op("dve", [lambda e: e.reciprocal(out=rstd, in_=rstd)], reads=[r_stat], writes=[r_stat])
            S.op("dve", [lambda e: e.scalar_tensor_tensor(out=nmr, in0=mv[:, 0:1], scalar=-1.0, in1=rstd,
                                                          op0=ALU.mult, op1=ALU.mult)], reads=[r_stat], writes=[r_stat])
            S.op("act", [lambda e: e.activation(out=r_in, in_=r_in, func=AF.Identity, scale=rstd, bias=nmr)],
                 reads=[r_r, r_stat], writes=[r_r])
            S.op("dve", [lambda e: e.tensor_tensor(out=r_in, in0=r_in, in1=GB[:, 0, :], op=ALU.mult)],
                 reads=[r_r, r_GB], writes=[r_r])
            S.op("pool", [lambda e: e.tensor_tensor(out=X[:, tile, :], in0=r_in, in1=GB[:, 1, :], op=ALU.add)],
                 reads=[r_r, r_GB], writes=[r_X[tile]])
            if final_out:
                S.dma("sp", out_d[tile * 128:(tile + 1) * 128, :], X[:, tile, :], reads=[r_X[tile]], key="out%d" % (tile % 4))
                return None
            else:
                i4 = tile % 4
                xb = xbuf[i4]
                S.op("act", [lambda e: e.activation(out=xb, in_=X[:, tile, :], func=AF.Identity)],
                     reads=[r_X[tile]], writes=[r_xbuf[i4]])
                return lambda: transpose_to_XT(xb, tile, r_xbuf[i4])

        def mem_attention(toff):
            PT = [AV(toff + i * K_, [512], BF16) for i in range(4)]
            r_PT = [Res("mpt%d" % i) for i in range(4)]
            rden = AV(toff + 4 * K_, [512], F32)
            r_rden = Res("mrden")
            for hp in range(2):
                for qt in range(4):
                    qs = slice(qt * 512, (qt + 1) * 512)
                    for hh in range(2):
                        h = 2 * hp + hh
                        prow = slice(hh * 64, (hh + 1) * 64)
                        pts = []
                        for mt in range(2):
                            b = nxt([0, 1, 2, 3])
                            mm_group(bank(b), [(memKT[prow, hp, mt * 128:(mt + 1) * 128], MQ[prow, hp, qs])],
                                     reads=[r_memKV, r_MQ], writes=[PB[b]])
                            k = (hh * 2 + mt)
                            S.op("act", [lambda e, k=k, b=b: e.activation(out=PT[k], in_=bank(b), func=AF.Exp, scale=0.125)],
                                 reads=[PB[b]], writes=[r_PT[k]])
                            pts.append(k)
                        bo, bd = 4, 5
                        mm_group(bank(bo, 512, 0, 64), [(memV[:, mt, h * 64:(h + 1) * 64], PT[pts[mt]]) for mt in range(2)],
                                 reads=[r_memKV, r_PT[pts[0]], r_PT[pts[1]]], writes=[PB[bo]])
                        mm_group(bank(bd, 512, 0, 64), [(ones64[:], PT[pts[mt]]) for mt in range(2)],
                                 reads=[r_consts, r_PT[pts[0]], r_PT[pts[1]]], writes=[PB[bd]])
                        S.op("dve", [lambda e: e.reciprocal(out=rden[0:64, :], in_=bank(bd, 512, 0, 64))],
                             reads=[PB[bd]], writes=[r_rden])
                        S.op("dve", [lambda e, prow=prow, hp=hp, qs=qs: e.tensor_tensor(
                            out=YM[prow, hp, qs], in0=bank(bo, 512, 0, 64), in1=rden[0:64, :], op=ALU.mult)],
                             reads=[PB[bo], r_rden], writes=[r_YM])

        def out_proj_ln1(layer):
            wout = AV(R_T, [8, 1024], BF16)
            r_wout = Res("wout")
            xs = [AV(R_T + 16 * K_ + i * 4 * K_, [1024], F32) for i in range(2)]
            r_xs = [Res("xs%d" % i) for i in range(2)]
            S.dma("pool", wout, w_out_d[layer].rearrange("(kc p) n -> p kc n", p=128), writes=[r_wout], key="wout")
            load_ln(layer, 0)
            pend = []
            for tile in range(16):
                ts_ = slice(tile * 128, (tile + 1) * 128)
                bp = [0, 2, 4][tile % 3]
                for half in range(2):
                    b = bp + half
                    pairs = []
                    for kc in range(8):
                        lhs = YT[:, kc, ts_] if kc < 6 else YM[:, kc - 6, ts_]
                        pairs.append((lhs, wout[:, kc, half * 512:(half + 1) * 512]))
                    mm_group(bank(b), pairs, reads=[r_YT, r_YM, r_wout], writes=[PB[b]])
                i = tile % 2
                ri = tile % NRB
                if layer == 0:
                    S.dma("sp", xs[i], x_d[ts_, :], writes=[r_xs[i]], key="xs%d" % i)
                    xin, rxin = xs[i], r_xs[i]
                else:
                    xin, rxin = X[:, tile, :], r_X[tile]
                rb = rbuf[ri]
                S.op("dve", [lambda e, xin=xin, rb=rb, bp=bp: e.scalar_tensor_tensor(
                    out=rb, in0=xin, scalar=ALPHA, in1=ps[:, bp * 512:bp * 512 + 1024], op0=ALU.mult, op1=ALU.add)],
                     reads=[rxin, PB[bp], PB[bp + 1]], writes=[r_rbuf[ri]])
                pend.append(ln_epilogue(tile, rb, r_rbuf[ri], False))
                if len(pend) > LN_LAG:
                    pend.pop(0)()
            for f_ in pend:
                f_()

        def ffn_ln2(layer, final):
            blocks = [4, 4, 4, 4, 3, 3]
            S.dma("sp", fcwb[:], fcwb_d[layer][:, :, :], writes=[r_fcwb], key="fcwb")
            load_ln(layer, 1)
            GT = AV(R_Y, [4, 2048], BF16)
            r_GT = Res("GT")
            wdn = [AV(R_T + i * 8 * K_, [4, 1024], BF16) for i in range(2)]
            r_wdn = [Res("wdn%d" % i) for i in range(2)]
            wup = [AV(R_T + 16 * K_ + i * 4 * K_, [8, 2, 128], BF16) for i in range(3)]
            r_wup = [Res("wup%d" % i) for i in range(3)]
            hsb = [AV(R_T + 28 * K_ + i * 8208, [2052], F32) for i in range(2)]
            r_hsb = [Res("hsb%d" % i) for i in range(2)]
            for i in range(2):
                S.op("pool", [lambda e, i=i: e.memset(hsb[i][:, 0:1], 0.0)], writes=[r_hsb[i]])
                S.op("pool", [lambda e, i=i: e.memset(hsb[i][:, 2049:2050], 0.0)], writes=[r_hsb[i]])
            acc_a = AV(R_YM, [2048], F32)
            acc_g = AV(R_MQ, [2048], F32)
            r_acca, r_accg = Res("acca"), Res("accg")
            wd_v = w_dn_d[layer]
            wu_v = w_up_d[layer].rearrange("(kc p) n -> p kc n", p=128)
            j0 = 0
            for bi, nb in enumerate(blocks):
                wd = wdn[bi % 2]
                S.dma("pool", wd[:, 0:nb, :], wd_v[j0 * 128:(j0 + nb) * 128, :].rearrange("(j p) n -> p j n", p=128),
                      writes=[r_wdn[bi % 2]], key="wdn%d" % (bi % 2))
                for jj in range(nb):
                    j = j0 + jj
                    wi = j % 3
                    wu = wup[wi]
                    S.dma("pool", wu[:, :, 0, :], wu_v[:, :, j * 128:(j + 1) * 128], writes=[r_wup[wi]], key="wup%da" % wi)
                    S.dma("pool", wu[:, :, 1, :], wu_v[:, :, DFF + j * 128:DFF + (j + 1) * 128], writes=[r_wup[wi]], key="wup%db" % wi)
                    for ag in range(2):
                        hs_ = hsb[ag]
                        cj = ag * NFF + j
                        for tq in range(4):
                            b = nxt([0, 1, 2, 3])
                            mm_group(bank(b), [(wu[:, kc, ag, :], XT[:, kc, tq * 512:(tq + 1) * 512]) for kc in range(8)],
                                     reads=[r_wup[wi], r_XT], writes=[PB[b]])
                            S.op("act", [lambda e, hs_=hs_, tq=tq, b=b: e.activation(
                                out=hs_[:, 1 + tq * 512:1 + (tq + 1) * 512], in_=bank(b), func=AF.Identity)],
                                 reads=[PB[b]], writes=[r_hsb[ag]])
                        acc, r_acc = (acc_a, r_acca) if ag == 0 else (acc_g, r_accg)
                        S.op("act", [lambda e, acc=acc, hs_=hs_, cj=cj: e.activation(
                            out=acc, in_=hs_[:, 1:2049], func=AF.Identity, scale=fcwb[:, cj, 1:2], bias=fcwb[:, cj, 3:4])],
                             reads=[r_hsb[ag], r_fcwb], writes=[r_acc])
                        S.op("dve", [lambda e, acc=acc, hs_=hs_, cj=cj: e.scalar_tensor_tensor(
                            out=acc, in0=hs_[:, 0:2048], scalar=fcwb[:, cj, 0:1], in1=acc, op0=ALU.mult, op1=ALU.add)],
                             reads=[r_hsb[ag], r_fcwb, r_acc], writes=[r_acc])
                        S.op("dve", [lambda e, acc=acc, hs_=hs_, cj=cj: e.scalar_tensor_tensor(
                            out=acc, in0=hs_[:, 2:2050], scalar=fcwb[:, cj, 2:3], in1=acc, op0=ALU.mult, op1=ALU.add)],
                             reads=[r_hsb[ag], r_fcwb, r_acc], writes=[r_acc])
                    S.op("act", [lambda e: e.activation(out=acc_g, in_=acc_g, func=AF.Silu)], reads=[r_accg], writes=[r_accg])
                    S.op("dve", [lambda e, jj=jj: e.tensor_tensor(out=GT[:, jj, :], in0=acc_g, in1=acc_a, op=ALU.mult)],
                         reads=[r_accg, r_acca], writes=[r_GT])
                last = bi == len(blocks) - 1
                if last:
                    S.barrier()
                pend = []
                for tile in range(16):
                    ts_ = slice(tile * 128, (tile + 1) * 128)
                    bp = [0, 2, 4][tile % 3] if last else [4, 6][tile % 2]
                    for half in range(2):
                        mm_group(bank(bp + half), [(GT[:, jj, ts_], wd[:, jj, half * 512:(half + 1) * 512]) for jj in range(nb)],
                                 reads=[r_GT, r_wdn[bi % 2]], writes=[PB[bp + half]])
                    pin = ps[:, bp * 512:bp * 512 + 1024]
                    if not last:
                        if bi == 0:
                            S.op("dve", [lambda e, tile=tile, pin=pin: e.scalar_tensor_tensor(
                                out=X[:, tile, :], in0=X[:, tile, :], scalar=ALPHA, in1=pin, op0=ALU.mult, op1=ALU.add)],
                                 reads=[PB[bp], PB[bp + 1]], writes=[r_X[tile]])
                        else:
                            S.op("dve", [lambda e, tile=tile, pin=pin: e.tensor_tensor(
                                out=X[:, tile, :], in0=X[:, tile, :], in1=pin, op=ALU.add)],
                                 reads=[PB[bp], PB[bp + 1]], writes=[r_X[tile]])
                    else:
                        ri = tile % NRB
                        rb = rbuf[ri]
                        S.op("dve", [lambda e, tile=tile, pin=pin, rb=rb: e.tensor_tensor(
                            out=rb, in0=X[:, tile, :], in1=pin, op=ALU.add)],
                             reads=[PB[bp], PB[bp + 1], r_X[tile]], writes=[r_rbuf[ri]])
                        fB = ln_epilogue(tile, rb, r_rbuf[ri], final)
                        if fB is not None:
                            pend.append(fB)
                            if len(pend) > LN_LAG:
                                pend.pop(0)()
                for f_ in pend:
                    f_()
                j0 += nb

        memf = AV(R_X, [2, 1024], F32)
        memb = AV(R_X + 8 * K_, [2, 1024], BF16)
        memT = AV(R_X + 12 * K_, [8, 256], BF16)
        wkv = AV(R_X + 16 * K_, [8, 512], BF16)
        r_memf, r_memb, r_memT, r_wkv = Res("memf"), Res("memb"), Res("memT"), Res("wkv")
        S.dma("sp", memf, mem_d.rearrange("(mt p) d -> p mt d", p=128), writes=[r_memf], key="memf")
        S.dma("pool", wkv, wkv_d.rearrange("(kc p) n -> p kc n", p=128), writes=[r_wkv], key="wkv")
        act_copy(memb, memf, [r_memf], [r_memb])
        for mt in range(2):
            b = nxt([6, 7])
            bb = bank_bf(b)
            fns = [lambda e, kc=kc, mt=mt, bb=bb: e.transpose(bb[:, kc * 128:(kc + 1) * 128], memb[:, mt, kc * 128:(kc + 1) * 128], ident[:])
                   for kc in range(8)]
            S.op("pe", fns, reads=[r_memb, r_consts], writes=[PB[b]])
            S.op("act", [lambda e, mt=mt, bb=bb: e.activation(out=memT[:, :, mt * 128:(mt + 1) * 128],
                                                             in_=bb.rearrange("p (k m) -> p k m", k=8), func=AF.Identity)],
                 reads=[PB[b]], writes=[r_memT])
        for hp in range(2):
            b = nxt([0, 1])
            mm_group(bank(b, 256), [(wkv[:, kc, hp * 128:(hp + 1) * 128], memT[:, kc, :]) for kc in range(8)],
                     reads=[r_wkv, r_memT], writes=[PB[b]])
            act_copy(memKT[:, hp, :], bank(b, 256), [PB[b]], [r_memKV])
        for mt in range(2):
            b = nxt([0, 1])
            mm_group(bank(b, 256), [(memT[:, kc, mt * 128:(mt + 1) * 128], wkv[:, kc, 256:512]) for kc in range(8)],
                     reads=[r_wkv, r_memT], writes=[PB[b]])
            act_copy(memV[:, mt, :], bank(b, 256), [PB[b]], [r_memKV])
        S.barrier()

        if debug == "s_m":
            S.barrier()
            S.emit()
            return nc
        xs0 = [AV(R_T + 16 * K_ + i * 4 * K_, [1024], F32) for i in range(2)]
        r_xs0 = [Res("xs0_%d" % i) for i in range(2)]
        for tile in range(16):
            i = tile % 2
            S.dma("sp", xs0[i], x_d[tile * 128:(tile + 1) * 128, :], writes=[r_xs0[i]], key="xs%d" % i)
            i4 = tile % 4
            S.op("act", [lambda e, i=i, i4=i4: e.activation(out=xbuf[i4], in_=xs0[i], func=AF.Identity)],
                 reads=[r_xs0[i]], writes=[r_xbuf[i4]])
            transpose_to_XT(xbuf[i4], tile, r_xbuf[i4])

        if debug == "s_x":
            S.barrier()
            S.emit()
            return nc
        HS = AV(R_X, [2, 16, 768], BF16)
        r_HS = Res("HS")
        hA = AV(R_X + 48 * K_, [2048], F32)
        hB = AV(R_X + 56 * K_, [2048], F32)
        r_hA, r_hB = Res("hA"), Res("hB")
        zf = AV(R_Y, [2048], F32)
        fw = AV(R_Y + 8 * K_, [3, 64], F32)
        fwo = AV(R_Y + 9 * K_, [1536], F32)
        fbf = AV(R_Y + 15 * K_, [8], F32)
        dbc = AV(R_Y + 16 * K_, [768], F32)
        dlt = AV(R_YM, [768], F32)
        dec = AV(R_YM + 3 * K_, [768], F32)
        ntn = AV(R_YM + 6 * K_, [16], F32)
        fsb = AV(R_MQ, [768], F32)
        wtmp = AV(R_MQ + 3 * K_, [512], F32)
        wtm2 = AV(R_MQ + 5 * K_, [512], F32)
        r_f = Res("filt_in")
        r_dec, r_fsb, r_wtmp, r_wtm2, r_dbc = Res("dec"), Res("fsb"), Res("wtmp"), Res("wtm2"), Res("dbc")
        S.dma("sp", zf[0:33, :], zfT_d[:, :], writes=[r_f], key="f0")
        S.dma("sp", fw[0:33, 0, :], fw1_d[:, :], writes=[r_f], key="f1")
        S.dma("sp", fw[0:64, 1, :], fw2_d[:, :], writes=[r_f], key="f2")
        S.dma("sp", fw[0:64, 2, :], fw3_d[:, :], writes=[r_f], key="f3")
        S.dma("sp", fwo[0:64, :], fwo_d[:, :], writes=[r_f], key="f4")
        S.dma("sp", fbf[0:64, 0:6], fbf_d[:, :], writes=[r_f], key="f5")
        S.dma("sp", dbc, hd_d.partition_broadcast(128), writes=[r_dbc], key="f6")
        S.dma("sp", dlt, deltas_d.partition_broadcast(128), writes=[r_f], key="f7")
        S.dma("sp", ntn, ntn_d[:, :], writes=[r_f], key="f8")
        fbs = stat[0:64, 120:123]
        for l in range(3):
            S.op("dve", [lambda e, l=l: e.tensor_tensor(out=fbs[:, l:l + 1], in0=fbf[0:64, 2 * l:2 * l + 1],
                                                        in1=fbf[0:64, 2 * l + 1:2 * l + 2], op=ALU.mult)],
                 reads=[r_f], writes=[r_stat])
        srcs = [(zf, 33, r_f), (hA, 64, r_hA), (hB, 64, r_hB)]
        dsts = [(hA, r_hA), (hB, r_hB), (hA, r_hA)]
        for l in range(3):
            src, kk, r_src = srcs[l]
            dst, r_dst = dsts[l]
            for tq in range(4):
                b = nxt([0, 1, 2, 3])
                cs = slice(tq * 512, (tq + 1) * 512)
                mm_group(bank(b, 512, 0, 64), [(fw[0:kk, l, :], src[0:kk, cs])], reads=[r_f, r_src], writes=[PB[b]])
                S.op("dve", [lambda e, b=b, l=l: e.tensor_scalar(out=wtmp[0:64, :], in0=bank(b, 512, 0, 64),
                                                                scalar1=fbf[0:64, 2 * l + 1:2 * l + 2], scalar2=fbs[:, l:l + 1],
                                                                op0=ALU.mult, op1=ALU.add)],
                     reads=[PB[b], r_f, r_stat], writes=[r_wtmp])
                S.op("dve", [lambda e: e.tensor_scalar(out=wtm2[0:64, :], in0=wtmp[0:64, :], scalar1=-PI, scalar2=2 * PI,
                                                       op0=ALU.is_lt, op1=ALU.mult)], reads=[r_wtmp], writes=[r_wtm2])
                S.op("dve", [lambda e: e.tensor_tensor(out=wtmp[0:64, :], in0=wtmp[0:64, :], in1=wtm2[0:64, :], op=ALU.add)],
                     reads=[r_wtmp, r_wtm2], writes=[r_wtmp])
                S.op("dve", [lambda e: e.tensor_scalar(out=wtm2[0:64, :], in0=wtmp[0:64, :], scalar1=PI, scalar2=-2 * PI,
                                                       op0=ALU.is_gt, op1=ALU.mult)], reads=[r_wtmp], writes=[r_wtm2])
                S.op("dve", [lambda e: e.tensor_tensor(out=wtmp[0:64, :], in0=wtmp[0:64, :], in1=wtm2[0:64, :], op=ALU.add)],
                     reads=[r_wtmp, r_wtm2], writes=[r_wtmp])
                S.op("act", [lambda e, dst=dst, cs=cs: e.activation(out=dst[0:64, cs], in_=wtmp[0:64, :], func=AF.Sin)],
                     reads=[r_wtmp], writes=[r_dst])
        for tile in range(16):
            ts_ = slice(tile * 128, (tile + 1) * 128)
            for q3 in range(3):
                mm_group(bank(q3), [(hA[0:64, ts_], fwo[0:64, q3 * 512:(q3 + 1) * 512])], reads=[r_hA, r_f], writes=[PB[q3]])
            S.op("act", [lambda e, tile=tile: e.activation(out=dec, in_=dlt, func=AF.Exp, scale=ntn[:, tile:tile + 1])],
                 reads=[r_f], writes=[r_dec])
            act_copy(fsb, ps[:, 0:768], [PB[0], PB[1]], [r_fsb])
            S.op("dve", [lambda e: e.tensor_tensor(out=hB[:, 0:768], in0=fsb, in1=ps[:, 768:1536], op=ALU.add)],
                 reads=[r_fsb, PB[1], PB[2]], writes=[r_hB])
            S.op("dve", [lambda e: e.tensor_tensor(out=hB[:, 768:1536], in0=fsb, in1=ps[:, 768:1536], op=ALU.subtract)],
                 reads=[r_fsb, PB[1], PB[2]], writes=[r_hB])
            S.op("dve", [lambda e, tile=tile: e.tensor_tensor(out=HS[:, 0, tile, :], in0=hB[:, 0:768], in1=dec, op=ALU.mult)],
                 reads=[r_hB, r_dec], writes=[r_HS])
            S.op("dve", [lambda e, tile=tile: e.tensor_tensor(out=HS[:, 1, tile, :], in0=hB[:, 768:1536], in1=dec, op=ALU.mult)],
                 reads=[r_hB, r_dec], writes=[r_HS])
        S.barrier()

        if debug == "s_f":
            S.barrier()
            S.emit()
            return nc
        KS = AV(R_T, [2, 16, 768], BF16)
        r_KS = Res("KS")

        def fwd_pass(rhs_re, rhs_im, r_rhs, ftab, r_ftab, epilogue):
            for fc in range(16):
                si = fc % 2
                ft = ftab[si]
                S.dma("sp", ft.rearrange("p a b c -> p (a b c)"), fwd_tab_d[fc], writes=[r_ftab[si]], key="ftab%d" % si)
                bs = [0, 1, 2, 3] if fc % 2 == 0 else [4, 5, 6, 7]
                for ri in range(2):
                    rhs = rhs_re if ri == 0 else rhs_im
                    o0 = bs[0] * 512 + ri * 1024
                    fns = []
                    for tc in range(16):
                        lhs = ft[:, ri, tc, :]
                        fns.append(lambda e, lhs=lhs, tc=tc, rhs=rhs, o0=o0: e.matmul(
                            ps[:, o0:o0 + 512], lhs, rhs[:, tc, 0:512], start=(tc == 0), stop=(tc == 15)))
                        fns.append(lambda e, lhs=lhs, tc=tc, rhs=rhs, o0=o0: e.matmul(
                            ps[:, o0 + 512:o0 + 768], lhs, rhs[:, tc, 512:768], start=(tc == 0), stop=(tc == 15)))
                    S.op("pe", fns, reads=[r_ftab[si], r_rhs], writes=[PB[bs[2 * ri]], PB[bs[2 * ri + 1]]])
                pre = ps[:, bs[0] * 512:bs[0] * 512 + 768]
                pim = ps[:, bs[2] * 512:bs[2] * 512 + 768]
                epilogue(fc, pre, pim, [PB[b] for b in bs])

        ftab = [AV(R_YM, [2, 16, 128], BF16), AV(R_MQ, [2, 16, 128], BF16)]
        r_ftab = [Res("ftab0"), Res("ftab1")]

        def k_epilogue(fc, pre, pim, rbs):
            S.op("dve", [lambda e: e.tensor_tensor(out=KS[:, 0, fc, :], in0=pre, in1=dbc, op=ALU.add)],
                 reads=rbs[0:2] + [r_dbc], writes=[r_KS])
            act_copy(KS[:, 1, fc, :], pim, rbs[2:4], [r_KS])

        fwd_pass(HS[:, 0], HS[:, 1], r_HS, ftab, r_ftab, k_epilogue)
        S.barrier()

        if debug == "s_k":
            S.barrier()
            S.emit()
            return nc
        Z = AV(R_X, [16, 768], BF16)
        r_Z = Res("Z")
        hsb0 = [AV(R_X + 24 * K_ + i * 8208, [2052], F32) for i in range(2)]
        r_hsb0 = [Res("hsb0_%d" % i) for i in range(2)]
        accx = AV(R_X + 24 * K_ + 16416, [2048], F32)
        accv = AV(R_X + 24 * K_ + 16416 + 8192, [2048], F32)
        zT = AV(R_X + 24 * K_ + 16416 + 16384, [2048], BF16)
        r_accx, r_accv, r_zT = Res("accx"), Res("accv"), Res("zT")
        wch = [AV(R_T + 48 * K_ + i * 2 * K_, [8, 128], BF16) for i in range(3)]
        r_wch = [Res("wch%d" % i) for i in range(3)]
        for i in range(2):
            S.op("pool", [lambda e, i=i: e.memset(hsb0[i][:, 0:1], 0.0)], writes=[r_hsb0[i]])
            S.op("pool", [lambda e, i=i: e.memset(hsb0[i][:, 2049:2050], 0.0)], writes=[r_hsb0[i]])
        w0v = w_in_d[0].rearrange("(kc p) n -> p kc n", p=128)
        order = [18, 19] + [0, 1, 2, 3, 4, 5]
        for i in range(6):
            order += [6 + i, 12 + i]
        _wc = [0]

        def proj_chunk(wv, col0, sink_fn, extra_reads=()):
            wi = _wc[0] % 3
            _wc[0] += 1
            S.dma("pool", wch[wi], wv[:, :, col0:col0 + 128], writes=[r_wch[wi]], key="wch%d" % wi)
            for tq in range(4):
                b = nxt([0, 1, 2, 3])
                mm_group(bank(b), [(wch[wi][:, kc, :], XT[:, kc, tq * 512:(tq + 1) * 512]) for kc in range(8)],
                         reads=[r_wch[wi], r_XT], writes=[PB[b]])
                sink_fn(tq, b)

        def conv_chunk(c, hs_, r_hs, acc_out, r_acc_list, out_final):
            S.op("act", [lambda e: e.activation(out=acc_out, in_=hs_[:, 1:2049], func=AF.Identity,
                                                scale=cwb[:, c, 1:2], bias=cwb[:, c, 3:4])],
                 reads=[r_hs, r_cwb], writes=r_acc_list)
            S.op("dve", [lambda e: e.scalar_tensor_tensor(out=acc_out, in0=hs_[:, 0:2048], scalar=cwb[:, c, 0:1],
                                                          in1=acc_out, op0=ALU.mult, op1=ALU.add)],
                 reads=[r_hs, r_cwb] + r_acc_list, writes=r_acc_list)
            S.op("dve", [lambda e: e.scalar_tensor_tensor(out=out_final[0], in0=hs_[:, 2:2050], scalar=cwb[:, c, 2:3],
                                                          in1=acc_out, op0=ALU.mult, op1=ALU.add)],
                 reads=[r_hs, r_cwb] + r_acc_list, writes=out_final[1])

        for ci_, c in enumerate(order):
            if debug and debug.startswith('s_p') and ci_ == int(debug[3:]):
                S.barrier()
                S.emit()
                return nc
            if c >= 18:
                hp = c - 18

                def sink_mq(tq, b, hp=hp):
                    act_copy(MQ[:, hp, tq * 512:(tq + 1) * 512], bank(b), [PB[b]], [r_MQ])
                proj_chunk(w0v, c * 128, sink_mq)
                continue
            si = c % 2
            hs_ = hsb0[si]

            def sink_h(tq, b, hs_=hs_, si=si):
                act_copy(hs_[:, 1 + tq * 512:1 + (tq + 1) * 512], bank(b), [PB[b]], [r_hsb0[si]])
            proj_chunk(w0v, c * 128, sink_h)
            if c < 6:
                conv_chunk(c, hs_, r_hsb0[si], accv, [r_accv], (YT[:, c, :], [r_YT]))
            elif c < 12:
                conv_chunk(c, hs_, r_hsb0[si], accx, [r_accx], (accx, [r_accx]))
            else:
                i6 = c - 12
                conv_chunk(c, hs_, r_hsb0[si], accv, [r_accv], (accv, [r_accv]))
                S.op("dve", [lambda e: e.tensor_tensor(out=zT, in0=accv, in1=accx, op=ALU.mult)],
                     reads=[r_accv, r_accx], writes=[r_zT])
                for g8 in range(2):
                    b = nxt([6, 7])
                    bb = bank_bf(b)
                    fns = [lambda e, t8=t8, bb=bb, g8=g8: e.transpose(bb[:, t8 * 128:(t8 + 1) * 128],
                                                                     zT[:, (g8 * 8 + t8) * 128:(g8 * 8 + t8 + 1) * 128], ident[:])
                           for t8 in range(8)]
                    S.op("pe", fns, reads=[r_zT, r_consts], writes=[PB[b]])
                    S.op("act", [lambda e, g8=g8, bb=bb, i6=i6: e.activation(
                        out=Z[:, g8 * 8:(g8 + 1) * 8, i6 * 128:(i6 + 1) * 128],
                        in_=bb.rearrange("p (k m) -> p k m", k=8), func=AF.Identity)],
                         reads=[PB[b]], writes=[r_Z])
        S.barrier()
        if debug == "z":
            for tile in range(16):
                S.op("act", [lambda e, tile=tile: e.activation(out=X[:, tile, 0:768] if False else hsb0[0][:, 0:768], in_=Z[:, tile, :], func=AF.Identity)],
                     reads=[r_Z], writes=[r_hsb0[0]])
                S.dma("sp", dbg_d[tile * 128:(tile + 1) * 128, 0:768], hsb0[0][:, 0:768], reads=[r_hsb0[0]], key="dbg")
            S.barrier()
            S.emit()
            return nc
        mem_attention(R_X + 24 * K_)
        S.barrier()

        YRE = AV(R_XT, [16, 768], BF16)
        YIM = AV(R_X + 24 * K_, [16, 768], BF16)
        r_Y = Res("Yspec")
        ct = [AV(R_X + 48 * K_ + i * 3 * K_, [768], F32) for i in range(4)]
        r_ct = [Res("ct%d" % i) for i in range(4)]
        ftabU = [AV(R_T + 48 * K_, [2, 16, 128], BF16), AV(R_MQ, [2, 16, 128], BF16)]
        r_ftabU = [Res("ftabU0"), Res("ftabU1")]

        def u_epilogue(fc, pre, pim, rbs):
            kre, kim = KS[:, 0, fc, :], KS[:, 1, fc, :]
            S.op("dve", [lambda e: e.tensor_tensor(out=ct[0], in0=pre, in1=kre, op=ALU.mult)], reads=rbs[0:2] + [r_KS], writes=[r_ct[0]])
            S.op("dve", [lambda e: e.tensor_tensor(out=ct[1], in0=pim, in1=kim, op=ALU.mult)], reads=rbs[2:4] + [r_KS], writes=[r_ct[1]])
            S.op("dve", [lambda e: e.tensor_tensor(out=ct[2], in0=pre, in1=kim, op=ALU.mult)], reads=rbs[0:2] + [r_KS], writes=[r_ct[2]])
            S.op("dve", [lambda e: e.tensor_tensor(out=ct[3], in0=pim, in1=kre, op=ALU.mult)], reads=rbs[2:4] + [r_KS], writes=[r_ct[3]])
            S.op("pool", [lambda e: e.tensor_tensor(out=YRE[:, fc, :], in0=ct[0], in1=ct[1], op=ALU.subtract)],
                 reads=[r_ct[0], r_ct[1]], writes=[r_Y])
            S.op("pool", [lambda e: e.tensor_tensor(out=YIM[:, fc, :], in0=ct[2], in1=ct[3], op=ALU.add)],
                 reads=[r_ct[2], r_ct[3]], writes=[r_Y])

        fwd_pass(Z, Z, r_Z, ftabU, r_ftabU, u_epilogue)
        S.barrier()

        itab = [AV(R_T + 48 * K_, [4, 2, 512], BF16), AV(R_MQ, [4, 2, 512], BF16)]
        r_itab = [Res("itab0"), Res("itab1")]
        _it = 0
        for tt in range(4):
            fn_all = []
            for fg in range(4):
                si = _it % 2
                _it += 1
                S.dma("sp", itab[si].rearrange("p a b c -> p (a b c)"), inv_tab_d[tt, fg], writes=[r_itab[si]], key="itab%d" % si)
                fns = []
                for fi in range(4):
                    fc = fg * 4 + fi
                    for ri in range(2):
                        Ysrc = YRE if ri == 0 else YIM
                        for cc in range(6):
                            first = (fc == 0 and ri == 0)
                            lastm = (fc == 15 and ri == 1)
                            fns.append(lambda e, cc=cc, Ysrc=Ysrc, fc=fc, si=si, fi=fi, ri=ri, first=first, lastm=lastm: e.matmul(
                                bank(cc), Ysrc[:, fc, cc * 128:(cc + 1) * 128], itab[si][:, fi, ri, :], start=first, stop=lastm))
                S.op("pe", fns, reads=[r_itab[si], r_Y], writes=[PB[cc] for cc in range(6)])
            for cc in range(6):
                S.op("dve", [lambda e, cc=cc, tt=tt: e.tensor_tensor(out=YT[:, cc, tt * 512:(tt + 1) * 512], in0=bank(cc),
                                                                    in1=YT[:, cc, tt * 512:(tt + 1) * 512], op=ALU.mult)],
                     reads=[PB[cc], r_YT], writes=[r_YT])
        S.barrier()

        if debug == "mix0":
            for c in range(8):
                src = YT[:, c, 0:1024] if c < 6 else YM[:, c - 6, 0:1024]
                S.op("act", [lambda e, src=src: e.activation(out=rbuf[0], in_=src, func=AF.Identity)], reads=[r_YT, r_YM], writes=[r_rbuf[0]])
                S.dma("sp", dbg_d[c * 128:(c + 1) * 128, :], rbuf[0], reads=[r_rbuf[0]], key="dbg")
            S.barrier()
            S.emit()
            return nc

        out_proj_ln1(0)
        S.barrier()
        if debug == "ln1_0":
            for tile in range(16):
                S.dma("sp", dbg_d[tile * 128:(tile + 1) * 128, :], X[:, tile, :], reads=[r_X[tile]], key="dbg")
            S.barrier()
            S.emit()
            return nc
        ffn_ln2(0, final=(debug == "l0"))
        S.barrier()

        if debug is None or debug.startswith("l1"):
            if debug == "l1_s":
                S.barrier()
                S.emit()
                return nc
            w1v = w_in_d[1].rearrange("(kc p) n -> p kc n", p=128)
            QT = YT
            r_QT = [Res("QT%d" % i) for i in range(12)]
            KT = AV(R_T, [4, 2048], BF16)
            r_KT = Res("KT")
            VT = AV(R_T + 16 * K_, [16, 256], BF16)
            r_VT = Res("VT")
            ropec = AV(R_T + 24 * K_, [2048], F32)
            ropes = AV(R_T + 32 * K_, [2048], F32)
            r_rope = Res("rope")
            wch1 = [AV(R_T + 40 * K_ + i * 2 * K_, [8, 128], BF16) for i in range(3)]
            r_wch1 = [Res("wch1_%d" % i) for i in range(3)]
            qsb = [AV(R_T + 46 * K_ + i * K_, [512], BF16) for i in range(2)]
            r_qsb = [Res("qsb%d" % i) for i in range(2)]
            rt1 = AV(R_T + 48 * K_, [512], F32)
            rt2 = AV(R_T + 50 * K_, [512], F32)
            r_rt1, r_rt2 = Res("rt1"), Res("rt2")
            wvt = AV(R_T + 52 * K_, [8, 256], BF16)
            r_wvt = Res("wvt")
            esk = small("esk", [64, 12], F32)
            r_esk = Res("esk")
            import os
            SK = os.environ.get("SKIP", "")
            if "r" not in SK:
                S.dma("sp", ropec, ropec_d[:, :], writes=[r_rope], key="rope0")
                S.dma("sp", ropes, ropes_d[:, :], writes=[r_rope], key="rope1")
            if "e" not in SK:
                S.dma("sp", esk[:], sink_d.partition_broadcast(64), writes=[r_esk], key="esk")
            if "x" not in SK:
                S.op("act", [lambda e: e.activation(out=esk[:], in_=esk[:], func=(AF.Identity if "I" in SK else AF.Exp))], reads=[r_esk], writes=[r_esk])
            if "w" not in SK:
                S.dma("pool", wvt, w1v[:, :, 1024:1280], writes=[r_wvt], key="wvt")
            if debug == "l1_p0":
                S.barrier()
                S.emit()
                return nc
            _w1 = [0]
            _rq = [0]

            def proj1(loads, sink_fn):
                wi = _w1[0] % 3
                _w1[0] += 1
                for k_, (dst0, dst1, c0, c1) in enumerate(loads):
                    S.dma("pool", wch1[wi][:, :, dst0:dst1], w1v[:, :, c0:c1], writes=[r_wch1[wi]], key="wch1_%d_%d" % (wi, k_))
                for tq in range(4):
                    b = nxt([0, 1, 2, 3])
                    mm_group(bank(b), [(wch1[wi][:, kc, :], XT[:, kc, tq * 512:(tq + 1) * 512]) for kc in range(8)],
                             reads=[r_wch1[wi], r_XT], writes=[PB[b]])
                    sink_fn(tq, b)

            def rope_sink(dest, r_dest):
                def sink(tq, b):
                    cs = slice(tq * 512, (tq + 1) * 512)
                    i = _rq[0] % 2
                    _rq[0] += 1
                    S.op("act", [lambda e: e.activation(out=qsb[i], in_=bank(b), func=AF.Identity)],
                         reads=[PB[b]], writes=[r_qsb[i]])
                    b2 = [4, 5][i]
                    mm_group(bank(b2), [(pm[:], qsb[i])], reads=[r_consts, r_qsb[i]], writes=[PB[b2]])
                    S.op("dve", [lambda e: e.tensor_tensor(out=rt1, in0=bank(b), in1=ropec[:, cs], op=ALU.mult)],
                         reads=[PB[b], r_rope], writes=[r_rt1])
                    S.op("dve", [lambda e: e.tensor_tensor(out=rt2, in0=bank(b2), in1=ropes[:, cs], op=ALU.mult)],
                         reads=[PB[b2], r_rope], writes=[r_rt2])
                    S.op("dve", [lambda e: e.tensor_tensor(out=dest[:, cs], in0=rt1, in1=rt2, op=ALU.add)],
                         reads=[r_rt1, r_rt2], writes=r_dest)
                return sink

            for c in range(6):
                proj1([(0, 128, c * 128, (c + 1) * 128)], rope_sink(QT[:, c, :], [r_QT[2 * c], r_QT[2 * c + 1]]))
            if debug == "l1_p1":
                S.barrier()
                S.emit()
                return nc
            for g in range(4):
                c0 = 768 + g * 64
                proj1([(0, 64, c0, c0 + 64), (64, 128, c0, c0 + 64)], rope_sink(KT[:, g, :], [r_KT]))
            if debug == "l1_p2":
                S.barrier()
                S.emit()
                return nc
            for hp in range(2):
                def sink_mq1(tq, b, hp=hp):
                    act_copy(MQ[:, hp, tq * 512:(tq + 1) * 512], bank(b), [PB[b]], [r_MQ])
                proj1([(0, 128, 1280 + hp * 128, 1280 + (hp + 1) * 128)], sink_mq1)
            for tile in range(16):
                b = nxt([0, 1, 2, 3])
                mm_group(bank(b, 256), [(XT[:, kc, tile * 128:(tile + 1) * 128], wvt[:, kc, :]) for kc in range(8)],
                         reads=[r_XT, r_wvt], writes=[PB[b]])
                act_copy(VT[:, tile, :], bank(b, 256), [PB[b]], [r_VT])
            S.barrier()

            if debug == "l1_p":
                S.barrier()
                S.emit()
                return nc
            PTs = [AV(R_XT + i * 12 * K_, [16, 384], BF16) for i in range(2)]
            r_PTs = [Res("PTs%d" % i) for i in range(2)]
            rden1 = AV(R_XT + 24 * K_, [512], F32)
            r_rden1 = Res("rden1")
            for h in range(12):
                g = h // 3
                hh = h % 2
                c = h // 2
                prow = slice(hh * 64, (hh + 1) * 64)
                pt = PTs[h % 2]
                r_pt = r_PTs[h % 2]
                for j in range(16):
                    qlo = max(0, j - 1) * 128
                    qhi = min(16, j + 2) * 128
                    n = qhi - qlo
                    moff = qlo - (j - 1) * 128
                    b = nxt([0, 1, 2, 3])
                    fns = [
                        lambda e, b=b, n=n, j=j, qlo=qlo, qhi=qhi, prow=prow, g=g, c=c: e.matmul(
                            bank(b, n), KT[prow, g, j * 128:(j + 1) * 128], QT[prow, c, qlo:qhi], start=True, stop=False),
                        lambda e, b=b, n=n, moff=moff: e.matmul(
                            bank(b, n), ident[:], maskb[:, moff:moff + n], start=False, stop=True),
                    ]
                    S.op("pe", fns, reads=[r_KT, r_QT[h], r_consts], writes=[PB[b]])
                    S.op("act", [lambda e, b=b, n=n, j=j, pt=pt: e.activation(out=pt[:, j, 0:n], in_=bank(b, n), func=AF.Exp, scale=0.125)],
                         reads=[PB[b]], writes=[r_pt])
                for qt in range(4):
                    bo, bd = [4, 6][qt % 2], [5, 7][qt % 2]
                    fo, fd = [], []
                    for i4 in range(4):
                        qb = 4 * qt + i4
                        js = [j for j in (qb - 1, qb, qb + 1) if 0 <= j < 16]
                        for k_, j in enumerate(js):
                            lc = (qb - max(0, j - 1)) * 128
                            st_, sp_ = (k_ == 0), (k_ == len(js) - 1)
                            fo.append(lambda e, i4=i4, j=j, lc=lc, st_=st_, sp_=sp_, bo=bo, pt=pt, g=g: e.matmul(
                                ps[0:64, bo * 512 + i4 * 128:bo * 512 + (i4 + 1) * 128], VT[:, j, g * 64:(g + 1) * 64],
                                pt[:, j, lc:lc + 128], start=st_, stop=sp_))
                            fd.append(lambda e, i4=i4, j=j, lc=lc, st_=st_, sp_=sp_, bd=bd, pt=pt: e.matmul(
                                ps[0:64, bd * 512 + i4 * 128:bd * 512 + (i4 + 1) * 128], ones64[:],
                                pt[:, j, lc:lc + 128], start=st_, stop=sp_))
                    S.op("pe", fo, reads=[r_pt, r_VT], writes=[PB[bo]])
                    S.op("pe", fd, reads=[r_pt, r_consts], writes=[PB[bd]])
                    S.op("dve", [lambda e, bd=bd, h=h: e.tensor_scalar(out=rden1[0:64, :], in0=bank(bd, 512, 0, 64),
                                                                     scalar1=esk[:, h:h + 1], scalar2=None, op0=ALU.add)],
                         reads=[PB[bd], r_esk], writes=[r_rden1])
                    S.op("dve", [lambda e: e.reciprocal(out=rden1[0:64, :], in_=rden1[0:64, :])], reads=[r_rden1], writes=[r_rden1])
                    S.op("dve", [lambda e, bo=bo, qt=qt, prow=prow, c=c: e.tensor_tensor(
                        out=QT[prow, c, qt * 512:(qt + 1) * 512], in0=bank(bo, 512, 0, 64), in1=rden1[0:64, :], op=ALU.mult)],
                         reads=[PB[bo], r_rden1], writes=[r_QT[h]])
            S.barrier()
            if debug == "l1_a":
                S.barrier()
                S.emit()
                return nc
            mem_attention(R_XT + 26 * K_ - 2 * K_)
            S.barrier()
            if debug == "l1mix":
                for c in range(8):
                    src = YT[:, c, 0:1024] if c < 6 else YM[:, c - 6, 0:1024]
                    S.op("act", [lambda e, src=src: e.activation(out=rbuf[0], in_=src, func=AF.Identity)], reads=[r_YM], writes=[r_rbuf[0]])
                    S.dma("sp", dbg_d[c * 128:(c + 1) * 128, :], rbuf[0], reads=[r_rbuf[0]], key="dbg")
                S.barrier()
                S.emit()
                return nc
            out_proj_ln1(1)
            S.barrier()
            ffn_ln2(1, final=True)
            S.barrier()

        S.barrier()
        S.emit()
    return nc


def prep_shared(inputs):
    f32 = np.float32
    sh = {}
    for k in ("w_mem_kv", "l0_w_in", "l1_w_in", "l0_w_out", "l1_w_out", "l0_ffn_w_up", "l1_ffn_w_up",
              "l0_ffn_w_down", "l1_ffn_w_down", "l0_filt_w1", "l0_filt_w2", "l0_filt_w3", "l0_filt_w_out",
              "l0_hyena_d", "l1_sink"):
        sh[k] = np.ascontiguousarray(np.asarray(inputs[k], dtype=f32))
    for i in range(2):
        for n in ("ln1_g", "ln1_b", "ln2_g", "ln2_b"):
            sh["l%d_%s" % (i, n)] = np.ascontiguousarray(np.asarray(inputs["l%d_%s" % (i, n)], dtype=f32))
        cw = np.asarray(inputs["l%d_ffn_conv_w" % i], f32)
        cb = np.asarray(inputs["l%d_ffn_conv_b" % i], f32)
        a = np.concatenate([cw, cb[None, :]], axis=0)
        sh["l%d_fcwb" % i] = np.ascontiguousarray(a.reshape(4, 44, 128).transpose(2, 1, 0))
    cw = np.asarray(inputs["l0_conv_w"], f32)
    cb = np.asarray(inputs["l0_conv_b"], f32)
    a = np.concatenate([cw, cb[None, :]], axis=0)
    sh["l0_cwb"] = np.ascontiguousarray(a.reshape(4, 18, 128).transpose(2, 1, 0))
    sh["l0_fbf"] = np.ascontiguousarray(np.stack(
        [np.asarray(inputs["l0_filt_%s%d" % (n, l)], f32) for l in (1, 2, 3) for n in ("b", "f")], axis=1))
    sh.update(const_tables())
    return sh


_NC_CACHE = {}


def kernel(**inputs):
    sh = prep_shared(inputs)
    x = np.asarray(inputs["x"], np.float32)
    mem = np.asarray(inputs["mem"], np.float32)
    if "nc" not in _NC_CACHE:
        _NC_CACHE["nc"] = build_program()
    nc = _NC_CACHE["nc"]
    in_maps = []
    for b in range(8):
        m = dict(sh)
        m["x"] = np.ascontiguousarray(x[b])
        m["mem"] = np.ascontiguousarray(mem[b])
        in_maps.append(m)
    res = run_bass_kernel_spmd(nc, in_maps, core_ids=list(range(8)))
    return np.stack([np.asarray(r["out"], np.float32) for r in res.results], axis=0)
```

```python
import math
from contextlib import ExitStack

import numpy as np
import ml_dtypes
import concourse.bass as bass
import concourse.mybir as mybir
from concourse.bass_utils import run_bass_kernel_spmd

F32 = mybir.dt.float32
BF16 = mybir.dt.bfloat16
AF = mybir.ActivationFunctionType
ALU = mybir.AluOpType
AX = mybir.AxisListType

L = 2048
D = 1024
NT = 16
KC = 8
DFF = 2816
NFF = 22
ALPHA = 4.0 ** 0.25
EPS = 1e-5
PI = float(np.pi)


class Res:
    __slots__ = ("name", "w", "r", "excl")

    def __init__(self, name, excl=False):
        self.name = name
        self.w = None
        self.r = []
        self.excl = excl


class Sched:
    def __init__(self, nc, es):
        self.nc = nc
        self.es = es
        self.engs = {}
        for n in ("pe", "act", "dve", "pool", "sp"):
            sem = es.enter_context(nc.semaphore("s_" + n))
            self.engs[n] = dict(sem=sem, count=0, known={}, ops=[])
        self.dma_sems = {}
        self.n_dma_sems = 0

    def dma_sem(self, key):
        if key not in self.dma_sems:
            sem = self.es.enter_context(self.nc.semaphore("d%d" % self.n_dma_sems))
            self.n_dma_sems += 1
            self.dma_sems[key] = [sem, 0]
        return self.dma_sems[key]

    def op(self, eng, fns, reads=(), writes=(), dma_key=None):
        E = self.engs[eng]
        deps = {}

        def add(ev):
            if ev is None:
                return
            sem, val = ev
            if deps.get(sem, 0) < val:
                deps[sem] = val

        excl_reads = [r for r in reads if r.excl]
        writes = list(writes) + [r for r in excl_reads if r not in writes]
        reads = [r for r in reads if not r.excl]
        for r in reads:
            add(r.w)
        for w in writes:
            add(w.w)
            for ev in w.r:
                add(ev)
        waits = []
        for sem, val in deps.items():
            if sem is E["sem"] and eng == "pe" and dma_key is None:
                continue
            if E["known"].get(sem, 0) >= val:
                continue
            E["known"][sem] = val
            waits.append((sem, val))
        if dma_key is not None:
            ds = self.dma_sem(dma_key)
            ds[1] += 16
            ev = (ds[0], ds[1])
            inc = (ds[0], 16)
        else:
            E["count"] += 1
            ev = (E["sem"], E["count"])
            inc = (E["sem"], 1)
        for r in reads:
            r.r.append(ev)
        for w in writes:
            w.w = ev
            w.r = []
        if not isinstance(fns, (list, tuple)):
            fns = [fns]
        E["ops"].append((waits, list(fns), inc))
        return ev

    def dma(self, queue, out, in_, reads=(), writes=(), key=None):
        assert key is not None
        return self.op(queue, [lambda e: e.dma_start(out=out, in_=in_)], reads, writes, dma_key=key)

    def barrier(self):
        evs = [(E["sem"], E["count"]) for E in self.engs.values() if E["count"] > 0]
        evs += [(s, c) for (s, c) in self.dma_sems.values() if c > 0]
        for n, E in self.engs.items():
            waits = []
            for sem, val in evs:
                if E["known"].get(sem, 0) >= val:
                    continue
                if sem is E["sem"] and n == "pe":
                    continue
                E["known"][sem] = val
                waits.append((sem, val))
            if waits:
                E["ops"].append((waits, [], None))

    def emit(self):
        nc = self.nc
        import sys
        print("SCHED counts", {n: E["count"] for n, E in self.engs.items()}, "dma", {k: v[1] for k, v in self.dma_sems.items()}, file=sys.stderr)

        def run(name):
            def f(e):
                for waits, fns, inc in self.engs[name]["ops"]:
                    for sem, val in waits:
                        e.wait_ge(sem, val)
                    n = len(fns)
                    for i, fn in enumerate(fns):
                        ins = fn(e)
                        if i == n - 1:
                            ins.then_inc(inc[0], inc[1])
            return f

        with nc.Block() as block:
            block.tensor(run("pe"))
            block.scalar(run("act"))
            block.vector(run("dve"))
            block.gpsimd(run("pool"))
            block.sync(run("sp"))


def _bf(a):
    return np.ascontiguousarray(a.astype(ml_dtypes.bfloat16))


_CONST_CACHE = {}


def const_tables():
    if _CONST_CACHE:
        return _CONST_CACHE
    N = 4096
    f = np.arange(2048, dtype=np.float64)
    t = np.arange(2048, dtype=np.float64)
    m = np.mod(np.outer(2 * f + 1, t), 2 * N)
    ang = np.pi * m / N
    C = np.cos(ang)
    Sn = np.sin(ang)
    CT = C.T.reshape(16, 128, 16, 128)
    ST = (-Sn).T.reshape(16, 128, 16, 128)
    fwd = np.stack([CT, ST], axis=0)
    fwd = fwd.transpose(3, 2, 0, 1, 4)
    _CONST_CACHE["fwd_tab"] = _bf(fwd.reshape(16, 128, 2 * 16 * 128))
    Ci = (C / 2048.0).reshape(4, 4, 128, 4, 512)
    Si = (-Sn / 2048.0).reshape(4, 4, 128, 4, 512)
    inv = np.stack([Ci, Si], axis=0)
    inv = inv.transpose(4, 1, 3, 2, 0, 5)
    _CONST_CACHE["inv_tab"] = _bf(inv.reshape(4, 4, 128, 4 * 2 * 512))
    f32 = np.float32
    tl = np.linspace(0.0, 1.0, L, dtype=f32)[:, None]
    w = (f32(2.0 * math.pi) * np.arange(L, dtype=f32)[:, None] / f32(L)).astype(f32)
    fr = np.linspace(1e-4, 15, 16, dtype=f32)[None, :]
    z = np.concatenate([tl, np.cos(fr * w), -np.sin(fr * w)], axis=-1).astype(f32)
    _CONST_CACHE["zfT"] = np.ascontiguousarray(z.T)
    _CONST_CACHE["ntn"] = np.ascontiguousarray((-tl[:, 0]).reshape(16, 128).T.astype(f32))
    min_decay = math.log(1e-2) / 1.5
    max_decay = math.log(1e-2) / 0.3
    _CONST_CACHE["deltas"] = np.abs(np.linspace(min_decay, max_decay, 768, dtype=f32)).astype(f32)
    inv_f = (10000.0 ** (-np.arange(0, 64, 2, dtype=f32) / f32(64))).astype(f32)
    angr = (np.arange(L, dtype=f32)[:, None] * inv_f[None, :]).astype(f32)
    angr = np.concatenate([angr, angr], axis=-1)
    cosT = np.cos(angr).T.astype(f32)
    sinT = np.sin(angr).T.astype(f32)
    _CONST_CACHE["ropec"] = np.ascontiguousarray(np.concatenate([cosT, cosT], axis=0))
    _CONST_CACHE["ropes"] = np.ascontiguousarray(np.concatenate([sinT, sinT], axis=0))
    Pm = np.zeros((128, 128), np.float32)
    for po in range(128):
        d = po % 64
        if d < 32:
            Pm[po + 32, po] = -1.0
        else:
            Pm[po - 32, po] = 1.0
    _CONST_CACHE["pm"] = _bf(Pm)
    _CONST_CACHE["ident"] = _bf(np.eye(128, dtype=np.float32))
    k = np.arange(128)[:, None]
    q = np.arange(128)[None, :]
    NEG = -30000.0
    m_next = np.where(k <= q, 0.0, NEG)
    m_prev = np.where(k >= q, 0.0, NEG)
    _CONST_CACHE["maskb"] = _bf(np.concatenate([m_next, np.zeros((128, 128)), m_prev], axis=1))
    return _CONST_CACHE


def build_program(debug=None):
    nc = bass.Bass("TRN2", target_bir_lowering=False)
    dbg = {}

    def din(name, shape, dt=F32):
        return nc.dram_tensor(name, list(shape), dt, kind="ExternalInput").ap()

    x_d = din("x", [L, D])
    mem_d = din("mem", [256, D])
    wkv_d = din("w_mem_kv", [D, 512])
    w_in_d = [din("l0_w_in", [D, 2560]), din("l1_w_in", [D, 1536])]
    w_out_d = [din("l0_w_out", [D, D]), din("l1_w_out", [D, D])]
    w_up_d = [din("l%d_ffn_w_up" % i, [D, 2 * DFF]) for i in range(2)]
    w_dn_d = [din("l%d_ffn_w_down" % i, [DFF, D]) for i in range(2)]
    ln_d = [[din("l%d_%s" % (i, n), [D]) for n in ("ln1_g", "ln1_b", "ln2_g", "ln2_b")] for i in range(2)]
    fcwb_d = [din("l%d_fcwb" % i, [128, 44, 4]) for i in range(2)]
    cwb_d = din("l0_cwb", [128, 18, 4])
    fw1_d = din("l0_filt_w1", [33, 64])
    fw2_d = din("l0_filt_w2", [64, 64])
    fw3_d = din("l0_filt_w3", [64, 64])
    fwo_d = din("l0_filt_w_out", [64, 1536])
    fbf_d = din("l0_fbf", [64, 6])
    hd_d = din("l0_hyena_d", [768])
    sink_d = din("l1_sink", [12])
    fwd_tab_d = din("fwd_tab", [16, 128, 4096], BF16)
    inv_tab_d = din("inv_tab", [4, 4, 128, 4096], BF16)
    zfT_d = din("zfT", [33, L])
    ntn_d = din("ntn", [128, 16])
    deltas_d = din("deltas", [768])
    ropec_d = din("ropec", [128, L])
    ropes_d = din("ropes", [128, L])
    pm_d = din("pm", [128, 128], BF16)
    ident_d = din("ident", [128, 128], BF16)
    maskb_d = din("maskb", [128, 384], BF16)
    out_d = nc.dram_tensor("out", [L, D], F32, kind="ExternalOutput").ap()
    if debug:
        dbg_d = nc.dram_tensor("dbg", [L, D], F32, kind="ExternalOutput").ap()

    es = ExitStack()
    with es:
        S = Sched(nc, es)
        AR_BYTES = 194 * 1024
        arena = es.enter_context(nc.sbuf_tensor("arena", [128, AR_BYTES // 2], BF16))
        ps = es.enter_context(nc.psum_tensor("ps", [128, 4096], F32))
        PB = [Res("pb%d" % i, excl=True) for i in range(8)]

        def bank(b, n=512, p0=0, p1=128):
            return ps[p0:p1, b * 512:b * 512 + n]

        def bank_bf(b):
            return ps[:, b * 512:(b + 1) * 512].bitcast(BF16)

        def AV(off, shape, dt):
            n = int(np.prod(shape))
            if dt == F32:
                v = arena[:, off // 2: off // 2 + 2 * n].bitcast(F32)
            else:
                v = arena[:, off // 2: off // 2 + n]
            if len(shape) == 2:
                v = v.rearrange("p (a b) -> p a b", a=shape[0])
            elif len(shape) == 3:
                v = v.rearrange("p (a b c) -> p a b c", a=shape[0], b=shape[1])
            elif len(shape) == 4:
                v = v.rearrange("p (a b c d) -> p a b c d", a=shape[0], b=shape[1], c=shape[2])
            return v

        K_ = 1024
        R_X, R_XT, R_Y, R_YM, R_MQ, R_T = 0, 64 * K_, 96 * K_, 120 * K_, 128 * K_, 136 * K_

        def small(name, shape, dt):
            return es.enter_context(nc.sbuf_tensor("sb_" + name, list(shape), dt))

        ident = small("ident", [128, 128], BF16)
        pm = small("pm", [128, 128], BF16)
        maskb = small("maskb", [128, 384], BF16)
        ones64 = small("ones64", [128, 64], BF16)
        memKT = small("memKT", [128, 2, 256], BF16)
        memV = small("memV", [128, 2, 256], BF16)
        cwb = small("cwb", [128, 18, 4], F32)
        fcwb = small("fcwb", [128, 44, 4], F32)
        stat = small("stat", [128, 128], F32)
        GB = small("GB", [128, 2, 1024], F32)
        r_consts = Res("consts")
        r_memKV = Res("memKV")
        r_cwb = Res("cwb")
        r_fcwb = Res("fcwb")
        r_GB = Res("GB")

        X = AV(R_X, [16, 1024], F32)
        XT = AV(R_XT, [8, 2048], BF16)
        YT = AV(R_Y, [6, 2048], BF16)
        YM = AV(R_YM, [2, 2048], BF16)
        MQ = AV(R_MQ, [2, 2048], BF16)
        r_X = [Res("X%d" % i) for i in range(16)]
        r_XT = Res("XT")
        r_YT = Res("YT")
        r_YM = Res("YM")
        r_MQ = Res("MQ")

        _bk = [0]

        def nxt(lst):
            b = lst[_bk[0] % len(lst)]
            _bk[0] += 1
            return b

        def mm_group(out, pairs, reads, writes):
            n = len(pairs)
            fns = []
            for i, (l, r) in enumerate(pairs):
                fns.append(lambda e, l=l, r=r, i=i: e.matmul(out, l, r, start=(i == 0), stop=(i == n - 1)))
            return S.op("pe", fns, reads, writes)

        def act_copy(out, in_, reads, writes):
            return S.op("act", [lambda e: e.activation(out=out, in_=in_, func=AF.Identity)], reads, writes)

        S.dma("sp", ident[:], ident_d[:, :], writes=[r_consts], key="c0")
        S.dma("sp", pm[:], pm_d[:, :], writes=[r_consts], key="c1")
        S.dma("sp", maskb[:], maskb_d[:, :], writes=[r_consts], key="c2")
        S.dma("sp", cwb[:], cwb_d[:, :, :], writes=[r_cwb], key="c3")
        S.op("dve", [lambda e: e.memset(ones64[:], 1.0)], writes=[r_consts])
        epst = small("epst", [128, 1], F32)
        S.op("dve", [lambda e: e.memset(epst[:], EPS)], writes=[r_consts])


        def transpose_to_XT(src_bf, tile, r_src):
            b = nxt([6, 7])
            bb = bank_bf(b)
            fns = [lambda e, kc=kc: e.transpose(bb[:, kc * 128:(kc + 1) * 128], src_bf[:, kc * 128:(kc + 1) * 128], ident[:])
                   for kc in range(8)]
            S.op("pe", fns, reads=[r_src, r_consts], writes=[PB[b]])
            S.op("act", [lambda e: e.activation(out=XT[:, :, tile * 128:(tile + 1) * 128],
                                                in_=bb.rearrange("p (k m) -> p k m", k=8), func=AF.Identity)],
                 reads=[PB[b]], writes=[r_XT])

        def load_ln(layer, which):
            g_d, b_d = ln_d[layer][2 * which], ln_d[layer][2 * which + 1]
            S.dma("sp", GB[:, 0, :], g_d.partition_broadcast(128), writes=[r_GB], key="gb0")
            S.dma("sp", GB[:, 1, :], b_d.partition_broadcast(128), writes=[r_GB], key="gb1")

        NRB = 6
        LN_LAG = 3
        _rboff = [46 * K_, 50 * K_, 28672, 28672 + 4096, 36880, 36880 + 4096]
        rbuf = [AV(R_T + _rboff[i], [1024], F32) for i in range(NRB)]
        _xboff = [54 * K_, 56 * K_, 24 * K_, 26 * K_]
        xbuf = [AV(R_T + _xboff[i], [1024], BF16) for i in range(4)]
        r_rbuf = [Res("rbuf%d" % i) for i in range(NRB)]
        r_xbuf = [Res("xbuf%d" % i) for i in range(4)]
        r_stat = Res("stat")
        r_stats = [Res("stat%d" % i) for i in range(8)]
        _ln = [0]

        def ln_epilogue(tile, r_in, r_r, final_out):
            i = _ln[0] % 2
            _ln[0] += 1
            so = (tile % NRB) * 16
            r_stat = r_stats[tile % NRB]
            st6 = stat[:, so:so + 12]
            mv = stat[:, so + 12:so + 14]
            rstd = stat[:, so + 14:so + 15]
            nmr = stat[:, so + 15:so + 16]
            S.op("dve", [lambda e: e.bn_stats(out=st6[:, 0:6], in_=r_in[:, 0:512])], reads=[r_r], writes=[r_stat])
            S.op("dve", [lambda e: e.bn_stats(out=st6[:, 6:12], in_=r_in[:, 512:1024])], reads=[r_r], writes=[r_stat])
            S.op("dve", [lambda e: e.bn_aggr(out=mv, in_=st6)], reads=[r_stat], writes=[r_stat])
            S.op("act", [lambda e: e.activation(out=rstd, in_=mv[:, 1:2], func=AF.Sqrt, bias=epst[:, 0:1])],
                 reads=[r_stat, r_consts], writes=[r_stat])
            S.op("dve", [lambda e: e.reciprocal(out=rstd, in_=rstd)], reads=[r_stat], writes=[r_stat])
            S.op("dve", [lambda e: e.scalar_tensor_tensor(out=nmr, in0=mv[:, 0:1], scalar=-1.0, in1=rstd,
                                                          op0=ALU.mult, op1=ALU.mult)], reads=[r_stat], writes=[r_stat])
            S.op("act", [lambda e: e.activation(out=r_in, in_=r_in, func=AF.Identity, scale=rstd, bias=nmr)],
                 reads=[r_r, r_stat], writes=[r_r])
            S.op("dve", [lambda e: e.tensor_tensor(out=r_in, in0=r_in, in1=GB[:, 0, :], op=ALU.mult)],
                 reads=[r_r, r_GB], writes=[r_r])
            S.op("pool", [lambda e: e.tensor_tensor(out=X[:, tile, :], in0=r_in, in1=GB[:, 1, :], op=ALU.add)],
                 reads=[r_r, r_GB], writes=[r_X[tile]])
            if final_out:
                S.dma("sp", out_d[tile * 128:(tile + 1) * 128, :], X[:, tile, :], reads=[r_X[tile]], key="out%d" % (tile % 4))
                return None
            else:
                i4 = tile % 4
                xb = xbuf[i4]
                S.op("act", [lambda e: e.activation(out=xb, in_=X[:, tile, :], func=AF.Identity)],
                     reads=[r_X[tile]], writes=[r_xbuf[i4]])
                return lambda: transpose_to_XT(xb, tile, r_xbuf[i4])

        def mem_attention(toff):
            PT = [AV(toff + i * K_, [512], BF16) for i in range(4)]
            r_PT = [Res("mpt%d" % i) for i in range(4)]
            rden = AV(toff + 4 * K_, [512], F32)
            r_rden = Res("mrden")
            for hp in range(2):
                for qt in range(4):
                    qs = slice(qt * 512, (qt + 1) * 512)
                    for hh in range(2):
                        h = 2 * hp + hh
                        prow = slice(hh * 64, (hh + 1) * 64)
                        pts = []
                        for mt in range(2):
                            b = nxt([0, 1, 2, 3])
                            mm_group(bank(b), [(memKT[prow, hp, mt * 128:(mt + 1) * 128], MQ[prow, hp, qs])],
                                     reads=[r_memKV, r_MQ], writes=[PB[b]])
                            k = (hh * 2 + mt)
                            S.op("act", [lambda e, k=k, b=b: e.activation(out=PT[k], in_=bank(b), func=AF.Exp, scale=0.125)],
                                 reads=[PB[b]], writes=[r_PT[k]])
                            pts.append(k)
                        bo, bd = 4, 5
                        mm_group(bank(bo, 512, 0, 64), [(memV[:, mt, h * 64:(h + 1) * 64], PT[pts[mt]]) for mt in range(2)],
                                 reads=[r_memKV, r_PT[pts[0]], r_PT[pts[1]]], writes=[PB[bo]])
                        mm_group(bank(bd, 512, 0, 64), [(ones64[:], PT[pts[mt]]) for mt in range(2)],
                                 reads=[r_consts, r_PT[pts[0]], r_PT[pts[1]]], writes=[PB[bd]])
                        S.op("dve", [lambda e: e.reciprocal(out=rden[0:64, :], in_=bank(bd, 512, 0, 64))],
                             reads=[PB[bd]], writes=[r_rden])
                        S.op("dve", [lambda e, prow=prow, hp=hp, qs=qs: e.tensor_tensor(
                            out=YM[prow, hp, qs], in0=bank(bo, 512, 0, 64), in1=rden[0:64, :], op=ALU.mult)],
                             reads=[PB[bo], r_rden], writes=[r_YM])

        def out_proj_ln1(layer):
            wout = AV(R_T, [8, 1024], BF16)
            r_wout = Res("wout")
            xs = [AV(R_T + 16 * K_ + i * 4 * K_, [1024], F32) for i in range(2)]
            r_xs = [Res("xs%d" % i) for i in range(2)]
            S.dma("pool", wout, w_out_d[layer].rearrange("(kc p) n -> p kc n", p=128), writes=[r_wout], key="wout")
            load_ln(layer, 0)
            pend = []
            for tile in range(16):
                ts_ = slice(tile * 128, (tile + 1) * 128)
                bp = [0, 2, 4][tile % 3]
                for half in range(2):
                    b = bp + half
                    pairs = []
                    for kc in range(8):
                        lhs = YT[:, kc, ts_] if kc < 6 else YM[:, kc - 6, ts_]
                        pairs.append((lhs, wout[:, kc, half * 512:(half + 1) * 512]))
                    mm_group(bank(b), pairs, reads=[r_YT, r_YM, r_wout], writes=[PB[b]])
                i = tile % 2
                ri = tile % NRB
                if layer == 0:
                    S.dma("sp", xs[i], x_d[ts_, :], writes=[r_xs[i]], key="xs%d" % i)
                    xin, rxin = xs[i], r_xs[i]
                else:
                    xin, rxin = X[:, tile, :], r_X[tile]
                rb = rbuf[ri]
                S.op("dve", [lambda e, xin=xin, rb=rb, bp=bp: e.scalar_tensor_tensor(
                    out=rb, in0=xin, scalar=ALPHA, in1=ps[:, bp * 512:bp * 512 + 1024], op0=ALU.mult, op1=ALU.add)],
                     reads=[rxin, PB[bp], PB[bp + 1]], writes=[r_rbuf[ri]])
                pend.append(ln_epilogue(tile, rb, r_rbuf[ri], False))
                if len(pend) > LN_LAG:
                    pend.pop(0)()
            for f_ in pend:
                f_()

        def ffn_ln2(layer, final):
            blocks = [4, 4, 4, 4, 3, 3]
            S.dma("sp", fcwb[:], fcwb_d[layer][:, :, :], writes=[r_fcwb], key="fcwb")
            load_ln(layer, 1)
            GT = AV(R_Y, [4, 2048], BF16)
            r_GT = Res("GT")
            wdn = [AV(R_T + i * 8 * K_, [4, 1024], BF16) for i in range(2)]
            r_wdn = [Res("wdn%d" % i) for i in range(2)]
            wup = [AV(R_T + 16 * K_ + i * 4 * K_, [8, 2, 128], BF16) for i in range(3)]
            r_wup = [Res("wup%d" % i) for i in range(3)]
            hsb = [AV(R_T + 28 * K_ + i * 8208, [2052], F32) for i in range(2)]
            r_hsb = [Res("hsb%d" % i) for i in range(2)]
            for i in range(2):
                S.op("pool", [lambda e, i=i: e.memset(hsb[i][:, 0:1], 0.0)], writes=[r_hsb[i]])
                S.op("pool", [lambda e, i=i: e.memset(hsb[i][:, 2049:2050], 0.0)], writes=[r_hsb[i]])
            acc_as = [AV(R_YM, [2048], F32), AV(R_Y + 16 * K_, [2048], F32)]
            acc_g = AV(R_MQ, [2048], F32)
            r_accas, r_accg = [Res("acca0"), Res("acca1")], Res("accg")
            wd_v = w_dn_d[layer]
            wu_v = w_up_d[layer].rearrange("(kc p) n -> p kc n", p=128)
            j0 = 0
            for bi, nb in enumerate(blocks):
                wd = wdn[bi % 2]
                S.dma("pool", wd[:, 0:nb, :], wd_v[j0 * 128:(j0 + nb) * 128, :].rearrange("(j p) n -> p j n", p=128),
                      writes=[r_wdn[bi % 2]], key="wdn%d" % (bi % 2))
                deferred = None
                for jj in range(nb):
                    j = j0 + jj
                    wi = j % 3
                    wu = wup[wi]
                    acc_a, r_acca = acc_as[j % 2], r_accas[j % 2]
                    S.dma("pool", wu[:, :, 0, :], wu_v[:, :, j * 128:(j + 1) * 128], writes=[r_wup[wi]], key="wup%da" % wi)
                    S.dma("pool", wu[:, :, 1, :], wu_v[:, :, DFF + j * 128:DFF + (j + 1) * 128], writes=[r_wup[wi]], key="wup%db" % wi)
                    for ag in range(2):
                        hs_ = hsb[ag]
                        cj = ag * NFF + j
                        for tq in range(4):
                            b = nxt([0, 1, 2, 3])
                            mm_group(bank(b), [(wu[:, kc, ag, :], XT[:, kc, tq * 512:(tq + 1) * 512]) for kc in range(8)],
                                     reads=[r_wup[wi], r_XT], writes=[PB[b]])
                            S.op("act", [lambda e, hs_=hs_, tq=tq, b=b: e.activation(
                                out=hs_[:, 1 + tq * 512:1 + (tq + 1) * 512], in_=bank(b), func=AF.Identity)],
                                 reads=[PB[b]], writes=[r_hsb[ag]])
                        acc, r_acc = (acc_a, r_acca) if ag == 0 else (acc_g, r_accg)
                        S.op("act", [lambda e, acc=acc, hs_=hs_, cj=cj: e.activation(
                            out=acc, in_=hs_[:, 1:2049], func=AF.Identity, scale=fcwb[:, cj, 1:2], bias=fcwb[:, cj, 3:4])],
                             reads=[r_hsb[ag], r_fcwb], writes=[r_acc])
                        S.op("dve", [lambda e, acc=acc, hs_=hs_, cj=cj: e.scalar_tensor_tensor(
                            out=acc, in0=hs_[:, 0:2048], scalar=fcwb[:, cj, 0:1], in1=acc, op0=ALU.mult, op1=ALU.add)],
                             reads=[r_hsb[ag], r_fcwb, r_acc], writes=[r_acc])
                        S.op("dve", [lambda e, acc=acc, hs_=hs_, cj=cj: e.scalar_tensor_tensor(
                            out=acc, in0=hs_[:, 2:2050], scalar=fcwb[:, cj, 2:3], in1=acc, op0=ALU.mult, op1=ALU.add)],
                             reads=[r_hsb[ag], r_fcwb, r_acc], writes=[r_acc])
                        if ag == 0 and deferred is not None:
                            deferred()
                            deferred = None

                    def _fin(jj=jj, acc_a=acc_a, r_acca=r_acca):
                        S.op("act", [lambda e: e.activation(out=acc_g, in_=acc_g, func=AF.Silu)], reads=[r_accg], writes=[r_accg])
                        S.op("dve", [lambda e: e.tensor_tensor(out=GT[:, jj, :], in0=acc_g, in1=acc_a, op=ALU.mult)],
                             reads=[r_accg, r_acca], writes=[r_GT])
                    deferred = _fin
                if deferred is not None:
                    deferred()
                    deferred = None
                last = bi == len(blocks) - 1
                if last:
                    S.barrier()
                pend = []
                for tile in range(16):
                    ts_ = slice(tile * 128, (tile + 1) * 128)
                    bp = [0, 2, 4][tile % 3] if last else [4, 6][tile % 2]
                    for half in range(2):
                        mm_group(bank(bp + half), [(GT[:, jj, ts_], wd[:, jj, half * 512:(half + 1) * 512]) for jj in range(nb)],
                                 reads=[r_GT, r_wdn[bi % 2]], writes=[PB[bp + half]])
                    pin = ps[:, bp * 512:bp * 512 + 1024]
                    if not last:
                        if bi == 0:
                            S.op("dve", [lambda e, tile=tile, pin=pin: e.scalar_tensor_tensor(
                                out=X[:, tile, :], in0=X[:, tile, :], scalar=ALPHA, in1=pin, op0=ALU.mult, op1=ALU.add)],
                                 reads=[PB[bp], PB[bp + 1]], writes=[r_X[tile]])
                        else:
                            S.op("dve", [lambda e, tile=tile, pin=pin: e.tensor_tensor(
                                out=X[:, tile, :], in0=X[:, tile, :], in1=pin, op=ALU.add)],
                                 reads=[PB[bp], PB[bp + 1]], writes=[r_X[tile]])
                    else:
                        ri = tile % NRB
                        rb = rbuf[ri]
                        S.op("dve", [lambda e, tile=tile, pin=pin, rb=rb: e.tensor_tensor(
                            out=rb, in0=X[:, tile, :], in1=pin, op=ALU.add)],
                             reads=[PB[bp], PB[bp + 1], r_X[tile]], writes=[r_rbuf[ri]])
                        fB = ln_epilogue(tile, rb, r_rbuf[ri], final)
                        if fB is not None:
                            pend.append(fB)
                            if len(pend) > LN_LAG:
                                pend.pop(0)()
                for f_ in pend:
                    f_()
                j0 += nb

        memf = AV(R_X, [2, 1024], F32)
        memb = AV(R_X + 8 * K_, [2, 1024], BF16)
        memT = AV(R_X + 12 * K_, [8, 256], BF16)
        wkv = AV(R_X + 16 * K_, [8, 512], BF16)
        r_memf, r_memb, r_memT, r_wkv = Res("memf"), Res("memb"), Res("memT"), Res("wkv")
        S.dma("sp", memf, mem_d.rearrange("(mt p) d -> p mt d", p=128), writes=[r_memf], key="memf")
        S.dma("pool", wkv, wkv_d.rearrange("(kc p) n -> p kc n", p=128), writes=[r_wkv], key="wkv")
        act_copy(memb, memf, [r_memf], [r_memb])
        for mt in range(2):
            b = nxt([6, 7])
            bb = bank_bf(b)
            fns = [lambda e, kc=kc, mt=mt, bb=bb: e.transpose(bb[:, kc * 128:(kc + 1) * 128], memb[:, mt, kc * 128:(kc + 1) * 128], ident[:])
                   for kc in range(8)]
            S.op("pe", fns, reads=[r_memb, r_consts], writes=[PB[b]])
            S.op("act", [lambda e, mt=mt, bb=bb: e.activation(out=memT[:, :, mt * 128:(mt + 1) * 128],
                                                             in_=bb.rearrange("p (k m) -> p k m", k=8), func=AF.Identity)],
                 reads=[PB[b]], writes=[r_memT])
        for hp in range(2):
            b = nxt([0, 1])
            mm_group(bank(b, 256), [(wkv[:, kc, hp * 128:(hp + 1) * 128], memT[:, kc, :]) for kc in range(8)],
                     reads=[r_wkv, r_memT], writes=[PB[b]])
            act_copy(memKT[:, hp, :], bank(b, 256), [PB[b]], [r_memKV])
        for mt in range(2):
            b = nxt([0, 1])
            mm_group(bank(b, 256), [(memT[:, kc, mt * 128:(mt + 1) * 128], wkv[:, kc, 256:512]) for kc in range(8)],
                     reads=[r_wkv, r_memT], writes=[PB[b]])
            act_copy(memV[:, mt, :], bank(b, 256), [PB[b]], [r_memKV])
        S.barrier()

        if debug == "s_m":
            S.barrier()
            S.emit()
            return nc
        xs0 = [AV(R_T + 16 * K_ + i * 4 * K_, [1024], F32) for i in range(2)]
        r_xs0 = [Res("xs0_%d" % i) for i in range(2)]
        for tile in range(16):
            i = tile % 2
            S.dma("sp", xs0[i], x_d[tile * 128:(tile + 1) * 128, :], writes=[r_xs0[i]], key="xs%d" % i)
            i4 = tile % 4
            S.op("act", [lambda e, i=i, i4=i4: e.activation(out=xbuf[i4], in_=xs0[i], func=AF.Identity)],
                 reads=[r_xs0[i]], writes=[r_xbuf[i4]])
            transpose_to_XT(xbuf[i4], tile, r_xbuf[i4])

        if debug == "s_x":
            S.barrier()
            S.emit()
            return nc
        HS = AV(R_X, [2, 16, 768], BF16)
        r_HS = Res("HS")
        hA = AV(R_X + 48 * K_, [2048], F32)
        hB = AV(R_X + 56 * K_, [2048], F32)
        r_hA, r_hB = Res("hA"), Res("hB")
        zf = AV(R_Y, [2048], F32)
        fw = AV(R_Y + 8 * K_, [3, 64], F32)
        fwo = AV(R_Y + 9 * K_, [1536], F32)
        fbf = AV(R_Y + 15 * K_, [8], F32)
        dbc = AV(R_Y + 16 * K_, [768], F32)
        dlt = AV(R_YM, [768], F32)
        dec = AV(R_YM + 3 * K_, [768], F32)
        ntn = AV(R_YM + 6 * K_, [16], F32)
        fsb = AV(R_MQ, [768], F32)
        wtmp = AV(R_MQ + 3 * K_, [512], F32)
        wtm2 = AV(R_MQ + 5 * K_, [512], F32)
        r_f = Res("filt_in")
        r_dec, r_fsb, r_wtmp, r_wtm2, r_dbc = Res("dec"), Res("fsb"), Res("wtmp"), Res("wtm2"), Res("dbc")
        S.dma("sp", zf[0:33, :], zfT_d[:, :], writes=[r_f], key="f0")
        S.dma("sp", fw[0:33, 0, :], fw1_d[:, :], writes=[r_f], key="f1")
        S.dma("sp", fw[0:64, 1, :], fw2_d[:, :], writes=[r_f], key="f2")
        S.dma("sp", fw[0:64, 2, :], fw3_d[:, :], writes=[r_f], key="f3")
        S.dma("sp", fwo[0:64, :], fwo_d[:, :], writes=[r_f], key="f4")
        S.dma("sp", fbf[0:64, 0:6], fbf_d[:, :], writes=[r_f], key="f5")
        S.dma("sp", dbc, hd_d.partition_broadcast(128), writes=[r_dbc], key="f6")
        S.dma("sp", dlt, deltas_d.partition_broadcast(128), writes=[r_f], key="f7")
        S.dma("sp", ntn, ntn_d[:, :], writes=[r_f], key="f8")
        fbs = stat[0:64, 120:123]
        for l in range(3):
            S.op("dve", [lambda e, l=l: e.tensor_tensor(out=fbs[:, l:l + 1], in0=fbf[0:64, 2 * l:2 * l + 1],
                                                        in1=fbf[0:64, 2 * l + 1:2 * l + 2], op=ALU.mult)],
                 reads=[r_f], writes=[r_stat])
        srcs = [(zf, 33, r_f), (hA, 64, r_hA), (hB, 64, r_hB)]
        dsts = [(hA, r_hA), (hB, r_hB), (hA, r_hA)]
        for l in range(3):
            src, kk, r_src = srcs[l]
            dst, r_dst = dsts[l]
            for tq in range(4):
                b = nxt([0, 1, 2, 3])
                cs = slice(tq * 512, (tq + 1) * 512)
                mm_group(bank(b, 512, 0, 64), [(fw[0:kk, l, :], src[0:kk, cs])], reads=[r_f, r_src], writes=[PB[b]])
                S.op("dve", [lambda e, b=b, l=l: e.tensor_scalar(out=wtmp[0:64, :], in0=bank(b, 512, 0, 64),
                                                                scalar1=fbf[0:64, 2 * l + 1:2 * l + 2], scalar2=fbs[:, l:l + 1],
                                                                op0=ALU.mult, op1=ALU.add)],
                     reads=[PB[b], r_f, r_stat], writes=[r_wtmp])
                S.op("dve", [lambda e: e.tensor_scalar(out=wtm2[0:64, :], in0=wtmp[0:64, :], scalar1=-PI, scalar2=2 * PI,
                                                       op0=ALU.is_lt, op1=ALU.mult)], reads=[r_wtmp], writes=[r_wtm2])
                S.op("dve", [lambda e: e.tensor_tensor(out=wtmp[0:64, :], in0=wtmp[0:64, :], in1=wtm2[0:64, :], op=ALU.add)],
                     reads=[r_wtmp, r_wtm2], writes=[r_wtmp])
                S.op("dve", [lambda e: e.tensor_scalar(out=wtm2[0:64, :], in0=wtmp[0:64, :], scalar1=PI, scalar2=-2 * PI,
                                                       op0=ALU.is_gt, op1=ALU.mult)], reads=[r_wtmp], writes=[r_wtm2])
                S.op("dve", [lambda e: e.tensor_tensor(out=wtmp[0:64, :], in0=wtmp[0:64, :], in1=wtm2[0:64, :], op=ALU.add)],
                     reads=[r_wtmp, r_wtm2], writes=[r_wtmp])
                S.op("act", [lambda e, dst=dst, cs=cs: e.activation(out=dst[0:64, cs], in_=wtmp[0:64, :], func=AF.Sin)],
                     reads=[r_wtmp], writes=[r_dst])
        for tile in range(16):
            ts_ = slice(tile * 128, (tile + 1) * 128)
            for q3 in range(3):
                mm_group(bank(q3), [(hA[0:64, ts_], fwo[0:64, q3 * 512:(q3 + 1) * 512])], reads=[r_hA, r_f], writes=[PB[q3]])
            S.op("act", [lambda e, tile=tile: e.activation(out=dec, in_=dlt, func=AF.Exp, scale=ntn[:, tile:tile + 1])],
                 reads=[r_f], writes=[r_dec])
            act_copy(fsb, ps[:, 0:768], [PB[0], PB[1]], [r_fsb])
            S.op("dve", [lambda e: e.tensor_tensor(out=hB[:, 0:768], in0=fsb, in1=ps[:, 768:1536], op=ALU.add)],
                 reads=[r_fsb, PB[1], PB[2]], writes=[r_hB])
            S.op("dve", [lambda e: e.tensor_tensor(out=hB[:, 768:1536], in0=fsb, in1=ps[:, 768:1536], op=ALU.subtract)],
                 reads=[r_fsb, PB[1], PB[2]], writes=[r_hB])
            S.op("dve", [lambda e, tile=tile: e.tensor_tensor(out=HS[:, 0, tile, :], in0=hB[:, 0:768], in1=dec, op=ALU.mult)],
                 reads=[r_hB, r_dec], writes=[r_HS])
            S.op("dve", [lambda e, tile=tile: e.tensor_tensor(out=HS[:, 1, tile, :], in0=hB[:, 768:1536], in1=dec, op=ALU.mult)],
                 reads=[r_hB, r_dec], writes=[r_HS])
        S.barrier()

        if debug == "s_f":
            S.barrier()
            S.emit()
            return nc
        KS = AV(R_T, [2, 16, 768], BF16)
        r_KS = Res("KS")

        def fwd_pass(rhs_re, rhs_im, r_rhs, ftab, r_ftab, epilogue):
            for fc in range(16):
                si = fc % 2
                ft = ftab[si]
                S.dma("sp", ft.rearrange("p a b c -> p (a b c)"), fwd_tab_d[fc], writes=[r_ftab[si]], key="ftab%d" % si)
                bs = [0, 1, 2, 3] if fc % 2 == 0 else [4, 5, 6, 7]
                for ri in range(2):
                    rhs = rhs_re if ri == 0 else rhs_im
                    o0 = bs[0] * 512 + ri * 1024
                    fns = []
                    for tc in range(16):
                        lhs = ft[:, ri, tc, :]
                        fns.append(lambda e, lhs=lhs, tc=tc, rhs=rhs, o0=o0: e.matmul(
                            ps[:, o0:o0 + 512], lhs, rhs[:, tc, 0:512], start=(tc == 0), stop=(tc == 15)))
                        fns.append(lambda e, lhs=lhs, tc=tc, rhs=rhs, o0=o0: e.matmul(
                            ps[:, o0 + 512:o0 + 768], lhs, rhs[:, tc, 512:768], start=(tc == 0), stop=(tc == 15)))
                    S.op("pe", fns, reads=[r_ftab[si], r_rhs], writes=[PB[bs[2 * ri]], PB[bs[2 * ri + 1]]])
                pre = ps[:, bs[0] * 512:bs[0] * 512 + 768]
                pim = ps[:, bs[2] * 512:bs[2] * 512 + 768]
                epilogue(fc, pre, pim, [PB[b] for b in bs])

        ftab = [AV(R_YM, [2, 16, 128], BF16), AV(R_MQ, [2, 16, 128], BF16)]
        r_ftab = [Res("ftab0"), Res("ftab1")]

        def k_epilogue(fc, pre, pim, rbs):
            S.op("dve", [lambda e: e.tensor_tensor(out=KS[:, 0, fc, :], in0=pre, in1=dbc, op=ALU.add)],
                 reads=rbs[0:2] + [r_dbc], writes=[r_KS])
            act_copy(KS[:, 1, fc, :], pim, rbs[2:4], [r_KS])

        fwd_pass(HS[:, 0], HS[:, 1], r_HS, ftab, r_ftab, k_epilogue)
        S.barrier()

        if debug == "s_k":
            S.barrier()
            S.emit()
            return nc
        Z = AV(R_X, [16, 768], BF16)
        r_Z = Res("Z")
        hsb0 = [AV(R_X + 24 * K_ + i * 8208, [2052], F32) for i in range(2)]
        r_hsb0 = [Res("hsb0_%d" % i) for i in range(2)]
        accx = AV(R_X + 24 * K_ + 16416, [2048], F32)
        accv = AV(R_X + 24 * K_ + 16416 + 8192, [2048], F32)
        zT = AV(R_X + 24 * K_ + 16416 + 16384, [2048], BF16)
        r_accx, r_accv, r_zT = Res("accx"), Res("accv"), Res("zT")
        wch = [AV(R_T + 48 * K_ + i * 2 * K_, [8, 128], BF16) for i in range(3)]
        r_wch = [Res("wch%d" % i) for i in range(3)]
        for i in range(2):
            S.op("pool", [lambda e, i=i: e.memset(hsb0[i][:, 0:1], 0.0)], writes=[r_hsb0[i]])
            S.op("pool", [lambda e, i=i: e.memset(hsb0[i][:, 2049:2050], 0.0)], writes=[r_hsb0[i]])
        w0v = w_in_d[0].rearrange("(kc p) n -> p kc n", p=128)
        order = [18, 19] + [0, 1, 2, 3, 4, 5]
        for i in range(6):
            order += [6 + i, 12 + i]
        _wc = [0]

        def proj_chunk(wv, col0, sink_fn, extra_reads=()):
            wi = _wc[0] % 3
            _wc[0] += 1
            S.dma("pool", wch[wi], wv[:, :, col0:col0 + 128], writes=[r_wch[wi]], key="wch%d" % wi)
            for tq in range(4):
                b = nxt([0, 1, 2, 3])
                mm_group(bank(b), [(wch[wi][:, kc, :], XT[:, kc, tq * 512:(tq + 1) * 512]) for kc in range(8)],
                         reads=[r_wch[wi], r_XT], writes=[PB[b]])
                sink_fn(tq, b)

        def conv_chunk(c, hs_, r_hs, acc_out, r_acc_list, out_final):
            S.op("act", [lambda e: e.activation(out=acc_out, in_=hs_[:, 1:2049], func=AF.Identity,
                                                scale=cwb[:, c, 1:2], bias=cwb[:, c, 3:4])],
                 reads=[r_hs, r_cwb], writes=r_acc_list)
            S.op("dve", [lambda e: e.scalar_tensor_tensor(out=acc_out, in0=hs_[:, 0:2048], scalar=cwb[:, c, 0:1],
                                                          in1=acc_out, op0=ALU.mult, op1=ALU.add)],
                 reads=[r_hs, r_cwb] + r_acc_list, writes=r_acc_list)
            S.op("dve", [lambda e: e.scalar_tensor_tensor(out=out_final[0], in0=hs_[:, 2:2050], scalar=cwb[:, c, 2:3],
                                                          in1=acc_out, op0=ALU.mult, op1=ALU.add)],
                 reads=[r_hs, r_cwb] + r_acc_list, writes=out_final[1])

        zdef = []
        for ci_, c in enumerate(order):
            if debug and debug.startswith('s_p') and ci_ == int(debug[3:]):
                S.barrier()
                S.emit()
                return nc
            if c >= 18:
                hp = c - 18

                def sink_mq(tq, b, hp=hp):
                    act_copy(MQ[:, hp, tq * 512:(tq + 1) * 512], bank(b), [PB[b]], [r_MQ])
                proj_chunk(w0v, c * 128, sink_mq)
                continue
            si = (c % 2) if c < 6 else (0 if c < 12 else 1)
            hs_ = hsb0[si]

            def sink_h(tq, b, hs_=hs_, si=si):
                act_copy(hs_[:, 1 + tq * 512:1 + (tq + 1) * 512], bank(b), [PB[b]], [r_hsb0[si]])
            proj_chunk(w0v, c * 128, sink_h)
            if c < 6:
                conv_chunk(c, hs_, r_hsb0[si], accv, [r_accv], (YT[:, c, :], [r_YT]))
            elif c < 12:
                conv_chunk(c, hs_, r_hsb0[si], accx, [r_accx], (accx, [r_accx]))
                while zdef:
                    zdef.pop(0)()
            else:
                i6 = c - 12
                conv_chunk(c, hs_, r_hsb0[si], accv, [r_accv], (accv, [r_accv]))
                S.op("dve", [lambda e: e.tensor_tensor(out=zT, in0=accv, in1=accx, op=ALU.mult)],
                     reads=[r_accv, r_accx], writes=[r_zT])
                def _ztr(i6=i6):
                    for g8 in range(2):
                        b = nxt([6, 7])
                        bb = bank_bf(b)
                        fns = [lambda e, t8=t8, bb=bb, g8=g8: e.transpose(bb[:, t8 * 128:(t8 + 1) * 128],
                                                                         zT[:, (g8 * 8 + t8) * 128:(g8 * 8 + t8 + 1) * 128], ident[:])
                               for t8 in range(8)]
                        S.op("pe", fns, reads=[r_zT, r_consts], writes=[PB[b]])
                        S.op("act", [lambda e, g8=g8, bb=bb, i6=i6: e.activation(
                            out=Z[:, g8 * 8:(g8 + 1) * 8, i6 * 128:(i6 + 1) * 128],
                            in_=bb.rearrange("p (k m) -> p k m", k=8), func=AF.Identity)],
                             reads=[PB[b]], writes=[r_Z])
                zdef.append(_ztr)
        while zdef:
            zdef.pop(0)()
        S.barrier()
        if debug == "z":
            for tile in range(16):
                S.op("act", [lambda e, tile=tile: e.activation(out=X[:, tile, 0:768] if False else hsb0[0][:, 0:768], in_=Z[:, tile, :], func=AF.Identity)],
                     reads=[r_Z], writes=[r_hsb0[0]])
                S.dma("sp", dbg_d[tile * 128:(tile + 1) * 128, 0:768], hsb0[0][:, 0:768], reads=[r_hsb0[0]], key="dbg")
            S.barrier()
            S.emit()
            return nc
        mem_attention(R_X + 24 * K_)
        S.barrier()

        YRE = AV(R_XT, [16, 768], BF16)
        YIM = AV(R_X + 24 * K_, [16, 768], BF16)
        r_Y = Res("Yspec")
        ct = [AV(R_X + 48 * K_ + i * 3 * K_, [768], F32) for i in range(4)]
        r_ct = [Res("ct%d" % i) for i in range(4)]
        ftabU = [AV(R_T + 48 * K_, [2, 16, 128], BF16), AV(R_MQ, [2, 16, 128], BF16)]
        r_ftabU = [Res("ftabU0"), Res("ftabU1")]

        def u_epilogue(fc, pre, pim, rbs):
            kre, kim = KS[:, 0, fc, :], KS[:, 1, fc, :]
            S.op("dve", [lambda e: e.tensor_tensor(out=ct[0], in0=pre, in1=kre, op=ALU.mult)], reads=rbs[0:2] + [r_KS], writes=[r_ct[0]])
            S.op("dve", [lambda e: e.tensor_tensor(out=ct[1], in0=pim, in1=kim, op=ALU.mult)], reads=rbs[2:4] + [r_KS], writes=[r_ct[1]])
            S.op("dve", [lambda e: e.tensor_tensor(out=ct[2], in0=pre, in1=kim, op=ALU.mult)], reads=rbs[0:2] + [r_KS], writes=[r_ct[2]])
            S.op("dve", [lambda e: e.tensor_tensor(out=ct[3], in0=pim, in1=kre, op=ALU.mult)], reads=rbs[2:4] + [r_KS], writes=[r_ct[3]])
            S.op("pool", [lambda e: e.tensor_tensor(out=YRE[:, fc, :], in0=ct[0], in1=ct[1], op=ALU.subtract)],
                 reads=[r_ct[0], r_ct[1]], writes=[r_Y])
            S.op("pool", [lambda e: e.tensor_tensor(out=YIM[:, fc, :], in0=ct[2], in1=ct[3], op=ALU.add)],
                 reads=[r_ct[2], r_ct[3]], writes=[r_Y])

        fwd_pass(Z, Z, r_Z, ftabU, r_ftabU, u_epilogue)
        S.barrier()

        itab = [AV(R_T + 48 * K_, [4, 2, 512], BF16), AV(R_MQ, [4, 2, 512], BF16)]
        r_itab = [Res("itab0"), Res("itab1")]
        _it = 0
        for tt in range(4):
            fn_all = []
            for fg in range(4):
                si = _it % 2
                _it += 1
                S.dma("sp", itab[si].rearrange("p a b c -> p (a b c)"), inv_tab_d[tt, fg], writes=[r_itab[si]], key="itab%d" % si)
                fns = []
                for fi in range(4):
                    fc = fg * 4 + fi
                    for ri in range(2):
                        Ysrc = YRE if ri == 0 else YIM
                        for cc in range(6):
                            first = (fc == 0 and ri == 0)
                            lastm = (fc == 15 and ri == 1)
                            fns.append(lambda e, cc=cc, Ysrc=Ysrc, fc=fc, si=si, fi=fi, ri=ri, first=first, lastm=lastm: e.matmul(
                                bank(cc), Ysrc[:, fc, cc * 128:(cc + 1) * 128], itab[si][:, fi, ri, :], start=first, stop=lastm))
                S.op("pe", fns, reads=[r_itab[si], r_Y], writes=[PB[cc] for cc in range(6)])
            for cc in range(6):
                S.op("dve", [lambda e, cc=cc, tt=tt: e.tensor_tensor(out=YT[:, cc, tt * 512:(tt + 1) * 512], in0=bank(cc),
                                                                    in1=YT[:, cc, tt * 512:(tt + 1) * 512], op=ALU.mult)],
                     reads=[PB[cc], r_YT], writes=[r_YT])
        S.barrier()

        if debug == "mix0":
            for c in range(8):
                src = YT[:, c, 0:1024] if c < 6 else YM[:, c - 6, 0:1024]
                S.op("act", [lambda e, src=src: e.activation(out=rbuf[0], in_=src, func=AF.Identity)], reads=[r_YT, r_YM], writes=[r_rbuf[0]])
                S.dma("sp", dbg_d[c * 128:(c + 1) * 128, :], rbuf[0], reads=[r_rbuf[0]], key="dbg")
            S.barrier()
            S.emit()
            return nc

        out_proj_ln1(0)
        S.barrier()
        if debug == "ln1_0":
            for tile in range(16):
                S.dma("sp", dbg_d[tile * 128:(tile + 1) * 128, :], X[:, tile, :], reads=[r_X[tile]], key="dbg")
            S.barrier()
            S.emit()
            return nc
        ffn_ln2(0, final=(debug == "l0"))
        S.barrier()

        if debug is None or debug.startswith("l1"):
            if debug == "l1_s":
                S.barrier()
                S.emit()
                return nc
            w1v = w_in_d[1].rearrange("(kc p) n -> p kc n", p=128)
            QT = YT
            r_QT = [Res("QT%d" % i) for i in range(12)]
            KT = AV(R_T, [4, 2048], BF16)
            r_KT = Res("KT")
            VT = AV(R_T + 16 * K_, [16, 256], BF16)
            r_VT = Res("VT")
            ropec = AV(R_T + 24 * K_, [2048], F32)
            ropes = AV(R_T + 32 * K_, [2048], F32)
            r_rope = Res("rope")
            wch1 = [AV(R_T + 40 * K_ + i * 2 * K_, [8, 128], BF16) for i in range(3)]
            r_wch1 = [Res("wch1_%d" % i) for i in range(3)]
            qsb = [AV(R_T + 46 * K_ + i * K_, [512], BF16) for i in range(2)]
            r_qsb = [Res("qsb%d" % i) for i in range(2)]
            rt1 = AV(R_T + 48 * K_, [512], F32)
            rt2 = AV(R_T + 50 * K_, [512], F32)
            r_rt1, r_rt2 = Res("rt1"), Res("rt2")
            wvt = AV(R_T + 52 * K_, [8, 256], BF16)
            r_wvt = Res("wvt")
            esk = small("esk", [64, 12], F32)
            r_esk = Res("esk")
            import os
            SK = os.environ.get("SKIP", "")
            if "r" not in SK:
                S.dma("sp", ropec, ropec_d[:, :], writes=[r_rope], key="rope0")
                S.dma("sp", ropes, ropes_d[:, :], writes=[r_rope], key="rope1")
            if "e" not in SK:
                S.dma("sp", esk[:], sink_d.partition_broadcast(64), writes=[r_esk], key="esk")
            if "x" not in SK:
                S.op("act", [lambda e: e.activation(out=esk[:], in_=esk[:], func=(AF.Identity if "I" in SK else AF.Exp))], reads=[r_esk], writes=[r_esk])
            if "w" not in SK:
                S.dma("pool", wvt, w1v[:, :, 1024:1280], writes=[r_wvt], key="wvt")
            if debug == "l1_p0":
                S.barrier()
                S.emit()
                return nc
            _w1 = [0]
            _rq = [0]

            def proj1(loads, sink_fn):
                wi = _w1[0] % 3
                _w1[0] += 1
                for k_, (dst0, dst1, c0, c1) in enumerate(loads):
                    S.dma("pool", wch1[wi][:, :, dst0:dst1], w1v[:, :, c0:c1], writes=[r_wch1[wi]], key="wch1_%d_%d" % (wi, k_))
                for tq in range(4):
                    b = nxt([0, 1, 2, 3])
                    mm_group(bank(b), [(wch1[wi][:, kc, :], XT[:, kc, tq * 512:(tq + 1) * 512]) for kc in range(8)],
                             reads=[r_wch1[wi], r_XT], writes=[PB[b]])
                    sink_fn(tq, b)

            def rope_sink(dest, r_dest):
                def sink(tq, b):
                    cs = slice(tq * 512, (tq + 1) * 512)
                    i = _rq[0] % 2
                    _rq[0] += 1
                    S.op("act", [lambda e: e.activation(out=qsb[i], in_=bank(b), func=AF.Identity)],
                         reads=[PB[b]], writes=[r_qsb[i]])
                    b2 = [4, 5][i]
                    mm_group(bank(b2), [(pm[:], qsb[i])], reads=[r_consts, r_qsb[i]], writes=[PB[b2]])
                    S.op("dve", [lambda e: e.tensor_tensor(out=rt1, in0=bank(b), in1=ropec[:, cs], op=ALU.mult)],
                         reads=[PB[b], r_rope], writes=[r_rt1])
                    S.op("dve", [lambda e: e.tensor_tensor(out=rt2, in0=bank(b2), in1=ropes[:, cs], op=ALU.mult)],
                         reads=[PB[b2], r_rope], writes=[r_rt2])
                    S.op("dve", [lambda e: e.tensor_tensor(out=dest[:, cs], in0=rt1, in1=rt2, op=ALU.add)],
                         reads=[r_rt1, r_rt2], writes=r_dest)
                return sink

            for c in range(6):
                proj1([(0, 128, c * 128, (c + 1) * 128)], rope_sink(QT[:, c, :], [r_QT[2 * c], r_QT[2 * c + 1]]))
            if debug == "l1_p1":
                S.barrier()
                S.emit()
                return nc
            for g in range(4):
                c0 = 768 + g * 64
                proj1([(0, 64, c0, c0 + 64), (64, 128, c0, c0 + 64)], rope_sink(KT[:, g, :], [r_KT]))
            if debug == "l1_p2":
                S.barrier()
                S.emit()
                return nc
            for hp in range(2):
                def sink_mq1(tq, b, hp=hp):
                    act_copy(MQ[:, hp, tq * 512:(tq + 1) * 512], bank(b), [PB[b]], [r_MQ])
                proj1([(0, 128, 1280 + hp * 128, 1280 + (hp + 1) * 128)], sink_mq1)
            for tile in range(16):
                b = nxt([0, 1, 2, 3])
                mm_group(bank(b, 256), [(XT[:, kc, tile * 128:(tile + 1) * 128], wvt[:, kc, :]) for kc in range(8)],
                         reads=[r_XT, r_wvt], writes=[PB[b]])
                act_copy(VT[:, tile, :], bank(b, 256), [PB[b]], [r_VT])
            S.barrier()

            if debug == "l1_p":
                S.barrier()
                S.emit()
                return nc
            PTs = [AV(R_XT + i * 12 * K_, [16, 384], BF16) for i in range(2)]
            r_PTs = [Res("PTs%d" % i) for i in range(2)]
            rden1 = AV(R_XT + 24 * K_, [512], F32)
            r_rden1 = Res("rden1")
            for h in range(12):
                g = h // 3
                hh = h % 2
                c = h // 2
                prow = slice(hh * 64, (hh + 1) * 64)
                pt = PTs[h % 2]
                r_pt = r_PTs[h % 2]
                for j in range(16):
                    qlo = max(0, j - 1) * 128
                    qhi = min(16, j + 2) * 128
                    n = qhi - qlo
                    moff = qlo - (j - 1) * 128
                    b = nxt([0, 1, 2, 3])
                    fns = [
                        lambda e, b=b, n=n, j=j, qlo=qlo, qhi=qhi, prow=prow, g=g, c=c: e.matmul(
                            bank(b, n), KT[prow, g, j * 128:(j + 1) * 128], QT[prow, c, qlo:qhi], start=True, stop=False),
                        lambda e, b=b, n=n, moff=moff: e.matmul(
                            bank(b, n), ident[:], maskb[:, moff:moff + n], start=False, stop=True),
                    ]
                    S.op("pe", fns, reads=[r_KT, r_QT[h], r_consts], writes=[PB[b]])
                    S.op("act", [lambda e, b=b, n=n, j=j, pt=pt: e.activation(out=pt[:, j, 0:n], in_=bank(b, n), func=AF.Exp, scale=0.125)],
                         reads=[PB[b]], writes=[r_pt])
                for qt in range(4):
                    bo, bd = [4, 6][qt % 2], [5, 7][qt % 2]
                    fo, fd = [], []
                    for i4 in range(4):
                        qb = 4 * qt + i4
                        js = [j for j in (qb - 1, qb, qb + 1) if 0 <= j < 16]
                        for k_, j in enumerate(js):
                            lc = (qb - max(0, j - 1)) * 128
                            st_, sp_ = (k_ == 0), (k_ == len(js) - 1)
                            fo.append(lambda e, i4=i4, j=j, lc=lc, st_=st_, sp_=sp_, bo=bo, pt=pt, g=g: e.matmul(
                                ps[0:64, bo * 512 + i4 * 128:bo * 512 + (i4 + 1) * 128], VT[:, j, g * 64:(g + 1) * 64],
                                pt[:, j, lc:lc + 128], start=st_, stop=sp_))
                            fd.append(lambda e, i4=i4, j=j, lc=lc, st_=st_, sp_=sp_, bd=bd, pt=pt: e.matmul(
                                ps[0:64, bd * 512 + i4 * 128:bd * 512 + (i4 + 1) * 128], ones64[:],
                                pt[:, j, lc:lc + 128], start=st_, stop=sp_))
                    S.op("pe", fo, reads=[r_pt, r_VT], writes=[PB[bo]])
                    S.op("pe", fd, reads=[r_pt, r_consts], writes=[PB[bd]])
                    S.op("dve", [lambda e, bd=bd, h=h: e.tensor_scalar(out=rden1[0:64, :], in0=bank(bd, 512, 0, 64),
                                                                     scalar1=esk[:, h:h + 1], scalar2=None, op0=ALU.add)],
                         reads=[PB[bd], r_esk], writes=[r_rden1])
                    S.op("dve", [lambda e: e.reciprocal(out=rden1[0:64, :], in_=rden1[0:64, :])], reads=[r_rden1], writes=[r_rden1])
                    S.op("dve", [lambda e, bo=bo, qt=qt, prow=prow, c=c: e.tensor_tensor(
                        out=QT[prow, c, qt * 512:(qt + 1) * 512], in0=bank(bo, 512, 0, 64), in1=rden1[0:64, :], op=ALU.mult)],
                         reads=[PB[bo], r_rden1], writes=[r_QT[h]])
            S.barrier()
            if debug == "l1_a":
                S.barrier()
                S.emit()
                return nc
            mem_attention(R_XT + 26 * K_ - 2 * K_)
            S.barrier()
            if debug == "l1mix":
                for c in range(8):
                    src = YT[:, c, 0:1024] if c < 6 else YM[:, c - 6, 0:1024]
                    S.op("act", [lambda e, src=src: e.activation(out=rbuf[0], in_=src, func=AF.Identity)], reads=[r_YM], writes=[r_rbuf[0]])
                    S.dma("sp", dbg_d[c * 128:(c + 1) * 128, :], rbuf[0], reads=[r_rbuf[0]], key="dbg")
                S.barrier()
                S.emit()
                return nc
            out_proj_ln1(1)
            S.barrier()
            ffn_ln2(1, final=True)
            S.barrier()

        S.barrier()
        S.emit()
    return nc


def prep_shared(inputs):
    f32 = np.float32
    sh = {}
    for k in ("w_mem_kv", "l0_w_in", "l1_w_in", "l0_w_out", "l1_w_out", "l0_ffn_w_up", "l1_ffn_w_up",
              "l0_ffn_w_down", "l1_ffn_w_down", "l0_filt_w1", "l0_filt_w2", "l0_filt_w3", "l0_filt_w_out",
              "l0_hyena_d", "l1_sink"):
        sh[k] = np.ascontiguousarray(np.asarray(inputs[k], dtype=f32))
    for i in range(2):
        for n in ("ln1_g", "ln1_b", "ln2_g", "ln2_b"):
            sh["l%d_%s" % (i, n)] = np.ascontiguousarray(np.asarray(inputs["l%d_%s" % (i, n)], dtype=f32))
        cw = np.asarray(inputs["l%d_ffn_conv_w" % i], f32)
        cb = np.asarray(inputs["l%d_ffn_conv_b" % i], f32)
        a = np.concatenate([cw, cb[None, :]], axis=0)
        sh["l%d_fcwb" % i] = np.ascontiguousarray(a.reshape(4, 44, 128).transpose(2, 1, 0))
    cw = np.asarray(inputs["l0_conv_w"], f32)
    cb = np.asarray(inputs["l0_conv_b"], f32)
    a = np.concatenate([cw, cb[None, :]], axis=0)
    sh["l0_cwb"] = np.ascontiguousarray(a.reshape(4, 18, 128).transpose(2, 1, 0))
    sh["l0_fbf"] = np.ascontiguousarray(np.stack(
        [np.asarray(inputs["l0_filt_%s%d" % (n, l)], f32) for l in (1, 2, 3) for n in ("b", "f")], axis=1))
    sh.update(const_tables())
    return sh


_NC_CACHE = {}


def kernel(**inputs):
    sh = prep_shared(inputs)
    x = np.asarray(inputs["x"], np.float32)
    mem = np.asarray(inputs["mem"], np.float32)
    if "nc" not in _NC_CACHE:
        _NC_CACHE["nc"] = build_program()
    nc = _NC_CACHE["nc"]
    in_maps = []
    for b in range(8):
        m = dict(sh)
        m["x"] = np.ascontiguousarray(x[b])
        m["mem"] = np.ascontiguousarray(mem[b])
        in_maps.append(m)
    res = run_bass_kernel_spmd(nc, in_maps, core_ids=list(range(8)))
    return np.stack([np.asarray(r["out"], np.float32) for r in res.results], axis=0)
```

```python
import math
from contextlib import ExitStack

import numpy as np
import ml_dtypes
import concourse.bass as bass
import concourse.mybir as mybir
from concourse.bass_utils import run_bass_kernel_spmd

F32 = mybir.dt.float32
BF16 = mybir.dt.bfloat16
AF = mybir.ActivationFunctionType
ALU = mybir.AluOpType
AX = mybir.AxisListType

L = 2048
D = 1024
NT = 16
KC = 8
DFF = 2816
NFF = 22
ALPHA = 4.0 ** 0.25
EPS = 1e-5
PI = float(np.pi)


class Res:
    __slots__ = ("name", "w", "r", "excl")

    def __init__(self, name, excl=False):
        self.name = name
        self.w = None
        self.r = []
        self.excl = excl


class Sched:
    def __init__(self, nc, es):
        self.nc = nc
        self.es = es
        self.engs = {}
        for n in ("pe", "act", "dve", "pool", "sp"):
            sem = es.enter_context(nc.semaphore("s_" + n))
            self.engs[n] = dict(sem=sem, count=0, known={}, ops=[])
        self.dma_sems = {}
        self.n_dma_sems = 0

    def dma_sem(self, key):
        if key not in self.dma_sems:
            sem = self.es.enter_context(self.nc.semaphore("d%d" % self.n_dma_sems))
            self.n_dma_sems += 1
            self.dma_sems[key] = [sem, 0]
        return self.dma_sems[key]

    def op(self, eng, fns, reads=(), writes=(), dma_key=None):
        E = self.engs[eng]
        deps = {}

        def add(ev):
            if ev is None:
                return
            sem, val = ev
            if deps.get(sem, 0) < val:
                deps[sem] = val

        excl_reads = [r for r in reads if r.excl]
        writes = list(writes) + [r for r in excl_reads if r not in writes]
        reads = [r for r in reads if not r.excl]
        for r in reads:
            add(r.w)
        for w in writes:
            add(w.w)
            for ev in w.r:
                add(ev)
        waits = []
        for sem, val in deps.items():
            if sem is E["sem"] and eng == "pe" and dma_key is None:
                continue
            if E["known"].get(sem, 0) >= val:
                continue
            E["known"][sem] = val
            waits.append((sem, val))
        if dma_key is not None:
            ds = self.dma_sem(dma_key)
            ds[1] += 16
            ev = (ds[0], ds[1])
            inc = (ds[0], 16)
        else:
            E["count"] += 1
            ev = (E["sem"], E["count"])
            inc = (E["sem"], 1)
        for r in reads:
            r.r.append(ev)
        for w in writes:
            w.w = ev
            w.r = []
        if not isinstance(fns, (list, tuple)):
            fns = [fns]
        E["ops"].append((waits, list(fns), inc))
        return ev

    def dma(self, queue, out, in_, reads=(), writes=(), key=None):
        assert key is not None
        return self.op(queue, [lambda e: e.dma_start(out=out, in_=in_)], reads, writes, dma_key=key)

    def barrier(self):
        evs = [(E["sem"], E["count"]) for E in self.engs.values() if E["count"] > 0]
        evs += [(s, c) for (s, c) in self.dma_sems.values() if c > 0]
        for n, E in self.engs.items():
            waits = []
            for sem, val in evs:
                if E["known"].get(sem, 0) >= val:
                    continue
                if sem is E["sem"] and n == "pe":
                    continue
                E["known"][sem] = val
                waits.append((sem, val))
            if waits:
                E["ops"].append((waits, [], None))

    def emit(self):
        nc = self.nc
        import sys
        print("SCHED counts", {n: E["count"] for n, E in self.engs.items()}, "dma", {k: v[1] for k, v in self.dma_sems.items()}, file=sys.stderr)

        def run(name):
            def f(e):
                for waits, fns, inc in self.engs[name]["ops"]:
                    for sem, val in waits:
                        e.wait_ge(sem, val)
                    n = len(fns)
                    for i, fn in enumerate(fns):
                        ins = fn(e)
                        if i == n - 1:
                            ins.then_inc(inc[0], inc[1])
            return f

        with nc.Block() as block:
            block.tensor(run("pe"))
            block.scalar(run("act"))
            block.vector(run("dve"))
            block.gpsimd(run("pool"))
            block.sync(run("sp"))


def _bf(a):
    return np.ascontiguousarray(a.astype(ml_dtypes.bfloat16))


_CONST_CACHE = {}


def const_tables():
    if _CONST_CACHE:
        return _CONST_CACHE
    N = 4096
    f = np.arange(2048, dtype=np.float64)
    t = np.arange(2048, dtype=np.float64)
    m = np.mod(np.outer(2 * f + 1, t), 2 * N)
    ang = np.pi * m / N
    C = np.cos(ang)
    Sn = np.sin(ang)
    CT = C.T.reshape(16, 128, 16, 128)
    ST = (-Sn).T.reshape(16, 128, 16, 128)
    fwd = np.stack([CT, ST], axis=0)
    fwd = fwd.transpose(3, 2, 0, 1, 4)
    _CONST_CACHE["fwd_tab"] = _bf(fwd.reshape(16, 128, 2 * 16 * 128))
    Ci = (C / 2048.0).reshape(4, 4, 128, 4, 512)
    Si = (-Sn / 2048.0).reshape(4, 4, 128, 4, 512)
    inv = np.stack([Ci, Si], axis=0)
    inv = inv.transpose(4, 1, 3, 2, 0, 5)
    _CONST_CACHE["inv_tab"] = _bf(inv.reshape(4, 4, 128, 4 * 2 * 512))
    f32 = np.float32
    tl = np.linspace(0.0, 1.0, L, dtype=f32)[:, None]
    w = (f32(2.0 * math.pi) * np.arange(L, dtype=f32)[:, None] / f32(L)).astype(f32)
    fr = np.linspace(1e-4, 15, 16, dtype=f32)[None, :]
    z = np.concatenate([tl, np.cos(fr * w), -np.sin(fr * w)], axis=-1).astype(f32)
    _CONST_CACHE["zfT"] = np.ascontiguousarray(z.T)
    _CONST_CACHE["ntn"] = np.ascontiguousarray((-tl[:, 0]).reshape(16, 128).T.astype(f32))
    min_decay = math.log(1e-2) / 1.5
    max_decay = math.log(1e-2) / 0.3
    _CONST_CACHE["deltas"] = np.abs(np.linspace(min_decay, max_decay, 768, dtype=f32)).astype(f32)
    inv_f = (10000.0 ** (-np.arange(0, 64, 2, dtype=f32) / f32(64))).astype(f32)
    angr = (np.arange(L, dtype=f32)[:, None] * inv_f[None, :]).astype(f32)
    angr = np.concatenate([angr, angr], axis=-1)
    cosT = np.cos(angr).T.astype(f32)
    sinT = np.sin(angr).T.astype(f32)
    _CONST_CACHE["ropec"] = np.ascontiguousarray(np.concatenate([cosT, cosT], axis=0))
    _CONST_CACHE["ropes"] = np.ascontiguousarray(np.concatenate([sinT, sinT], axis=0))
    Pm = np.zeros((128, 128), np.float32)
    for po in range(128):
        d = po % 64
        if d < 32:
            Pm[po + 32, po] = -1.0
        else:
            Pm[po - 32, po] = 1.0
    _CONST_CACHE["pm"] = _bf(Pm)
    _CONST_CACHE["ident"] = _bf(np.eye(128, dtype=np.float32))
    k = np.arange(128)[:, None]
    q = np.arange(128)[None, :]
    NEG = -30000.0
    m_next = np.where(k <= q, 0.0, NEG)
    m_prev = np.where(k >= q, 0.0, NEG)
    _CONST_CACHE["maskb"] = _bf(np.concatenate([m_next, np.zeros((128, 128)), m_prev], axis=1))
    return _CONST_CACHE


def build_program(debug=None):
    nc = bass.Bass("TRN2", target_bir_lowering=False)
    dbg = {}

    def din(name, shape, dt=F32):
        return nc.dram_tensor(name, list(shape), dt, kind="ExternalInput").ap()

    x_d = din("x", [L, D])
    mem_d = din("mem", [256, D])
    wkv_d = din("w_mem_kv", [D, 512])
    w_in_d = [din("l0_w_in", [D, 2560]), din("l1_w_in", [D, 1536])]
    w_out_d = [din("l0_w_out", [D, D]), din("l1_w_out", [D, D])]
    w_up_d = [din("l%d_ffn_w_up" % i, [D, 2 * DFF]) for i in range(2)]
    w_dn_d = [din("l%d_ffn_w_down" % i, [DFF, D]) for i in range(2)]
    ln_d = [[din("l%d_%s" % (i, n), [D]) for n in ("ln1_g", "ln1_b", "ln2_g", "ln2_b")] for i in range(2)]
    fcwb_d = [din("l%d_fcwb" % i, [128, 44, 4]) for i in range(2)]
    cwb_d = din("l0_cwb", [128, 18, 4])
    fw1_d = din("l0_filt_w1", [33, 64])
    fw2_d = din("l0_filt_w2", [64, 64])
    fw3_d = din("l0_filt_w3", [64, 64])
    fwo_d = din("l0_filt_w_out", [64, 1536])
    fbf_d = din("l0_fbf", [64, 6])
    hd_d = din("l0_hyena_d", [768])
    sink_d = din("l1_sink", [12])
    fwd_tab_d = din("fwd_tab", [16, 128, 4096], BF16)
    inv_tab_d = din("inv_tab", [4, 4, 128, 4096], BF16)
    zfT_d = din("zfT", [33, L])
    ntn_d = din("ntn", [128, 16])
    deltas_d = din("deltas", [768])
    ropec_d = din("ropec", [128, L])
    ropes_d = din("ropes", [128, L])
    pm_d = din("pm", [128, 128], BF16)
    ident_d = din("ident", [128, 128], BF16)
    maskb_d = din("maskb", [128, 384], BF16)
    out_d = nc.dram_tensor("out", [L, D], F32, kind="ExternalOutput").ap()
    if debug:
        dbg_d = nc.dram_tensor("dbg", [L, D], F32, kind="ExternalOutput").ap()

    es = ExitStack()
    with es:
        S = Sched(nc, es)
        AR_BYTES = 194 * 1024
        arena = es.enter_context(nc.sbuf_tensor("arena", [128, AR_BYTES // 2], BF16))
        ps = es.enter_context(nc.psum_tensor("ps", [128, 4096], F32))
        PB = [Res("pb%d" % i, excl=True) for i in range(8)]

        def bank(b, n=512, p0=0, p1=128):
            return ps[p0:p1, b * 512:b * 512 + n]

        def bank_bf(b):
            return ps[:, b * 512:(b + 1) * 512].bitcast(BF16)

        def AV(off, shape, dt):
            n = int(np.prod(shape))
            if dt == F32:
                v = arena[:, off // 2: off // 2 + 2 * n].bitcast(F32)
            else:
                v = arena[:, off // 2: off // 2 + n]
            if len(shape) == 2:
                v = v.rearrange("p (a b) -> p a b", a=shape[0])
            elif len(shape) == 3:
                v = v.rearrange("p (a b c) -> p a b c", a=shape[0], b=shape[1])
            elif len(shape) == 4:
                v = v.rearrange("p (a b c d) -> p a b c d", a=shape[0], b=shape[1], c=shape[2])
            return v

        K_ = 1024
        R_X, R_XT, R_Y, R_YM, R_MQ, R_T = 0, 64 * K_, 96 * K_, 120 * K_, 128 * K_, 136 * K_

        def small(name, shape, dt):
            return es.enter_context(nc.sbuf_tensor("sb_" + name, list(shape), dt))

        ident = small("ident", [128, 128], BF16)
        pm = small("pm", [128, 128], BF16)
        maskb = small("maskb", [128, 384], BF16)
        ones64 = small("ones64", [128, 64], BF16)
        memKT = small("memKT", [128, 2, 256], BF16)
        memV = small("memV", [128, 2, 256], BF16)
        cwb = small("cwb", [128, 18, 4], F32)
        fcwb = small("fcwb", [128, 44, 4], F32)
        stat = small("stat", [128, 128], F32)
        GB = small("GB", [128, 2, 1024], F32)
        r_consts = Res("consts")
        r_memKV = Res("memKV")
        r_cwb = Res("cwb")
        r_fcwb = Res("fcwb")
        r_GB = Res("GB")

        X = AV(R_X, [16, 1024], F32)
        XT = AV(R_XT, [8, 2048], BF16)
        YT = AV(R_Y, [6, 2048], BF16)
        YM = AV(R_YM, [2, 2048], BF16)
        MQ = AV(R_MQ, [2, 2048], BF16)
        r_X = [Res("X%d" % i) for i in range(16)]
        r_XT = Res("XT")
        r_YT = Res("YT")
        r_YM = Res("YM")
        r_MQ = Res("MQ")

        _bk = [0]

        def nxt(lst):
            b = lst[_bk[0] % len(lst)]
            _bk[0] += 1
            return b

        def mm_group(out, pairs, reads, writes):
            n = len(pairs)
            fns = []
            for i, (l, r) in enumerate(pairs):
                fns.append(lambda e, l=l, r=r, i=i: e.matmul(out, l, r, start=(i == 0), stop=(i == n - 1)))
            return S.op("pe", fns, reads, writes)

        def act_copy(out, in_, reads, writes):
            return S.op("act", [lambda e: e.activation(out=out, in_=in_, func=AF.Identity)], reads, writes)

        S.dma("sp", ident[:], ident_d[:, :], writes=[r_consts], key="c0")
        S.dma("sp", pm[:], pm_d[:, :], writes=[r_consts], key="c1")
        S.dma("sp", maskb[:], maskb_d[:, :], writes=[r_consts], key="c2")
        S.dma("sp", cwb[:], cwb_d[:, :, :], writes=[r_cwb], key="c3")
        S.op("dve", [lambda e: e.memset(ones64[:], 1.0)], writes=[r_consts])
        epst = small("epst", [128, 1], F32)
        S.op("dve", [lambda e: e.memset(epst[:], EPS)], writes=[r_consts])


        def transpose_to_XT(src_bf, tile, r_src):
            b = nxt([6, 7])
            bb = bank_bf(b)
            fns = [lambda e, kc=kc: e.transpose(bb[:, kc * 128:(kc + 1) * 128], src_bf[:, kc * 128:(kc + 1) * 128], ident[:])
                   for kc in range(8)]
            S.op("pe", fns, reads=[r_src, r_consts], writes=[PB[b]])
            S.op("act", [lambda e: e.activation(out=XT[:, :, tile * 128:(tile + 1) * 128],
                                                in_=bb.rearrange("p (k m) -> p k m", k=8), func=AF.Identity)],
                 reads=[PB[b]], writes=[r_XT])

        def load_ln(layer, which):
            g_d, b_d = ln_d[layer][2 * which], ln_d[layer][2 * which + 1]
            S.dma("sp", GB[:, 0, :], g_d.partition_broadcast(128), writes=[r_GB], key="gb0")
            S.dma("sp", GB[:, 1, :], b_d.partition_broadcast(128), writes=[r_GB], key="gb1")

        NRB = 6
        LN_LAG = 3
        _rboff = [46 * K_, 50 * K_, 28672, 28672 + 4096, 36880, 36880 + 4096]
        rbuf = [AV(R_T + _rboff[i], [1024], F32) for i in range(NRB)]
        _xboff = [54 * K_, 56 * K_, 24 * K_, 26 * K_]
        xbuf = [AV(R_T + _xboff[i], [1024], BF16) for i in range(4)]
        r_rbuf = [Res("rbuf%d" % i) for i in range(NRB)]
        r_xbuf = [Res("xbuf%d" % i) for i in range(4)]
        r_stat = Res("stat")
        r_stats = [Res("stat%d" % i) for i in range(8)]
        _ln = [0]

        class LNPipe:
            def __init__(self, final_out):
                self.final_out = final_out
                self.q = []

            def push(self, tile, r_in, r_r):
                ri = tile % NRB
                so = ri * 16
                r_stat = r_stats[ri]
                st6 = stat[:, so:so + 12]
                mv = stat[:, so + 12:so + 14]
                rstd = stat[:, so + 14:so + 15]
                final_out = self.final_out

                def s1():
                    S.op("dve", [lambda e: e.bn_stats(out=st6[:, 0:6], in_=r_in[:, 0:512])], reads=[r_r], writes=[r_stat])
                    S.op("dve", [lambda e: e.bn_stats(out=st6[:, 6:12], in_=r_in[:, 512:1024])], reads=[r_r], writes=[r_stat])
                    S.op("dve", [lambda e: e.bn_aggr(out=mv, in_=st6)], reads=[r_stat], writes=[r_stat])
                    S.op("act", [lambda e: e.activation(out=rstd, in_=mv[:, 1:2], func=AF.Sqrt, bias=epst[:, 0:1])],
                         reads=[r_stat, r_consts], writes=[r_stat])

                def s2():
                    S.op("dve", [lambda e: e.reciprocal(out=rstd, in_=rstd)], reads=[r_stat], writes=[r_stat])
                    S.op("dve", [lambda e: e.tensor_scalar(out=r_in, in0=r_in, scalar1=mv[:, 0:1], scalar2=rstd,
                                                           op0=ALU.subtract, op1=ALU.mult)],
                         reads=[r_r, r_stat], writes=[r_r])
                    S.op("dve", [lambda e: e.tensor_tensor(out=r_in, in0=r_in, in1=GB[:, 0, :], op=ALU.mult)],
                         reads=[r_r, r_GB], writes=[r_r])
                    S.op("pool", [lambda e: e.tensor_tensor(out=X[:, tile, :], in0=r_in, in1=GB[:, 1, :], op=ALU.add)],
                         reads=[r_r, r_GB], writes=[r_X[tile]])

                def s3():
                    if final_out:
                        S.dma("sp", out_d[tile * 128:(tile + 1) * 128, :], X[:, tile, :], reads=[r_X[tile]], key="out%d" % (tile % 4))
                    else:
                        i4 = tile % 4
                        S.op("act", [lambda e: e.activation(out=xbuf[i4], in_=X[:, tile, :], func=AF.Identity)],
                             reads=[r_X[tile]], writes=[r_xbuf[i4]])

                def s4():
                    if not final_out:
                        i4 = tile % 4
                        transpose_to_XT(xbuf[i4], tile, r_xbuf[i4])

                s1()
                for k_, f_ in enumerate((s2, s3, s4)):
                    self.q.append([k_ + 1, f_])
                self._tick()

            def _tick(self):
                keep = []
                for item in self.q:
                    item[0] -= 0
                ready = [it for it in self.q if it[0] <= 0]
                for it in ready:
                    it[1]()
                self.q = [it for it in self.q if it[0] > 0]
                for it in self.q:
                    it[0] -= 1

            def flush(self):
                while self.q:
                    self._tick()

        def mem_attention(toff):
            PT = [AV(toff + i * K_, [512], BF16) for i in range(4)]
            r_PT = [Res("mpt%d" % i) for i in range(4)]
            rden = AV(toff + 4 * K_, [512], F32)
            r_rden = Res("mrden")
            for hp in range(2):
                for qt in range(4):
                    qs = slice(qt * 512, (qt + 1) * 512)
                    for hh in range(2):
                        h = 2 * hp + hh
                        prow = slice(hh * 64, (hh + 1) * 64)
                        pts = []
                        for mt in range(2):
                            b = nxt([0, 1, 2, 3])
                            mm_group(bank(b), [(memKT[prow, hp, mt * 128:(mt + 1) * 128], MQ[prow, hp, qs])],
                                     reads=[r_memKV, r_MQ], writes=[PB[b]])
                            k = (hh * 2 + mt)
                            S.op("act", [lambda e, k=k, b=b: e.activation(out=PT[k], in_=bank(b), func=AF.Exp, scale=0.125)],
                                 reads=[PB[b]], writes=[r_PT[k]])
                            pts.append(k)
                        bo, bd = 4, 5
                        mm_group(bank(bo, 512, 0, 64), [(memV[:, mt, h * 64:(h + 1) * 64], PT[pts[mt]]) for mt in range(2)],
                                 reads=[r_memKV, r_PT[pts[0]], r_PT[pts[1]]], writes=[PB[bo]])
                        mm_group(bank(bd, 512, 0, 64), [(ones64[:], PT[pts[mt]]) for mt in range(2)],
                                 reads=[r_consts, r_PT[pts[0]], r_PT[pts[1]]], writes=[PB[bd]])
                        S.op("dve", [lambda e: e.reciprocal(out=rden[0:64, :], in_=bank(bd, 512, 0, 64))],
                             reads=[PB[bd]], writes=[r_rden])
                        S.op("dve", [lambda e, prow=prow, hp=hp, qs=qs: e.tensor_tensor(
                            out=YM[prow, hp, qs], in0=bank(bo, 512, 0, 64), in1=rden[0:64, :], op=ALU.mult)],
                             reads=[PB[bo], r_rden], writes=[r_YM])

        def out_proj_ln1(layer):
            wout = AV(R_T, [8, 1024], BF16)
            r_wout = Res("wout")
            xs = [AV(R_T + 16 * K_ + i * 4 * K_, [1024], F32) for i in range(2)]
            r_xs = [Res("xs%d" % i) for i in range(2)]
            S.dma("pool", wout, w_out_d[layer].rearrange("(kc p) n -> p kc n", p=128), writes=[r_wout], key="wout")
            load_ln(layer, 0)
            lnp = LNPipe(False)
            for tile in range(16):
                ts_ = slice(tile * 128, (tile + 1) * 128)
                bp = [0, 2, 4][tile % 3]
                for half in range(2):
                    b = bp + half
                    pairs = []
                    for kc in range(8):
                        lhs = YT[:, kc, ts_] if kc < 6 else YM[:, kc - 6, ts_]
                        pairs.append((lhs, wout[:, kc, half * 512:(half + 1) * 512]))
                    mm_group(bank(b), pairs, reads=[r_YT, r_YM, r_wout], writes=[PB[b]])
                i = tile % 2
                ri = tile % NRB
                if layer == 0:
                    S.dma("sp", xs[i], x_d[ts_, :], writes=[r_xs[i]], key="xs%d" % i)
                    xin, rxin = xs[i], r_xs[i]
                else:
                    xin, rxin = X[:, tile, :], r_X[tile]
                rb = rbuf[ri]
                S.op("dve", [lambda e, xin=xin, rb=rb, bp=bp: e.scalar_tensor_tensor(
                    out=rb, in0=xin, scalar=ALPHA, in1=ps[:, bp * 512:bp * 512 + 1024], op0=ALU.mult, op1=ALU.add)],
                     reads=[rxin, PB[bp], PB[bp + 1]], writes=[r_rbuf[ri]])
                lnp.push(tile, rb, r_rbuf[ri])
            lnp.flush()

        def ffn_ln2(layer, final):
            blocks = [4, 4, 4, 4, 3, 3]
            S.dma("sp", fcwb[:], fcwb_d[layer][:, :, :], writes=[r_fcwb], key="fcwb")
            load_ln(layer, 1)
            GT = AV(R_Y, [4, 2048], BF16)
            r_GT = Res("GT")
            wdn = [AV(R_T + i * 8 * K_, [4, 1024], BF16) for i in range(2)]
            r_wdn = [Res("wdn%d" % i) for i in range(2)]
            wup = [AV(R_T + 16 * K_ + i * 4 * K_, [8, 2, 128], BF16) for i in range(3)]
            r_wup = [Res("wup%d" % i) for i in range(3)]
            hsb = [AV(R_T + 28 * K_ + i * 8208, [2052], F32) for i in range(2)]
            r_hsb = [Res("hsb%d" % i) for i in range(2)]
            for i in range(2):
                S.op("pool", [lambda e, i=i: e.memset(hsb[i][:, 0:1], 0.0)], writes=[r_hsb[i]])
                S.op("pool", [lambda e, i=i: e.memset(hsb[i][:, 2049:2050], 0.0)], writes=[r_hsb[i]])
            acc_as = [AV(R_YM, [2048], F32), AV(R_Y + 16 * K_, [2048], F32)]
            acc_g = AV(R_MQ, [2048], F32)
            r_accas, r_accg = [Res("acca0"), Res("acca1")], Res("accg")
            wd_v = w_dn_d[layer]
            wu_v = w_up_d[layer].rearrange("(kc p) n -> p kc n", p=128)
            j0 = 0
            for bi, nb in enumerate(blocks):
                wd = wdn[bi % 2]
                S.dma("pool", wd[:, 0:nb, :], wd_v[j0 * 128:(j0 + nb) * 128, :].rearrange("(j p) n -> p j n", p=128),
                      writes=[r_wdn[bi % 2]], key="wdn%d" % (bi % 2))
                deferred = None
                for jj in range(nb):
                    j = j0 + jj
                    wi = j % 3
                    wu = wup[wi]
                    acc_a, r_acca = acc_as[j % 2], r_accas[j % 2]
                    S.dma("pool", wu[:, :, 0, :], wu_v[:, :, j * 128:(j + 1) * 128], writes=[r_wup[wi]], key="wup%da" % wi)
                    S.dma("pool", wu[:, :, 1, :], wu_v[:, :, DFF + j * 128:DFF + (j + 1) * 128], writes=[r_wup[wi]], key="wup%db" % wi)
                    for ag in range(2):
                        hs_ = hsb[ag]
                        cj = ag * NFF + j
                        for tq in range(4):
                            b = nxt([0, 1, 2, 3])
                            mm_group(bank(b), [(wu[:, kc, ag, :], XT[:, kc, tq * 512:(tq + 1) * 512]) for kc in range(8)],
                                     reads=[r_wup[wi], r_XT], writes=[PB[b]])
                            S.op("act", [lambda e, hs_=hs_, tq=tq, b=b: e.activation(
                                out=hs_[:, 1 + tq * 512:1 + (tq + 1) * 512], in_=bank(b), func=AF.Identity)],
                                 reads=[PB[b]], writes=[r_hsb[ag]])
                        acc, r_acc = (acc_a, r_acca) if ag == 0 else (acc_g, r_accg)
                        S.op("act", [lambda e, acc=acc, hs_=hs_, cj=cj: e.activation(
                            out=acc, in_=hs_[:, 1:2049], func=AF.Identity, scale=fcwb[:, cj, 1:2], bias=fcwb[:, cj, 3:4])],
                             reads=[r_hsb[ag], r_fcwb], writes=[r_acc])
                        S.op("dve", [lambda e, acc=acc, hs_=hs_, cj=cj: e.scalar_tensor_tensor(
                            out=acc, in0=hs_[:, 0:2048], scalar=fcwb[:, cj, 0:1], in1=acc, op0=ALU.mult, op1=ALU.add)],
                             reads=[r_hsb[ag], r_fcwb, r_acc], writes=[r_acc])
                        S.op("dve", [lambda e, acc=acc, hs_=hs_, cj=cj: e.scalar_tensor_tensor(
                            out=acc, in0=hs_[:, 2:2050], scalar=fcwb[:, cj, 2:3], in1=acc, op0=ALU.mult, op1=ALU.add)],
                             reads=[r_hsb[ag], r_fcwb, r_acc], writes=[r_acc])
                        if ag == 0 and deferred is not None:
                            deferred()
                            deferred = None

                    def _fin(jj=jj, acc_a=acc_a, r_acca=r_acca):
                        S.op("act", [lambda e: e.activation(out=acc_g, in_=acc_g, func=AF.Silu)], reads=[r_accg], writes=[r_accg])
                        S.op("dve", [lambda e: e.tensor_tensor(out=GT[:, jj, :], in0=acc_g, in1=acc_a, op=ALU.mult)],
                             reads=[r_accg, r_acca], writes=[r_GT])
                    deferred = _fin
                if deferred is not None:
                    deferred()
                    deferred = None
                last = bi == len(blocks) - 1
                if last:
                    S.barrier()
                lnp = LNPipe(final)
                for tile in range(16):
                    ts_ = slice(tile * 128, (tile + 1) * 128)
                    bp = [0, 2, 4][tile % 3] if last else [4, 6][tile % 2]
                    for half in range(2):
                        mm_group(bank(bp + half), [(GT[:, jj, ts_], wd[:, jj, half * 512:(half + 1) * 512]) for jj in range(nb)],
                                 reads=[r_GT, r_wdn[bi % 2]], writes=[PB[bp + half]])
                    pin = ps[:, bp * 512:bp * 512 + 1024]
                    if not last:
                        if bi == 0:
                            S.op("dve", [lambda e, tile=tile, pin=pin: e.scalar_tensor_tensor(
                                out=X[:, tile, :], in0=X[:, tile, :], scalar=ALPHA, in1=pin, op0=ALU.mult, op1=ALU.add)],
                                 reads=[PB[bp], PB[bp + 1]], writes=[r_X[tile]])
                        else:
                            S.op("dve", [lambda e, tile=tile, pin=pin: e.tensor_tensor(
                                out=X[:, tile, :], in0=X[:, tile, :], in1=pin, op=ALU.add)],
                                 reads=[PB[bp], PB[bp + 1]], writes=[r_X[tile]])
                    else:
                        ri = tile % NRB
                        rb = rbuf[ri]
                        S.op("dve", [lambda e, tile=tile, pin=pin, rb=rb: e.tensor_tensor(
                            out=rb, in0=X[:, tile, :], in1=pin, op=ALU.add)],
                             reads=[PB[bp], PB[bp + 1], r_X[tile]], writes=[r_rbuf[ri]])
                        lnp.push(tile, rb, r_rbuf[ri])
                lnp.flush()
                j0 += nb

        memf = AV(R_X, [2, 1024], F32)
        memb = AV(R_X + 8 * K_, [2, 1024], BF16)
        memT = AV(R_X + 12 * K_, [8, 256], BF16)
        wkv = AV(R_X + 16 * K_, [8, 512], BF16)
        r_memf, r_memb, r_memT, r_wkv = Res("memf"), Res("memb"), Res("memT"), Res("wkv")
        S.dma("sp", memf, mem_d.rearrange("(mt p) d -> p mt d", p=128), writes=[r_memf], key="memf")
        S.dma("pool", wkv, wkv_d.rearrange("(kc p) n -> p kc n", p=128), writes=[r_wkv], key="wkv")
        act_copy(memb, memf, [r_memf], [r_memb])
        for mt in range(2):
            b = nxt([6, 7])
            bb = bank_bf(b)
            fns = [lambda e, kc=kc, mt=mt, bb=bb: e.transpose(bb[:, kc * 128:(kc + 1) * 128], memb[:, mt, kc * 128:(kc + 1) * 128], ident[:])
                   for kc in range(8)]
            S.op("pe", fns, reads=[r_memb, r_consts], writes=[PB[b]])
            S.op("act", [lambda e, mt=mt, bb=bb: e.activation(out=memT[:, :, mt * 128:(mt + 1) * 128],
                                                             in_=bb.rearrange("p (k m) -> p k m", k=8), func=AF.Identity)],
                 reads=[PB[b]], writes=[r_memT])
        for hp in range(2):
            b = nxt([0, 1])
            mm_group(bank(b, 256), [(wkv[:, kc, hp * 128:(hp + 1) * 128], memT[:, kc, :]) for kc in range(8)],
                     reads=[r_wkv, r_memT], writes=[PB[b]])
            act_copy(memKT[:, hp, :], bank(b, 256), [PB[b]], [r_memKV])
        for mt in range(2):
            b = nxt([0, 1])
            mm_group(bank(b, 256), [(memT[:, kc, mt * 128:(mt + 1) * 128], wkv[:, kc, 256:512]) for kc in range(8)],
                     reads=[r_wkv, r_memT], writes=[PB[b]])
            act_copy(memV[:, mt, :], bank(b, 256), [PB[b]], [r_memKV])
        S.barrier()

        if debug == "s_m":
            S.barrier()
            S.emit()
            return nc
        xs0 = [AV(R_T + 16 * K_ + i * 4 * K_, [1024], F32) for i in range(2)]
        r_xs0 = [Res("xs0_%d" % i) for i in range(2)]
        for tile in range(16):
            i = tile % 2
            S.dma("sp", xs0[i], x_d[tile * 128:(tile + 1) * 128, :], writes=[r_xs0[i]], key="xs%d" % i)
            i4 = tile % 4
            S.op("act", [lambda e, i=i, i4=i4: e.activation(out=xbuf[i4], in_=xs0[i], func=AF.Identity)],
                 reads=[r_xs0[i]], writes=[r_xbuf[i4]])
            transpose_to_XT(xbuf[i4], tile, r_xbuf[i4])

        if debug == "s_x":
            S.barrier()
            S.emit()
            return nc
        HS = AV(R_X, [2, 16, 768], BF16)
        r_HS = Res("HS")
        hA = AV(R_X + 48 * K_, [2048], F32)
        hB = AV(R_X + 56 * K_, [2048], F32)
        r_hA, r_hB = Res("hA"), Res("hB")
        zf = AV(R_Y, [2048], F32)
        fw = AV(R_Y + 8 * K_, [3, 64], F32)
        fwo = AV(R_Y + 9 * K_, [1536], F32)
        fbf = AV(R_Y + 15 * K_, [8], F32)
        dbc = AV(R_Y + 16 * K_, [768], F32)
        dlt = AV(R_YM, [768], F32)
        dec = AV(R_YM + 3 * K_, [768], F32)
        ntn = AV(R_YM + 6 * K_, [16], F32)
        fsb = AV(R_MQ, [768], F32)
        wtmp = AV(R_MQ + 3 * K_, [512], F32)
        wtm2 = AV(R_MQ + 5 * K_, [512], F32)
        r_f = Res("filt_in")
        r_dec, r_fsb, r_wtmp, r_wtm2, r_dbc = Res("dec"), Res("fsb"), Res("wtmp"), Res("wtm2"), Res("dbc")
        S.dma("sp", zf[0:33, :], zfT_d[:, :], writes=[r_f], key="f0")
        S.dma("sp", fw[0:33, 0, :], fw1_d[:, :], writes=[r_f], key="f1")
        S.dma("sp", fw[0:64, 1, :], fw2_d[:, :], writes=[r_f], key="f2")
        S.dma("sp", fw[0:64, 2, :], fw3_d[:, :], writes=[r_f], key="f3")
        S.dma("sp", fwo[0:64, :], fwo_d[:, :], writes=[r_f], key="f4")
        S.dma("sp", fbf[0:64, 0:6], fbf_d[:, :], writes=[r_f], key="f5")
        S.dma("sp", dbc, hd_d.partition_broadcast(128), writes=[r_dbc], key="f6")
        S.dma("sp", dlt, deltas_d.partition_broadcast(128), writes=[r_f], key="f7")
        S.dma("sp", ntn, ntn_d[:, :], writes=[r_f], key="f8")
        fbs = stat[0:64, 120:123]
        for l in range(3):
            S.op("dve", [lambda e, l=l: e.tensor_tensor(out=fbs[:, l:l + 1], in0=fbf[0:64, 2 * l:2 * l + 1],
                                                        in1=fbf[0:64, 2 * l + 1:2 * l + 2], op=ALU.mult)],
                 reads=[r_f], writes=[r_stat])
        srcs = [(zf, 33, r_f), (hA, 64, r_hA), (hB, 64, r_hB)]
        dsts = [(hA, r_hA), (hB, r_hB), (hA, r_hA)]
        for l in range(3):
            src, kk, r_src = srcs[l]
            dst, r_dst = dsts[l]
            for tq in range(4):
                b = nxt([0, 1, 2, 3])
                cs = slice(tq * 512, (tq + 1) * 512)
                mm_group(bank(b, 512, 0, 64), [(fw[0:kk, l, :], src[0:kk, cs])], reads=[r_f, r_src], writes=[PB[b]])
                S.op("dve", [lambda e, b=b, l=l: e.tensor_scalar(out=wtmp[0:64, :], in0=bank(b, 512, 0, 64),
                                                                scalar1=fbf[0:64, 2 * l + 1:2 * l + 2], scalar2=fbs[:, l:l + 1],
                                                                op0=ALU.mult, op1=ALU.add)],
                     reads=[PB[b], r_f, r_stat], writes=[r_wtmp])
                S.op("dve", [lambda e: e.tensor_scalar(out=wtm2[0:64, :], in0=wtmp[0:64, :], scalar1=-PI, scalar2=2 * PI,
                                                       op0=ALU.is_lt, op1=ALU.mult)], reads=[r_wtmp], writes=[r_wtm2])
                S.op("dve", [lambda e: e.tensor_tensor(out=wtmp[0:64, :], in0=wtmp[0:64, :], in1=wtm2[0:64, :], op=ALU.add)],
                     reads=[r_wtmp, r_wtm2], writes=[r_wtmp])
                S.op("dve", [lambda e: e.tensor_scalar(out=wtm2[0:64, :], in0=wtmp[0:64, :], scalar1=PI, scalar2=-2 * PI,
                                                       op0=ALU.is_gt, op1=ALU.mult)], reads=[r_wtmp], writes=[r_wtm2])
                S.op("dve", [lambda e: e.tensor_tensor(out=wtmp[0:64, :], in0=wtmp[0:64, :], in1=wtm2[0:64, :], op=ALU.add)],
                     reads=[r_wtmp, r_wtm2], writes=[r_wtmp])
                S.op("act", [lambda e, dst=dst, cs=cs: e.activation(out=dst[0:64, cs], in_=wtmp[0:64, :], func=AF.Sin)],
                     reads=[r_wtmp], writes=[r_dst])
        for tile in range(16):
            ts_ = slice(tile * 128, (tile + 1) * 128)
            for q3 in range(3):
                mm_group(bank(q3), [(hA[0:64, ts_], fwo[0:64, q3 * 512:(q3 + 1) * 512])], reads=[r_hA, r_f], writes=[PB[q3]])
            S.op("act", [lambda e, tile=tile: e.activation(out=dec, in_=dlt, func=AF.Exp, scale=ntn[:, tile:tile + 1])],
                 reads=[r_f], writes=[r_dec])
            act_copy(fsb, ps[:, 0:768], [PB[0], PB[1]], [r_fsb])
            S.op("dve", [lambda e: e.tensor_tensor(out=hB[:, 0:768], in0=fsb, in1=ps[:, 768:1536], op=ALU.add)],
                 reads=[r_fsb, PB[1], PB[2]], writes=[r_hB])
            S.op("dve", [lambda e: e.tensor_tensor(out=hB[:, 768:1536], in0=fsb, in1=ps[:, 768:1536], op=ALU.subtract)],
                 reads=[r_fsb, PB[1], PB[2]], writes=[r_hB])
            S.op("dve", [lambda e, tile=tile: e.tensor_tensor(out=HS[:, 0, tile, :], in0=hB[:, 0:768], in1=dec, op=ALU.mult)],
                 reads=[r_hB, r_dec], writes=[r_HS])
            S.op("dve", [lambda e, tile=tile: e.tensor_tensor(out=HS[:, 1, tile, :], in0=hB[:, 768:1536], in1=dec, op=ALU.mult)],
                 reads=[r_hB, r_dec], writes=[r_HS])
        S.barrier()

        if debug == "s_f":
            S.barrier()
            S.emit()
            return nc
        KS = AV(R_T, [2, 16, 768], BF16)
        r_KS = Res("KS")

        def fwd_pass(rhs_re, rhs_im, r_rhs, ftab, r_ftab, epilogue):
            for fc in range(16):
                si = fc % 2
                ft = ftab[si]
                S.dma("sp", ft.rearrange("p a b c -> p (a b c)"), fwd_tab_d[fc], writes=[r_ftab[si]], key="ftab%d" % si)
                bs = [0, 1, 2, 3] if fc % 2 == 0 else [4, 5, 6, 7]
                for ri in range(2):
                    rhs = rhs_re if ri == 0 else rhs_im
                    o0 = bs[0] * 512 + ri * 1024
                    fns = []
                    for tc in range(16):
                        lhs = ft[:, ri, tc, :]
                        fns.append(lambda e, lhs=lhs, tc=tc, rhs=rhs, o0=o0: e.matmul(
                            ps[:, o0:o0 + 512], lhs, rhs[:, tc, 0:512], start=(tc == 0), stop=(tc == 15)))
                        fns.append(lambda e, lhs=lhs, tc=tc, rhs=rhs, o0=o0: e.matmul(
                            ps[:, o0 + 512:o0 + 768], lhs, rhs[:, tc, 512:768], start=(tc == 0), stop=(tc == 15)))
                    S.op("pe", fns, reads=[r_ftab[si], r_rhs], writes=[PB[bs[2 * ri]], PB[bs[2 * ri + 1]]])
                pre = ps[:, bs[0] * 512:bs[0] * 512 + 768]
                pim = ps[:, bs[2] * 512:bs[2] * 512 + 768]
                epilogue(fc, pre, pim, [PB[b] for b in bs])

        ftab = [AV(R_YM, [2, 16, 128], BF16), AV(R_MQ, [2, 16, 128], BF16)]
        r_ftab = [Res("ftab0"), Res("ftab1")]

        def k_epilogue(fc, pre, pim, rbs):
            S.op("dve", [lambda e: e.tensor_tensor(out=KS[:, 0, fc, :], in0=pre, in1=dbc, op=ALU.add)],
                 reads=rbs[0:2] + [r_dbc], writes=[r_KS])
            act_copy(KS[:, 1, fc, :], pim, rbs[2:4], [r_KS])

        fwd_pass(HS[:, 0], HS[:, 1], r_HS, ftab, r_ftab, k_epilogue)
        S.barrier()

        if debug == "s_k":
            S.barrier()
            S.emit()
            return nc
        Z = AV(R_X, [16, 768], BF16)
        r_Z = Res("Z")
        hsb0 = [AV(R_X + 24 * K_ + i * 8208, [2052], F32) for i in range(2)]
        r_hsb0 = [Res("hsb0_%d" % i) for i in range(2)]
        accx = AV(R_X + 24 * K_ + 16416, [2048], F32)
        accv = AV(R_X + 24 * K_ + 16416 + 8192, [2048], F32)
        zT = AV(R_X + 24 * K_ + 16416 + 16384, [2048], BF16)
        r_accx, r_accv, r_zT = Res("accx"), Res("accv"), Res("zT")
        wch = [AV(R_T + 48 * K_ + i * 2 * K_, [8, 128], BF16) for i in range(3)]
        r_wch = [Res("wch%d" % i) for i in range(3)]
        for i in range(2):
            S.op("pool", [lambda e, i=i: e.memset(hsb0[i][:, 0:1], 0.0)], writes=[r_hsb0[i]])
            S.op("pool", [lambda e, i=i: e.memset(hsb0[i][:, 2049:2050], 0.0)], writes=[r_hsb0[i]])
        w0v = w_in_d[0].rearrange("(kc p) n -> p kc n", p=128)
        order = [18, 19] + [0, 1, 2, 3, 4, 5]
        for i in range(6):
            order += [6 + i, 12 + i]
        _wc = [0]

        def proj_chunk(wv, col0, sink_fn, extra_reads=()):
            wi = _wc[0] % 3
            _wc[0] += 1
            S.dma("pool", wch[wi], wv[:, :, col0:col0 + 128], writes=[r_wch[wi]], key="wch%d" % wi)
            for tq in range(4):
                b = nxt([0, 1, 2, 3])
                mm_group(bank(b), [(wch[wi][:, kc, :], XT[:, kc, tq * 512:(tq + 1) * 512]) for kc in range(8)],
                         reads=[r_wch[wi], r_XT], writes=[PB[b]])
                sink_fn(tq, b)

        def conv_chunk(c, hs_, r_hs, acc_out, r_acc_list, out_final):
            S.op("act", [lambda e: e.activation(out=acc_out, in_=hs_[:, 1:2049], func=AF.Identity,
                                                scale=cwb[:, c, 1:2], bias=cwb[:, c, 3:4])],
                 reads=[r_hs, r_cwb], writes=r_acc_list)
            S.op("dve", [lambda e: e.scalar_tensor_tensor(out=acc_out, in0=hs_[:, 0:2048], scalar=cwb[:, c, 0:1],
                                                          in1=acc_out, op0=ALU.mult, op1=ALU.add)],
                 reads=[r_hs, r_cwb] + r_acc_list, writes=r_acc_list)
            S.op("dve", [lambda e: e.scalar_tensor_tensor(out=out_final[0], in0=hs_[:, 2:2050], scalar=cwb[:, c, 2:3],
                                                          in1=acc_out, op0=ALU.mult, op1=ALU.add)],
                 reads=[r_hs, r_cwb] + r_acc_list, writes=out_final[1])

        zdef = []
        for ci_, c in enumerate(order):
            if debug and debug.startswith('s_p') and ci_ == int(debug[3:]):
                S.barrier()
                S.emit()
                return nc
            if c >= 18:
                hp = c - 18

                def sink_mq(tq, b, hp=hp):
                    act_copy(MQ[:, hp, tq * 512:(tq + 1) * 512], bank(b), [PB[b]], [r_MQ])
                proj_chunk(w0v, c * 128, sink_mq)
                continue
            si = (c % 2) if c < 6 else (0 if c < 12 else 1)
            hs_ = hsb0[si]

            def sink_h(tq, b, hs_=hs_, si=si):
                act_copy(hs_[:, 1 + tq * 512:1 + (tq + 1) * 512], bank(b), [PB[b]], [r_hsb0[si]])
            proj_chunk(w0v, c * 128, sink_h)
            if c < 6:
                conv_chunk(c, hs_, r_hsb0[si], accv, [r_accv], (YT[:, c, :], [r_YT]))
            elif c < 12:
                conv_chunk(c, hs_, r_hsb0[si], accx, [r_accx], (accx, [r_accx]))
                while zdef:
                    zdef.pop(0)()
            else:
                i6 = c - 12
                conv_chunk(c, hs_, r_hsb0[si], accv, [r_accv], (accv, [r_accv]))
                S.op("dve", [lambda e: e.tensor_tensor(out=zT, in0=accv, in1=accx, op=ALU.mult)],
                     reads=[r_accv, r_accx], writes=[r_zT])
                def _ztr(i6=i6):
                    for g8 in range(2):
                        b = nxt([6, 7])
                        bb = bank_bf(b)
                        fns = [lambda e, t8=t8, bb=bb, g8=g8: e.transpose(bb[:, t8 * 128:(t8 + 1) * 128],
                                                                         zT[:, (g8 * 8 + t8) * 128:(g8 * 8 + t8 + 1) * 128], ident[:])
                               for t8 in range(8)]
                        S.op("pe", fns, reads=[r_zT, r_consts], writes=[PB[b]])
                        S.op("act", [lambda e, g8=g8, bb=bb, i6=i6: e.activation(
                            out=Z[:, g8 * 8:(g8 + 1) * 8, i6 * 128:(i6 + 1) * 128],
                            in_=bb.rearrange("p (k m) -> p k m", k=8), func=AF.Identity)],
                             reads=[PB[b]], writes=[r_Z])
                zdef.append(_ztr)
        while zdef:
            zdef.pop(0)()
        S.barrier()
        if debug == "z":
            for tile in range(16):
                S.op("act", [lambda e, tile=tile: e.activation(out=X[:, tile, 0:768] if False else hsb0[0][:, 0:768], in_=Z[:, tile, :], func=AF.Identity)],
                     reads=[r_Z], writes=[r_hsb0[0]])
                S.dma("sp", dbg_d[tile * 128:(tile + 1) * 128, 0:768], hsb0[0][:, 0:768], reads=[r_hsb0[0]], key="dbg")
            S.barrier()
            S.emit()
            return nc
        mem_attention(R_X + 24 * K_)
        S.barrier()

        YRE = AV(R_XT, [16, 768], BF16)
        YIM = AV(R_X + 24 * K_, [16, 768], BF16)
        r_Y = Res("Yspec")
        ct = [AV(R_X + 48 * K_ + i * 3 * K_, [768], F32) for i in range(4)]
        r_ct = [Res("ct%d" % i) for i in range(4)]
        ftabU = [AV(R_T + 48 * K_, [2, 16, 128], BF16), AV(R_MQ, [2, 16, 128], BF16)]
        r_ftabU = [Res("ftabU0"), Res("ftabU1")]

        def u_epilogue(fc, pre, pim, rbs):
            kre, kim = KS[:, 0, fc, :], KS[:, 1, fc, :]
            S.op("dve", [lambda e: e.tensor_tensor(out=ct[0], in0=pre, in1=kre, op=ALU.mult)], reads=rbs[0:2] + [r_KS], writes=[r_ct[0]])
            S.op("dve", [lambda e: e.tensor_tensor(out=ct[1], in0=pim, in1=kim, op=ALU.mult)], reads=rbs[2:4] + [r_KS], writes=[r_ct[1]])
            S.op("dve", [lambda e: e.tensor_tensor(out=ct[2], in0=pre, in1=kim, op=ALU.mult)], reads=rbs[0:2] + [r_KS], writes=[r_ct[2]])
            S.op("dve", [lambda e: e.tensor_tensor(out=ct[3], in0=pim, in1=kre, op=ALU.mult)], reads=rbs[2:4] + [r_KS], writes=[r_ct[3]])
            S.op("pool", [lambda e: e.tensor_tensor(out=YRE[:, fc, :], in0=ct[0], in1=ct[1], op=ALU.subtract)],
                 reads=[r_ct[0], r_ct[1]], writes=[r_Y])
            S.op("pool", [lambda e: e.tensor_tensor(out=YIM[:, fc, :], in0=ct[2], in1=ct[3], op=ALU.add)],
                 reads=[r_ct[2], r_ct[3]], writes=[r_Y])

        fwd_pass(Z, Z, r_Z, ftabU, r_ftabU, u_epilogue)
        S.barrier()

        itab = [AV(R_T + 48 * K_, [4, 2, 512], BF16), AV(R_MQ, [4, 2, 512], BF16)]
        r_itab = [Res("itab0"), Res("itab1")]
        _it = 0
        for tt in range(4):
            fn_all = []
            for fg in range(4):
                si = _it % 2
                _it += 1
                S.dma("sp", itab[si].rearrange("p a b c -> p (a b c)"), inv_tab_d[tt, fg], writes=[r_itab[si]], key="itab%d" % si)
                fns = []
                for fi in range(4):
                    fc = fg * 4 + fi
                    for ri in range(2):
                        Ysrc = YRE if ri == 0 else YIM
                        for cc in range(6):
                            first = (fc == 0 and ri == 0)
                            lastm = (fc == 15 and ri == 1)
                            fns.append(lambda e, cc=cc, Ysrc=Ysrc, fc=fc, si=si, fi=fi, ri=ri, first=first, lastm=lastm: e.matmul(
                                bank(cc), Ysrc[:, fc, cc * 128:(cc + 1) * 128], itab[si][:, fi, ri, :], start=first, stop=lastm))
                S.op("pe", fns, reads=[r_itab[si], r_Y], writes=[PB[cc] for cc in range(6)])
            for cc in range(6):
                S.op("dve", [lambda e, cc=cc, tt=tt: e.tensor_tensor(out=YT[:, cc, tt * 512:(tt + 1) * 512], in0=bank(cc),
                                                                    in1=YT[:, cc, tt * 512:(tt + 1) * 512], op=ALU.mult)],
                     reads=[PB[cc], r_YT], writes=[r_YT])
        S.barrier()

        if debug == "mix0":
            for c in range(8):
                src = YT[:, c, 0:1024] if c < 6 else YM[:, c - 6, 0:1024]
                S.op("act", [lambda e, src=src: e.activation(out=rbuf[0], in_=src, func=AF.Identity)], reads=[r_YT, r_YM], writes=[r_rbuf[0]])
                S.dma("sp", dbg_d[c * 128:(c + 1) * 128, :], rbuf[0], reads=[r_rbuf[0]], key="dbg")
            S.barrier()
            S.emit()
            return nc

        out_proj_ln1(0)
        S.barrier()
        if debug == "ln1_0":
            for tile in range(16):
                S.dma("sp", dbg_d[tile * 128:(tile + 1) * 128, :], X[:, tile, :], reads=[r_X[tile]], key="dbg")
            S.barrier()
            S.emit()
            return nc
        ffn_ln2(0, final=(debug == "l0"))
        S.barrier()

        if debug is None or debug.startswith("l1"):
            if debug == "l1_s":
                S.barrier()
                S.emit()
                return nc
            w1v = w_in_d[1].rearrange("(kc p) n -> p kc n", p=128)
            QT = YT
            r_QT = [Res("QT%d" % i) for i in range(12)]
            KT = AV(R_T, [4, 2048], BF16)
            r_KT = Res("KT")
            VT = AV(R_T + 16 * K_, [16, 256], BF16)
            r_VT = Res("VT")
            ropec = AV(R_T + 24 * K_, [2048], F32)
            ropes = AV(R_T + 32 * K_, [2048], F32)
            r_rope = Res("rope")
            wch1 = [AV(R_T + 40 * K_ + i * 2 * K_, [8, 128], BF16) for i in range(3)]
            r_wch1 = [Res("wch1_%d" % i) for i in range(3)]
            qsb = [AV(R_T + 46 * K_ + i * K_, [512], BF16) for i in range(2)]
            r_qsb = [Res("qsb%d" % i) for i in range(2)]
            rt1 = AV(R_T + 48 * K_, [512], F32)
            rt2 = AV(R_T + 50 * K_, [512], F32)
            r_rt1, r_rt2 = Res("rt1"), Res("rt2")
            wvt = AV(R_T + 52 * K_, [8, 256], BF16)
            r_wvt = Res("wvt")
            esk = small("esk", [64, 12], F32)
            r_esk = Res("esk")
            import os
            SK = os.environ.get("SKIP", "")
            if "r" not in SK:
                S.dma("sp", ropec, ropec_d[:, :], writes=[r_rope], key="rope0")
                S.dma("sp", ropes, ropes_d[:, :], writes=[r_rope], key="rope1")
            if "e" not in SK:
                S.dma("sp", esk[:], sink_d.partition_broadcast(64), writes=[r_esk], key="esk")
            if "x" not in SK:
                S.op("act", [lambda e: e.activation(out=esk[:], in_=esk[:], func=(AF.Identity if "I" in SK else AF.Exp))], reads=[r_esk], writes=[r_esk])
            if "w" not in SK:
                S.dma("pool", wvt, w1v[:, :, 1024:1280], writes=[r_wvt], key="wvt")
            if debug == "l1_p0":
                S.barrier()
                S.emit()
                return nc
            _w1 = [0]
            _rq = [0]

            def proj1(loads, sink_fn):
                wi = _w1[0] % 3
                _w1[0] += 1
                for k_, (dst0, dst1, c0, c1) in enumerate(loads):
                    S.dma("pool", wch1[wi][:, :, dst0:dst1], w1v[:, :, c0:c1], writes=[r_wch1[wi]], key="wch1_%d_%d" % (wi, k_))
                for tq in range(4):
                    b = nxt([0, 1, 2, 3])
                    mm_group(bank(b), [(wch1[wi][:, kc, :], XT[:, kc, tq * 512:(tq + 1) * 512]) for kc in range(8)],
                             reads=[r_wch1[wi], r_XT], writes=[PB[b]])
                    sink_fn(tq, b)

            def rope_sink(dest, r_dest):
                def sink(tq, b):
                    cs = slice(tq * 512, (tq + 1) * 512)
                    i = _rq[0] % 2
                    _rq[0] += 1
                    S.op("act", [lambda e: e.activation(out=qsb[i], in_=bank(b), func=AF.Identity)],
                         reads=[PB[b]], writes=[r_qsb[i]])
                    b2 = [4, 5][i]
                    mm_group(bank(b2), [(pm[:], qsb[i])], reads=[r_consts, r_qsb[i]], writes=[PB[b2]])
                    S.op("dve", [lambda e: e.tensor_tensor(out=rt1, in0=bank(b), in1=ropec[:, cs], op=ALU.mult)],
                         reads=[PB[b], r_rope], writes=[r_rt1])
                    S.op("dve", [lambda e: e.tensor_tensor(out=rt2, in0=bank(b2), in1=ropes[:, cs], op=ALU.mult)],
                         reads=[PB[b2], r_rope], writes=[r_rt2])
                    S.op("dve", [lambda e: e.tensor_tensor(out=dest[:, cs], in0=rt1, in1=rt2, op=ALU.add)],
                         reads=[r_rt1, r_rt2], writes=r_dest)
                return sink

            for c in range(6):
                proj1([(0, 128, c * 128, (c + 1) * 128)], rope_sink(QT[:, c, :], [r_QT[2 * c], r_QT[2 * c + 1]]))
            if debug == "l1_p1":
                S.barrier()
                S.emit()
                return nc
            for g in range(4):
                c0 = 768 + g * 64
                proj1([(0, 64, c0, c0 + 64), (64, 128, c0, c0 + 64)], rope_sink(KT[:, g, :], [r_KT]))
            if debug == "l1_p2":
                S.barrier()
                S.emit()
                return nc
            for hp in range(2):
                def sink_mq1(tq, b, hp=hp):
                    act_copy(MQ[:, hp, tq * 512:(tq + 1) * 512], bank(b), [PB[b]], [r_MQ])
                proj1([(0, 128, 1280 + hp * 128, 1280 + (hp + 1) * 128)], sink_mq1)
            for tile in range(16):
                b = nxt([0, 1, 2, 3])
                mm_group(bank(b, 256), [(XT[:, kc, tile * 128:(tile + 1) * 128], wvt[:, kc, :]) for kc in range(8)],
                         reads=[r_XT, r_wvt], writes=[PB[b]])
                act_copy(VT[:, tile, :], bank(b, 256), [PB[b]], [r_VT])
            S.barrier()

            if debug == "l1_p":
                S.barrier()
                S.emit()
                return nc
            PTs = [AV(R_XT + i * 12 * K_, [16, 384], BF16) for i in range(2)]
            r_PTs = [Res("PTs%d" % i) for i in range(2)]
            rden1 = AV(R_XT + 24 * K_, [512], F32)
            r_rden1 = Res("rden1")
            for h in range(12):
                g = h // 3
                hh = h % 2
                c = h // 2
                prow = slice(hh * 64, (hh + 1) * 64)
                pt = PTs[h % 2]
                r_pt = r_PTs[h % 2]
                for j in range(16):
                    qlo = max(0, j - 1) * 128
                    qhi = min(16, j + 2) * 128
                    n = qhi - qlo
                    moff = qlo - (j - 1) * 128
                    b = nxt([0, 1, 2, 3])
                    fns = [
                        lambda e, b=b, n=n, j=j, qlo=qlo, qhi=qhi, prow=prow, g=g, c=c: e.matmul(
                            bank(b, n), KT[prow, g, j * 128:(j + 1) * 128], QT[prow, c, qlo:qhi], start=True, stop=False),
                        lambda e, b=b, n=n, moff=moff: e.matmul(
                            bank(b, n), ident[:], maskb[:, moff:moff + n], start=False, stop=True),
                    ]
                    S.op("pe", fns, reads=[r_KT, r_QT[h], r_consts], writes=[PB[b]])
                    S.op("act", [lambda e, b=b, n=n, j=j, pt=pt: e.activation(out=pt[:, j, 0:n], in_=bank(b, n), func=AF.Exp, scale=0.125)],
                         reads=[PB[b]], writes=[r_pt])
                for qt in range(4):
                    bo, bd = [4, 6][qt % 2], [5, 7][qt % 2]
                    fo, fd = [], []
                    for i4 in range(4):
                        qb = 4 * qt + i4
                        js = [j for j in (qb - 1, qb, qb + 1) if 0 <= j < 16]
                        for k_, j in enumerate(js):
                            lc = (qb - max(0, j - 1)) * 128
                            st_, sp_ = (k_ == 0), (k_ == len(js) - 1)
                            fo.append(lambda e, i4=i4, j=j, lc=lc, st_=st_, sp_=sp_, bo=bo, pt=pt, g=g: e.matmul(
                                ps[0:64, bo * 512 + i4 * 128:bo * 512 + (i4 + 1) * 128], VT[:, j, g * 64:(g + 1) * 64],
                                pt[:, j, lc:lc + 128], start=st_, stop=sp_))
                            fd.append(lambda e, i4=i4, j=j, lc=lc, st_=st_, sp_=sp_, bd=bd, pt=pt: e.matmul(
                                ps[0:64, bd * 512 + i4 * 128:bd * 512 + (i4 + 1) * 128], ones64[:],
                                pt[:, j, lc:lc + 128], start=st_, stop=sp_))
                    S.op("pe", fo, reads=[r_pt, r_VT], writes=[PB[bo]])
                    S.op("pe", fd, reads=[r_pt, r_consts], writes=[PB[bd]])
                    S.op("dve", [lambda e, bd=bd, h=h: e.tensor_scalar(out=rden1[0:64, :], in0=bank(bd, 512, 0, 64),
                                                                     scalar1=esk[:, h:h + 1], scalar2=None, op0=ALU.add)],
                         reads=[PB[bd], r_esk], writes=[r_rden1])
                    S.op("dve", [lambda e: e.reciprocal(out=rden1[0:64, :], in_=rden1[0:64, :])], reads=[r_rden1], writes=[r_rden1])
                    S.op("dve", [lambda e, bo=bo, qt=qt, prow=prow, c=c: e.tensor_tensor(
                        out=QT[prow, c, qt * 512:(qt + 1) * 512], in0=bank(bo, 512, 0, 64), in1=rden1[0:64, :], op=ALU.mult)],
                         reads=[PB[bo], r_rden1], writes=[r_QT[h]])
            S.barrier()
            if debug == "l1_a":
                S.barrier()
                S.emit()
                return nc
            mem_attention(R_XT + 26 * K_ - 2 * K_)
            S.barrier()
            if debug == "l1mix":
                for c in range(8):
                    src = YT[:, c, 0:1024] if c < 6 else YM[:, c - 6, 0:1024]
                    S.op("act", [lambda e, src=src: e.activation(out=rbuf[0], in_=src, func=AF.Identity)], reads=[r_YM], writes=[r_rbuf[0]])
                    S.dma("sp", dbg_d[c * 128:(c + 1) * 128, :], rbuf[0], reads=[r_rbuf[0]], key="dbg")
                S.barrier()
                S.emit()
                return nc
            out_proj_ln1(1)
            S.barrier()
            ffn_ln2(1, final=True)
            S.barrier()

        S.barrier()
        S.emit()
    return nc


def prep_shared(inputs):
    f32 = np.float32
    sh = {}
    for k in ("w_mem_kv", "l0_w_in", "l1_w_in", "l0_w_out", "l1_w_out", "l0_ffn_w_up", "l1_ffn_w_up",
              "l0_ffn_w_down", "l1_ffn_w_down", "l0_filt_w1", "l0_filt_w2", "l0_filt_w3", "l0_filt_w_out",
              "l0_hyena_d", "l1_sink"):
        sh[k] = np.ascontiguousarray(np.asarray(inputs[k], dtype=f32))
    for i in range(2):
        for n in ("ln1_g", "ln1_b", "ln2_g", "ln2_b"):
            sh["l%d_%s" % (i, n)] = np.ascontiguousarray(np.asarray(inputs["l%d_%s" % (i, n)], dtype=f32))
        cw = np.asarray(inputs["l%d_ffn_conv_w" % i], f32)
        cb = np.asarray(inputs["l%d_ffn_conv_b" % i], f32)
        a = np.concatenate([cw, cb[None, :]], axis=0)
        sh["l%d_fcwb" % i] = np.ascontiguousarray(a.reshape(4, 44, 128).transpose(2, 1, 0))
    cw = np.asarray(inputs["l0_conv_w"], f32)
    cb = np.asarray(inputs["l0_conv_b"], f32)
    a = np.concatenate([cw, cb[None, :]], axis=0)
    sh["l0_cwb"] = np.ascontiguousarray(a.reshape(4, 18, 128).transpose(2, 1, 0))
    sh["l0_fbf"] = np.ascontiguousarray(np.stack(
        [np.asarray(inputs["l0_filt_%s%d" % (n, l)], f32) for l in (1, 2, 3) for n in ("b", "f")], axis=1))
    sh.update(const_tables())
    return sh


_NC_CACHE = {}


def kernel(**inputs):
    sh = prep_shared(inputs)
    x = np.asarray(inputs["x"], np.float32)
    mem = np.asarray(inputs["mem"], np.float32)
    if "nc" not in _NC_CACHE:
        _NC_CACHE["nc"] = build_program()
    nc = _NC_CACHE["nc"]
    in_maps = []
    for b in range(8):
        m = dict(sh)
        m["x"] = np.ascontiguousarray(x[b])
        m["mem"] = np.ascontiguousarray(mem[b])
        in_maps.append(m)
    res = run_bass_kernel_spmd(nc, in_maps, core_ids=list(range(8)))
    return np.stack([np.asarray(r["out"], np.float32) for r in res.results], axis=0)
```

```python
import math
from contextlib import ExitStack

import numpy as np
import ml_dtypes
import concourse.bass as bass
import concourse.mybir as mybir
from concourse.bass_utils import run_bass_kernel_spmd

F32 = mybir.dt.float32
BF16 = mybir.dt.bfloat16
AF = mybir.ActivationFunctionType
ALU = mybir.AluOpType
AX = mybir.AxisListType

L = 2048
D = 1024
NT = 16
KC = 8
DFF = 2816
NFF = 22
ALPHA = 4.0 ** 0.25
EPS = 1e-5
PI = float(np.pi)


class Res:
    __slots__ = ("name", "w", "r", "excl")

    def __init__(self, name, excl=False):
        self.name = name
        self.w = None
        self.r = []
        self.excl = excl


class Sched:
    def __init__(self, nc, es):
        self.nc = nc
        self.es = es
        self.engs = {}
        for n in ("pe", "act", "dve", "pool", "sp"):
            sem = es.enter_context(nc.semaphore("s_" + n))
            self.engs[n] = dict(sem=sem, count=0, known={}, ops=[])
        self.dma_sems = {}
        self.n_dma_sems = 0

    def dma_sem(self, key):
        if key not in self.dma_sems:
            sem = self.es.enter_context(self.nc.semaphore("d%d" % self.n_dma_sems))
            self.n_dma_sems += 1
            self.dma_sems[key] = [sem, 0]
        return self.dma_sems[key]

    def op(self, eng, fns, reads=(), writes=(), dma_key=None):
        E = self.engs[eng]
        deps = {}

        def add(ev):
            if ev is None:
                return
            sem, val = ev
            if deps.get(sem, 0) < val:
                deps[sem] = val

        excl_reads = [r for r in reads if r.excl]
        writes = list(writes) + [r for r in excl_reads if r not in writes]
        reads = [r for r in reads if not r.excl]
        for r in reads:
            add(r.w)
        for w in writes:
            add(w.w)
            for ev in w.r:
                add(ev)
        waits = []
        for sem, val in deps.items():
            if sem is E["sem"] and eng == "pe" and dma_key is None:
                continue
            if E["known"].get(sem, 0) >= val:
                continue
            E["known"][sem] = val
            waits.append((sem, val))
        if dma_key is not None:
            ds = self.dma_sem(dma_key)
            ds[1] += 16
            ev = (ds[0], ds[1])
            inc = (ds[0], 16)
        else:
            E["count"] += 1
            ev = (E["sem"], E["count"])
            inc = (E["sem"], 1)
        for r in reads:
            r.r.append(ev)
        for w in writes:
            w.w = ev
            w.r = []
        if not isinstance(fns, (list, tuple)):
            fns = [fns]
        E["ops"].append((waits, list(fns), inc))
        return ev

    def dma(self, queue, out, in_, reads=(), writes=(), key=None):
        assert key is not None
        return self.op(queue, [lambda e: e.dma_start(out=out, in_=in_)], reads, writes, dma_key=key)

    def barrier(self):
        evs = [(E["sem"], E["count"]) for E in self.engs.values() if E["count"] > 0]
        evs += [(s, c) for (s, c) in self.dma_sems.values() if c > 0]
        for n, E in self.engs.items():
            waits = []
            for sem, val in evs:
                if E["known"].get(sem, 0) >= val:
                    continue
                if sem is E["sem"] and n == "pe":
                    continue
                E["known"][sem] = val
                waits.append((sem, val))
            if waits:
                E["ops"].append((waits, [], None))

    def emit(self):
        nc = self.nc
        import sys
        print("SCHED counts", {n: E["count"] for n, E in self.engs.items()}, "dma", {k: v[1] for k, v in self.dma_sems.items()}, file=sys.stderr)

        def run(name):
            def f(e):
                for waits, fns, inc in self.engs[name]["ops"]:
                    for sem, val in waits:
                        e.wait_ge(sem, val)
                    n = len(fns)
                    for i, fn in enumerate(fns):
                        ins = fn(e)
                        if i == n - 1:
                            ins.then_inc(inc[0], inc[1])
            return f

        with nc.Block() as block:
            block.tensor(run("pe"))
            block.scalar(run("act"))
            block.vector(run("dve"))
            block.gpsimd(run("pool"))
            block.sync(run("sp"))


def _bf(a):
    return np.ascontiguousarray(a.astype(ml_dtypes.bfloat16))


_CONST_CACHE = {}


def const_tables():
    if _CONST_CACHE:
        return _CONST_CACHE
    N = 4096
    f = np.arange(2048, dtype=np.float64)
    t = np.arange(2048, dtype=np.float64)
    m = np.mod(np.outer(2 * f + 1, t), 2 * N)
    ang = np.pi * m / N
    C = np.cos(ang)
    Sn = np.sin(ang)
    CT = C.T.reshape(16, 128, 16, 128)
    ST = (-Sn).T.reshape(16, 128, 16, 128)
    fwd = np.stack([CT, ST], axis=0)
    fwd = fwd.transpose(3, 2, 0, 1, 4)
    _CONST_CACHE["fwd_tab"] = _bf(fwd.reshape(16, 128, 2 * 16 * 128))
    Ci = (C / 2048.0).reshape(4, 4, 128, 4, 512)
    Si = (-Sn / 2048.0).reshape(4, 4, 128, 4, 512)
    inv = np.stack([Ci, Si], axis=0)
    inv = inv.transpose(4, 1, 3, 2, 0, 5)
    _CONST_CACHE["inv_tab"] = _bf(inv.reshape(4, 4, 128, 4 * 2 * 512))
    f32 = np.float32
    tl = np.linspace(0.0, 1.0, L, dtype=f32)[:, None]
    w = (f32(2.0 * math.pi) * np.arange(L, dtype=f32)[:, None] / f32(L)).astype(f32)
    fr = np.linspace(1e-4, 15, 16, dtype=f32)[None, :]
    z = np.concatenate([tl, np.cos(fr * w), -np.sin(fr * w)], axis=-1).astype(f32)
    _CONST_CACHE["zfT"] = np.ascontiguousarray(z.T)
    _CONST_CACHE["ntn"] = np.ascontiguousarray((-tl[:, 0]).reshape(16, 128).T.astype(f32))
    min_decay = math.log(1e-2) / 1.5
    max_decay = math.log(1e-2) / 0.3
    _CONST_CACHE["deltas"] = np.abs(np.linspace(min_decay, max_decay, 768, dtype=f32)).astype(f32)
    inv_f = (10000.0 ** (-np.arange(0, 64, 2, dtype=f32) / f32(64))).astype(f32)
    angr = (np.arange(L, dtype=f32)[:, None] * inv_f[None, :]).astype(f32)
    angr = np.concatenate([angr, angr], axis=-1)
    cosT = np.cos(angr).T.astype(f32)
    sinT = np.sin(angr).T.astype(f32)
    _CONST_CACHE["ropec"] = np.ascontiguousarray(np.concatenate([cosT, cosT], axis=0))
    _CONST_CACHE["ropes"] = np.ascontiguousarray(np.concatenate([sinT, sinT], axis=0))
    Pm = np.zeros((128, 128), np.float32)
    for po in range(128):
        d = po % 64
        if d < 32:
            Pm[po + 32, po] = -1.0
        else:
            Pm[po - 32, po] = 1.0
    _CONST_CACHE["pm"] = _bf(Pm)
    _CONST_CACHE["ident"] = _bf(np.eye(128, dtype=np.float32))
    k = np.arange(128)[:, None]
    q = np.arange(128)[None, :]
    NEG = -30000.0
    m_next = np.where(k <= q, 0.0, NEG)
    m_prev = np.where(k >= q, 0.0, NEG)
    _CONST_CACHE["maskb"] = _bf(np.concatenate([m_next, np.zeros((128, 128)), m_prev], axis=1))
    return _CONST_CACHE


def build_program(debug=None):
    nc = bass.Bass("TRN2", target_bir_lowering=False)
    dbg = {}

    def din(name, shape, dt=F32):
        return nc.dram_tensor(name, list(shape), dt, kind="ExternalInput").ap()

    x_d = din("x", [L, D])
    mem_d = din("mem", [256, D])
    wkv_d = din("w_mem_kv", [D, 512])
    w_in_d = [din("l0_w_in", [D, 2560]), din("l1_w_in", [D, 1536])]
    w_out_d = [din("l0_w_out", [D, D]), din("l1_w_out", [D, D])]
    w_up_d = [din("l%d_ffn_w_up" % i, [D, 2 * DFF]) for i in range(2)]
    w_dn_d = [din("l%d_ffn_w_down" % i, [DFF, D]) for i in range(2)]
    ln_d = [[din("l%d_%s" % (i, n), [D]) for n in ("ln1_g", "ln1_b", "ln2_g", "ln2_b")] for i in range(2)]
    fcwb_d = [din("l%d_fcwb" % i, [128, 44, 4]) for i in range(2)]
    cwb_d = din("l0_cwb", [128, 18, 4])
    fw1_d = din("l0_filt_w1", [33, 64])
    fw2_d = din("l0_filt_w2", [64, 64])
    fw3_d = din("l0_filt_w3", [64, 64])
    fwo_d = din("l0_filt_w_out", [64, 1536])
    fbf_d = din("l0_fbf", [64, 6])
    hd_d = din("l0_hyena_d", [768])
    sink_d = din("l1_sink", [12])
    fwd_tab_d = din("fwd_tab", [16, 128, 4096], BF16)
    inv_tab_d = din("inv_tab", [4, 4, 128, 4096], BF16)
    zfT_d = din("zfT", [33, L])
    ntn_d = din("ntn", [128, 16])
    deltas_d = din("deltas", [768])
    ropec_d = din("ropec", [128, L])
    ropes_d = din("ropes", [128, L])
    pm_d = din("pm", [128, 128], BF16)
    ident_d = din("ident", [128, 128], BF16)
    maskb_d = din("maskb", [128, 384], BF16)
    out_d = nc.dram_tensor("out", [L, D], F32, kind="ExternalOutput").ap()
    if debug:
        dbg_d = nc.dram_tensor("dbg", [L, D], F32, kind="ExternalOutput").ap()

    es = ExitStack()
    with es:
        S = Sched(nc, es)
        AR_BYTES = 194 * 1024
        arena = es.enter_context(nc.sbuf_tensor("arena", [128, AR_BYTES // 2], BF16))
        ps = es.enter_context(nc.psum_tensor("ps", [128, 4096], F32))
        PB = [Res("pb%d" % i, excl=True) for i in range(8)]

        def bank(b, n=512, p0=0, p1=128):
            return ps[p0:p1, b * 512:b * 512 + n]

        def bank_bf(b):
            return ps[:, b * 512:(b + 1) * 512].bitcast(BF16)

        def AV(off, shape, dt):
            n = int(np.prod(shape))
            if dt == F32:
                v = arena[:, off // 2: off // 2 + 2 * n].bitcast(F32)
            else:
                v = arena[:, off // 2: off // 2 + n]
            if len(shape) == 2:
                v = v.rearrange("p (a b) -> p a b", a=shape[0])
            elif len(shape) == 3:
                v = v.rearrange("p (a b c) -> p a b c", a=shape[0], b=shape[1])
            elif len(shape) == 4:
                v = v.rearrange("p (a b c d) -> p a b c d", a=shape[0], b=shape[1], c=shape[2])
            return v

        K_ = 1024
        R_X, R_XT, R_Y, R_YM, R_MQ, R_T = 0, 64 * K_, 96 * K_, 120 * K_, 128 * K_, 136 * K_

        def small(name, shape, dt):
            return es.enter_context(nc.sbuf_tensor("sb_" + name, list(shape), dt))

        ident = small("ident", [128, 128], BF16)
        pm = small("pm", [128, 128], BF16)
        maskb = small("maskb", [128, 384], BF16)
        ones64 = small("ones64", [128, 64], BF16)
        memKT = small("memKT", [128, 2, 256], BF16)
        memV = small("memV", [128, 2, 256], BF16)
        cwb = small("cwb", [128, 18, 4], F32)
        fcwb = small("fcwb", [128, 44, 4], F32)
        stat = small("stat", [128, 128], F32)
        GB = small("GB", [128, 2, 1024], F32)
        r_consts = Res("consts")
        r_memKV = Res("memKV")
        r_cwb = Res("cwb")
        r_fcwb = Res("fcwb")
        r_GB = Res("GB")

        X = AV(R_X, [16, 1024], F32)
        XT = AV(R_XT, [8, 2048], BF16)
        YT = AV(R_Y, [6, 2048], BF16)
        YM = AV(R_YM, [2, 2048], BF16)
        MQ = AV(R_MQ, [2, 2048], BF16)
        r_X = [Res("X%d" % i) for i in range(16)]
        r_XT = Res("XT")
        r_YT = Res("YT")
        r_YM = Res("YM")
        r_MQ = Res("MQ")

        _bk = [0]

        def nxt(lst):
            b = lst[_bk[0] % len(lst)]
            _bk[0] += 1
            return b

        def mm_group(out, pairs, reads, writes):
            n = len(pairs)
            fns = []
            for i, (l, r) in enumerate(pairs):
                fns.append(lambda e, l=l, r=r, i=i: e.matmul(out, l, r, start=(i == 0), stop=(i == n - 1)))
            return S.op("pe", fns, reads, writes)

        def act_copy(out, in_, reads, writes):
            return S.op("act", [lambda e: e.activation(out=out, in_=in_, func=AF.Identity)], reads, writes)

        S.dma("sp", ident[:], ident_d[:, :], writes=[r_consts], key="c0")
        S.dma("sp", pm[:], pm_d[:, :], writes=[r_consts], key="c1")
        S.dma("sp", maskb[:], maskb_d[:, :], writes=[r_consts], key="c2")
        S.dma("sp", cwb[:], cwb_d[:, :, :], writes=[r_cwb], key="c3")
        S.op("dve", [lambda e: e.memset(ones64[:], 1.0)], writes=[r_consts])
        epst = small("epst", [128, 1], F32)
        S.op("dve", [lambda e: e.memset(epst[:], EPS)], writes=[r_consts])


        def transpose_to_XT(src_bf, tile, r_src):
            b = nxt([6, 7])
            bb = bank_bf(b)
            fns = [lambda e, kc=kc: e.transpose(bb[:, kc * 128:(kc + 1) * 128], src_bf[:, kc * 128:(kc + 1) * 128], ident[:])
                   for kc in range(8)]
            S.op("pe", fns, reads=[r_src, r_consts], writes=[PB[b]])
            S.op("act", [lambda e: e.activation(out=XT[:, :, tile * 128:(tile + 1) * 128],
                                                in_=bb.rearrange("p (k m) -> p k m", k=8), func=AF.Identity)],
                 reads=[PB[b]], writes=[r_XT])

        def load_ln(layer, which):
            g_d, b_d = ln_d[layer][2 * which], ln_d[layer][2 * which + 1]
            S.dma("sp", GB[:, 0, :], g_d.partition_broadcast(128), writes=[r_GB], key="gb0")
            S.dma("sp", GB[:, 1, :], b_d.partition_broadcast(128), writes=[r_GB], key="gb1")

        NRB = 6
        LN_LAG = 3
        _rboff = [46 * K_, 50 * K_, 28672, 28672 + 4096, 36880, 36880 + 4096]
        rbuf = [AV(R_T + _rboff[i], [1024], F32) for i in range(NRB)]
        _xboff = [54 * K_, 56 * K_, 24 * K_, 26 * K_]
        xbuf = [AV(R_T + _xboff[i], [1024], BF16) for i in range(4)]
        r_rbuf = [Res("rbuf%d" % i) for i in range(NRB)]
        r_xbuf = [Res("xbuf%d" % i) for i in range(4)]
        r_stat = Res("stat")
        r_stats = [Res("stat%d" % i) for i in range(8)]
        _ln = [0]

        class LNPipe:
            def __init__(self, final_out):
                self.final_out = final_out
                self.q = []

            def push(self, tile, r_in, r_r):
                ri = tile % NRB
                so = ri * 16
                r_stat = r_stats[ri]
                st6 = stat[:, so:so + 12]
                mv = stat[:, so + 12:so + 14]
                rstd = stat[:, so + 14:so + 15]
                final_out = self.final_out

                def s1():
                    S.op("dve", [lambda e: e.bn_stats(out=st6[:, 0:6], in_=r_in[:, 0:512])], reads=[r_r], writes=[r_stat])
                    S.op("dve", [lambda e: e.bn_stats(out=st6[:, 6:12], in_=r_in[:, 512:1024])], reads=[r_r], writes=[r_stat])
                    S.op("dve", [lambda e: e.bn_aggr(out=mv, in_=st6)], reads=[r_stat], writes=[r_stat])
                    S.op("act", [lambda e: e.activation(out=rstd, in_=mv[:, 1:2], func=AF.Sqrt, bias=epst[:, 0:1])],
                         reads=[r_stat, r_consts], writes=[r_stat])

                def s2():
                    S.op("dve", [lambda e: e.reciprocal(out=rstd, in_=rstd)], reads=[r_stat], writes=[r_stat])
                    S.op("dve", [lambda e: e.tensor_scalar(out=r_in, in0=r_in, scalar1=mv[:, 0:1], scalar2=rstd,
                                                           op0=ALU.subtract, op1=ALU.mult)],
                         reads=[r_r, r_stat], writes=[r_r])
                    S.op("dve", [lambda e: e.tensor_tensor(out=r_in, in0=r_in, in1=GB[:, 0, :], op=ALU.mult)],
                         reads=[r_r, r_GB], writes=[r_r])
                    S.op("pool", [lambda e: e.tensor_tensor(out=X[:, tile, :], in0=r_in, in1=GB[:, 1, :], op=ALU.add)],
                         reads=[r_r, r_GB], writes=[r_X[tile]])

                def s3():
                    if final_out:
                        S.dma("sp", out_d[tile * 128:(tile + 1) * 128, :], X[:, tile, :], reads=[r_X[tile]], key="out%d" % (tile % 4))
                    else:
                        i4 = tile % 4
                        S.op("act", [lambda e: e.activation(out=xbuf[i4], in_=X[:, tile, :], func=AF.Identity)],
                             reads=[r_X[tile]], writes=[r_xbuf[i4]])

                def s4():
                    if not final_out:
                        i4 = tile % 4
                        transpose_to_XT(xbuf[i4], tile, r_xbuf[i4])

                s1()
                for k_, f_ in enumerate((s2, s3, s4)):
                    self.q.append([k_ + 1, f_])
                self._tick()

            def _tick(self):
                keep = []
                for item in self.q:
                    item[0] -= 0
                ready = [it for it in self.q if it[0] <= 0]
                for it in ready:
                    it[1]()
                self.q = [it for it in self.q if it[0] > 0]
                for it in self.q:
                    it[0] -= 1

            def flush(self):
                while self.q:
                    self._tick()

        def mem_attention(toff):
            PT = [AV(toff + i * K_, [512], BF16) for i in range(8)]
            r_PT = [Res("mpt%d" % i) for i in range(8)]
            rden = AV(toff + 8 * K_, [512], F32)
            r_rden = Res("mrden")
            its = [(hp, qt, hh) for hp in range(2) for qt in range(4) for hh in range(2)]

            def stage1(n):
                hp, qt, hh = its[n]
                qs = slice(qt * 512, (qt + 1) * 512)
                prow = slice(hh * 64, (hh + 1) * 64)
                for mt in range(2):
                    b = nxt([0, 1, 2, 3])
                    mm_group(bank(b), [(memKT[prow, hp, mt * 128:(mt + 1) * 128], MQ[prow, hp, qs])],
                             reads=[r_memKV, r_MQ], writes=[PB[b]])
                    k = (n % 2) * 4 + mt
                    S.op("act", [lambda e, k=k, b=b: e.activation(out=PT[k], in_=bank(b), func=AF.Exp, scale=0.125)],
                         reads=[PB[b]], writes=[r_PT[k]])

            def stage2(n):
                hp, qt, hh = its[n]
                h = 2 * hp + hh
                qs = slice(qt * 512, (qt + 1) * 512)
                prow = slice(hh * 64, (hh + 1) * 64)
                pts = [(n % 2) * 4 + mt for mt in range(2)]
                bo, bd = [4, 6][n % 2], [5, 7][n % 2]
                mm_group(bank(bo, 512, 0, 64), [(memV[:, mt, h * 64:(h + 1) * 64], PT[pts[mt]]) for mt in range(2)],
                         reads=[r_memKV, r_PT[pts[0]], r_PT[pts[1]]], writes=[PB[bo]])
                mm_group(bank(bd, 512, 0, 64), [(ones64[:], PT[pts[mt]]) for mt in range(2)],
                         reads=[r_consts, r_PT[pts[0]], r_PT[pts[1]]], writes=[PB[bd]])
                S.op("dve", [lambda e: e.reciprocal(out=rden[0:64, :], in_=bank(bd, 512, 0, 64))],
                     reads=[PB[bd]], writes=[r_rden])
                S.op("dve", [lambda e: e.tensor_tensor(
                    out=YM[prow, hp, qs], in0=bank(bo, 512, 0, 64), in1=rden[0:64, :], op=ALU.mult)],
                     reads=[PB[bo], r_rden], writes=[r_YM])

            for n in range(len(its) + 1):
                if n < len(its):
                    stage1(n)
                if n >= 1:
                    stage2(n - 1)

        def out_proj_ln1(layer):
            wout = AV(R_T, [8, 1024], BF16)
            r_wout = Res("wout")
            xs = [AV(R_T + 16 * K_ + i * 4 * K_, [1024], F32) for i in range(2)]
            r_xs = [Res("xs%d" % i) for i in range(2)]
            S.dma("pool", wout, w_out_d[layer].rearrange("(kc p) n -> p kc n", p=128), writes=[r_wout], key="wout")
            load_ln(layer, 0)
            lnp = LNPipe(False)
            for tile in range(16):
                ts_ = slice(tile * 128, (tile + 1) * 128)
                bp = [0, 2, 4][tile % 3]
                for half in range(2):
                    b = bp + half
                    pairs = []
                    for kc in range(8):
                        lhs = YT[:, kc, ts_] if kc < 6 else YM[:, kc - 6, ts_]
                        pairs.append((lhs, wout[:, kc, half * 512:(half + 1) * 512]))
                    mm_group(bank(b), pairs, reads=[r_YT, r_YM, r_wout], writes=[PB[b]])
                i = tile % 2
                ri = tile % NRB
                if layer == 0:
                    S.dma("sp", xs[i], x_d[ts_, :], writes=[r_xs[i]], key="xs%d" % i)
                    xin, rxin = xs[i], r_xs[i]
                else:
                    xin, rxin = X[:, tile, :], r_X[tile]
                rb = rbuf[ri]
                S.op("dve", [lambda e, xin=xin, rb=rb, bp=bp: e.scalar_tensor_tensor(
                    out=rb, in0=xin, scalar=ALPHA, in1=ps[:, bp * 512:bp * 512 + 1024], op0=ALU.mult, op1=ALU.add)],
                     reads=[rxin, PB[bp], PB[bp + 1]], writes=[r_rbuf[ri]])
                lnp.push(tile, rb, r_rbuf[ri])
            lnp.flush()

        def ffn_ln2(layer, final):
            blocks = [4, 4, 4, 4, 3, 3]
            S.dma("sp", fcwb[:], fcwb_d[layer][:, :, :], writes=[r_fcwb], key="fcwb")
            load_ln(layer, 1)
            GT = AV(R_Y, [4, 2048], BF16)
            r_GT = Res("GT")
            wdn = [AV(R_T + i * 8 * K_, [4, 1024], BF16) for i in range(2)]
            r_wdn = [Res("wdn%d" % i) for i in range(2)]
            wup = [AV(R_T + 16 * K_ + i * 4 * K_, [8, 2, 128], BF16) for i in range(3)]
            r_wup = [Res("wup%d" % i) for i in range(3)]
            hsb = [AV(R_T + 28 * K_ + i * 8208, [2052], F32) for i in range(2)]
            r_hsb = [Res("hsb%d" % i) for i in range(2)]
            for i in range(2):
                S.op("pool", [lambda e, i=i: e.memset(hsb[i][:, 0:1], 0.0)], writes=[r_hsb[i]])
                S.op("pool", [lambda e, i=i: e.memset(hsb[i][:, 2049:2050], 0.0)], writes=[r_hsb[i]])
            acc_as = [AV(R_YM, [2048], F32), AV(R_Y + 16 * K_, [2048], F32)]
            acc_g = AV(R_MQ, [2048], F32)
            r_accas, r_accg = [Res("acca0"), Res("acca1")], Res("accg")
            wd_v = w_dn_d[layer]
            wu_v = w_up_d[layer].rearrange("(kc p) n -> p kc n", p=128)
            j0 = 0
            for bi, nb in enumerate(blocks):
                wd = wdn[bi % 2]
                S.dma("pool", wd[:, 0:nb, :], wd_v[j0 * 128:(j0 + nb) * 128, :].rearrange("(j p) n -> p j n", p=128),
                      writes=[r_wdn[bi % 2]], key="wdn%d" % (bi % 2))
                deferred = None
                for jj in range(nb):
                    j = j0 + jj
                    wi = j % 3
                    wu = wup[wi]
                    acc_a, r_acca = acc_as[j % 2], r_accas[j % 2]
                    S.dma("pool", wu[:, :, 0, :], wu_v[:, :, j * 128:(j + 1) * 128], writes=[r_wup[wi]], key="wup%da" % wi)
                    S.dma("pool", wu[:, :, 1, :], wu_v[:, :, DFF + j * 128:DFF + (j + 1) * 128], writes=[r_wup[wi]], key="wup%db" % wi)
                    for ag in range(2):
                        hs_ = hsb[ag]
                        cj = ag * NFF + j
                        for tq in range(4):
                            b = nxt([0, 1, 2, 3])
                            mm_group(bank(b), [(wu[:, kc, ag, :], XT[:, kc, tq * 512:(tq + 1) * 512]) for kc in range(8)],
                                     reads=[r_wup[wi], r_XT], writes=[PB[b]])
                            S.op("act", [lambda e, hs_=hs_, tq=tq, b=b: e.activation(
                                out=hs_[:, 1 + tq * 512:1 + (tq + 1) * 512], in_=bank(b), func=AF.Identity)],
                                 reads=[PB[b]], writes=[r_hsb[ag]])
                        acc, r_acc = (acc_a, r_acca) if ag == 0 else (acc_g, r_accg)
                        S.op("act", [lambda e, acc=acc, hs_=hs_, cj=cj: e.activation(
                            out=acc, in_=hs_[:, 1:2049], func=AF.Identity, scale=fcwb[:, cj, 1:2], bias=fcwb[:, cj, 3:4])],
                             reads=[r_hsb[ag], r_fcwb], writes=[r_acc])
                        S.op("dve", [lambda e, acc=acc, hs_=hs_, cj=cj: e.scalar_tensor_tensor(
                            out=acc, in0=hs_[:, 0:2048], scalar=fcwb[:, cj, 0:1], in1=acc, op0=ALU.mult, op1=ALU.add)],
                             reads=[r_hsb[ag], r_fcwb, r_acc], writes=[r_acc])
                        S.op("dve", [lambda e, acc=acc, hs_=hs_, cj=cj: e.scalar_tensor_tensor(
                            out=acc, in0=hs_[:, 2:2050], scalar=fcwb[:, cj, 2:3], in1=acc, op0=ALU.mult, op1=ALU.add)],
                             reads=[r_hsb[ag], r_fcwb, r_acc], writes=[r_acc])
                        if ag == 0 and deferred is not None:
                            deferred()
                            deferred = None

                    def _fin(jj=jj, acc_a=acc_a, r_acca=r_acca):
                        S.op("act", [lambda e: e.activation(out=acc_g, in_=acc_g, func=AF.Silu)], reads=[r_accg], writes=[r_accg])
                        S.op("dve", [lambda e: e.tensor_tensor(out=GT[:, jj, :], in0=acc_g, in1=acc_a, op=ALU.mult)],
                             reads=[r_accg, r_acca], writes=[r_GT])
                    deferred = _fin
                if deferred is not None:
                    deferred()
                    deferred = None
                last = bi == len(blocks) - 1
                if last:
                    S.barrier()
                lnp = LNPipe(final)
                for tile in range(16):
                    ts_ = slice(tile * 128, (tile + 1) * 128)
                    bp = [0, 2, 4][tile % 3] if last else [4, 6][tile % 2]
                    for half in range(2):
                        mm_group(bank(bp + half), [(GT[:, jj, ts_], wd[:, jj, half * 512:(half + 1) * 512]) for jj in range(nb)],
                                 reads=[r_GT, r_wdn[bi % 2]], writes=[PB[bp + half]])
                    pin = ps[:, bp * 512:bp * 512 + 1024]
                    if not last:
                        if bi == 0:
                            S.op("dve", [lambda e, tile=tile, pin=pin: e.scalar_tensor_tensor(
                                out=X[:, tile, :], in0=X[:, tile, :], scalar=ALPHA, in1=pin, op0=ALU.mult, op1=ALU.add)],
                                 reads=[PB[bp], PB[bp + 1]], writes=[r_X[tile]])
                        else:
                            S.op("dve", [lambda e, tile=tile, pin=pin: e.tensor_tensor(
                                out=X[:, tile, :], in0=X[:, tile, :], in1=pin, op=ALU.add)],
                                 reads=[PB[bp], PB[bp + 1]], writes=[r_X[tile]])
                    else:
                        ri = tile % NRB
                        rb = rbuf[ri]
                        S.op("dve", [lambda e, tile=tile, pin=pin, rb=rb: e.tensor_tensor(
                            out=rb, in0=X[:, tile, :], in1=pin, op=ALU.add)],
                             reads=[PB[bp], PB[bp + 1], r_X[tile]], writes=[r_rbuf[ri]])
                        lnp.push(tile, rb, r_rbuf[ri])
                lnp.flush()
                j0 += nb

        memf = AV(R_T + 8 * K_, [2, 1024], F32)
        memb = AV(R_T + 28 * K_, [2, 1024], BF16)
        memT = AV(R_T + 32 * K_, [8, 256], BF16)
        wkv = AV(R_T, [8, 512], BF16)
        r_memf, r_memb, r_memT, r_wkv = Res("memf"), Res("memb"), Res("memT"), Res("wkv")
        S.dma("sp", memf, mem_d.rearrange("(mt p) d -> p mt d", p=128), writes=[r_memf], key="memf")
        S.dma("pool", wkv, wkv_d.rearrange("(kc p) n -> p kc n", p=128), writes=[r_wkv], key="wkv")
        act_copy(memb, memf, [r_memf], [r_memb])
        for mt in range(2):
            b = nxt([6, 7])
            bb = bank_bf(b)
            fns = [lambda e, kc=kc, mt=mt, bb=bb: e.transpose(bb[:, kc * 128:(kc + 1) * 128], memb[:, mt, kc * 128:(kc + 1) * 128], ident[:])
                   for kc in range(8)]
            S.op("pe", fns, reads=[r_memb, r_consts], writes=[PB[b]])
            S.op("act", [lambda e, mt=mt, bb=bb: e.activation(out=memT[:, :, mt * 128:(mt + 1) * 128],
                                                             in_=bb.rearrange("p (k m) -> p k m", k=8), func=AF.Identity)],
                 reads=[PB[b]], writes=[r_memT])
        for hp in range(2):
            b = nxt([0, 1])
            mm_group(bank(b, 256), [(wkv[:, kc, hp * 128:(hp + 1) * 128], memT[:, kc, :]) for kc in range(8)],
                     reads=[r_wkv, r_memT], writes=[PB[b]])
            act_copy(memKT[:, hp, :], bank(b, 256), [PB[b]], [r_memKV])
        for mt in range(2):
            b = nxt([0, 1])
            mm_group(bank(b, 256), [(memT[:, kc, mt * 128:(mt + 1) * 128], wkv[:, kc, 256:512]) for kc in range(8)],
                     reads=[r_wkv, r_memT], writes=[PB[b]])
            act_copy(memV[:, mt, :], bank(b, 256), [PB[b]], [r_memKV])

        if debug == "s_m":
            S.barrier()
            S.emit()
            return nc
        xs0 = [AV(R_T + 16 * K_ + i * 4 * K_, [1024], F32) for i in range(2)]
        r_xs0 = [Res("xs0_%d" % i) for i in range(2)]
        for tile in range(16):
            i = tile % 2
            S.dma("sp", xs0[i], x_d[tile * 128:(tile + 1) * 128, :], writes=[r_xs0[i]], key="xs%d" % i)
            i4 = tile % 4
            S.op("act", [lambda e, i=i, i4=i4: e.activation(out=xbuf[i4], in_=xs0[i], func=AF.Identity)],
                 reads=[r_xs0[i]], writes=[r_xbuf[i4]])
            transpose_to_XT(xbuf[i4], tile, r_xbuf[i4])

        if debug == "s_x":
            S.barrier()
            S.emit()
            return nc
        HS = AV(R_X, [2, 16, 768], BF16)
        r_HS = Res("HS")
        hA = AV(R_X + 48 * K_, [2048], F32)
        hB = AV(R_X + 56 * K_, [2048], F32)
        r_hA, r_hB = Res("hA"), Res("hB")
        zf = AV(R_Y, [2048], F32)
        fw = AV(R_Y + 8 * K_, [3, 64], F32)
        fwo = AV(R_Y + 9 * K_, [1536], F32)
        fbf = AV(R_Y + 15 * K_, [8], F32)
        dbc = AV(R_Y + 16 * K_, [768], F32)
        dlt = AV(R_YM, [768], F32)
        dec = AV(R_YM + 3 * K_, [768], F32)
        ntn = AV(R_YM + 6 * K_, [16], F32)
        fsb = AV(R_MQ, [768], F32)
        wtmp = AV(R_MQ + 3 * K_, [512], F32)
        wtm2 = AV(R_MQ + 5 * K_, [512], F32)
        r_f = Res("filt_in")
        r_dec, r_fsb, r_wtmp, r_wtm2, r_dbc = Res("dec"), Res("fsb"), Res("wtmp"), Res("wtm2"), Res("dbc")
        S.dma("sp", zf[0:33, :], zfT_d[:, :], writes=[r_f], key="f0")
        S.dma("sp", fw[0:33, 0, :], fw1_d[:, :], writes=[r_f], key="f1")
        S.dma("sp", fw[0:64, 1, :], fw2_d[:, :], writes=[r_f], key="f2")
        S.dma("sp", fw[0:64, 2, :], fw3_d[:, :], writes=[r_f], key="f3")
        S.dma("sp", fwo[0:64, :], fwo_d[:, :], writes=[r_f], key="f4")
        S.dma("sp", fbf[0:64, 0:6], fbf_d[:, :], writes=[r_f], key="f5")
        S.dma("sp", dbc, hd_d.partition_broadcast(128), writes=[r_dbc], key="f6")
        S.dma("sp", dlt, deltas_d.partition_broadcast(128), writes=[r_f], key="f7")
        S.dma("sp", ntn, ntn_d[:, :], writes=[r_f], key="f8")
        fbs = stat[0:64, 120:123]
        for l in range(3):
            S.op("dve", [lambda e, l=l: e.tensor_tensor(out=fbs[:, l:l + 1], in0=fbf[0:64, 2 * l:2 * l + 1],
                                                        in1=fbf[0:64, 2 * l + 1:2 * l + 2], op=ALU.mult)],
                 reads=[r_f], writes=[r_stat])
        srcs = [(zf, 33, r_f), (hA, 64, r_hA), (hB, 64, r_hB)]
        dsts = [(hA, r_hA), (hB, r_hB), (hA, r_hA)]
        for l in range(3):
            src, kk, r_src = srcs[l]
            dst, r_dst = dsts[l]
            for tq in range(4):
                b = nxt([0, 1, 2, 3])
                cs = slice(tq * 512, (tq + 1) * 512)
                mm_group(bank(b, 512, 0, 64), [(fw[0:kk, l, :], src[0:kk, cs])], reads=[r_f, r_src], writes=[PB[b]])
                S.op("dve", [lambda e, b=b, l=l: e.tensor_scalar(out=wtmp[0:64, :], in0=bank(b, 512, 0, 64),
                                                                scalar1=fbf[0:64, 2 * l + 1:2 * l + 2], scalar2=fbs[:, l:l + 1],
                                                                op0=ALU.mult, op1=ALU.add)],
                     reads=[PB[b], r_f, r_stat], writes=[r_wtmp])
                S.op("dve", [lambda e: e.tensor_scalar(out=wtm2[0:64, :], in0=wtmp[0:64, :], scalar1=-PI, scalar2=2 * PI,
                                                       op0=ALU.is_lt, op1=ALU.mult)], reads=[r_wtmp], writes=[r_wtm2])
                S.op("dve", [lambda e: e.tensor_tensor(out=wtmp[0:64, :], in0=wtmp[0:64, :], in1=wtm2[0:64, :], op=ALU.add)],
                     reads=[r_wtmp, r_wtm2], writes=[r_wtmp])
                S.op("dve", [lambda e: e.tensor_scalar(out=wtm2[0:64, :], in0=wtmp[0:64, :], scalar1=PI, scalar2=-2 * PI,
                                                       op0=ALU.is_gt, op1=ALU.mult)], reads=[r_wtmp], writes=[r_wtm2])
                S.op("dve", [lambda e: e.tensor_tensor(out=wtmp[0:64, :], in0=wtmp[0:64, :], in1=wtm2[0:64, :], op=ALU.add)],
                     reads=[r_wtmp, r_wtm2], writes=[r_wtmp])
                S.op("act", [lambda e, dst=dst, cs=cs: e.activation(out=dst[0:64, cs], in_=wtmp[0:64, :], func=AF.Sin)],
                     reads=[r_wtmp], writes=[r_dst])
        for tile in range(16):
            ts_ = slice(tile * 128, (tile + 1) * 128)
            for q3 in range(3):
                mm_group(bank(q3), [(hA[0:64, ts_], fwo[0:64, q3 * 512:(q3 + 1) * 512])], reads=[r_hA, r_f], writes=[PB[q3]])
            S.op("act", [lambda e, tile=tile: e.activation(out=dec, in_=dlt, func=AF.Exp, scale=ntn[:, tile:tile + 1])],
                 reads=[r_f], writes=[r_dec])
            act_copy(fsb, ps[:, 0:768], [PB[0], PB[1]], [r_fsb])
            S.op("dve", [lambda e: e.tensor_tensor(out=hB[:, 0:768], in0=fsb, in1=ps[:, 768:1536], op=ALU.add)],
                 reads=[r_fsb, PB[1], PB[2]], writes=[r_hB])
            S.op("dve", [lambda e: e.tensor_tensor(out=hB[:, 768:1536], in0=fsb, in1=ps[:, 768:1536], op=ALU.subtract)],
                 reads=[r_fsb, PB[1], PB[2]], writes=[r_hB])
            S.op("dve", [lambda e, tile=tile: e.tensor_tensor(out=HS[:, 0, tile, :], in0=hB[:, 0:768], in1=dec, op=ALU.mult)],
                 reads=[r_hB, r_dec], writes=[r_HS])
            S.op("dve", [lambda e, tile=tile: e.tensor_tensor(out=HS[:, 1, tile, :], in0=hB[:, 768:1536], in1=dec, op=ALU.mult)],
                 reads=[r_hB, r_dec], writes=[r_HS])
        S.barrier()

        if debug == "s_f":
            S.barrier()
            S.emit()
            return nc
        KS = AV(R_T, [2, 16, 768], BF16)
        r_KS = Res("KS")

        def fwd_pass(rhs_re, rhs_im, r_rhs, ftab, r_ftab, epilogue):
            for fc in range(16):
                si = fc % 2
                ft = ftab[si]
                S.dma("sp", ft.rearrange("p a b c -> p (a b c)"), fwd_tab_d[fc], writes=[r_ftab[si]], key="ftab%d" % si)
                bs = [0, 1, 2, 3] if fc % 2 == 0 else [4, 5, 6, 7]
                for ri in range(2):
                    rhs = rhs_re if ri == 0 else rhs_im
                    o0 = bs[0] * 512 + ri * 1024
                    fns = []
                    for tc in range(16):
                        lhs = ft[:, ri, tc, :]
                        fns.append(lambda e, lhs=lhs, tc=tc, rhs=rhs, o0=o0: e.matmul(
                            ps[:, o0:o0 + 512], lhs, rhs[:, tc, 0:512], start=(tc == 0), stop=(tc == 15)))
                        fns.append(lambda e, lhs=lhs, tc=tc, rhs=rhs, o0=o0: e.matmul(
                            ps[:, o0 + 512:o0 + 768], lhs, rhs[:, tc, 512:768], start=(tc == 0), stop=(tc == 15)))
                    S.op("pe", fns, reads=[r_ftab[si], r_rhs], writes=[PB[bs[2 * ri]], PB[bs[2 * ri + 1]]])
                pre = ps[:, bs[0] * 512:bs[0] * 512 + 768]
                pim = ps[:, bs[2] * 512:bs[2] * 512 + 768]
                epilogue(fc, pre, pim, [PB[b] for b in bs])

        ftab = [AV(R_YM, [2, 16, 128], BF16), AV(R_MQ, [2, 16, 128], BF16)]
        r_ftab = [Res("ftab0"), Res("ftab1")]

        def k_epilogue(fc, pre, pim, rbs):
            S.op("dve", [lambda e: e.tensor_tensor(out=KS[:, 0, fc, :], in0=pre, in1=dbc, op=ALU.add)],
                 reads=rbs[0:2] + [r_dbc], writes=[r_KS])
            act_copy(KS[:, 1, fc, :], pim, rbs[2:4], [r_KS])

        fwd_pass(HS[:, 0], HS[:, 1], r_HS, ftab, r_ftab, k_epilogue)
        S.barrier()

        if debug == "s_k":
            S.barrier()
            S.emit()
            return nc
        Z = AV(R_X, [16, 768], BF16)
        r_Z = Res("Z")
        hsb0 = [AV(R_X + 24 * K_ + i * 8208, [2052], F32) for i in range(2)]
        r_hsb0 = [Res("hsb0_%d" % i) for i in range(2)]
        accx = AV(R_X + 24 * K_ + 16416, [2048], F32)
        accv = AV(R_X + 24 * K_ + 16416 + 8192, [2048], F32)
        zT = AV(R_X + 24 * K_ + 16416 + 16384, [2048], BF16)
        r_accx, r_accv, r_zT = Res("accx"), Res("accv"), Res("zT")
        wch = [AV(R_T + 48 * K_ + i * 2 * K_, [8, 128], BF16) for i in range(3)]
        r_wch = [Res("wch%d" % i) for i in range(3)]
        for i in range(2):
            S.op("pool", [lambda e, i=i: e.memset(hsb0[i][:, 0:1], 0.0)], writes=[r_hsb0[i]])
            S.op("pool", [lambda e, i=i: e.memset(hsb0[i][:, 2049:2050], 0.0)], writes=[r_hsb0[i]])
        w0v = w_in_d[0].rearrange("(kc p) n -> p kc n", p=128)
        order = [18, 19] + [0, 1, 2, 3, 4, 5]
        for i in range(6):
            order += [6 + i, 12 + i]
        _wc = [0]

        def proj_chunk(wv, col0, sink_fn, extra_reads=()):
            wi = _wc[0] % 3
            _wc[0] += 1
            S.dma("pool", wch[wi], wv[:, :, col0:col0 + 128], writes=[r_wch[wi]], key="wch%d" % wi)
            for tq in range(4):
                b = nxt([0, 1, 2, 3])
                mm_group(bank(b), [(wch[wi][:, kc, :], XT[:, kc, tq * 512:(tq + 1) * 512]) for kc in range(8)],
                         reads=[r_wch[wi], r_XT], writes=[PB[b]])
                sink_fn(tq, b)

        def conv_chunk(c, hs_, r_hs, acc_out, r_acc_list, out_final):
            S.op("act", [lambda e: e.activation(out=acc_out, in_=hs_[:, 1:2049], func=AF.Identity,
                                                scale=cwb[:, c, 1:2], bias=cwb[:, c, 3:4])],
                 reads=[r_hs, r_cwb], writes=r_acc_list)
            S.op("dve", [lambda e: e.scalar_tensor_tensor(out=acc_out, in0=hs_[:, 0:2048], scalar=cwb[:, c, 0:1],
                                                          in1=acc_out, op0=ALU.mult, op1=ALU.add)],
                 reads=[r_hs, r_cwb] + r_acc_list, writes=r_acc_list)
            S.op("dve", [lambda e: e.scalar_tensor_tensor(out=out_final[0], in0=hs_[:, 2:2050], scalar=cwb[:, c, 2:3],
                                                          in1=acc_out, op0=ALU.mult, op1=ALU.add)],
                 reads=[r_hs, r_cwb] + r_acc_list, writes=out_final[1])

        zdef = []
        for ci_, c in enumerate(order):
            if debug and debug.startswith('s_p') and ci_ == int(debug[3:]):
                S.barrier()
                S.emit()
                return nc
            if c >= 18:
                hp = c - 18

                def sink_mq(tq, b, hp=hp):
                    act_copy(MQ[:, hp, tq * 512:(tq + 1) * 512], bank(b), [PB[b]], [r_MQ])
                proj_chunk(w0v, c * 128, sink_mq)
                continue
            si = (c % 2) if c < 6 else (0 if c < 12 else 1)
            hs_ = hsb0[si]

            def sink_h(tq, b, hs_=hs_, si=si):
                act_copy(hs_[:, 1 + tq * 512:1 + (tq + 1) * 512], bank(b), [PB[b]], [r_hsb0[si]])
            proj_chunk(w0v, c * 128, sink_h)
            if c < 6:
                conv_chunk(c, hs_, r_hsb0[si], accv, [r_accv], (YT[:, c, :], [r_YT]))
            elif c < 12:
                conv_chunk(c, hs_, r_hsb0[si], accx, [r_accx], (accx, [r_accx]))
                while zdef:
                    zdef.pop(0)()
            else:
                i6 = c - 12
                conv_chunk(c, hs_, r_hsb0[si], accv, [r_accv], (accv, [r_accv]))
                S.op("dve", [lambda e: e.tensor_tensor(out=zT, in0=accv, in1=accx, op=ALU.mult)],
                     reads=[r_accv, r_accx], writes=[r_zT])
                def _ztr(i6=i6):
                    for g8 in range(2):
                        b = nxt([6, 7])
                        bb = bank_bf(b)
                        fns = [lambda e, t8=t8, bb=bb, g8=g8: e.transpose(bb[:, t8 * 128:(t8 + 1) * 128],
                                                                         zT[:, (g8 * 8 + t8) * 128:(g8 * 8 + t8 + 1) * 128], ident[:])
                               for t8 in range(8)]
                        S.op("pe", fns, reads=[r_zT, r_consts], writes=[PB[b]])
                        S.op("act", [lambda e, g8=g8, bb=bb, i6=i6: e.activation(
                            out=Z[:, g8 * 8:(g8 + 1) * 8, i6 * 128:(i6 + 1) * 128],
                            in_=bb.rearrange("p (k m) -> p k m", k=8), func=AF.Identity)],
                             reads=[PB[b]], writes=[r_Z])
                zdef.append(_ztr)
        while zdef:
            zdef.pop(0)()
        S.barrier()
        if debug == "z":
            for tile in range(16):
                S.op("act", [lambda e, tile=tile: e.activation(out=X[:, tile, 0:768] if False else hsb0[0][:, 0:768], in_=Z[:, tile, :], func=AF.Identity)],
                     reads=[r_Z], writes=[r_hsb0[0]])
                S.dma("sp", dbg_d[tile * 128:(tile + 1) * 128, 0:768], hsb0[0][:, 0:768], reads=[r_hsb0[0]], key="dbg")
            S.barrier()
            S.emit()
            return nc
        mem_attention(R_X + 24 * K_)
        S.barrier()

        YRE = AV(R_XT, [16, 768], BF16)
        YIM = AV(R_X + 24 * K_, [16, 768], BF16)
        r_Y = Res("Yspec")
        ct = [AV(R_X + 48 * K_ + i * 3 * K_, [768], F32) for i in range(4)]
        r_ct = [Res("ct%d" % i) for i in range(4)]
        ftabU = [AV(R_T + 48 * K_, [2, 16, 128], BF16), AV(R_MQ, [2, 16, 128], BF16)]
        r_ftabU = [Res("ftabU0"), Res("ftabU1")]

        def u_epilogue(fc, pre, pim, rbs):
            kre, kim = KS[:, 0, fc, :], KS[:, 1, fc, :]
            S.op("dve", [lambda e: e.tensor_tensor(out=ct[0], in0=pre, in1=kre, op=ALU.mult)], reads=rbs[0:2] + [r_KS], writes=[r_ct[0]])
            S.op("dve", [lambda e: e.tensor_tensor(out=ct[1], in0=pim, in1=kim, op=ALU.mult)], reads=rbs[2:4] + [r_KS], writes=[r_ct[1]])
            S.op("dve", [lambda e: e.tensor_tensor(out=ct[2], in0=pre, in1=kim, op=ALU.mult)], reads=rbs[0:2] + [r_KS], writes=[r_ct[2]])
            S.op("dve", [lambda e: e.tensor_tensor(out=ct[3], in0=pim, in1=kre, op=ALU.mult)], reads=rbs[2:4] + [r_KS], writes=[r_ct[3]])
            S.op("pool", [lambda e: e.tensor_tensor(out=YRE[:, fc, :], in0=ct[0], in1=ct[1], op=ALU.subtract)],
                 reads=[r_ct[0], r_ct[1]], writes=[r_Y])
            S.op("pool", [lambda e: e.tensor_tensor(out=YIM[:, fc, :], in0=ct[2], in1=ct[3], op=ALU.add)],
                 reads=[r_ct[2], r_ct[3]], writes=[r_Y])

        fwd_pass(Z, Z, r_Z, ftabU, r_ftabU, u_epilogue)
        S.barrier()

        itab = [AV(R_T + 48 * K_, [4, 2, 512], BF16), AV(R_MQ, [4, 2, 512], BF16)]
        r_itab = [Res("itab0"), Res("itab1")]
        _it = 0
        for tt in range(4):
            fn_all = []
            for fg in range(4):
                si = _it % 2
                _it += 1
                S.dma("sp", itab[si].rearrange("p a b c -> p (a b c)"), inv_tab_d[tt, fg], writes=[r_itab[si]], key="itab%d" % si)
                fns = []
                for fi in range(4):
                    fc = fg * 4 + fi
                    for ri in range(2):
                        Ysrc = YRE if ri == 0 else YIM
                        for cc in range(6):
                            first = (fc == 0 and ri == 0)
                            lastm = (fc == 15 and ri == 1)
                            fns.append(lambda e, cc=cc, Ysrc=Ysrc, fc=fc, si=si, fi=fi, ri=ri, first=first, lastm=lastm: e.matmul(
                                bank(cc), Ysrc[:, fc, cc * 128:(cc + 1) * 128], itab[si][:, fi, ri, :], start=first, stop=lastm))
                S.op("pe", fns, reads=[r_itab[si], r_Y], writes=[PB[cc] for cc in range(6)])
            for cc in range(6):
                S.op("dve", [lambda e, cc=cc, tt=tt: e.tensor_tensor(out=YT[:, cc, tt * 512:(tt + 1) * 512], in0=bank(cc),
                                                                    in1=YT[:, cc, tt * 512:(tt + 1) * 512], op=ALU.mult)],
                     reads=[PB[cc], r_YT], writes=[r_YT])
        S.barrier()

        if debug == "mix0":
            for c in range(8):
                src = YT[:, c, 0:1024] if c < 6 else YM[:, c - 6, 0:1024]
                S.op("act", [lambda e, src=src: e.activation(out=rbuf[0], in_=src, func=AF.Identity)], reads=[r_YT, r_YM], writes=[r_rbuf[0]])
                S.dma("sp", dbg_d[c * 128:(c + 1) * 128, :], rbuf[0], reads=[r_rbuf[0]], key="dbg")
            S.barrier()
            S.emit()
            return nc

        out_proj_ln1(0)
        S.barrier()
        if debug == "ln1_0":
            for tile in range(16):
                S.dma("sp", dbg_d[tile * 128:(tile + 1) * 128, :], X[:, tile, :], reads=[r_X[tile]], key="dbg")
            S.barrier()
            S.emit()
            return nc
        ffn_ln2(0, final=(debug == "l0"))
        S.barrier()

        if debug is None or debug.startswith("l1"):
            if debug == "l1_s":
                S.barrier()
                S.emit()
                return nc
            w1v = w_in_d[1].rearrange("(kc p) n -> p kc n", p=128)
            QT = YT
            r_QT = [Res("QT%d" % i) for i in range(12)]
            KT = AV(R_T, [4, 2048], BF16)
            r_KT = Res("KT")
            VT = AV(R_T + 16 * K_, [16, 256], BF16)
            r_VT = Res("VT")
            ropec = AV(R_T + 24 * K_, [2048], F32)
            ropes = AV(R_T + 32 * K_, [2048], F32)
            r_rope = Res("rope")
            wch1 = [AV(R_T + 40 * K_ + i * 2 * K_, [8, 128], BF16) for i in range(3)]
            r_wch1 = [Res("wch1_%d" % i) for i in range(3)]
            qsb = [AV(R_T + 46 * K_ + i * K_, [512], BF16) for i in range(2)]
            r_qsb = [Res("qsb%d" % i) for i in range(2)]
            rt1 = AV(R_T + 48 * K_, [512], F32)
            rt2 = AV(R_T + 50 * K_, [512], F32)
            r_rt1, r_rt2 = Res("rt1"), Res("rt2")
            wvt = AV(R_T + 52 * K_, [8, 256], BF16)
            r_wvt = Res("wvt")
            esk = small("esk", [64, 12], F32)
            r_esk = Res("esk")
            import os
            SK = os.environ.get("SKIP", "")
            if "r" not in SK:
                S.dma("sp", ropec, ropec_d[:, :], writes=[r_rope], key="rope0")
                S.dma("sp", ropes, ropes_d[:, :], writes=[r_rope], key="rope1")
            if "e" not in SK:
                S.dma("sp", esk[:], sink_d.partition_broadcast(64), writes=[r_esk], key="esk")
            if "x" not in SK:
                S.op("act", [lambda e: e.activation(out=esk[:], in_=esk[:], func=(AF.Identity if "I" in SK else AF.Exp))], reads=[r_esk], writes=[r_esk])
            if "w" not in SK:
                S.dma("pool", wvt, w1v[:, :, 1024:1280], writes=[r_wvt], key="wvt")
            if debug == "l1_p0":
                S.barrier()
                S.emit()
                return nc
            _w1 = [0]
            _rq = [0]

            def proj1(loads, sink_fn):
                wi = _w1[0] % 3
                _w1[0] += 1
                for k_, (dst0, dst1, c0, c1) in enumerate(loads):
                    S.dma("pool", wch1[wi][:, :, dst0:dst1], w1v[:, :, c0:c1], writes=[r_wch1[wi]], key="wch1_%d_%d" % (wi, k_))
                for tq in range(4):
                    b = nxt([0, 1, 2, 3])
                    mm_group(bank(b), [(wch1[wi][:, kc, :], XT[:, kc, tq * 512:(tq + 1) * 512]) for kc in range(8)],
                             reads=[r_wch1[wi], r_XT], writes=[PB[b]])
                    sink_fn(tq, b)

            def rope_sink(dest, r_dest):
                def sink(tq, b):
                    cs = slice(tq * 512, (tq + 1) * 512)
                    i = _rq[0] % 2
                    _rq[0] += 1
                    S.op("act", [lambda e: e.activation(out=qsb[i], in_=bank(b), func=AF.Identity)],
                         reads=[PB[b]], writes=[r_qsb[i]])
                    b2 = [4, 5][i]
                    mm_group(bank(b2), [(pm[:], qsb[i])], reads=[r_consts, r_qsb[i]], writes=[PB[b2]])
                    S.op("dve", [lambda e: e.tensor_tensor(out=rt1, in0=bank(b), in1=ropec[:, cs], op=ALU.mult)],
                         reads=[PB[b], r_rope], writes=[r_rt1])
                    S.op("dve", [lambda e: e.tensor_tensor(out=rt2, in0=bank(b2), in1=ropes[:, cs], op=ALU.mult)],
                         reads=[PB[b2], r_rope], writes=[r_rt2])
                    S.op("dve", [lambda e: e.tensor_tensor(out=dest[:, cs], in0=rt1, in1=rt2, op=ALU.add)],
                         reads=[r_rt1, r_rt2], writes=r_dest)
                return sink

            for c in range(6):
                proj1([(0, 128, c * 128, (c + 1) * 128)], rope_sink(QT[:, c, :], [r_QT[2 * c], r_QT[2 * c + 1]]))
            if debug == "l1_p1":
                S.barrier()
                S.emit()
                return nc
            for g in range(4):
                c0 = 768 + g * 64
                proj1([(0, 64, c0, c0 + 64), (64, 128, c0, c0 + 64)], rope_sink(KT[:, g, :], [r_KT]))
            if debug == "l1_p2":
                S.barrier()
                S.emit()
                return nc
            for hp in range(2):
                def sink_mq1(tq, b, hp=hp):
                    act_copy(MQ[:, hp, tq * 512:(tq + 1) * 512], bank(b), [PB[b]], [r_MQ])
                proj1([(0, 128, 1280 + hp * 128, 1280 + (hp + 1) * 128)], sink_mq1)
            for tile in range(16):
                b = nxt([0, 1, 2, 3])
                mm_group(bank(b, 256), [(XT[:, kc, tile * 128:(tile + 1) * 128], wvt[:, kc, :]) for kc in range(8)],
                         reads=[r_XT, r_wvt], writes=[PB[b]])
                act_copy(VT[:, tile, :], bank(b, 256), [PB[b]], [r_VT])
            S.barrier()

            if debug == "l1_p":
                S.barrier()
                S.emit()
                return nc
            PTs = [AV(R_XT + i * 12 * K_, [16, 384], BF16) for i in range(2)]
            r_PTs = [Res("PTs%d" % i) for i in range(2)]
            rden1 = AV(R_XT + 24 * K_, [512], F32)
            r_rden1 = Res("rden1")
            def st_exp(h, j):
                g = h // 3
                hh = h % 2
                c = h // 2
                prow = slice(hh * 64, (hh + 1) * 64)
                pt = PTs[h % 2]
                r_pt = r_PTs[h % 2]
                qlo = max(0, j - 1) * 128
                qhi = min(16, j + 2) * 128
                n = qhi - qlo
                moff = qlo - (j - 1) * 128
                b = nxt([0, 1, 2, 3])
                fns = [
                    lambda e: e.matmul(bank(b, n), KT[prow, g, j * 128:(j + 1) * 128], QT[prow, c, qlo:qhi], start=True, stop=False),
                    lambda e: e.matmul(bank(b, n), ident[:], maskb[:, moff:moff + n], start=False, stop=True),
                ]
                S.op("pe", fns, reads=[r_KT, r_QT[h], r_consts], writes=[PB[b]])
                S.op("act", [lambda e: e.activation(out=pt[:, j, 0:n], in_=bank(b, n), func=AF.Exp, scale=0.125)],
                     reads=[PB[b]], writes=[r_pt])

            def pv_norm(h, qt):
                g = h // 3
                hh = h % 2
                c = h // 2
                prow = slice(hh * 64, (hh + 1) * 64)
                pt = PTs[h % 2]
                r_pt = r_PTs[h % 2]
                bo, bd = [4, 6][qt % 2], [5, 7][qt % 2]
                fo, fd = [], []
                for i4 in range(4):
                    qb = 4 * qt + i4
                    js = [j for j in (qb - 1, qb, qb + 1) if 0 <= j < 16]
                    for k_, j in enumerate(js):
                        lc = (qb - max(0, j - 1)) * 128
                        st_, sp_ = (k_ == 0), (k_ == len(js) - 1)
                        fo.append(lambda e, i4=i4, j=j, lc=lc, st_=st_, sp_=sp_: e.matmul(
                            ps[0:64, bo * 512 + i4 * 128:bo * 512 + (i4 + 1) * 128], VT[:, j, g * 64:(g + 1) * 64],
                            pt[:, j, lc:lc + 128], start=st_, stop=sp_))
                        fd.append(lambda e, i4=i4, j=j, lc=lc, st_=st_, sp_=sp_: e.matmul(
                            ps[0:64, bd * 512 + i4 * 128:bd * 512 + (i4 + 1) * 128], ones64[:],
                            pt[:, j, lc:lc + 128], start=st_, stop=sp_))
                S.op("pe", fo, reads=[r_pt, r_VT], writes=[PB[bo]])
                S.op("pe", fd, reads=[r_pt, r_consts], writes=[PB[bd]])
                S.op("dve", [lambda e: e.tensor_scalar(out=rden1[0:64, :], in0=bank(bd, 512, 0, 64),
                                                       scalar1=esk[:, h:h + 1], scalar2=None, op0=ALU.add)],
                     reads=[PB[bd], r_esk], writes=[r_rden1])
                S.op("dve", [lambda e: e.reciprocal(out=rden1[0:64, :], in_=rden1[0:64, :])], reads=[r_rden1], writes=[r_rden1])
                S.op("dve", [lambda e: e.tensor_tensor(
                    out=QT[prow, c, qt * 512:(qt + 1) * 512], in0=bank(bo, 512, 0, 64), in1=rden1[0:64, :], op=ALU.mult)],
                     reads=[PB[bo], r_rden1], writes=[r_QT[h]])

            for h in range(13):
                for qt in range(4):
                    if h < 12:
                        for j in range(4 * qt, 4 * qt + 4):
                            st_exp(h, j)
                    if h >= 1:
                        pv_norm(h - 1, qt)
            S.barrier()
            if debug == "l1_a":
                S.barrier()
                S.emit()
                return nc
            mem_attention(R_XT)
            S.barrier()
            if debug == "l1mix":
                for c in range(8):
                    src = YT[:, c, 0:1024] if c < 6 else YM[:, c - 6, 0:1024]
                    S.op("act", [lambda e, src=src: e.activation(out=rbuf[0], in_=src, func=AF.Identity)], reads=[r_YM], writes=[r_rbuf[0]])
                    S.dma("sp", dbg_d[c * 128:(c + 1) * 128, :], rbuf[0], reads=[r_rbuf[0]], key="dbg")
                S.barrier()
                S.emit()
                return nc
            out_proj_ln1(1)
            S.barrier()
            ffn_ln2(1, final=True)
            S.barrier()

        S.barrier()
        S.emit()
    return nc


def prep_shared(inputs):
    f32 = np.float32
    sh = {}
    for k in ("w_mem_kv", "l0_w_in", "l1_w_in", "l0_w_out", "l1_w_out", "l0_ffn_w_up", "l1_ffn_w_up",
              "l0_ffn_w_down", "l1_ffn_w_down", "l0_filt_w1", "l0_filt_w2", "l0_filt_w3", "l0_filt_w_out",
              "l0_hyena_d", "l1_sink"):
        sh[k] = np.ascontiguousarray(np.asarray(inputs[k], dtype=f32))
    for i in range(2):
        for n in ("ln1_g", "ln1_b", "ln2_g", "ln2_b"):
            sh["l%d_%s" % (i, n)] = np.ascontiguousarray(np.asarray(inputs["l%d_%s" % (i, n)], dtype=f32))
        cw = np.asarray(inputs["l%d_ffn_conv_w" % i], f32)
        cb = np.asarray(inputs["l%d_ffn_conv_b" % i], f32)
        a = np.concatenate([cw, cb[None, :]], axis=0)
        sh["l%d_fcwb" % i] = np.ascontiguousarray(a.reshape(4, 44, 128).transpose(2, 1, 0))
    cw = np.asarray(inputs["l0_conv_w"], f32)
    cb = np.asarray(inputs["l0_conv_b"], f32)
    a = np.concatenate([cw, cb[None, :]], axis=0)
    sh["l0_cwb"] = np.ascontiguousarray(a.reshape(4, 18, 128).transpose(2, 1, 0))
    sh["l0_fbf"] = np.ascontiguousarray(np.stack(
        [np.asarray(inputs["l0_filt_%s%d" % (n, l)], f32) for l in (1, 2, 3) for n in ("b", "f")], axis=1))
    sh.update(const_tables())
    return sh


_NC_CACHE = {}


def kernel(**inputs):
    sh = prep_shared(inputs)
    x = np.asarray(inputs["x"], np.float32)
    mem = np.asarray(inputs["mem"], np.float32)
    if "nc" not in _NC_CACHE:
        _NC_CACHE["nc"] = build_program()
    nc = _NC_CACHE["nc"]
    in_maps = []
    for b in range(8):
        m = dict(sh)
        m["x"] = np.ascontiguousarray(x[b])
        m["mem"] = np.ascontiguousarray(mem[b])
        in_maps.append(m)
    res = run_bass_kernel_spmd(nc, in_maps, core_ids=list(range(8)))
    return np.stack([np.asarray(r["out"], np.float32) for r in res.results], axis=0)
```

```python
import math
from contextlib import ExitStack

import numpy as np
import ml_dtypes
import concourse.bass as bass
import concourse.mybir as mybir
from concourse.bass_utils import run_bass_kernel_spmd

F32 = mybir.dt.float32
BF16 = mybir.dt.bfloat16
AF = mybir.ActivationFunctionType
ALU = mybir.AluOpType
AX = mybir.AxisListType

L = 2048
D = 1024
NT = 16
KC = 8
DFF = 2816
NFF = 22
ALPHA = 4.0 ** 0.25
EPS = 1e-5
PI = float(np.pi)


class Res:
    __slots__ = ("name", "w", "r", "excl")

    def __init__(self, name, excl=False):
        self.name = name
        self.w = None
        self.r = []
        self.excl = excl


class Sched:
    def __init__(self, nc, es):
        self.nc = nc
        self.es = es
        self.engs = {}
        for n in ("pe", "act", "dve", "pool", "sp"):
            sem = es.enter_context(nc.semaphore("s_" + n))
            self.engs[n] = dict(sem=sem, count=0, known={}, ops=[])
        self.dma_sems = {}
        self.n_dma_sems = 0

    def dma_sem(self, key):
        if key not in self.dma_sems:
            sem = self.es.enter_context(self.nc.semaphore("d%d" % self.n_dma_sems))
            self.n_dma_sems += 1
            self.dma_sems[key] = [sem, 0]
        return self.dma_sems[key]

    def op(self, eng, fns, reads=(), writes=(), dma_key=None):
        E = self.engs[eng]
        deps = {}

        def add(ev):
            if ev is None:
                return
            sem, val = ev
            if deps.get(sem, 0) < val:
                deps[sem] = val

        excl_reads = [r for r in reads if r.excl]
        writes = list(writes) + [r for r in excl_reads if r not in writes]
        reads = [r for r in reads if not r.excl]
        for r in reads:
            add(r.w)
        for w in writes:
            add(w.w)
            for ev in w.r:
                add(ev)
        waits = []
        for sem, val in deps.items():
            if sem is E["sem"] and eng == "pe" and dma_key is None:
                continue
            if E["known"].get(sem, 0) >= val:
                continue
            E["known"][sem] = val
            waits.append((sem, val))
        if dma_key is not None:
            ds = self.dma_sem(dma_key)
            ds[1] += 16
            ev = (ds[0], ds[1])
            inc = (ds[0], 16)
        else:
            E["count"] += 1
            ev = (E["sem"], E["count"])
            inc = (E["sem"], 1)
        for r in reads:
            r.r.append(ev)
        for w in writes:
            w.w = ev
            w.r = []
        if not isinstance(fns, (list, tuple)):
            fns = [fns]
        E["ops"].append((waits, list(fns), inc))
        return ev

    def dma(self, queue, out, in_, reads=(), writes=(), key=None):
        assert key is not None
        return self.op(queue, [lambda e: e.dma_start(out=out, in_=in_)], reads, writes, dma_key=key)

    def barrier(self):
        evs = [(E["sem"], E["count"]) for E in self.engs.values() if E["count"] > 0]
        evs += [(s, c) for (s, c) in self.dma_sems.values() if c > 0]
        for n, E in self.engs.items():
            waits = []
            for sem, val in evs:
                if E["known"].get(sem, 0) >= val:
                    continue
                if sem is E["sem"] and n == "pe":
                    continue
                E["known"][sem] = val
                waits.append((sem, val))
            if waits:
                E["ops"].append((waits, [], None))

    def emit(self):
        nc = self.nc
        import sys
        print("SCHED counts", {n: E["count"] for n, E in self.engs.items()}, "dma", {k: v[1] for k, v in self.dma_sems.items()}, file=sys.stderr)

        def run(name):
            def f(e):
                for waits, fns, inc in self.engs[name]["ops"]:
                    for sem, val in waits:
                        e.wait_ge(sem, val)
                    n = len(fns)
                    for i, fn in enumerate(fns):
                        ins = fn(e)
                        if i == n - 1:
                            ins.then_inc(inc[0], inc[1])
            return f

        with nc.Block() as block:
            block.tensor(run("pe"))
            block.scalar(run("act"))
            block.vector(run("dve"))
            block.gpsimd(run("pool"))
            block.sync(run("sp"))


def _bf(a):
    return np.ascontiguousarray(a.astype(ml_dtypes.bfloat16))


_CONST_CACHE = {}


def const_tables():
    if _CONST_CACHE:
        return _CONST_CACHE
    N = 4096
    f = np.arange(2048, dtype=np.float64)
    t = np.arange(2048, dtype=np.float64)
    m = np.mod(np.outer(2 * f + 1, t), 2 * N)
    ang = np.pi * m / N
    C = np.cos(ang)
    Sn = np.sin(ang)
    CT = C.T.reshape(16, 128, 16, 128)
    ST = (-Sn).T.reshape(16, 128, 16, 128)
    fwd = np.stack([CT, ST], axis=0)
    fwd = fwd.transpose(3, 2, 0, 1, 4)
    _CONST_CACHE["fwd_tab"] = _bf(fwd.reshape(16, 128, 2 * 16 * 128))
    Ci = (C / 2048.0).reshape(4, 4, 128, 4, 512)
    Si = (-Sn / 2048.0).reshape(4, 4, 128, 4, 512)
    inv = np.stack([Ci, Si], axis=0)
    inv = inv.transpose(4, 1, 3, 2, 0, 5)
    _CONST_CACHE["inv_tab"] = _bf(inv.reshape(4, 4, 128, 4 * 2 * 512))
    f32 = np.float32
    tl = np.linspace(0.0, 1.0, L, dtype=f32)[:, None]
    w = (f32(2.0 * math.pi) * np.arange(L, dtype=f32)[:, None] / f32(L)).astype(f32)
    fr = np.linspace(1e-4, 15, 16, dtype=f32)[None, :]
    z = np.concatenate([tl, np.cos(fr * w), -np.sin(fr * w)], axis=-1).astype(f32)
    _CONST_CACHE["zfT"] = np.ascontiguousarray(z.T)
    _CONST_CACHE["ntn"] = np.ascontiguousarray((-tl[:, 0]).reshape(16, 128).T.astype(f32))
    min_decay = math.log(1e-2) / 1.5
    max_decay = math.log(1e-2) / 0.3
    _CONST_CACHE["deltas"] = np.abs(np.linspace(min_decay, max_decay, 768, dtype=f32)).astype(f32)
    inv_f = (10000.0 ** (-np.arange(0, 64, 2, dtype=f32) / f32(64))).astype(f32)
    angr = (np.arange(L, dtype=f32)[:, None] * inv_f[None, :]).astype(f32)
    angr = np.concatenate([angr, angr], axis=-1)
    cosT = np.cos(angr).T.astype(f32)
    sinT = np.sin(angr).T.astype(f32)
    _CONST_CACHE["ropec"] = np.ascontiguousarray(np.concatenate([cosT, cosT], axis=0))
    _CONST_CACHE["ropes"] = np.ascontiguousarray(np.concatenate([sinT, sinT], axis=0))
    Pm = np.zeros((128, 128), np.float32)
    for po in range(128):
        d = po % 64
        if d < 32:
            Pm[po + 32, po] = -1.0
        else:
            Pm[po - 32, po] = 1.0
    _CONST_CACHE["pm"] = _bf(Pm)
    _CONST_CACHE["ident"] = _bf(np.eye(128, dtype=np.float32))
    k = np.arange(128)[:, None]
    q = np.arange(128)[None, :]
    NEG = -30000.0
    m_next = np.where(k <= q, 0.0, NEG)
    m_prev = np.where(k >= q, 0.0, NEG)
    _CONST_CACHE["maskb"] = _bf(np.concatenate([m_next, np.zeros((128, 128)), m_prev], axis=1))
    return _CONST_CACHE


def build_program(debug=None):
    nc = bass.Bass("TRN2", target_bir_lowering=False)
    dbg = {}

    def din(name, shape, dt=F32):
        return nc.dram_tensor(name, list(shape), dt, kind="ExternalInput").ap()

    x_d = din("x", [L, D])
    mem_d = din("mem", [256, D])
    wkv_d = din("w_mem_kv", [D, 512])
    w_in_d = [din("l0_w_in", [D, 2560]), din("l1_w_in", [D, 1536])]
    w_out_d = [din("l0_w_out", [D, D]), din("l1_w_out", [D, D])]
    w_up_d = [din("l%d_ffn_w_up" % i, [D, 2 * DFF]) for i in range(2)]
    w_dn_d = [din("l%d_ffn_w_down" % i, [DFF, D]) for i in range(2)]
    ln_d = [[din("l%d_%s" % (i, n), [D]) for n in ("ln1_g", "ln1_b", "ln2_g", "ln2_b")] for i in range(2)]
    fcwb_d = [din("l%d_fcwb" % i, [128, 44, 4]) for i in range(2)]
    cwb_d = din("l0_cwb", [128, 18, 4])
    fw1_d = din("l0_filt_w1", [33, 64])
    fw2_d = din("l0_filt_w2", [64, 64])
    fw3_d = din("l0_filt_w3", [64, 64])
    fwo_d = din("l0_filt_w_out", [64, 1536])
    fbf_d = din("l0_fbf", [64, 6])
    hd_d = din("l0_hyena_d", [768])
    sink_d = din("l1_sink", [12])
    fwd_tab_d = din("fwd_tab", [16, 128, 4096], BF16)
    inv_tab_d = din("inv_tab", [4, 4, 128, 4096], BF16)
    zfT_d = din("zfT", [33, L])
    ntn_d = din("ntn", [128, 16])
    deltas_d = din("deltas", [768])
    ropec_d = din("ropec", [128, L])
    ropes_d = din("ropes", [128, L])
    pm_d = din("pm", [128, 128], BF16)
    ident_d = din("ident", [128, 128], BF16)
    maskb_d = din("maskb", [128, 384], BF16)
    out_d = nc.dram_tensor("out", [L, D], F32, kind="ExternalOutput").ap()
    if debug:
        dbg_d = nc.dram_tensor("dbg", [L, D], F32, kind="ExternalOutput").ap()

    es = ExitStack()
    with es:
        S = Sched(nc, es)
        AR_BYTES = 194 * 1024
        arena = es.enter_context(nc.sbuf_tensor("arena", [128, AR_BYTES // 2], BF16))
        ps = es.enter_context(nc.psum_tensor("ps", [128, 4096], F32))
        PB = [Res("pb%d" % i, excl=True) for i in range(8)]

        def bank(b, n=512, p0=0, p1=128):
            return ps[p0:p1, b * 512:b * 512 + n]

        def bank_bf(b):
            return ps[:, b * 512:(b + 1) * 512].bitcast(BF16)

        def AV(off, shape, dt):
            n = int(np.prod(shape))
            if dt == F32:
                v = arena[:, off // 2: off // 2 + 2 * n].bitcast(F32)
            else:
                v = arena[:, off // 2: off // 2 + n]
            if len(shape) == 2:
                v = v.rearrange("p (a b) -> p a b", a=shape[0])
            elif len(shape) == 3:
                v = v.rearrange("p (a b c) -> p a b c", a=shape[0], b=shape[1])
            elif len(shape) == 4:
                v = v.rearrange("p (a b c d) -> p a b c d", a=shape[0], b=shape[1], c=shape[2])
            return v

        K_ = 1024
        R_X, R_XT, R_Y, R_YM, R_MQ, R_T = 0, 64 * K_, 96 * K_, 120 * K_, 128 * K_, 136 * K_

        def small(name, shape, dt):
            return es.enter_context(nc.sbuf_tensor("sb_" + name, list(shape), dt))

        ident = small("ident", [128, 128], BF16)
        pm = small("pm", [128, 128], BF16)
        maskb = small("maskb", [128, 384], BF16)
        ones64 = small("ones64", [128, 64], BF16)
        memKT = small("memKT", [128, 2, 256], BF16)
        memV = small("memV", [128, 2, 256], BF16)
        fcwb = small("fcwb", [128, 44, 4], F32)
        cwb = fcwb[:, 0:18, :]
        stat = small("stat", [128, 128], F32)
        GB = small("GB", [128, 2, 1024], F32)
        r_consts = Res("consts")
        r_memKV = Res("memKV")
        r_cwb = Res("cwb")
        r_fcwb = Res("fcwb")
        r_GB = Res("GB")

        X = AV(R_X, [16, 1024], F32)
        XT = AV(R_XT, [8, 2048], BF16)
        YT = AV(R_Y, [6, 2048], BF16)
        YM = AV(R_YM, [2, 2048], BF16)
        MQ = AV(R_MQ, [2, 2048], BF16)
        r_X = [Res("X%d" % i) for i in range(16)]
        r_XT = Res("XT")
        r_YT = Res("YT")
        r_YM = Res("YM")
        r_MQ = Res("MQ")

        _bk = [0]

        def nxt(lst):
            b = lst[_bk[0] % len(lst)]
            _bk[0] += 1
            return b

        def mm_group(out, pairs, reads, writes):
            n = len(pairs)
            fns = []
            for i, (l, r) in enumerate(pairs):
                fns.append(lambda e, l=l, r=r, i=i: e.matmul(out, l, r, start=(i == 0), stop=(i == n - 1)))
            return S.op("pe", fns, reads, writes)

        def act_copy(out, in_, reads, writes):
            return S.op("act", [lambda e: e.activation(out=out, in_=in_, func=AF.Identity)], reads, writes)

        S.dma("sp", ident[:], ident_d[:, :], writes=[r_consts], key="c0")
        S.dma("sp", pm[:], pm_d[:, :], writes=[r_consts], key="c1")
        S.dma("sp", maskb[:], maskb_d[:, :], writes=[r_consts], key="c2")
        S.dma("sp", cwb, cwb_d[:, :, :], writes=[r_cwb], key="c3")
        S.op("dve", [lambda e: e.memset(ones64[:], 1.0)], writes=[r_consts])
        epst = small("epst", [128, 1], F32)
        identf = small("identf", [128, 128], F32)
        aident = small("aident", [128, 128], F32)
        S.op("act", [lambda e: e.activation(out=identf[:], in_=ident[:], func=AF.Identity)], reads=[r_consts], writes=[r_consts])
        S.op("act", [lambda e: e.activation(out=aident[:], in_=ident[:], func=AF.Identity, scale=ALPHA)], reads=[r_consts], writes=[r_consts])
        S.op("dve", [lambda e: e.memset(epst[:], EPS)], writes=[r_consts])

        HS = AV(R_X, [2, 16, 768], BF16)
        r_HS = Res("HS")
        hA = AV(R_X + 48 * K_, [2048], F32)
        hB = AV(R_X + 56 * K_, [2048], F32)
        r_hA, r_hB = Res("hA"), Res("hB")
        zf = AV(R_Y, [2048], F32)
        fw = AV(R_Y + 8 * K_, [3, 64], F32)
        fwo = AV(R_Y + 9 * K_, [1536], F32)
        fbf = AV(R_Y + 15 * K_, [8], F32)
        dbc = AV(R_Y + 16 * K_, [768], F32)
        dlt = AV(R_YM, [768], F32)
        dec = AV(R_YM + 3 * K_, [768], F32)
        ntn = AV(R_YM + 6 * K_, [16], F32)
        fsb = AV(R_MQ, [768], F32)
        wtmp = AV(R_MQ + 3 * K_, [512], F32)
        wtm2 = AV(R_MQ + 5 * K_, [512], F32)
        r_f = Res("filt_in")
        r_dec, r_fsb, r_wtmp, r_wtm2, r_dbc = Res("dec"), Res("fsb"), Res("wtmp"), Res("wtm2"), Res("dbc")
        S.dma("sp", zf[0:33, :], zfT_d[:, :], writes=[r_f], key="f0")
        S.dma("sp", fw[0:33, 0, :], fw1_d[:, :], writes=[r_f], key="f1")
        S.dma("sp", fw[0:64, 1, :], fw2_d[:, :], writes=[r_f], key="f2")
        S.dma("sp", fw[0:64, 2, :], fw3_d[:, :], writes=[r_f], key="f3")
        S.dma("sp", fwo[0:64, :], fwo_d[:, :], writes=[r_f], key="f4")
        S.dma("sp", fbf[0:64, 0:6], fbf_d[:, :], writes=[r_f], key="f5")
        S.dma("sp", dbc, hd_d.partition_broadcast(128), writes=[r_dbc], key="f6")
        S.dma("sp", dlt, deltas_d.partition_broadcast(128), writes=[r_f], key="f7")
        S.dma("sp", ntn, ntn_d[:, :], writes=[r_f], key="f8")

        def transpose_to_XT(src_bf, tile, r_src):
            b = nxt([6, 7])
            bb = bank_bf(b)
            fns = [lambda e, kc=kc: e.transpose(bb[:, kc * 128:(kc + 1) * 128], src_bf[:, kc * 128:(kc + 1) * 128], ident[:])
                   for kc in range(8)]
            S.op("pe", fns, reads=[r_src, r_consts], writes=[PB[b]])
            S.op("act", [lambda e: e.activation(out=XT[:, :, tile * 128:(tile + 1) * 128],
                                                in_=bb.rearrange("p (k m) -> p k m", k=8), func=AF.Identity)],
                 reads=[PB[b]], writes=[r_XT])

        def load_ln(layer, which):
            g_d, b_d = ln_d[layer][2 * which], ln_d[layer][2 * which + 1]
            S.dma("sp", GB[:, 0, :], g_d.partition_broadcast(128), writes=[r_GB], key="gb0")
            S.dma("sp", GB[:, 1, :], b_d.partition_broadcast(128), writes=[r_GB], key="gb1")

        NRB = 6
        LN_LAG = 3
        _rboff = [46 * K_, 50 * K_, 28672, 28672 + 4096, 36880, 36880 + 4096]
        rbuf = [AV(R_T + _rboff[i], [1024], F32) for i in range(NRB)]
        _xboff = [54 * K_, 56 * K_, 24 * K_, 26 * K_]
        xbuf = [AV(R_T + _xboff[i], [1024], BF16) for i in range(4)]
        r_rbuf = [Res("rbuf%d" % i) for i in range(NRB)]
        r_xbuf = [Res("xbuf%d" % i) for i in range(4)]
        r_stat = Res("stat")
        r_stats = [Res("stat%d" % i) for i in range(8)]
        _ln = [0]

        class LNPipe:
            def __init__(self, final_out):
                self.final_out = final_out
                self.q = []

            def push(self, tile, pin, r_pin, r_in, r_r):
                ri = tile % NRB
                so = ri * 16
                r_stat = r_stats[ri]
                st6 = stat[:, so:so + 12]
                mv = stat[:, so + 12:so + 14]
                rstd = stat[:, so + 14:so + 15]
                nmr = stat[:, so + 15:so + 16]
                final_out = self.final_out

                def s1():
                    S.op("dve", [lambda e: e.bn_stats(out=st6[:, 0:6], in_=pin[:, 0:512])], reads=[r_pin[0]], writes=[r_stat])
                    S.op("dve", [lambda e: e.bn_stats(out=st6[:, 6:12], in_=pin[:, 512:1024])], reads=[r_pin[1]], writes=[r_stat])
                    S.op("dve", [lambda e: e.bn_aggr(out=mv, in_=st6)], reads=[r_stat], writes=[r_stat])
                    S.op("act", [lambda e: e.activation(out=rstd, in_=mv[:, 1:2], func=AF.Sqrt, bias=epst[:, 0:1])],
                         reads=[r_stat, r_consts], writes=[r_stat])

                def s2():
                    S.op("dve", [lambda e: e.reciprocal(out=rstd, in_=rstd)], reads=[r_stat], writes=[r_stat])
                    S.op("dve", [lambda e: e.scalar_tensor_tensor(out=nmr, in0=mv[:, 0:1], scalar=-1.0, in1=rstd,
                                                                  op0=ALU.mult, op1=ALU.mult)], reads=[r_stat], writes=[r_stat])
                    S.op("act", [lambda e: e.activation(out=r_in, in_=pin, func=AF.Identity, scale=rstd, bias=nmr)],
                         reads=list(r_pin) + [r_stat], writes=[r_r])
                    S.op("dve", [lambda e: e.tensor_tensor(out=r_in, in0=r_in, in1=GB[:, 0, :], op=ALU.mult)],
                         reads=[r_r, r_GB], writes=[r_r])
                    S.op("pool", [lambda e: e.tensor_tensor(out=X[:, tile, :], in0=r_in, in1=GB[:, 1, :], op=ALU.add)],
                         reads=[r_r, r_GB], writes=[r_X[tile]])

                def s3():
                    if final_out:
                        S.dma("sp", out_d[tile * 128:(tile + 1) * 128, :], X[:, tile, :], reads=[r_X[tile]], key="out%d" % (tile % 4))
                    else:
                        i4 = tile % 4
                        S.op("act", [lambda e: e.activation(out=xbuf[i4], in_=X[:, tile, :], func=AF.Identity)],
                             reads=[r_X[tile]], writes=[r_xbuf[i4]])

                def s4():
                    if not final_out:
                        i4 = tile % 4
                        transpose_to_XT(xbuf[i4], tile, r_xbuf[i4])

                s1()
                for k_, f_ in enumerate((s2, s3, s4)):
                    self.q.append([k_ + 1, f_])
                self._tick()

            def _tick(self):
                keep = []
                for item in self.q:
                    item[0] -= 0
                ready = [it for it in self.q if it[0] <= 0]
                for it in ready:
                    it[1]()
                self.q = [it for it in self.q if it[0] > 0]
                for it in self.q:
                    it[0] -= 1

            def flush(self):
                while self.q:
                    self._tick()

        def mem_attention(toff):
            PT = [AV(toff + i * K_, [512], BF16) for i in range(8)]
            r_PT = [Res("mpt%d" % i) for i in range(8)]
            rdens = [AV(toff + 8 * K_ + i * 2 * K_, [512], F32) for i in range(2)]
            r_rdens = [Res("mrden%d" % i) for i in range(2)]
            its = [(hp, qt, hh) for hp in range(2) for qt in range(4) for hh in range(2)]

            def stage1(n):
                hp, qt, hh = its[n]
                qs = slice(qt * 512, (qt + 1) * 512)
                prow = slice(hh * 64, (hh + 1) * 64)
                for mt in range(2):
                    b = nxt([0, 1, 2, 3])
                    mm_group(bank(b), [(memKT[prow, hp, mt * 128:(mt + 1) * 128], MQ[prow, hp, qs])],
                             reads=[r_memKV, r_MQ], writes=[PB[b]])
                    k = (n % 2) * 4 + mt
                    S.op("act", [lambda e, k=k, b=b: e.activation(out=PT[k], in_=bank(b), func=AF.Exp, scale=0.125)],
                         reads=[PB[b]], writes=[r_PT[k]])

            def stage2(n):
                hp, qt, hh = its[n]
                h = 2 * hp + hh
                qs = slice(qt * 512, (qt + 1) * 512)
                prow = slice(hh * 64, (hh + 1) * 64)
                pts = [(n % 2) * 4 + mt for mt in range(2)]
                bo, bd = [4, 6][n % 2], [5, 7][n % 2]
                mm_group(bank(bo, 512, 0, 64), [(memV[:, mt, h * 64:(h + 1) * 64], PT[pts[mt]]) for mt in range(2)],
                         reads=[r_memKV, r_PT[pts[0]], r_PT[pts[1]]], writes=[PB[bo]])
                mm_group(bank(bd, 512, 0, 64), [(ones64[:], PT[pts[mt]]) for mt in range(2)],
                         reads=[r_consts, r_PT[pts[0]], r_PT[pts[1]]], writes=[PB[bd]])
                rd = rdens[n % 2]
                r_rd = r_rdens[n % 2]
                S.op("dve", [lambda e: e.reciprocal(out=rd[0:64, :], in_=bank(bd, 512, 0, 64))],
                     reads=[PB[bd]], writes=[r_rd])
                S.op("dve", [lambda e: e.tensor_tensor(
                    out=YM[prow, hp, qs], in0=bank(bo, 512, 0, 64), in1=rd[0:64, :], op=ALU.mult)],
                     reads=[PB[bo], r_rd], writes=[r_YM])

            for n in range(len(its) + 1):
                if n < len(its):
                    stage1(n)
                if n >= 1:
                    stage2(n - 1)

        def out_proj_ln1(layer):
            wout = AV(R_T, [8, 1024], BF16)
            r_wout = Res("wout")
            xs = [AV(R_T + 16 * K_ + i * 4 * K_, [1024], F32) for i in range(2)]
            r_xs = [Res("xs%d" % i) for i in range(2)]
            S.dma("pool", wout, w_out_d[layer].rearrange("(kc p) n -> p kc n", p=128),
                  writes=[r_wout] + ([r_KS] if layer == 0 else []), key="wout")
            load_ln(layer, 0)
            lnp = LNPipe(False)
            for tile in range(16):
                ts_ = slice(tile * 128, (tile + 1) * 128)
                bp = [0, 2, 4][tile % 3]
                i = tile % 2
                ri = tile % NRB
                if layer == 0:
                    S.dma("sp", xs[i], x_d[ts_, :], writes=[r_xs[i]] + ([r_KS] if tile < 2 else []), key="xs%d" % i)
                    xin, rxin = xs[i], r_xs[i]
                else:
                    xin, rxin = X[:, tile, :], r_X[tile]
                for half in range(2):
                    b = bp + half
                    hsl = slice(half * 512, (half + 1) * 512)
                    fns = []
                    for kc in range(8):
                        lhs = YT[:, kc, ts_] if kc < 6 else YM[:, kc - 6, ts_]
                        fns.append(lambda e, lhs=lhs, kc=kc, b=b, hsl=hsl: e.matmul(bank(b), lhs, wout[:, kc, hsl], start=(kc == 0), stop=False))
                    fns.append(lambda e, b=b, hsl=hsl, xin=xin: e.matmul(bank(b), aident[:], xin[:, hsl], start=False, stop=True))
                    S.op("pe", fns, reads=[r_YT, r_YM, r_wout, rxin, r_consts], writes=[PB[b]])
                lnp.push(tile, ps[:, bp * 512:bp * 512 + 1024], [PB[bp], PB[bp + 1]], rbuf[ri], r_rbuf[ri])
            lnp.flush()

        def ffn_ln2(layer, final):
            blocks = [4, 4, 4, 4, 3, 3]
            S.dma("sp", fcwb[:], fcwb_d[layer][:, :, :], writes=[r_fcwb], key="fcwb")
            load_ln(layer, 1)
            GT = AV(R_Y, [4, 2048], BF16)
            r_GT = Res("GT")
            wdn = [AV(R_T + i * 8 * K_, [4, 1024], BF16) for i in range(2)]
            r_wdn = [Res("wdn%d" % i) for i in range(2)]
            wup = [AV(R_T + 16 * K_ + i * 4 * K_, [8, 2, 128], BF16) for i in range(3)]
            r_wup = [Res("wup%d" % i) for i in range(3)]
            hsb = [AV(R_T + 28 * K_ + i * 8208, [2052], F32) for i in range(2)]
            r_hsb = [Res("hsb%d" % i) for i in range(2)]
            for i in range(2):
                S.op("pool", [lambda e, i=i: e.memset(hsb[i][:, 0:1], 0.0)], writes=[r_hsb[i]])
                S.op("pool", [lambda e, i=i: e.memset(hsb[i][:, 2049:2050], 0.0)], writes=[r_hsb[i]])
            acc_as = [AV(R_YM, [2048], F32), AV(R_Y + 16 * K_, [2048], F32)]
            acc_g = AV(R_MQ, [2048], F32)
            r_accas, r_accg = [Res("acca0"), Res("acca1")], Res("accg")
            wd_v = w_dn_d[layer]
            wu_v = w_up_d[layer].rearrange("(kc p) n -> p kc n", p=128)
            def down_proj(bi, nb, wd):
                last = bi == len(blocks) - 1
                if last:
                    S.barrier()
                lnp = LNPipe(final)
                for tile in range(16):
                    ts_ = slice(tile * 128, (tile + 1) * 128)
                    bp = [0, 2, 4][tile % 3] if last else [4, 6][tile % 2]
                    pin = ps[:, bp * 512:bp * 512 + 1024]
                    if not last:
                        for half in range(2):
                            mm_group(bank(bp + half), [(GT[:, jj, ts_], wd[:, jj, half * 512:(half + 1) * 512]) for jj in range(nb)],
                                     reads=[r_GT, r_wdn[bi % 2]], writes=[PB[bp + half]])
                        if bi == 0:
                            S.op("dve", [lambda e, tile=tile, pin=pin: e.scalar_tensor_tensor(
                                out=X[:, tile, :], in0=X[:, tile, :], scalar=ALPHA, in1=pin, op0=ALU.mult, op1=ALU.add)],
                                 reads=[PB[bp], PB[bp + 1]], writes=[r_X[tile]])
                        else:
                            S.op("dve", [lambda e, tile=tile, pin=pin: e.tensor_tensor(
                                out=X[:, tile, :], in0=X[:, tile, :], in1=pin, op=ALU.add)],
                                 reads=[PB[bp], PB[bp + 1]], writes=[r_X[tile]])
                    else:
                        for half in range(2):
                            b = bp + half
                            hsl = slice(half * 512, (half + 1) * 512)
                            fns = []
                            for jj in range(nb):
                                fns.append(lambda e, jj=jj, b=b, hsl=hsl, ts_=ts_: e.matmul(bank(b), GT[:, jj, ts_], wd[:, jj, hsl], start=(jj == 0), stop=False))
                            fns.append(lambda e, b=b, hsl=hsl, tile=tile: e.matmul(bank(b), identf[:], X[:, tile, hsl], start=False, stop=True))
                            S.op("pe", fns, reads=[r_GT, r_wdn[bi % 2], r_X[tile], r_consts], writes=[PB[b]])
                        ri = tile % NRB
                        lnp.push(tile, pin, [PB[bp], PB[bp + 1]], rbuf[ri], r_rbuf[ri])
                lnp.flush()

            j0 = 0
            deferred = None
            for bi, nb in enumerate(blocks):
                wd = wdn[bi % 2]
                S.dma("pool", wd[:, 0:nb, :], wd_v[j0 * 128:(j0 + nb) * 128, :].rearrange("(j p) n -> p j n", p=128),
                      writes=[r_wdn[bi % 2]], key="wdn%d" % (bi % 2))
                for jj in range(nb):
                    j = j0 + jj
                    wi = j % 3
                    wu = wup[wi]
                    acc_a, r_acca = acc_as[j % 2], r_accas[j % 2]
                    S.dma("pool", wu[:, :, 0, :], wu_v[:, :, j * 128:(j + 1) * 128], writes=[r_wup[wi]], key="wup%da" % wi)
                    S.dma("pool", wu[:, :, 1, :], wu_v[:, :, DFF + j * 128:DFF + (j + 1) * 128], writes=[r_wup[wi]], key="wup%db" % wi)
                    for ag in range(2):
                        hs_ = hsb[ag]
                        cj = ag * NFF + j
                        for tq in range(4):
                            b = nxt([0, 1, 2, 3])
                            mm_group(bank(b), [(wu[:, kc, ag, :], XT[:, kc, tq * 512:(tq + 1) * 512]) for kc in range(8)],
                                     reads=[r_wup[wi], r_XT], writes=[PB[b]])
                            S.op("act", [lambda e, hs_=hs_, tq=tq, b=b: e.activation(
                                out=hs_[:, 1 + tq * 512:1 + (tq + 1) * 512], in_=bank(b), func=AF.Identity)],
                                 reads=[PB[b]], writes=[r_hsb[ag]])
                        acc, r_acc = (acc_a, r_acca) if ag == 0 else (acc_g, r_accg)
                        S.op("act", [lambda e, acc=acc, hs_=hs_, cj=cj: e.activation(
                            out=acc, in_=hs_[:, 1:2049], func=AF.Identity, scale=fcwb[:, cj, 1:2], bias=fcwb[:, cj, 3:4])],
                             reads=[r_hsb[ag], r_fcwb], writes=[r_acc])
                        S.op("dve", [lambda e, acc=acc, hs_=hs_, cj=cj: e.scalar_tensor_tensor(
                            out=acc, in0=hs_[:, 0:2048], scalar=fcwb[:, cj, 0:1], in1=acc, op0=ALU.mult, op1=ALU.add)],
                             reads=[r_hsb[ag], r_fcwb, r_acc], writes=[r_acc])
                        S.op("dve", [lambda e, acc=acc, hs_=hs_, cj=cj: e.scalar_tensor_tensor(
                            out=acc, in0=hs_[:, 2:2050], scalar=fcwb[:, cj, 2:3], in1=acc, op0=ALU.mult, op1=ALU.add)],
                             reads=[r_hsb[ag], r_fcwb, r_acc], writes=[r_acc])
                        if ag == 0 and deferred is not None:
                            deferred()
                            deferred = None

                    def _fin(jj=jj, acc_a=acc_a, r_acca=r_acca, endblk=(jj == nb - 1), bi=bi, nb=nb, wd=wd):
                        S.op("act", [lambda e: e.activation(out=acc_g, in_=acc_g, func=AF.Silu)], reads=[r_accg], writes=[r_accg])
                        S.op("dve", [lambda e: e.tensor_tensor(out=GT[:, jj, :], in0=acc_g, in1=acc_a, op=ALU.mult)],
                             reads=[r_accg, r_acca], writes=[r_GT])
                        if endblk:
                            down_proj(bi, nb, wd)
                    deferred = _fin
                j0 += nb
            if deferred is not None:
                deferred()
                deferred = None

        memf = AV(R_T + 8 * K_, [2, 1024], F32)
        memb = AV(R_T + 28 * K_, [2, 1024], BF16)
        memT = AV(R_T + 32 * K_, [8, 256], BF16)
        wkv = AV(R_T, [8, 512], BF16)
        r_memf, r_memb, r_memT, r_wkv = Res("memf"), Res("memb"), Res("memT"), Res("wkv")
        S.dma("sp", memf, mem_d.rearrange("(mt p) d -> p mt d", p=128), writes=[r_memf], key="memf")
        S.dma("pool", wkv, wkv_d.rearrange("(kc p) n -> p kc n", p=128), writes=[r_wkv], key="wkv")
        act_copy(memb, memf, [r_memf], [r_memb])
        for mt in range(2):
            b = nxt([6, 7])
            bb = bank_bf(b)
            fns = [lambda e, kc=kc, mt=mt, bb=bb: e.transpose(bb[:, kc * 128:(kc + 1) * 128], memb[:, mt, kc * 128:(kc + 1) * 128], ident[:])
                   for kc in range(8)]
            S.op("pe", fns, reads=[r_memb, r_consts], writes=[PB[b]])
            S.op("act", [lambda e, mt=mt, bb=bb: e.activation(out=memT[:, :, mt * 128:(mt + 1) * 128],
                                                             in_=bb.rearrange("p (k m) -> p k m", k=8), func=AF.Identity)],
                 reads=[PB[b]], writes=[r_memT])
        for hp in range(2):
            b = nxt([0, 1])
            mm_group(bank(b, 256), [(wkv[:, kc, hp * 128:(hp + 1) * 128], memT[:, kc, :]) for kc in range(8)],
                     reads=[r_wkv, r_memT], writes=[PB[b]])
            act_copy(memKT[:, hp, :], bank(b, 256), [PB[b]], [r_memKV])
        for mt in range(2):
            b = nxt([0, 1])
            mm_group(bank(b, 256), [(memT[:, kc, mt * 128:(mt + 1) * 128], wkv[:, kc, 256:512]) for kc in range(8)],
                     reads=[r_wkv, r_memT], writes=[PB[b]])
            act_copy(memV[:, mt, :], bank(b, 256), [PB[b]], [r_memKV])

        if debug == "s_m":
            S.barrier()
            S.emit()
            return nc
        xs0 = [AV(R_T + 16 * K_ + i * 4 * K_, [1024], F32) for i in range(2)]
        r_xs0 = [Res("xs0_%d" % i) for i in range(2)]

        def x_step(tile):
            i = tile % 2
            i4 = tile % 4
            S.dma("sp", xs0[i], x_d[tile * 128:(tile + 1) * 128, :], writes=[r_xs0[i]], key="xs%d" % i)
            S.op("act", [lambda e: e.activation(out=xbuf[i4], in_=xs0[i], func=AF.Identity)],
                 reads=[r_xs0[i]], writes=[r_xbuf[i4]])
            transpose_to_XT(xbuf[i4], tile, r_xbuf[i4])
        x_steps = [(lambda t=t: x_step(t)) for t in range(16)]
        if debug == "s_x":
            S.barrier()
            S.emit()
            return nc
        fbs = stat[0:64, 120:123]
        for l in range(3):
            S.op("dve", [lambda e, l=l: e.tensor_tensor(out=fbs[:, l:l + 1], in0=fbf[0:64, 2 * l:2 * l + 1],
                                                        in1=fbf[0:64, 2 * l + 1:2 * l + 2], op=ALU.mult)],
                 reads=[r_f], writes=[r_stat])
        srcs = [(zf, 33, r_f), (hA, 64, r_hA), (hB, 64, r_hB)]
        dsts = [(hA, r_hA), (hB, r_hB), (hA, r_hA)]
        wtW = AV(R_T + 36 * K_, [2048], F32)
        wt2W = AV(R_T + 44 * K_, [2048], F32)
        r_wtW, r_wt2W = Res("wtW"), Res("wt2W")
        dec2 = [dec, AV(R_MQ, [768], F32)]
        r_dec2 = [Res("dec0"), Res("dec1")]
        f_steps = []

        def mlp_layer(l):
            src, kk, r_src = srcs[l]
            dst, r_dst = dsts[l]
            for tq in range(4):
                cs = slice(tq * 512, (tq + 1) * 512)
                mm_group(bank(tq, 512, 0, 64), [(fw[0:kk, l, :], src[0:kk, cs])], reads=[r_f, r_src], writes=[PB[tq]])
            pall = ps[0:64, 0:2048]
            S.op("dve", [lambda e: e.tensor_scalar(out=wtW[0:64, :], in0=pall, scalar1=fbf[0:64, 2 * l + 1:2 * l + 2],
                                                   scalar2=fbs[:, l:l + 1], op0=ALU.mult, op1=ALU.add)],
                 reads=[PB[0], PB[1], PB[2], PB[3], r_f, r_stat], writes=[r_wtW])
            S.op("dve", [lambda e: e.tensor_scalar(out=wt2W[0:64, :], in0=wtW[0:64, :], scalar1=-PI, scalar2=2 * PI,
                                                   op0=ALU.is_lt, op1=ALU.mult)], reads=[r_wtW], writes=[r_wt2W])
            S.op("dve", [lambda e: e.tensor_tensor(out=wtW[0:64, :], in0=wtW[0:64, :], in1=wt2W[0:64, :], op=ALU.add)],
                 reads=[r_wtW, r_wt2W], writes=[r_wtW])
            S.op("dve", [lambda e: e.tensor_scalar(out=wt2W[0:64, :], in0=wtW[0:64, :], scalar1=PI, scalar2=-2 * PI,
                                                   op0=ALU.is_gt, op1=ALU.mult)], reads=[r_wtW], writes=[r_wt2W])
            S.op("dve", [lambda e: e.tensor_tensor(out=wtW[0:64, :], in0=wtW[0:64, :], in1=wt2W[0:64, :], op=ALU.add)],
                 reads=[r_wtW, r_wt2W], writes=[r_wtW])
            S.op("act", [lambda e: e.activation(out=dst[0:64, :], in_=wtW[0:64, :], func=AF.Sin)],
                 reads=[r_wtW], writes=[r_dst])

        def wcomb():
            S.op("dve", [lambda e: e.tensor_tensor(out=fsb[0:64, :], in0=fwo[0:64, 0:768], in1=fwo[0:64, 768:1536], op=ALU.add)],
                 reads=[r_f], writes=[r_fsb])
            S.op("dve", [lambda e: e.tensor_tensor(out=fwo[0:64, 768:1536], in0=fwo[0:64, 0:768], in1=fwo[0:64, 768:1536], op=ALU.subtract)],
                 reads=[r_f], writes=[r_f])
            S.op("dve", [lambda e: e.tensor_copy(out=fwo[0:64, 0:768], in_=fsb[0:64, :])], reads=[r_fsb], writes=[r_f])

        def hfull(tile):
            ts_ = slice(tile * 128, (tile + 1) * 128)
            b0 = 4 if tile % 2 else 0
            dc, r_dc = dec2[tile % 2], r_dec2[tile % 2]
            for q3 in range(3):
                mm_group(bank(b0 + q3), [(hA[0:64, ts_], fwo[0:64, q3 * 512:(q3 + 1) * 512])], reads=[r_hA, r_f], writes=[PB[b0 + q3]])
            S.op("act", [lambda e: e.activation(out=dc, in_=dlt, func=AF.Exp, scale=ntn[:, tile:tile + 1])],
                 reads=[r_f], writes=[r_dc])
            S.op("dve", [lambda e: e.tensor_tensor(out=HS[:, 0, tile, :], in0=ps[:, b0 * 512:b0 * 512 + 768], in1=dc, op=ALU.mult)],
                 reads=[PB[b0], PB[b0 + 1], r_dc], writes=[r_HS])
            S.op("dve", [lambda e: e.tensor_tensor(out=HS[:, 1, tile, :], in0=ps[:, b0 * 512 + 768:b0 * 512 + 1536], in1=dc, op=ALU.mult)],
                 reads=[PB[b0 + 1], PB[b0 + 2], r_dc], writes=[r_HS])

        f_steps.append(wcomb)
        for l in range(3):
            f_steps.append(lambda l=l: mlp_layer(l))
        for tile in range(16):
            f_steps.append(lambda tile=tile: hfull(tile))
        xi = 0
        for k_, fs in enumerate(f_steps):
            fs()
            if k_ >= 1 and xi < 16:
                x_steps[xi]()
                xi += 1
        while xi < 16:
            x_steps[xi]()
            xi += 1
        S.barrier()

        if debug == "s_f":
            S.barrier()
            S.emit()
            return nc
        KS = AV(R_T, [2, 16, 768], BF16)
        r_KS = Res("KS")

        def fwd_pass(rhs_re, rhs_im, r_rhs, ftab, r_ftab, epilogue, extra_w=((), ())):
            for fc in range(16):
                si = fc % 2
                ft = ftab[si]
                S.dma("sp", ft.rearrange("p a b c -> p (a b c)"), fwd_tab_d[fc],
                      writes=[r_ftab[si]] + (list(extra_w[si]) if fc < 2 else []), key="ftab%d" % si)
                bs = [0, 1, 2, 3] if fc % 2 == 0 else [4, 5, 6, 7]
                for ri in range(2):
                    rhs = rhs_re if ri == 0 else rhs_im
                    o0 = bs[0] * 512 + ri * 1024
                    fns = []
                    for tc in range(16):
                        lhs = ft[:, ri, tc, :]
                        fns.append(lambda e, lhs=lhs, tc=tc, rhs=rhs, o0=o0: e.matmul(
                            ps[:, o0:o0 + 512], lhs, rhs[:, tc, 0:512], start=(tc == 0), stop=(tc == 15)))
                        fns.append(lambda e, lhs=lhs, tc=tc, rhs=rhs, o0=o0: e.matmul(
                            ps[:, o0 + 512:o0 + 768], lhs, rhs[:, tc, 512:768], start=(tc == 0), stop=(tc == 15)))
                    S.op("pe", fns, reads=[r_ftab[si], r_rhs], writes=[PB[bs[2 * ri]], PB[bs[2 * ri + 1]]])
                pre = ps[:, bs[0] * 512:bs[0] * 512 + 768]
                pim = ps[:, bs[2] * 512:bs[2] * 512 + 768]
                epilogue(fc, pre, pim, [PB[b] for b in bs])

        ftab = [AV(R_YM, [2, 16, 128], BF16), AV(R_MQ, [2, 16, 128], BF16)]
        r_ftab = [Res("ftab0"), Res("ftab1")]

        def k_epilogue(fc, pre, pim, rbs):
            S.op("dve", [lambda e: e.tensor_tensor(out=KS[:, 0, fc, :], in0=pre, in1=dbc, op=ALU.add)],
                 reads=rbs[0:2] + [r_dbc], writes=[r_KS])
            act_copy(KS[:, 1, fc, :], pim, rbs[2:4], [r_KS])

        fwd_pass(HS[:, 0], HS[:, 1], r_HS, ftab, r_ftab, k_epilogue)

        if debug == "s_k":
            S.barrier()
            S.emit()
            return nc
        Z = AV(R_X, [16, 768], BF16)
        r_Z = Res("Z")
        hsb0 = [AV(R_X + 24 * K_ + i * 8208, [2052], F32) for i in range(2)]
        r_hsb0 = [Res("hsb0_%d" % i) for i in range(2)]
        accx = AV(R_X + 24 * K_ + 16416, [2048], F32)
        accv = AV(R_X + 24 * K_ + 16416 + 8192, [2048], F32)
        zT = AV(R_X + 24 * K_ + 16416 + 16384, [2048], BF16)
        r_accx, r_accv, r_zT = Res("accx"), Res("accv"), Res("zT")
        wch = [AV(R_T + 48 * K_ + i * 2 * K_, [8, 128], BF16) for i in range(3)]
        r_wch = [Res("wch%d" % i) for i in range(3)]
        for i in range(2):
            S.op("pool", [lambda e, i=i: e.memset(hsb0[i][:, 0:1], 0.0)], writes=[r_hsb0[i], r_HS])
            S.op("pool", [lambda e, i=i: e.memset(hsb0[i][:, 2049:2050], 0.0)], writes=[r_hsb0[i], r_HS])
        w0v = w_in_d[0].rearrange("(kc p) n -> p kc n", p=128)
        order = [18, 19] + [0, 1, 2, 3, 4, 5]
        for i in range(6):
            order += [6 + i, 12 + i]
        _wc = [0]

        def proj_chunk(wv, col0, sink_fn, extra_reads=()):
            wi = _wc[0] % 3
            _wc[0] += 1
            S.dma("pool", wch[wi], wv[:, :, col0:col0 + 128], writes=[r_wch[wi]], key="wch%d" % wi)
            for tq in range(4):
                b = nxt([0, 1, 2, 3])
                mm_group(bank(b), [(wch[wi][:, kc, :], XT[:, kc, tq * 512:(tq + 1) * 512]) for kc in range(8)],
                         reads=[r_wch[wi], r_XT], writes=[PB[b]])
                sink_fn(tq, b)

        def conv_chunk(c, hs_, r_hs, acc_out, r_acc_list, out_final):
            S.op("act", [lambda e: e.activation(out=acc_out, in_=hs_[:, 1:2049], func=AF.Identity,
                                                scale=cwb[:, c, 1:2], bias=cwb[:, c, 3:4])],
                 reads=[r_hs, r_cwb], writes=r_acc_list)
            S.op("dve", [lambda e: e.scalar_tensor_tensor(out=acc_out, in0=hs_[:, 0:2048], scalar=cwb[:, c, 0:1],
                                                          in1=acc_out, op0=ALU.mult, op1=ALU.add)],
                 reads=[r_hs, r_cwb] + r_acc_list, writes=r_acc_list)
            S.op("dve", [lambda e: e.scalar_tensor_tensor(out=out_final[0], in0=hs_[:, 2:2050], scalar=cwb[:, c, 2:3],
                                                          in1=acc_out, op0=ALU.mult, op1=ALU.add)],
                 reads=[r_hs, r_cwb] + r_acc_list, writes=out_final[1])

        zdef = []
        for ci_, c in enumerate(order):
            if debug and debug.startswith('s_p') and ci_ == int(debug[3:]):
                S.barrier()
                S.emit()
                return nc
            if c >= 18:
                hp = c - 18

                def sink_mq(tq, b, hp=hp):
                    act_copy(MQ[:, hp, tq * 512:(tq + 1) * 512], bank(b), [PB[b]], [r_MQ])
                proj_chunk(w0v, c * 128, sink_mq)
                continue
            si = (c % 2) if c < 6 else (0 if c < 12 else 1)
            hs_ = hsb0[si]

            def sink_h(tq, b, hs_=hs_, si=si):
                act_copy(hs_[:, 1 + tq * 512:1 + (tq + 1) * 512], bank(b), [PB[b]], [r_hsb0[si]])
            proj_chunk(w0v, c * 128, sink_h)
            if c < 6:
                conv_chunk(c, hs_, r_hsb0[si], accv, [r_accv], (YT[:, c, :], [r_YT]))
            elif c < 12:
                conv_chunk(c, hs_, r_hsb0[si], accx, [r_accx], (accx, [r_accx]))
                while zdef:
                    zdef.pop(0)()
            else:
                i6 = c - 12
                conv_chunk(c, hs_, r_hsb0[si], accv, [r_accv], (accv, [r_accv]))
                S.op("dve", [lambda e: e.tensor_tensor(out=zT, in0=accv, in1=accx, op=ALU.mult)],
                     reads=[r_accv, r_accx], writes=[r_zT])
                def _ztr(i6=i6):
                    for g8 in range(2):
                        b = nxt([6, 7])
                        bb = bank_bf(b)
                        fns = [lambda e, t8=t8, bb=bb, g8=g8: e.transpose(bb[:, t8 * 128:(t8 + 1) * 128],
                                                                         zT[:, (g8 * 8 + t8) * 128:(g8 * 8 + t8 + 1) * 128], ident[:])
                               for t8 in range(8)]
                        S.op("pe", fns, reads=[r_zT, r_consts], writes=[PB[b]])
                        S.op("act", [lambda e, g8=g8, bb=bb, i6=i6: e.activation(
                            out=Z[:, g8 * 8:(g8 + 1) * 8, i6 * 128:(i6 + 1) * 128],
                            in_=bb.rearrange("p (k m) -> p k m", k=8), func=AF.Identity)],
                             reads=[PB[b]], writes=[r_Z])
                zdef.append(_ztr)
        while zdef:
            zdef.pop(0)()
        if debug == "z":
            for tile in range(16):
                S.op("act", [lambda e, tile=tile: e.activation(out=X[:, tile, 0:768] if False else hsb0[0][:, 0:768], in_=Z[:, tile, :], func=AF.Identity)],
                     reads=[r_Z], writes=[r_hsb0[0]])
                S.dma("sp", dbg_d[tile * 128:(tile + 1) * 128, 0:768], hsb0[0][:, 0:768], reads=[r_hsb0[0]], key="dbg")
            S.barrier()
            S.emit()
            return nc
        mem_attention(R_X + 24 * K_)

        YRE = AV(R_XT, [16, 768], BF16)
        YIM = AV(R_X + 24 * K_, [16, 768], BF16)
        r_Y = Res("Yspec")
        ct = [AV(R_X + 48 * K_ + i * 3 * K_, [768], F32) for i in range(4)]
        r_ct = [Res("ct%d" % i) for i in range(4)]
        ftabU = [AV(R_T + 48 * K_, [2, 16, 128], BF16), AV(R_MQ, [2, 16, 128], BF16)]
        r_ftabU = [Res("ftabU0"), Res("ftabU1")]

        def u_epilogue(fc, pre, pim, rbs):
            kre, kim = KS[:, 0, fc, :], KS[:, 1, fc, :]
            S.op("dve", [lambda e: e.tensor_tensor(out=ct[0], in0=pre, in1=kre, op=ALU.mult)], reads=rbs[0:2] + [r_KS], writes=[r_ct[0]])
            S.op("dve", [lambda e: e.tensor_tensor(out=ct[1], in0=pim, in1=kim, op=ALU.mult)], reads=rbs[2:4] + [r_KS], writes=[r_ct[1]])
            S.op("dve", [lambda e: e.tensor_tensor(out=ct[2], in0=pre, in1=kim, op=ALU.mult)], reads=rbs[0:2] + [r_KS], writes=[r_ct[2]])
            S.op("dve", [lambda e: e.tensor_tensor(out=ct[3], in0=pim, in1=kre, op=ALU.mult)], reads=rbs[2:4] + [r_KS], writes=[r_ct[3]])
            S.op("pool", [lambda e: e.tensor_tensor(out=YRE[:, fc, :], in0=ct[0], in1=ct[1], op=ALU.subtract)],
                 reads=[r_ct[0], r_ct[1]], writes=[r_Y])
            S.op("pool", [lambda e: e.tensor_tensor(out=YIM[:, fc, :], in0=ct[2], in1=ct[3], op=ALU.add)],
                 reads=[r_ct[2], r_ct[3]], writes=[r_Y])

        fwd_pass(Z, Z, r_Z, ftabU, r_ftabU, u_epilogue, extra_w=(r_wch, [r_MQ]))

        itab = [AV(R_T + 48 * K_, [4, 2, 512], BF16), AV(R_MQ, [4, 2, 512], BF16)]
        r_itab = r_ftabU
        _it = 0
        for tt in range(4):
            fn_all = []
            for fg in range(4):
                si = _it % 2
                _it += 1
                S.dma("sp", itab[si].rearrange("p a b c -> p (a b c)"), inv_tab_d[tt, fg], writes=[r_itab[si]], key="itab%d" % si)
                fns = []
                for fi in range(4):
                    fc = fg * 4 + fi
                    for ri in range(2):
                        Ysrc = YRE if ri == 0 else YIM
                        for cc in range(6):
                            first = (fc == 0 and ri == 0)
                            lastm = (fc == 15 and ri == 1)
                            fns.append(lambda e, cc=cc, Ysrc=Ysrc, fc=fc, si=si, fi=fi, ri=ri, first=first, lastm=lastm: e.matmul(
                                bank(cc), Ysrc[:, fc, cc * 128:(cc + 1) * 128], itab[si][:, fi, ri, :], start=first, stop=lastm))
                S.op("pe", fns, reads=[r_itab[si], r_Y], writes=[PB[cc] for cc in range(6)])
            for cc in range(6):
                S.op("dve", [lambda e, cc=cc, tt=tt: e.tensor_tensor(out=YT[:, cc, tt * 512:(tt + 1) * 512], in0=bank(cc),
                                                                    in1=YT[:, cc, tt * 512:(tt + 1) * 512], op=ALU.mult)],
                     reads=[PB[cc], r_YT], writes=[r_YT])

        if debug == "mix0":
            for c in range(8):
                src = YT[:, c, 0:1024] if c < 6 else YM[:, c - 6, 0:1024]
                S.op("act", [lambda e, src=src: e.activation(out=rbuf[0], in_=src, func=AF.Identity)], reads=[r_YT, r_YM], writes=[r_rbuf[0]])
                S.dma("sp", dbg_d[c * 128:(c + 1) * 128, :], rbuf[0], reads=[r_rbuf[0]], key="dbg")
            S.barrier()
            S.emit()
            return nc

        out_proj_ln1(0)
        S.barrier()
        if debug == "ln1_0":
            for tile in range(16):
                S.dma("sp", dbg_d[tile * 128:(tile + 1) * 128, :], X[:, tile, :], reads=[r_X[tile]], key="dbg")
            S.barrier()
            S.emit()
            return nc
        ffn_ln2(0, final=(debug == "l0"))
        S.barrier()

        if debug is None or debug.startswith("l1"):
            if debug == "l1_s":
                S.barrier()
                S.emit()
                return nc
            w1v = w_in_d[1].rearrange("(kc p) n -> p kc n", p=128)
            QT = YT
            r_QT = [Res("QT%d" % i) for i in range(12)]
            KT = AV(R_T, [4, 2048], BF16)
            r_KT = Res("KT")
            VT = AV(R_T + 16 * K_, [16, 256], BF16)
            r_VT = Res("VT")
            ropec = AV(R_T + 24 * K_, [2048], F32)
            ropes = AV(R_T + 32 * K_, [2048], F32)
            r_rope = Res("rope")
            wch1 = [AV(R_T + 40 * K_ + i * 2 * K_, [8, 128], BF16) for i in range(3)]
            r_wch1 = [Res("wch1_%d" % i) for i in range(3)]
            qsb = [AV(R_T + 46 * K_ + i * K_, [512], BF16) for i in range(2)]
            r_qsb = [Res("qsb%d" % i) for i in range(2)]
            rt1 = AV(R_T + 48 * K_, [512], F32)
            rt2 = AV(R_T + 50 * K_, [512], F32)
            r_rt1, r_rt2 = Res("rt1"), Res("rt2")
            wvt = AV(R_T + 52 * K_, [8, 256], BF16)
            r_wvt = Res("wvt")
            esk = small("esk", [64, 12], F32)
            r_esk = Res("esk")
            import os
            SK = os.environ.get("SKIP", "")
            if "r" not in SK:
                S.dma("sp", ropec, ropec_d[:, :], writes=[r_rope], key="rope0")
                S.dma("sp", ropes, ropes_d[:, :], writes=[r_rope], key="rope1")
            if "e" not in SK:
                S.dma("sp", esk[:], sink_d.partition_broadcast(64), writes=[r_esk], key="esk")
            if "x" not in SK:
                S.op("act", [lambda e: e.activation(out=esk[:], in_=esk[:], func=(AF.Identity if "I" in SK else AF.Exp))], reads=[r_esk], writes=[r_esk])
            if "w" not in SK:
                S.dma("pool", wvt, w1v[:, :, 1024:1280], writes=[r_wvt], key="wvt")
            if debug == "l1_p0":
                S.barrier()
                S.emit()
                return nc
            _w1 = [0]
            _rq = [0]

            def proj1(loads, sink_fn):
                wi = _w1[0] % 3
                _w1[0] += 1
                for k_, (dst0, dst1, c0, c1) in enumerate(loads):
                    S.dma("pool", wch1[wi][:, :, dst0:dst1], w1v[:, :, c0:c1], writes=[r_wch1[wi]], key="wch1_%d_%d" % (wi, k_))
                for tq in range(4):
                    b = nxt([0, 1, 2, 3])
                    mm_group(bank(b), [(wch1[wi][:, kc, :], XT[:, kc, tq * 512:(tq + 1) * 512]) for kc in range(8)],
                             reads=[r_wch1[wi], r_XT], writes=[PB[b]])
                    sink_fn(tq, b)

            def rope_sink(dest, r_dest):
                def sink(tq, b):
                    cs = slice(tq * 512, (tq + 1) * 512)
                    i = _rq[0] % 2
                    _rq[0] += 1
                    S.op("act", [lambda e: e.activation(out=qsb[i], in_=bank(b), func=AF.Identity)],
                         reads=[PB[b]], writes=[r_qsb[i]])
                    b2 = [4, 5][i]
                    mm_group(bank(b2), [(pm[:], qsb[i])], reads=[r_consts, r_qsb[i]], writes=[PB[b2]])
                    S.op("dve", [lambda e: e.tensor_tensor(out=rt1, in0=bank(b), in1=ropec[:, cs], op=ALU.mult)],
                         reads=[PB[b], r_rope], writes=[r_rt1])
                    S.op("dve", [lambda e: e.tensor_tensor(out=rt2, in0=bank(b2), in1=ropes[:, cs], op=ALU.mult)],
                         reads=[PB[b2], r_rope], writes=[r_rt2])
                    S.op("dve", [lambda e: e.tensor_tensor(out=dest[:, cs], in0=rt1, in1=rt2, op=ALU.add)],
                         reads=[r_rt1, r_rt2], writes=r_dest)
                return sink

            for c in range(6):
                proj1([(0, 128, c * 128, (c + 1) * 128)], rope_sink(QT[:, c, :], [r_QT[2 * c], r_QT[2 * c + 1]]))
            if debug == "l1_p1":
                S.barrier()
                S.emit()
                return nc
            for g in range(4):
                c0 = 768 + g * 64
                proj1([(0, 64, c0, c0 + 64), (64, 128, c0, c0 + 64)], rope_sink(KT[:, g, :], [r_KT]))
            if debug == "l1_p2":
                S.barrier()
                S.emit()
                return nc
            for hp in range(2):
                def sink_mq1(tq, b, hp=hp):
                    act_copy(MQ[:, hp, tq * 512:(tq + 1) * 512], bank(b), [PB[b]], [r_MQ])
                proj1([(0, 128, 1280 + hp * 128, 1280 + (hp + 1) * 128)], sink_mq1)
            for tile in range(16):
                b = nxt([0, 1, 2, 3])
                mm_group(bank(b, 256), [(XT[:, kc, tile * 128:(tile + 1) * 128], wvt[:, kc, :]) for kc in range(8)],
                         reads=[r_XT, r_wvt], writes=[PB[b]])
                act_copy(VT[:, tile, :], bank(b, 256), [PB[b]], [r_VT])
            S.barrier()

            if debug == "l1_p":
                S.barrier()
                S.emit()
                return nc
            PTs = [AV(R_XT + i * 12 * K_, [16, 384], BF16) for i in range(2)]
            r_PTs = [Res("PTs%d" % i) for i in range(2)]
            rden1s = [AV(R_XT + 24 * K_ + i * 2 * K_, [512], F32) for i in range(2)]
            r_rden1s = [Res("rden1_%d" % i) for i in range(2)]
            def st_exp(h, j):
                g = h // 3
                hh = h % 2
                c = h // 2
                prow = slice(hh * 64, (hh + 1) * 64)
                pt = PTs[h % 2]
                r_pt = r_PTs[h % 2]
                qlo = max(0, j - 1) * 128
                qhi = min(16, j + 2) * 128
                n = qhi - qlo
                moff = qlo - (j - 1) * 128
                b = nxt([0, 1, 2, 3])
                fns = [
                    lambda e: e.matmul(bank(b, n), KT[prow, g, j * 128:(j + 1) * 128], QT[prow, c, qlo:qhi], start=True, stop=False),
                    lambda e: e.matmul(bank(b, n), ident[:], maskb[:, moff:moff + n], start=False, stop=True),
                ]
                S.op("pe", fns, reads=[r_KT, r_QT[h], r_consts], writes=[PB[b]])
                S.op("act", [lambda e: e.activation(out=pt[:, j, 0:n], in_=bank(b, n), func=AF.Exp, scale=0.125)],
                     reads=[PB[b]], writes=[r_pt])

            def pv_norm(h, qt):
                g = h // 3
                hh = h % 2
                c = h // 2
                prow = slice(hh * 64, (hh + 1) * 64)
                pt = PTs[h % 2]
                r_pt = r_PTs[h % 2]
                bo, bd = [4, 6][qt % 2], [5, 7][qt % 2]
                fo, fd = [], []
                for i4 in range(4):
                    qb = 4 * qt + i4
                    js = [j for j in (qb - 1, qb, qb + 1) if 0 <= j < 16]
                    for k_, j in enumerate(js):
                        lc = (qb - max(0, j - 1)) * 128
                        st_, sp_ = (k_ == 0), (k_ == len(js) - 1)
                        fo.append(lambda e, i4=i4, j=j, lc=lc, st_=st_, sp_=sp_: e.matmul(
                            ps[0:64, bo * 512 + i4 * 128:bo * 512 + (i4 + 1) * 128], VT[:, j, g * 64:(g + 1) * 64],
                            pt[:, j, lc:lc + 128], start=st_, stop=sp_))
                        fd.append(lambda e, i4=i4, j=j, lc=lc, st_=st_, sp_=sp_: e.matmul(
                            ps[0:64, bd * 512 + i4 * 128:bd * 512 + (i4 + 1) * 128], ones64[:],
                            pt[:, j, lc:lc + 128], start=st_, stop=sp_))
                S.op("pe", fo, reads=[r_pt, r_VT], writes=[PB[bo]])
                S.op("pe", fd, reads=[r_pt, r_consts], writes=[PB[bd]])
                rd = rden1s[qt % 2]
                r_rd = r_rden1s[qt % 2]
                S.op("act", [lambda e: e.activation(out=rd[0:64, :], in_=bank(bd, 512, 0, 64), func=AF.Identity,
                                                    bias=esk[:, h:h + 1])],
                     reads=[PB[bd], r_esk], writes=[r_rd])
                S.op("dve", [lambda e: e.reciprocal(out=rd[0:64, :], in_=rd[0:64, :])], reads=[r_rd], writes=[r_rd])
                S.op("dve", [lambda e: e.tensor_tensor(
                    out=QT[prow, c, qt * 512:(qt + 1) * 512], in0=bank(bo, 512, 0, 64), in1=rd[0:64, :], op=ALU.mult)],
                     reads=[PB[bo], r_rd], writes=[r_QT[h]])

            for h in range(13):
                for qt in range(4):
                    if h < 12:
                        for j in range(4 * qt, 4 * qt + 4):
                            st_exp(h, j)
                    if h >= 1:
                        pv_norm(h - 1, qt)
            S.barrier()
            if debug == "l1_a":
                S.barrier()
                S.emit()
                return nc
            mem_attention(R_XT)
            S.barrier()
            if debug == "l1mix":
                for c in range(8):
                    src = YT[:, c, 0:1024] if c < 6 else YM[:, c - 6, 0:1024]
                    S.op("act", [lambda e, src=src: e.activation(out=rbuf[0], in_=src, func=AF.Identity)], reads=[r_YM], writes=[r_rbuf[0]])
                    S.dma("sp", dbg_d[c * 128:(c + 1) * 128, :], rbuf[0], reads=[r_rbuf[0]], key="dbg")
                S.barrier()
                S.emit()
                return nc
            out_proj_ln1(1)
            S.barrier()
            ffn_ln2(1, final=True)
            S.barrier()

        S.barrier()
        S.emit()
    return nc


def prep_shared(inputs):
    f32 = np.float32
    sh = {}
    for k in ("w_mem_kv", "l0_w_in", "l1_w_in", "l0_w_out", "l1_w_out", "l0_ffn_w_up", "l1_ffn_w_up",
              "l0_ffn_w_down", "l1_ffn_w_down", "l0_filt_w1", "l0_filt_w2", "l0_filt_w3", "l0_filt_w_out",
              "l0_hyena_d", "l1_sink"):
        sh[k] = np.ascontiguousarray(np.asarray(inputs[k], dtype=f32))
    for i in range(2):
        for n in ("ln1_g", "ln1_b", "ln2_g", "ln2_b"):
            sh["l%d_%s" % (i, n)] = np.ascontiguousarray(np.asarray(inputs["l%d_%s" % (i, n)], dtype=f32))
        cw = np.asarray(inputs["l%d_ffn_conv_w" % i], f32)
        cb = np.asarray(inputs["l%d_ffn_conv_b" % i], f32)
        a = np.concatenate([cw, cb[None, :]], axis=0)
        sh["l%d_fcwb" % i] = np.ascontiguousarray(a.reshape(4, 44, 128).transpose(2, 1, 0))
    cw = np.asarray(inputs["l0_conv_w"], f32)
    cb = np.asarray(inputs["l0_conv_b"], f32)
    a = np.concatenate([cw, cb[None, :]], axis=0)
    sh["l0_cwb"] = np.ascontiguousarray(a.reshape(4, 18, 128).transpose(2, 1, 0))
    sh["l0_fbf"] = np.ascontiguousarray(np.stack(
        [np.asarray(inputs["l0_filt_%s%d" % (n, l)], f32) for l in (1, 2, 3) for n in ("b", "f")], axis=1))
    sh.update(const_tables())
    return sh


_NC_CACHE = {}


def kernel(**inputs):
    sh = prep_shared(inputs)
    x = np.asarray(inputs["x"], np.float32)
    mem = np.asarray(inputs["mem"], np.float32)
    if "nc" not in _NC_CACHE:
        _NC_CACHE["nc"] = build_program()
    nc = _NC_CACHE["nc"]
    in_maps = []
    for b in range(8):
        m = dict(sh)
        m["x"] = np.ascontiguousarray(x[b])
        m["mem"] = np.ascontiguousarray(mem[b])
        in_maps.append(m)
    res = run_bass_kernel_spmd(nc, in_maps, core_ids=list(range(8)))
    return np.stack([np.asarray(r["out"], np.float32) for r in res.results], axis=0)
```

```python
import math
from contextlib import ExitStack

import numpy as np
import ml_dtypes
import concourse.bass as bass
import concourse.mybir as mybir
from concourse.bass_utils import run_bass_kernel_spmd

F32 = mybir.dt.float32
BF16 = mybir.dt.bfloat16
AF = mybir.ActivationFunctionType
ALU = mybir.AluOpType
AX = mybir.AxisListType

L = 2048
D = 1024
NT = 16
KC = 8
DFF = 2816
NFF = 22
ALPHA = 4.0 ** 0.25
EPS = 1e-5
PI = float(np.pi)


class Res:
    __slots__ = ("name", "w", "r", "excl")

    def __init__(self, name, excl=False):
        self.name = name
        self.w = None
        self.r = []
        self.excl = excl


class Sched:
    def __init__(self, nc, es):
        self.nc = nc
        self.es = es
        self.engs = {}
        for n in ("pe", "act", "dve", "pool", "sp"):
            sem = es.enter_context(nc.semaphore("s_" + n))
            self.engs[n] = dict(sem=sem, count=0, known={}, ops=[])
        self.dma_sems = {}
        self.n_dma_sems = 0

    def dma_sem(self, key):
        if key not in self.dma_sems:
            sem = self.es.enter_context(self.nc.semaphore("d%d" % self.n_dma_sems))
            self.n_dma_sems += 1
            self.dma_sems[key] = [sem, 0]
        return self.dma_sems[key]

    def op(self, eng, fns, reads=(), writes=(), dma_key=None):
        E = self.engs[eng]
        deps = {}

        def add(ev):
            if ev is None:
                return
            sem, val = ev
            if deps.get(sem, 0) < val:
                deps[sem] = val

        excl_reads = [r for r in reads if r.excl]
        writes = list(writes) + [r for r in excl_reads if r not in writes]
        reads = [r for r in reads if not r.excl]
        for r in reads:
            add(r.w)
        for w in writes:
            add(w.w)
            for ev in w.r:
                add(ev)
        waits = []
        for sem, val in deps.items():
            if sem is E["sem"] and eng == "pe" and dma_key is None:
                continue
            if E["known"].get(sem, 0) >= val:
                continue
            E["known"][sem] = val
            waits.append((sem, val))
        if dma_key is not None:
            ds = self.dma_sem(dma_key)
            ds[1] += 16
            ev = (ds[0], ds[1])
            inc = (ds[0], 16)
        else:
            E["count"] += 1
            ev = (E["sem"], E["count"])
            inc = (E["sem"], 1)
        for r in reads:
            r.r.append(ev)
        for w in writes:
            w.w = ev
            w.r = []
        if not isinstance(fns, (list, tuple)):
            fns = [fns]
        E["ops"].append((waits, list(fns), inc))
        return ev

    def dma(self, queue, out, in_, reads=(), writes=(), key=None):
        assert key is not None
        return self.op(queue, [lambda e: e.dma_start(out=out, in_=in_)], reads, writes, dma_key=key)

    def barrier(self):
        evs = [(E["sem"], E["count"]) for E in self.engs.values() if E["count"] > 0]
        evs += [(s, c) for (s, c) in self.dma_sems.values() if c > 0]
        for n, E in self.engs.items():
            waits = []
            for sem, val in evs:
                if E["known"].get(sem, 0) >= val:
                    continue
                if sem is E["sem"] and n == "pe":
                    continue
                E["known"][sem] = val
                waits.append((sem, val))
            if waits:
                E["ops"].append((waits, [], None))

    def emit(self):
        nc = self.nc
        import sys
        print("SCHED counts", {n: E["count"] for n, E in self.engs.items()}, "dma", {k: v[1] for k, v in self.dma_sems.items()}, file=sys.stderr)

        def run(name):
            def f(e):
                for waits, fns, inc in self.engs[name]["ops"]:
                    for sem, val in waits:
                        e.wait_ge(sem, val)
                    n = len(fns)
                    for i, fn in enumerate(fns):
                        ins = fn(e)
                        if i == n - 1:
                            ins.then_inc(inc[0], inc[1])
            return f

        with nc.Block() as block:
            block.tensor(run("pe"))
            block.scalar(run("act"))
            block.vector(run("dve"))
            block.gpsimd(run("pool"))
            block.sync(run("sp"))


def _bf(a):
    return np.ascontiguousarray(a.astype(ml_dtypes.bfloat16))


_CONST_CACHE = {}


def const_tables():
    if _CONST_CACHE:
        return _CONST_CACHE
    N = 4096
    f = np.arange(2048, dtype=np.float64)
    t = np.arange(2048, dtype=np.float64)
    m = np.mod(np.outer(2 * f + 1, t), 2 * N)
    ang = np.pi * m / N
    C = np.cos(ang)
    Sn = np.sin(ang)
    CT = C.T.reshape(16, 128, 16, 128)
    ST = (-Sn).T.reshape(16, 128, 16, 128)
    fwd = np.stack([CT, ST], axis=0)
    fwd = fwd.transpose(3, 2, 0, 1, 4)
    _CONST_CACHE["fwd_tab"] = _bf(fwd.reshape(16, 128, 2 * 16 * 128))
    Ci = (C / 2048.0).reshape(4, 4, 128, 4, 512)
    Si = (-Sn / 2048.0).reshape(4, 4, 128, 4, 512)
    inv = np.stack([Ci, Si], axis=0)
    inv = inv.transpose(4, 1, 3, 2, 0, 5)
    _CONST_CACHE["inv_tab"] = _bf(inv.reshape(4, 4, 128, 4 * 2 * 512))
    f32 = np.float32
    tl = np.linspace(0.0, 1.0, L, dtype=f32)[:, None]
    w = (f32(2.0 * math.pi) * np.arange(L, dtype=f32)[:, None] / f32(L)).astype(f32)
    fr = np.linspace(1e-4, 15, 16, dtype=f32)[None, :]
    z = np.concatenate([tl, np.cos(fr * w), -np.sin(fr * w)], axis=-1).astype(f32)
    _CONST_CACHE["zfT"] = np.ascontiguousarray(z.T)
    _CONST_CACHE["ntn"] = np.ascontiguousarray((-tl[:, 0]).reshape(16, 128).T.astype(f32))
    min_decay = math.log(1e-2) / 1.5
    max_decay = math.log(1e-2) / 0.3
    _CONST_CACHE["deltas"] = np.abs(np.linspace(min_decay, max_decay, 768, dtype=f32)).astype(f32)
    inv_f = (10000.0 ** (-np.arange(0, 64, 2, dtype=f32) / f32(64))).astype(f32)
    angr = (np.arange(L, dtype=f32)[:, None] * inv_f[None, :]).astype(f32)
    angr = np.concatenate([angr, angr], axis=-1)
    cosT = np.cos(angr).T.astype(f32)
    sinT = np.sin(angr).T.astype(f32)
    _CONST_CACHE["ropec"] = np.ascontiguousarray(np.concatenate([cosT, cosT], axis=0))
    _CONST_CACHE["ropes"] = np.ascontiguousarray(np.concatenate([sinT, sinT], axis=0))
    Pm = np.zeros((128, 128), np.float32)
    for po in range(128):
        d = po % 64
        if d < 32:
            Pm[po + 32, po] = -1.0
        else:
            Pm[po - 32, po] = 1.0
    _CONST_CACHE["pm"] = _bf(Pm)
    _CONST_CACHE["ident"] = _bf(np.eye(128, dtype=np.float32))
    k = np.arange(128)[:, None]
    q = np.arange(128)[None, :]
    NEG = -30000.0
    m_next = np.where(k <= q, 0.0, NEG)
    m_prev = np.where(k >= q, 0.0, NEG)
    _CONST_CACHE["maskb"] = _bf(np.concatenate([m_next, np.zeros((128, 128)), m_prev], axis=1))
    return _CONST_CACHE


def build_program(debug=None):
    nc = bass.Bass("TRN2", target_bir_lowering=False)
    dbg = {}

    def din(name, shape, dt=F32):
        return nc.dram_tensor(name, list(shape), dt, kind="ExternalInput").ap()

    x_d = din("x", [L, D])
    mem_d = din("mem", [256, D])
    wkv_d = din("w_mem_kv", [D, 512])
    w_in_d = [din("l0_w_in", [D, 2560]), din("l1_w_in", [D, 1536])]
    w_out_d = [din("l0_w_out", [D, D]), din("l1_w_out", [D, D])]
    w_up_d = [din("l%d_ffn_w_up" % i, [D, 2 * DFF]) for i in range(2)]
    w_dn_d = [din("l%d_ffn_w_down" % i, [DFF, D]) for i in range(2)]
    ln_d = [[din("l%d_%s" % (i, n), [D]) for n in ("ln1_g", "ln1_b", "ln2_g", "ln2_b")] for i in range(2)]
    fcwb_d = [din("l%d_fcwb" % i, [128, 44, 4]) for i in range(2)]
    cwb_d = din("l0_cwb", [128, 18, 4])
    fw1_d = din("l0_filt_w1", [33, 64])
    fw2_d = din("l0_filt_w2", [64, 64])
    fw3_d = din("l0_filt_w3", [64, 64])
    fwo_d = din("l0_filt_w_out", [64, 1536])
    fbf_d = din("l0_fbf", [64, 6])
    hd_d = din("l0_hyena_d", [768])
    sink_d = din("l1_sink", [12])
    fwd_tab_d = din("fwd_tab", [16, 128, 4096], BF16)
    inv_tab_d = din("inv_tab", [4, 4, 128, 4096], BF16)
    zfT_d = din("zfT", [33, L])
    ntn_d = din("ntn", [128, 16])
    deltas_d = din("deltas", [768])
    ropec_d = din("ropec", [128, L])
    ropes_d = din("ropes", [128, L])
    pm_d = din("pm", [128, 128], BF16)
    ident_d = din("ident", [128, 128], BF16)
    maskb_d = din("maskb", [128, 384], BF16)
    out_d = nc.dram_tensor("out", [L, D], F32, kind="ExternalOutput").ap()
    if debug:
        dbg_d = nc.dram_tensor("dbg", [L, D], F32, kind="ExternalOutput").ap()

    es = ExitStack()
    with es:
        S = Sched(nc, es)
        AR_BYTES = 194 * 1024
        arena = es.enter_context(nc.sbuf_tensor("arena", [128, AR_BYTES // 2], BF16))
        ps = es.enter_context(nc.psum_tensor("ps", [128, 4096], F32))
        PB = [Res("pb%d" % i, excl=True) for i in range(8)]

        def bank(b, n=512, p0=0, p1=128):
            return ps[p0:p1, b * 512:b * 512 + n]

        def bank_bf(b):
            return ps[:, b * 512:(b + 1) * 512].bitcast(BF16)

        def AV(off, shape, dt):
            n = int(np.prod(shape))
            if dt == F32:
                v = arena[:, off // 2: off // 2 + 2 * n].bitcast(F32)
            else:
                v = arena[:, off // 2: off // 2 + n]
            if len(shape) == 2:
                v = v.rearrange("p (a b) -> p a b", a=shape[0])
            elif len(shape) == 3:
                v = v.rearrange("p (a b c) -> p a b c", a=shape[0], b=shape[1])
            elif len(shape) == 4:
                v = v.rearrange("p (a b c d) -> p a b c d", a=shape[0], b=shape[1], c=shape[2])
            return v

        K_ = 1024
        R_X, R_XT, R_Y, R_YM, R_MQ, R_T = 0, 64 * K_, 96 * K_, 120 * K_, 128 * K_, 136 * K_

        def small(name, shape, dt):
            return es.enter_context(nc.sbuf_tensor("sb_" + name, list(shape), dt))

        ident = small("ident", [128, 128], BF16)
        pm = small("pm", [128, 128], BF16)
        maskb = small("maskb", [128, 384], BF16)
        ones64 = small("ones64", [128, 64], BF16)
        memKT = small("memKT", [128, 2, 256], BF16)
        memV = small("memV", [128, 2, 256], BF16)
        fcwb = small("fcwb", [128, 44, 4], F32)
        cwb = fcwb[:, 0:18, :]
        stat = small("stat", [128, 128], F32)
        GB = small("GB", [128, 2, 1024], F32)
        r_consts = Res("consts")
        r_memKV = Res("memKV")
        r_cwb = Res("cwb")
        r_fcwb = Res("fcwb")
        r_GB = Res("GB")

        X = AV(R_X, [16, 1024], F32)
        XT = AV(R_XT, [8, 2048], BF16)
        YT = AV(R_Y, [6, 2048], BF16)
        YM = AV(R_YM, [2, 2048], BF16)
        MQ = AV(R_MQ, [2, 2048], BF16)
        r_X = [Res("X%d" % i) for i in range(16)]
        r_XT = Res("XT")
        r_YT = Res("YT")
        r_YM = Res("YM")
        r_MQ = Res("MQ")

        _bk = [0]

        def nxt(lst):
            b = lst[_bk[0] % len(lst)]
            _bk[0] += 1
            return b

        def mm_group(out, pairs, reads, writes):
            n = len(pairs)
            fns = []
            for i, (l, r) in enumerate(pairs):
                fns.append(lambda e, l=l, r=r, i=i: e.matmul(out, l, r, start=(i == 0), stop=(i == n - 1)))
            return S.op("pe", fns, reads, writes)

        def act_copy(out, in_, reads, writes):
            return S.op("act", [lambda e: e.activation(out=out, in_=in_, func=AF.Identity)], reads, writes)

        S.dma("sp", ident[:], ident_d[:, :], writes=[r_consts], key="c0")
        S.dma("sp", pm[:], pm_d[:, :], writes=[r_consts], key="c1")
        S.dma("sp", maskb[:], maskb_d[:, :], writes=[r_consts], key="c2")
        S.dma("sp", cwb, cwb_d[:, :, :], writes=[r_cwb], key="c3")
        S.op("dve", [lambda e: e.memset(ones64[:], 1.0)], writes=[r_consts])
        epst = small("epst", [128, 1], F32)
        identf = small("identf", [128, 128], F32)
        aident = small("aident", [128, 128], F32)
        S.op("act", [lambda e: e.activation(out=identf[:], in_=ident[:], func=AF.Identity)], reads=[r_consts], writes=[r_consts])
        S.op("act", [lambda e: e.activation(out=aident[:], in_=ident[:], func=AF.Identity, scale=ALPHA)], reads=[r_consts], writes=[r_consts])
        S.op("dve", [lambda e: e.memset(epst[:], EPS)], writes=[r_consts])

        HS = AV(R_X, [2, 16, 768], BF16)
        r_HS = Res("HS")
        hA = AV(R_X + 48 * K_, [2048], F32)
        hB = AV(R_X + 56 * K_, [2048], F32)
        r_hA, r_hB = Res("hA"), Res("hB")
        zf = AV(R_Y, [2048], F32)
        fw = AV(R_Y + 8 * K_, [3, 64], F32)
        fwo = AV(R_Y + 9 * K_, [1536], F32)
        fbf = AV(R_Y + 15 * K_, [8], F32)
        dbc = AV(R_Y + 16 * K_, [768], F32)
        dlt = AV(R_YM, [768], F32)
        dec = AV(R_YM + 3 * K_, [768], F32)
        ntn = AV(R_YM + 6 * K_, [16], F32)
        fsb = AV(R_MQ, [768], F32)
        wtmp = AV(R_MQ + 3 * K_, [512], F32)
        wtm2 = AV(R_MQ + 5 * K_, [512], F32)
        r_f = Res("filt_in")
        r_dec, r_fsb, r_wtmp, r_wtm2, r_dbc = Res("dec"), Res("fsb"), Res("wtmp"), Res("wtm2"), Res("dbc")
        S.dma("sp", zf[0:33, :], zfT_d[:, :], writes=[r_f], key="f0")
        S.dma("sp", fw[0:33, 0, :], fw1_d[:, :], writes=[r_f], key="f1")
        S.dma("sp", fw[0:64, 1, :], fw2_d[:, :], writes=[r_f], key="f2")
        S.dma("sp", fw[0:64, 2, :], fw3_d[:, :], writes=[r_f], key="f3")
        S.dma("sp", fwo[0:64, :], fwo_d[:, :], writes=[r_f], key="f4")
        S.dma("sp", fbf[0:64, 0:6], fbf_d[:, :], writes=[r_f], key="f5")
        S.dma("sp", dbc, hd_d.partition_broadcast(128), writes=[r_dbc], key="f6")
        S.dma("sp", dlt, deltas_d.partition_broadcast(128), writes=[r_f], key="f7")
        S.dma("sp", ntn, ntn_d[:, :], writes=[r_f], key="f8")

        def transpose_to_XT(src_bf, tile, r_src):
            b = nxt([6, 7])
            bb = bank_bf(b)
            fns = [lambda e, kc=kc: e.transpose(bb[:, kc * 128:(kc + 1) * 128], src_bf[:, kc * 128:(kc + 1) * 128], ident[:])
                   for kc in range(8)]
            S.op("pe", fns, reads=[r_src, r_consts], writes=[PB[b]])
            S.op("act", [lambda e: e.activation(out=XT[:, :, tile * 128:(tile + 1) * 128],
                                                in_=bb.rearrange("p (k m) -> p k m", k=8), func=AF.Identity)],
                 reads=[PB[b]], writes=[r_XT])

        def load_ln(layer, which):
            g_d, b_d = ln_d[layer][2 * which], ln_d[layer][2 * which + 1]
            S.dma("sp", GB[:, 0, :], g_d.partition_broadcast(128), writes=[r_GB], key="gb0")
            S.dma("sp", GB[:, 1, :], b_d.partition_broadcast(128), writes=[r_GB], key="gb1")

        NRB = 6
        LN_LAG = 3
        _rboff = [46 * K_, 50 * K_, 28672, 28672 + 4096, 36880, 36880 + 4096]
        rbuf = [AV(R_T + _rboff[i], [1024], F32) for i in range(NRB)]
        _xboff = [54 * K_, 56 * K_, 24 * K_, 26 * K_]
        xbuf = [AV(R_T + _xboff[i], [1024], BF16) for i in range(4)]
        r_rbuf = [Res("rbuf%d" % i) for i in range(NRB)]
        r_xbuf = [Res("xbuf%d" % i) for i in range(4)]
        r_stat = Res("stat")
        r_stats = [Res("stat%d" % i) for i in range(8)]
        _ln = [0]

        class LNPipe:
            def __init__(self, final_out):
                self.final_out = final_out
                self.q = []

            def push(self, tile, pin, r_pin, r_in, r_r):
                ri = tile % NRB
                so = ri * 16
                r_stat = r_stats[ri]
                st6 = stat[:, so:so + 12]
                mv = stat[:, so + 12:so + 14]
                rstd = stat[:, so + 14:so + 15]
                nmr = stat[:, so + 15:so + 16]
                final_out = self.final_out

                def s1():
                    S.op("dve", [lambda e: e.bn_stats(out=st6[:, 0:6], in_=pin[:, 0:512])], reads=[r_pin[0]], writes=[r_stat])
                    S.op("dve", [lambda e: e.bn_stats(out=st6[:, 6:12], in_=pin[:, 512:1024])], reads=[r_pin[1]], writes=[r_stat])
                    S.op("dve", [lambda e: e.bn_aggr(out=mv, in_=st6)], reads=[r_stat], writes=[r_stat])

                def s2():
                    S.op("act", [lambda e: e.activation(out=rstd, in_=mv[:, 1:2], func=AF.Sqrt, bias=epst[:, 0:1])],
                         reads=[r_stat, r_consts], writes=[r_stat])

                def s3():
                    S.op("dve", [lambda e: e.reciprocal(out=rstd, in_=rstd)], reads=[r_stat], writes=[r_stat])
                    S.op("dve", [lambda e: e.scalar_tensor_tensor(out=nmr, in0=mv[:, 0:1], scalar=-1.0, in1=rstd,
                                                                  op0=ALU.mult, op1=ALU.mult)], reads=[r_stat], writes=[r_stat])

                def s4():
                    S.op("act", [lambda e: e.activation(out=r_in, in_=pin, func=AF.Identity, scale=rstd, bias=nmr)],
                         reads=list(r_pin) + [r_stat], writes=[r_r])

                def s5():
                    S.op("dve", [lambda e: e.tensor_tensor(out=r_in, in0=r_in, in1=GB[:, 0, :], op=ALU.mult)],
                         reads=[r_r, r_GB], writes=[r_r])
                    S.op("pool", [lambda e: e.tensor_tensor(out=X[:, tile, :], in0=r_in, in1=GB[:, 1, :], op=ALU.add)],
                         reads=[r_r, r_GB], writes=[r_X[tile]])

                def s6():
                    if final_out:
                        S.dma("sp", out_d[tile * 128:(tile + 1) * 128, :], X[:, tile, :], reads=[r_X[tile]], key="out%d" % (tile % 4))
                    else:
                        i4 = tile % 4
                        S.op("act", [lambda e: e.activation(out=xbuf[i4], in_=X[:, tile, :], func=AF.Identity)],
                             reads=[r_X[tile]], writes=[r_xbuf[i4]])

                def s7():
                    if not final_out:
                        i4 = tile % 4
                        transpose_to_XT(xbuf[i4], tile, r_xbuf[i4])

                s1()
                for d_, f_ in ((1, s2), (1, s3), (2, s4), (3, s5), (4, s6), (5, s7)):
                    self.q.append([d_, f_])
                self._tick()

            def _tick(self):
                keep = []
                for item in self.q:
                    item[0] -= 0
                ready = [it for it in self.q if it[0] <= 0]
                for it in ready:
                    it[1]()
                self.q = [it for it in self.q if it[0] > 0]
                for it in self.q:
                    it[0] -= 1

            def flush(self):
                while self.q:
                    self._tick()

        def mem_attention(toff):
            PT = [AV(toff + i * K_, [512], BF16) for i in range(8)]
            r_PT = [Res("mpt%d" % i) for i in range(8)]
            rdens = [AV(toff + 8 * K_ + i * 2 * K_, [512], F32) for i in range(2)]
            r_rdens = [Res("mrden%d" % i) for i in range(2)]
            its = [(hp, qt, hh) for hp in range(2) for qt in range(4) for hh in range(2)]

            def stage1(n):
                hp, qt, hh = its[n]
                qs = slice(qt * 512, (qt + 1) * 512)
                prow = slice(hh * 64, (hh + 1) * 64)
                for mt in range(2):
                    b = nxt([0, 1, 2, 3])
                    mm_group(bank(b), [(memKT[prow, hp, mt * 128:(mt + 1) * 128], MQ[prow, hp, qs])],
                             reads=[r_memKV, r_MQ], writes=[PB[b]])
                    k = (n % 2) * 4 + mt
                    S.op("act", [lambda e, k=k, b=b: e.activation(out=PT[k], in_=bank(b), func=AF.Exp, scale=0.125)],
                         reads=[PB[b]], writes=[r_PT[k]])

            def stage2(n):
                hp, qt, hh = its[n]
                h = 2 * hp + hh
                qs = slice(qt * 512, (qt + 1) * 512)
                prow = slice(hh * 64, (hh + 1) * 64)
                pts = [(n % 2) * 4 + mt for mt in range(2)]
                bo, bd = [4, 6][n % 2], [5, 7][n % 2]
                mm_group(bank(bo, 512, 0, 64), [(memV[:, mt, h * 64:(h + 1) * 64], PT[pts[mt]]) for mt in range(2)],
                         reads=[r_memKV, r_PT[pts[0]], r_PT[pts[1]]], writes=[PB[bo]])
                mm_group(bank(bd, 512, 0, 64), [(ones64[:], PT[pts[mt]]) for mt in range(2)],
                         reads=[r_consts, r_PT[pts[0]], r_PT[pts[1]]], writes=[PB[bd]])
                rd = rdens[n % 2]
                r_rd = r_rdens[n % 2]
                S.op("dve", [lambda e: e.reciprocal(out=rd[0:64, :], in_=bank(bd, 512, 0, 64))],
                     reads=[PB[bd]], writes=[r_rd])
                S.op("dve", [lambda e: e.tensor_tensor(
                    out=YM[prow, hp, qs], in0=bank(bo, 512, 0, 64), in1=rd[0:64, :], op=ALU.mult)],
                     reads=[PB[bo], r_rd], writes=[r_YM])

            for n in range(len(its) + 1):
                if n < len(its):
                    stage1(n)
                if n >= 1:
                    stage2(n - 1)

        def out_proj_ln1(layer):
            wout = AV(R_T, [8, 1024], BF16)
            r_wout = Res("wout")
            xs = [AV(R_T + 16 * K_ + i * 4 * K_, [1024], F32) for i in range(2)]
            r_xs = [Res("xs%d" % i) for i in range(2)]
            S.dma("pool", wout, w_out_d[layer].rearrange("(kc p) n -> p kc n", p=128),
                  writes=[r_wout] + ([r_KS] if layer == 0 else []), key="wout")
            load_ln(layer, 0)
            lnp = LNPipe(False)
            for tile in range(16):
                ts_ = slice(tile * 128, (tile + 1) * 128)
                bp = [0, 2, 4][tile % 3]
                i = tile % 2
                ri = tile % NRB
                if layer == 0:
                    S.dma("sp", xs[i], x_d[ts_, :], writes=[r_xs[i]] + ([r_KS] if tile < 2 else []), key="xs%d" % i)
                    xin, rxin = xs[i], r_xs[i]
                else:
                    xin, rxin = X[:, tile, :], r_X[tile]
                for half in range(2):
                    b = bp + half
                    pairs = []
                    for kc in range(8):
                        lhs = YT[:, kc, ts_] if kc < 6 else YM[:, kc - 6, ts_]
                        pairs.append((lhs, wout[:, kc, half * 512:(half + 1) * 512]))
                    mm_group(bank(b), pairs, reads=[r_YT, r_YM, r_wout], writes=[PB[b]])
                rb = rbuf[ri]
                S.op("dve", [lambda e, xin=xin, rb=rb, bp=bp: e.scalar_tensor_tensor(
                    out=rb, in0=xin, scalar=ALPHA, in1=ps[:, bp * 512:bp * 512 + 1024], op0=ALU.mult, op1=ALU.add)],
                     reads=[rxin, PB[bp], PB[bp + 1]], writes=[r_rbuf[ri]])
                lnp.push(tile, rb, [r_rbuf[ri], r_rbuf[ri]], rb, r_rbuf[ri])
            lnp.flush()

        def ffn_ln2(layer, final):
            blocks = [4, 4, 4, 4, 3, 3]
            S.dma("sp", fcwb[:], fcwb_d[layer][:, :, :], writes=[r_fcwb], key="fcwb")
            load_ln(layer, 1)
            GT = AV(R_Y, [4, 2048], BF16)
            r_GT = Res("GT")
            wdn = [AV(R_T + i * 8 * K_, [4, 1024], BF16) for i in range(2)]
            r_wdn = [Res("wdn%d" % i) for i in range(2)]
            wup = [AV(R_T + 16 * K_ + i * 4 * K_, [8, 2, 128], BF16) for i in range(3)]
            r_wup = [Res("wup%d" % i) for i in range(3)]
            hsb = [AV(R_T + 28 * K_ + i * 8208, [2052], F32) for i in range(2)]
            r_hsb = [Res("hsb%d" % i) for i in range(2)]
            for i in range(2):
                S.op("pool", [lambda e, i=i: e.memset(hsb[i][:, 0:1], 0.0)], writes=[r_hsb[i]])
                S.op("pool", [lambda e, i=i: e.memset(hsb[i][:, 2049:2050], 0.0)], writes=[r_hsb[i]])
            acc_as = [AV(R_YM, [2048], F32), AV(R_Y + 16 * K_, [2048], F32)]
            acc_g = AV(R_MQ, [2048], F32)
            r_accas, r_accg = [Res("acca0"), Res("acca1")], Res("accg")
            wd_v = w_dn_d[layer]
            wu_v = w_up_d[layer].rearrange("(kc p) n -> p kc n", p=128)
            def down_proj(bi, nb, wd):
                last = bi == len(blocks) - 1
                if last:
                    S.barrier()
                lnp = LNPipe(final)
                for tile in range(16):
                    ts_ = slice(tile * 128, (tile + 1) * 128)
                    bp = [0, 2, 4][tile % 3] if last else [4, 6][tile % 2]
                    pin = ps[:, bp * 512:bp * 512 + 1024]
                    if not last:
                        for half in range(2):
                            mm_group(bank(bp + half), [(GT[:, jj, ts_], wd[:, jj, half * 512:(half + 1) * 512]) for jj in range(nb)],
                                     reads=[r_GT, r_wdn[bi % 2]], writes=[PB[bp + half]])
                        if bi == 0:
                            S.op("dve", [lambda e, tile=tile, pin=pin: e.scalar_tensor_tensor(
                                out=X[:, tile, :], in0=X[:, tile, :], scalar=ALPHA, in1=pin, op0=ALU.mult, op1=ALU.add)],
                                 reads=[PB[bp], PB[bp + 1]], writes=[r_X[tile]])
                        else:
                            S.op("dve", [lambda e, tile=tile, pin=pin: e.tensor_tensor(
                                out=X[:, tile, :], in0=X[:, tile, :], in1=pin, op=ALU.add)],
                                 reads=[PB[bp], PB[bp + 1]], writes=[r_X[tile]])
                    else:
                        for half in range(2):
                            b = bp + half
                            hsl = slice(half * 512, (half + 1) * 512)
                            fns = []
                            for jj in range(nb):
                                fns.append(lambda e, jj=jj, b=b, hsl=hsl, ts_=ts_: e.matmul(bank(b), GT[:, jj, ts_], wd[:, jj, hsl], start=(jj == 0), stop=False))
                            fns.append(lambda e, b=b, hsl=hsl, tile=tile: e.matmul(bank(b), identf[:], X[:, tile, hsl], start=False, stop=True))
                            S.op("pe", fns, reads=[r_GT, r_wdn[bi % 2], r_X[tile], r_consts], writes=[PB[b]])
                        ri = tile % NRB
                        lnp.push(tile, pin, [PB[bp], PB[bp + 1]], rbuf[ri], r_rbuf[ri])
                lnp.flush()

            j0 = 0
            deferred = None
            for bi, nb in enumerate(blocks):
                wd = wdn[bi % 2]
                S.dma("pool", wd[:, 0:nb, :], wd_v[j0 * 128:(j0 + nb) * 128, :].rearrange("(j p) n -> p j n", p=128),
                      writes=[r_wdn[bi % 2]], key="wdn%d" % (bi % 2))
                for jj in range(nb):
                    j = j0 + jj
                    wi = j % 3
                    wu = wup[wi]
                    acc_a, r_acca = acc_as[j % 2], r_accas[j % 2]
                    S.dma("pool", wu[:, :, 0, :], wu_v[:, :, j * 128:(j + 1) * 128], writes=[r_wup[wi]], key="wup%da" % wi)
                    S.dma("pool", wu[:, :, 1, :], wu_v[:, :, DFF + j * 128:DFF + (j + 1) * 128], writes=[r_wup[wi]], key="wup%db" % wi)
                    for ag in range(2):
                        hs_ = hsb[ag]
                        cj = ag * NFF + j
                        for tq in range(4):
                            b = nxt([0, 1, 2, 3])
                            mm_group(bank(b), [(wu[:, kc, ag, :], XT[:, kc, tq * 512:(tq + 1) * 512]) for kc in range(8)],
                                     reads=[r_wup[wi], r_XT], writes=[PB[b]])
                            S.op("act", [lambda e, hs_=hs_, tq=tq, b=b: e.activation(
                                out=hs_[:, 1 + tq * 512:1 + (tq + 1) * 512], in_=bank(b), func=AF.Identity)],
                                 reads=[PB[b]], writes=[r_hsb[ag]])
                        acc, r_acc = (acc_a, r_acca) if ag == 0 else (acc_g, r_accg)
                        S.op("act", [lambda e, acc=acc, hs_=hs_, cj=cj: e.activation(
                            out=acc, in_=hs_[:, 1:2049], func=AF.Identity, scale=fcwb[:, cj, 1:2], bias=fcwb[:, cj, 3:4])],
                             reads=[r_hsb[ag], r_fcwb], writes=[r_acc])
                        S.op("dve", [lambda e, acc=acc, hs_=hs_, cj=cj: e.scalar_tensor_tensor(
                            out=acc, in0=hs_[:, 0:2048], scalar=fcwb[:, cj, 0:1], in1=acc, op0=ALU.mult, op1=ALU.add)],
                             reads=[r_hsb[ag], r_fcwb, r_acc], writes=[r_acc])
                        S.op("dve", [lambda e, acc=acc, hs_=hs_, cj=cj: e.scalar_tensor_tensor(
                            out=acc, in0=hs_[:, 2:2050], scalar=fcwb[:, cj, 2:3], in1=acc, op0=ALU.mult, op1=ALU.add)],
                             reads=[r_hsb[ag], r_fcwb, r_acc], writes=[r_acc])
                        if ag == 0 and deferred is not None:
                            deferred()
                            deferred = None

                    def _fin(jj=jj, acc_a=acc_a, r_acca=r_acca, endblk=(jj == nb - 1), bi=bi, nb=nb, wd=wd):
                        S.op("act", [lambda e: e.activation(out=acc_g, in_=acc_g, func=AF.Silu)], reads=[r_accg], writes=[r_accg])
                        S.op("dve", [lambda e: e.tensor_tensor(out=GT[:, jj, :], in0=acc_g, in1=acc_a, op=ALU.mult)],
                             reads=[r_accg, r_acca], writes=[r_GT])
                        if endblk:
                            down_proj(bi, nb, wd)
                    deferred = _fin
                j0 += nb
            if deferred is not None:
                deferred()
                deferred = None

        memf = AV(R_T + 8 * K_, [2, 1024], F32)
        memb = AV(R_T + 28 * K_, [2, 1024], BF16)
        memT = AV(R_T + 32 * K_, [8, 256], BF16)
        wkv = AV(R_T, [8, 512], BF16)
        r_memf, r_memb, r_memT, r_wkv = Res("memf"), Res("memb"), Res("memT"), Res("wkv")
        S.dma("sp", memf, mem_d.rearrange("(mt p) d -> p mt d", p=128), writes=[r_memf], key="memf")
        S.dma("pool", wkv, wkv_d.rearrange("(kc p) n -> p kc n", p=128), writes=[r_wkv], key="wkv")
        act_copy(memb, memf, [r_memf], [r_memb])
        for mt in range(2):
            b = nxt([6, 7])
            bb = bank_bf(b)
            fns = [lambda e, kc=kc, mt=mt, bb=bb: e.transpose(bb[:, kc * 128:(kc + 1) * 128], memb[:, mt, kc * 128:(kc + 1) * 128], ident[:])
                   for kc in range(8)]
            S.op("pe", fns, reads=[r_memb, r_consts], writes=[PB[b]])
            S.op("act", [lambda e, mt=mt, bb=bb: e.activation(out=memT[:, :, mt * 128:(mt + 1) * 128],
                                                             in_=bb.rearrange("p (k m) -> p k m", k=8), func=AF.Identity)],
                 reads=[PB[b]], writes=[r_memT])
        for hp in range(2):
            b = nxt([0, 1])
            mm_group(bank(b, 256), [(wkv[:, kc, hp * 128:(hp + 1) * 128], memT[:, kc, :]) for kc in range(8)],
                     reads=[r_wkv, r_memT], writes=[PB[b]])
            act_copy(memKT[:, hp, :], bank(b, 256), [PB[b]], [r_memKV])
        for mt in range(2):
            b = nxt([0, 1])
            mm_group(bank(b, 256), [(memT[:, kc, mt * 128:(mt + 1) * 128], wkv[:, kc, 256:512]) for kc in range(8)],
                     reads=[r_wkv, r_memT], writes=[PB[b]])
            act_copy(memV[:, mt, :], bank(b, 256), [PB[b]], [r_memKV])

        if debug == "s_m":
            S.barrier()
            S.emit()
            return nc
        xs0 = [AV(R_T + 16 * K_ + i * 4 * K_, [1024], F32) for i in range(2)]
        r_xs0 = [Res("xs0_%d" % i) for i in range(2)]

        def x_step(tile):
            i = tile % 2
            i4 = tile % 4
            S.dma("sp", xs0[i], x_d[tile * 128:(tile + 1) * 128, :], writes=[r_xs0[i]], key="xs%d" % i)
            S.op("act", [lambda e: e.activation(out=xbuf[i4], in_=xs0[i], func=AF.Identity)],
                 reads=[r_xs0[i]], writes=[r_xbuf[i4]])
            transpose_to_XT(xbuf[i4], tile, r_xbuf[i4])
        x_steps = [(lambda t=t: x_step(t)) for t in range(16)]
        if debug == "s_x":
            S.barrier()
            S.emit()
            return nc
        fbs = stat[0:64, 120:123]
        for l in range(3):
            S.op("dve", [lambda e, l=l: e.tensor_tensor(out=fbs[:, l:l + 1], in0=fbf[0:64, 2 * l:2 * l + 1],
                                                        in1=fbf[0:64, 2 * l + 1:2 * l + 2], op=ALU.mult)],
                 reads=[r_f], writes=[r_stat])
        srcs = [(zf, 33, r_f), (hA, 64, r_hA), (hB, 64, r_hB)]
        dsts = [(hA, r_hA), (hB, r_hB), (hA, r_hA)]
        wtW = AV(R_T + 36 * K_, [2048], F32)
        wt2W = AV(R_T + 44 * K_, [2048], F32)
        r_wtW, r_wt2W = Res("wtW"), Res("wt2W")
        dec2 = [dec, AV(R_MQ, [768], F32)]
        r_dec2 = [Res("dec0"), Res("dec1")]
        f_steps = []

        def mlp_layer(l):
            src, kk, r_src = srcs[l]
            dst, r_dst = dsts[l]
            for tq in range(4):
                cs = slice(tq * 512, (tq + 1) * 512)
                mm_group(bank(tq, 512, 0, 64), [(fw[0:kk, l, :], src[0:kk, cs])], reads=[r_f, r_src], writes=[PB[tq]])
            pall = ps[0:64, 0:2048]
            S.op("dve", [lambda e: e.tensor_scalar(out=wtW[0:64, :], in0=pall, scalar1=fbf[0:64, 2 * l + 1:2 * l + 2],
                                                   scalar2=fbs[:, l:l + 1], op0=ALU.mult, op1=ALU.add)],
                 reads=[PB[0], PB[1], PB[2], PB[3], r_f, r_stat], writes=[r_wtW])
            S.op("dve", [lambda e: e.tensor_scalar(out=wt2W[0:64, :], in0=wtW[0:64, :], scalar1=-PI, scalar2=2 * PI,
                                                   op0=ALU.is_lt, op1=ALU.mult)], reads=[r_wtW], writes=[r_wt2W])
            S.op("dve", [lambda e: e.tensor_tensor(out=wtW[0:64, :], in0=wtW[0:64, :], in1=wt2W[0:64, :], op=ALU.add)],
                 reads=[r_wtW, r_wt2W], writes=[r_wtW])
            S.op("dve", [lambda e: e.tensor_scalar(out=wt2W[0:64, :], in0=wtW[0:64, :], scalar1=PI, scalar2=-2 * PI,
                                                   op0=ALU.is_gt, op1=ALU.mult)], reads=[r_wtW], writes=[r_wt2W])
            S.op("dve", [lambda e: e.tensor_tensor(out=wtW[0:64, :], in0=wtW[0:64, :], in1=wt2W[0:64, :], op=ALU.add)],
                 reads=[r_wtW, r_wt2W], writes=[r_wtW])
            S.op("act", [lambda e: e.activation(out=dst[0:64, :], in_=wtW[0:64, :], func=AF.Sin)],
                 reads=[r_wtW], writes=[r_dst])

        def wcomb():
            S.op("dve", [lambda e: e.tensor_tensor(out=fsb[0:64, :], in0=fwo[0:64, 0:768], in1=fwo[0:64, 768:1536], op=ALU.add)],
                 reads=[r_f], writes=[r_fsb])
            S.op("dve", [lambda e: e.tensor_tensor(out=fwo[0:64, 768:1536], in0=fwo[0:64, 0:768], in1=fwo[0:64, 768:1536], op=ALU.subtract)],
                 reads=[r_f], writes=[r_f])
            S.op("dve", [lambda e: e.tensor_copy(out=fwo[0:64, 0:768], in_=fsb[0:64, :])], reads=[r_fsb], writes=[r_f])

        def hfull(tile):
            ts_ = slice(tile * 128, (tile + 1) * 128)
            b0 = 4 if tile % 2 else 0
            dc, r_dc = dec2[tile % 2], r_dec2[tile % 2]
            for q3 in range(3):
                mm_group(bank(b0 + q3), [(hA[0:64, ts_], fwo[0:64, q3 * 512:(q3 + 1) * 512])], reads=[r_hA, r_f], writes=[PB[b0 + q3]])
            S.op("act", [lambda e: e.activation(out=dc, in_=dlt, func=AF.Exp, scale=ntn[:, tile:tile + 1])],
                 reads=[r_f], writes=[r_dc])
            S.op("dve", [lambda e: e.tensor_tensor(out=HS[:, 0, tile, :], in0=ps[:, b0 * 512:b0 * 512 + 768], in1=dc, op=ALU.mult)],
                 reads=[PB[b0], PB[b0 + 1], r_dc], writes=[r_HS])
            S.op("dve", [lambda e: e.tensor_tensor(out=HS[:, 1, tile, :], in0=ps[:, b0 * 512 + 768:b0 * 512 + 1536], in1=dc, op=ALU.mult)],
                 reads=[PB[b0 + 1], PB[b0 + 2], r_dc], writes=[r_HS])

        f_steps.append(wcomb)
        for l in range(3):
            f_steps.append(lambda l=l: mlp_layer(l))
        for tile in range(16):
            f_steps.append(lambda tile=tile: hfull(tile))
        xi = 0
        for k_, fs in enumerate(f_steps):
            fs()
            if k_ >= 1 and xi < 16:
                x_steps[xi]()
                xi += 1
        while xi < 16:
            x_steps[xi]()
            xi += 1
        S.barrier()

        if debug == "s_f":
            S.barrier()
            S.emit()
            return nc
        KS = AV(R_T, [2, 16, 768], BF16)
        r_KS = Res("KS")

        def fwd_pass(rhs_re, rhs_im, r_rhs, ftab, r_ftab, epilogue, extra_w=((), ())):
            for fc in range(16):
                si = fc % 2
                ft = ftab[si]
                S.dma("sp", ft.rearrange("p a b c -> p (a b c)"), fwd_tab_d[fc],
                      writes=[r_ftab[si]] + (list(extra_w[si]) if fc < 2 else []), key="ftab%d" % si)
                bs = [0, 1, 2, 3] if fc % 2 == 0 else [4, 5, 6, 7]
                for ri in range(2):
                    rhs = rhs_re if ri == 0 else rhs_im
                    o0 = bs[0] * 512 + ri * 1024
                    fns = []
                    for tc in range(16):
                        lhs = ft[:, ri, tc, :]
                        fns.append(lambda e, lhs=lhs, tc=tc, rhs=rhs, o0=o0: e.matmul(
                            ps[:, o0:o0 + 512], lhs, rhs[:, tc, 0:512], start=(tc == 0), stop=(tc == 15)))
                        fns.append(lambda e, lhs=lhs, tc=tc, rhs=rhs, o0=o0: e.matmul(
                            ps[:, o0 + 512:o0 + 768], lhs, rhs[:, tc, 512:768], start=(tc == 0), stop=(tc == 15)))
                    S.op("pe", fns, reads=[r_ftab[si], r_rhs], writes=[PB[bs[2 * ri]], PB[bs[2 * ri + 1]]])
                pre = ps[:, bs[0] * 512:bs[0] * 512 + 768]
                pim = ps[:, bs[2] * 512:bs[2] * 512 + 768]
                epilogue(fc, pre, pim, [PB[b] for b in bs])

        ftab = [AV(R_YM, [2, 16, 128], BF16), AV(R_MQ, [2, 16, 128], BF16)]
        r_ftab = [Res("ftab0"), Res("ftab1")]

        def k_epilogue(fc, pre, pim, rbs):
            S.op("dve", [lambda e: e.tensor_tensor(out=KS[:, 0, fc, :], in0=pre, in1=dbc, op=ALU.add)],
                 reads=rbs[0:2] + [r_dbc], writes=[r_KS])
            act_copy(KS[:, 1, fc, :], pim, rbs[2:4], [r_KS])

        fwd_pass(HS[:, 0], HS[:, 1], r_HS, ftab, r_ftab, k_epilogue)

        if debug == "s_k":
            S.barrier()
            S.emit()
            return nc
        Z = AV(R_X, [16, 768], BF16)
        r_Z = Res("Z")
        hsb0 = [AV(R_X + 24 * K_ + i * 8208, [2052], F32) for i in range(2)]
        r_hsb0 = [Res("hsb0_%d" % i) for i in range(2)]
        accx = AV(R_X + 24 * K_ + 16416, [2048], F32)
        accv = AV(R_X + 24 * K_ + 16416 + 8192, [2048], F32)
        zT = AV(R_X + 24 * K_ + 16416 + 16384, [2048], BF16)
        r_accx, r_accv, r_zT = Res("accx"), Res("accv"), Res("zT")
        wch = [AV(R_T + 48 * K_ + i * 2 * K_, [8, 128], BF16) for i in range(3)]
        r_wch = [Res("wch%d" % i) for i in range(3)]
        for i in range(2):
            S.op("pool", [lambda e, i=i: e.memset(hsb0[i][:, 0:1], 0.0)], writes=[r_hsb0[i], r_HS])
            S.op("pool", [lambda e, i=i: e.memset(hsb0[i][:, 2049:2050], 0.0)], writes=[r_hsb0[i], r_HS])
        w0v = w_in_d[0].rearrange("(kc p) n -> p kc n", p=128)
        order = [18, 19] + [0, 1, 2, 3, 4, 5]
        for i in range(6):
            order += [6 + i, 12 + i]
        _wc = [0]

        def proj_chunk(wv, col0, sink_fn, extra_reads=()):
            wi = _wc[0] % 3
            _wc[0] += 1
            S.dma("pool", wch[wi], wv[:, :, col0:col0 + 128], writes=[r_wch[wi]], key="wch%d" % wi)
            for tq in range(4):
                b = nxt([0, 1, 2, 3])
                mm_group(bank(b), [(wch[wi][:, kc, :], XT[:, kc, tq * 512:(tq + 1) * 512]) for kc in range(8)],
                         reads=[r_wch[wi], r_XT], writes=[PB[b]])
                sink_fn(tq, b)

        def conv_chunk(c, hs_, r_hs, acc_out, r_acc_list, out_final):
            S.op("act", [lambda e: e.activation(out=acc_out, in_=hs_[:, 1:2049], func=AF.Identity,
                                                scale=cwb[:, c, 1:2], bias=cwb[:, c, 3:4])],
                 reads=[r_hs, r_cwb], writes=r_acc_list)
            S.op("dve", [lambda e: e.scalar_tensor_tensor(out=acc_out, in0=hs_[:, 0:2048], scalar=cwb[:, c, 0:1],
                                                          in1=acc_out, op0=ALU.mult, op1=ALU.add)],
                 reads=[r_hs, r_cwb] + r_acc_list, writes=r_acc_list)
            S.op("dve", [lambda e: e.scalar_tensor_tensor(out=out_final[0], in0=hs_[:, 2:2050], scalar=cwb[:, c, 2:3],
                                                          in1=acc_out, op0=ALU.mult, op1=ALU.add)],
                 reads=[r_hs, r_cwb] + r_acc_list, writes=out_final[1])

        zdef = []
        for ci_, c in enumerate(order):
            if debug and debug.startswith('s_p') and ci_ == int(debug[3:]):
                S.barrier()
                S.emit()
                return nc
            if c >= 18:
                hp = c - 18

                def sink_mq(tq, b, hp=hp):
                    act_copy(MQ[:, hp, tq * 512:(tq + 1) * 512], bank(b), [PB[b]], [r_MQ])
                proj_chunk(w0v, c * 128, sink_mq)
                continue
            si = (c % 2) if c < 6 else (0 if c < 12 else 1)
            hs_ = hsb0[si]

            def sink_h(tq, b, hs_=hs_, si=si):
                act_copy(hs_[:, 1 + tq * 512:1 + (tq + 1) * 512], bank(b), [PB[b]], [r_hsb0[si]])
            proj_chunk(w0v, c * 128, sink_h)
            if c < 6:
                conv_chunk(c, hs_, r_hsb0[si], accv, [r_accv], (YT[:, c, :], [r_YT]))
            elif c < 12:
                conv_chunk(c, hs_, r_hsb0[si], accx, [r_accx], (accx, [r_accx]))
                while zdef:
                    zdef.pop(0)()
            else:
                i6 = c - 12
                conv_chunk(c, hs_, r_hsb0[si], accv, [r_accv], (accv, [r_accv]))
                S.op("dve", [lambda e: e.tensor_tensor(out=zT, in0=accv, in1=accx, op=ALU.mult)],
                     reads=[r_accv, r_accx], writes=[r_zT])
                def _ztr(i6=i6):
                    for g8 in range(2):
                        b = nxt([6, 7])
                        bb = bank_bf(b)
                        fns = [lambda e, t8=t8, bb=bb, g8=g8: e.transpose(bb[:, t8 * 128:(t8 + 1) * 128],
                                                                         zT[:, (g8 * 8 + t8) * 128:(g8 * 8 + t8 + 1) * 128], ident[:])
                               for t8 in range(8)]
                        S.op("pe", fns, reads=[r_zT, r_consts], writes=[PB[b]])
                        S.op("act", [lambda e, g8=g8, bb=bb, i6=i6: e.activation(
                            out=Z[:, g8 * 8:(g8 + 1) * 8, i6 * 128:(i6 + 1) * 128],
                            in_=bb.rearrange("p (k m) -> p k m", k=8), func=AF.Identity)],
                             reads=[PB[b]], writes=[r_Z])
                zdef.append(_ztr)
        while zdef:
            zdef.pop(0)()
        if debug == "z":
            for tile in range(16):
                S.op("act", [lambda e, tile=tile: e.activation(out=X[:, tile, 0:768] if False else hsb0[0][:, 0:768], in_=Z[:, tile, :], func=AF.Identity)],
                     reads=[r_Z], writes=[r_hsb0[0]])
                S.dma("sp", dbg_d[tile * 128:(tile + 1) * 128, 0:768], hsb0[0][:, 0:768], reads=[r_hsb0[0]], key="dbg")
            S.barrier()
            S.emit()
            return nc
        mem_attention(R_X + 24 * K_)

        YRE = AV(R_XT, [16, 768], BF16)
        YIM = AV(R_X + 24 * K_, [16, 768], BF16)
        r_Y = Res("Yspec")
        ct = [AV(R_X + 48 * K_ + i * 3 * K_, [768], F32) for i in range(4)]
        r_ct = [Res("ct%d" % i) for i in range(4)]
        ftabU = [AV(R_T + 48 * K_, [2, 16, 128], BF16), AV(R_MQ, [2, 16, 128], BF16)]
        r_ftabU = [Res("ftabU0"), Res("ftabU1")]

        def u_epilogue(fc, pre, pim, rbs):
            kre, kim = KS[:, 0, fc, :], KS[:, 1, fc, :]
            S.op("dve", [lambda e: e.tensor_tensor(out=ct[0], in0=pre, in1=kre, op=ALU.mult)], reads=rbs[0:2] + [r_KS], writes=[r_ct[0]])
            S.op("dve", [lambda e: e.tensor_tensor(out=ct[1], in0=pim, in1=kim, op=ALU.mult)], reads=rbs[2:4] + [r_KS], writes=[r_ct[1]])
            S.op("dve", [lambda e: e.tensor_tensor(out=ct[2], in0=pre, in1=kim, op=ALU.mult)], reads=rbs[0:2] + [r_KS], writes=[r_ct[2]])
            S.op("dve", [lambda e: e.tensor_tensor(out=ct[3], in0=pim, in1=kre, op=ALU.mult)], reads=rbs[2:4] + [r_KS], writes=[r_ct[3]])
            S.op("pool", [lambda e: e.tensor_tensor(out=YRE[:, fc, :], in0=ct[0], in1=ct[1], op=ALU.subtract)],
                 reads=[r_ct[0], r_ct[1]], writes=[r_Y])
            S.op("pool", [lambda e: e.tensor_tensor(out=YIM[:, fc, :], in0=ct[2], in1=ct[3], op=ALU.add)],
                 reads=[r_ct[2], r_ct[3]], writes=[r_Y])

        fwd_pass(Z, Z, r_Z, ftabU, r_ftabU, u_epilogue, extra_w=(r_wch, [r_MQ]))

        itab = [AV(R_T + 48 * K_, [4, 2, 512], BF16), AV(R_MQ, [4, 2, 512], BF16)]
        r_itab = r_ftabU
        _it = 0
        for tt in range(4):
            fn_all = []
            for fg in range(4):
                si = _it % 2
                _it += 1
                S.dma("sp", itab[si].rearrange("p a b c -> p (a b c)"), inv_tab_d[tt, fg], writes=[r_itab[si]], key="itab%d" % si)
                fns = []
                for fi in range(4):
                    fc = fg * 4 + fi
                    for ri in range(2):
                        Ysrc = YRE if ri == 0 else YIM
                        for cc in range(6):
                            first = (fc == 0 and ri == 0)
                            lastm = (fc == 15 and ri == 1)
                            fns.append(lambda e, cc=cc, Ysrc=Ysrc, fc=fc, si=si, fi=fi, ri=ri, first=first, lastm=lastm: e.matmul(
                                bank(cc), Ysrc[:, fc, cc * 128:(cc + 1) * 128], itab[si][:, fi, ri, :], start=first, stop=lastm))
                S.op("pe", fns, reads=[r_itab[si], r_Y], writes=[PB[cc] for cc in range(6)])
            for cc in range(6):
                S.op("dve", [lambda e, cc=cc, tt=tt: e.tensor_tensor(out=YT[:, cc, tt * 512:(tt + 1) * 512], in0=bank(cc),
                                                                    in1=YT[:, cc, tt * 512:(tt + 1) * 512], op=ALU.mult)],
                     reads=[PB[cc], r_YT], writes=[r_YT])

        if debug == "mix0":
            for c in range(8):
                src = YT[:, c, 0:1024] if c < 6 else YM[:, c - 6, 0:1024]
                S.op("act", [lambda e, src=src: e.activation(out=rbuf[0], in_=src, func=AF.Identity)], reads=[r_YT, r_YM], writes=[r_rbuf[0]])
                S.dma("sp", dbg_d[c * 128:(c + 1) * 128, :], rbuf[0], reads=[r_rbuf[0]], key="dbg")
            S.barrier()
            S.emit()
            return nc

        out_proj_ln1(0)
        S.barrier()
        if debug == "ln1_0":
            for tile in range(16):
                S.dma("sp", dbg_d[tile * 128:(tile + 1) * 128, :], X[:, tile, :], reads=[r_X[tile]], key="dbg")
            S.barrier()
            S.emit()
            return nc
        ffn_ln2(0, final=(debug == "l0"))
        S.barrier()

        if debug is None or debug.startswith("l1"):
            if debug == "l1_s":
                S.barrier()
                S.emit()
                return nc
            w1v = w_in_d[1].rearrange("(kc p) n -> p kc n", p=128)
            QT = YT
            r_QT = [Res("QT%d" % i) for i in range(12)]
            KT = AV(R_T, [4, 2048], BF16)
            r_KT = Res("KT")
            VT = AV(R_T + 16 * K_, [16, 256], BF16)
            r_VT = Res("VT")
            ropec = AV(R_T + 24 * K_, [2048], F32)
            ropes = AV(R_T + 32 * K_, [2048], F32)
            r_rope = Res("rope")
            wch1 = [AV(R_T + 40 * K_ + i * 2 * K_, [8, 128], BF16) for i in range(3)]
            r_wch1 = [Res("wch1_%d" % i) for i in range(3)]
            qsb = [AV(R_T + 46 * K_ + i * K_, [512], BF16) for i in range(2)]
            r_qsb = [Res("qsb%d" % i) for i in range(2)]
            rt1 = AV(R_T + 48 * K_, [512], F32)
            rt2 = AV(R_T + 50 * K_, [512], F32)
            r_rt1, r_rt2 = Res("rt1"), Res("rt2")
            wvt = AV(R_T + 52 * K_, [8, 256], BF16)
            r_wvt = Res("wvt")
            esk = small("esk", [64, 12], F32)
            r_esk = Res("esk")
            import os
            SK = os.environ.get("SKIP", "")
            if "r" not in SK:
                S.dma("sp", ropec, ropec_d[:, :], writes=[r_rope], key="rope0")
                S.dma("sp", ropes, ropes_d[:, :], writes=[r_rope], key="rope1")
            if "e" not in SK:
                S.dma("sp", esk[:], sink_d.partition_broadcast(64), writes=[r_esk], key="esk")
            if "x" not in SK:
                S.op("act", [lambda e: e.activation(out=esk[:], in_=esk[:], func=(AF.Identity if "I" in SK else AF.Exp))], reads=[r_esk], writes=[r_esk])
            if "w" not in SK:
                S.dma("pool", wvt, w1v[:, :, 1024:1280], writes=[r_wvt], key="wvt")
            if debug == "l1_p0":
                S.barrier()
                S.emit()
                return nc
            _w1 = [0]
            _rq = [0]

            def proj1(loads, sink_fn):
                wi = _w1[0] % 3
                _w1[0] += 1
                for k_, (dst0, dst1, c0, c1) in enumerate(loads):
                    S.dma("pool", wch1[wi][:, :, dst0:dst1], w1v[:, :, c0:c1], writes=[r_wch1[wi]], key="wch1_%d_%d" % (wi, k_))
                for tq in range(4):
                    b = nxt([0, 1, 2, 3])
                    mm_group(bank(b), [(wch1[wi][:, kc, :], XT[:, kc, tq * 512:(tq + 1) * 512]) for kc in range(8)],
                             reads=[r_wch1[wi], r_XT], writes=[PB[b]])
                    sink_fn(tq, b)

            def rope_sink(dest, r_dest):
                def sink(tq, b):
                    cs = slice(tq * 512, (tq + 1) * 512)
                    i = _rq[0] % 2
                    _rq[0] += 1
                    S.op("act", [lambda e: e.activation(out=qsb[i], in_=bank(b), func=AF.Identity)],
                         reads=[PB[b]], writes=[r_qsb[i]])
                    b2 = [4, 5][i]
                    mm_group(bank(b2), [(pm[:], qsb[i])], reads=[r_consts, r_qsb[i]], writes=[PB[b2]])
                    S.op("dve", [lambda e: e.tensor_tensor(out=rt1, in0=bank(b), in1=ropec[:, cs], op=ALU.mult)],
                         reads=[PB[b], r_rope], writes=[r_rt1])
                    S.op("dve", [lambda e: e.tensor_tensor(out=rt2, in0=bank(b2), in1=ropes[:, cs], op=ALU.mult)],
                         reads=[PB[b2], r_rope], writes=[r_rt2])
                    S.op("dve", [lambda e: e.tensor_tensor(out=dest[:, cs], in0=rt1, in1=rt2, op=ALU.add)],
                         reads=[r_rt1, r_rt2], writes=r_dest)
                return sink

            for c in range(6):
                proj1([(0, 128, c * 128, (c + 1) * 128)], rope_sink(QT[:, c, :], [r_QT[2 * c], r_QT[2 * c + 1]]))
            if debug == "l1_p1":
                S.barrier()
                S.emit()
                return nc
            for g in range(4):
                c0 = 768 + g * 64
                proj1([(0, 64, c0, c0 + 64), (64, 128, c0, c0 + 64)], rope_sink(KT[:, g, :], [r_KT]))
            if debug == "l1_p2":
                S.barrier()
                S.emit()
                return nc
            for hp in range(2):
                def sink_mq1(tq, b, hp=hp):
                    act_copy(MQ[:, hp, tq * 512:(tq + 1) * 512], bank(b), [PB[b]], [r_MQ])
                proj1([(0, 128, 1280 + hp * 128, 1280 + (hp + 1) * 128)], sink_mq1)
            for tile in range(16):
                b = nxt([0, 1, 2, 3])
                mm_group(bank(b, 256), [(XT[:, kc, tile * 128:(tile + 1) * 128], wvt[:, kc, :]) for kc in range(8)],
                         reads=[r_XT, r_wvt], writes=[PB[b]])
                act_copy(VT[:, tile, :], bank(b, 256), [PB[b]], [r_VT])
            S.barrier()

            if debug == "l1_p":
                S.barrier()
                S.emit()
                return nc
            PTs = [AV(R_XT + i * 12 * K_, [16, 384], BF16) for i in range(2)]
            r_PTs = [Res("PTs%d" % i) for i in range(2)]
            rden1s = [AV(R_XT + 24 * K_ + i * 2 * K_, [512], F32) for i in range(2)]
            r_rden1s = [Res("rden1_%d" % i) for i in range(2)]
            def st_exp(h, j):
                g = h // 3
                hh = h % 2
                c = h // 2
                prow = slice(hh * 64, (hh + 1) * 64)
                pt = PTs[h % 2]
                r_pt = r_PTs[h % 2]
                qlo = max(0, j - 1) * 128
                qhi = min(16, j + 2) * 128
                n = qhi - qlo
                moff = qlo - (j - 1) * 128
                b = nxt([0, 1, 2, 3])
                fns = [
                    lambda e: e.matmul(bank(b, n), KT[prow, g, j * 128:(j + 1) * 128], QT[prow, c, qlo:qhi], start=True, stop=False),
                    lambda e: e.matmul(bank(b, n), ident[:], maskb[:, moff:moff + n], start=False, stop=True),
                ]
                S.op("pe", fns, reads=[r_KT, r_QT[h], r_consts], writes=[PB[b]])
                S.op("act", [lambda e: e.activation(out=pt[:, j, 0:n], in_=bank(b, n), func=AF.Exp, scale=0.125)],
                     reads=[PB[b]], writes=[r_pt])

            def pv_norm(h, qt):
                g = h // 3
                hh = h % 2
                c = h // 2
                prow = slice(hh * 64, (hh + 1) * 64)
                pt = PTs[h % 2]
                r_pt = r_PTs[h % 2]
                bo, bd = [4, 6][qt % 2], [5, 7][qt % 2]
                fo, fd = [], []
                for i4 in range(4):
                    qb = 4 * qt + i4
                    js = [j for j in (qb - 1, qb, qb + 1) if 0 <= j < 16]
                    for k_, j in enumerate(js):
                        lc = (qb - max(0, j - 1)) * 128
                        st_, sp_ = (k_ == 0), (k_ == len(js) - 1)
                        fo.append(lambda e, i4=i4, j=j, lc=lc, st_=st_, sp_=sp_: e.matmul(
                            ps[0:64, bo * 512 + i4 * 128:bo * 512 + (i4 + 1) * 128], VT[:, j, g * 64:(g + 1) * 64],
                            pt[:, j, lc:lc + 128], start=st_, stop=sp_))
                        fd.append(lambda e, i4=i4, j=j, lc=lc, st_=st_, sp_=sp_: e.matmul(
                            ps[0:64, bd * 512 + i4 * 128:bd * 512 + (i4 + 1) * 128], ones64[:],
                            pt[:, j, lc:lc + 128], start=st_, stop=sp_))
                S.op("pe", fo, reads=[r_pt, r_VT], writes=[PB[bo]])
                S.op("pe", fd, reads=[r_pt, r_consts], writes=[PB[bd]])
                rd = rden1s[qt % 2]
                r_rd = r_rden1s[qt % 2]
                S.op("act", [lambda e: e.activation(out=rd[0:64, :], in_=bank(bd, 512, 0, 64), func=AF.Identity,
                                                    bias=esk[:, h:h + 1])],
                     reads=[PB[bd], r_esk], writes=[r_rd])
                S.op("dve", [lambda e: e.reciprocal(out=rd[0:64, :], in_=rd[0:64, :])], reads=[r_rd], writes=[r_rd])
                S.op("dve", [lambda e: e.tensor_tensor(
                    out=QT[prow, c, qt * 512:(qt + 1) * 512], in0=bank(bo, 512, 0, 64), in1=rd[0:64, :], op=ALU.mult)],
                     reads=[PB[bo], r_rd], writes=[r_QT[h]])

            for h in range(13):
                for qt in range(4):
                    if h < 12:
                        for j in range(4 * qt, 4 * qt + 4):
                            st_exp(h, j)
                    if h >= 1:
                        pv_norm(h - 1, qt)
            S.barrier()
            if debug == "l1_a":
                S.barrier()
                S.emit()
                return nc
            mem_attention(R_XT)
            S.barrier()
            if debug == "l1mix":
                for c in range(8):
                    src = YT[:, c, 0:1024] if c < 6 else YM[:, c - 6, 0:1024]
                    S.op("act", [lambda e, src=src: e.activation(out=rbuf[0], in_=src, func=AF.Identity)], reads=[r_YM], writes=[r_rbuf[0]])
                    S.dma("sp", dbg_d[c * 128:(c + 1) * 128, :], rbuf[0], reads=[r_rbuf[0]], key="dbg")
                S.barrier()
                S.emit()
                return nc
            out_proj_ln1(1)
            S.barrier()
            ffn_ln2(1, final=True)
            S.barrier()

        S.barrier()
        S.emit()
    return nc


def prep_shared(inputs):
    f32 = np.float32
    sh = {}
    for k in ("w_mem_kv", "l0_w_in", "l1_w_in", "l0_w_out", "l1_w_out", "l0_ffn_w_up", "l1_ffn_w_up",
              "l0_ffn_w_down", "l1_ffn_w_down", "l0_filt_w1", "l0_filt_w2", "l0_filt_w3", "l0_filt_w_out",
              "l0_hyena_d", "l1_sink"):
        sh[k] = np.ascontiguousarray(np.asarray(inputs[k], dtype=f32))
    for i in range(2):
        for n in ("ln1_g", "ln1_b", "ln2_g", "ln2_b"):
            sh["l%d_%s" % (i, n)] = np.ascontiguousarray(np.asarray(inputs["l%d_%s" % (i, n)], dtype=f32))
        cw = np.asarray(inputs["l%d_ffn_conv_w" % i], f32)
        cb = np.asarray(inputs["l%d_ffn_conv_b" % i], f32)
        a = np.concatenate([cw, cb[None, :]], axis=0)
        sh["l%d_fcwb" % i] = np.ascontiguousarray(a.reshape(4, 44, 128).transpose(2, 1, 0))
    cw = np.asarray(inputs["l0_conv_w"], f32)
    cb = np.asarray(inputs["l0_conv_b"], f32)
    a = np.concatenate([cw, cb[None, :]], axis=0)
    sh["l0_cwb"] = np.ascontiguousarray(a.reshape(4, 18, 128).transpose(2, 1, 0))
    sh["l0_fbf"] = np.ascontiguousarray(np.stack(
        [np.asarray(inputs["l0_filt_%s%d" % (n, l)], f32) for l in (1, 2, 3) for n in ("b", "f")], axis=1))
    sh.update(const_tables())
    return sh


_NC_CACHE = {}


def kernel(**inputs):
    sh = prep_shared(inputs)
    x = np.asarray(inputs["x"], np.float32)
    mem = np.asarray(inputs["mem"], np.float32)
    if "nc" not in _NC_CACHE:
        _NC_CACHE["nc"] = build_program()
    nc = _NC_CACHE["nc"]
    in_maps = []
    for b in range(8):
        m = dict(sh)
        m["x"] = np.ascontiguousarray(x[b])
        m["mem"] = np.ascontiguousarray(mem[b])
        in_maps.append(m)
    res = run_bass_kernel_spmd(nc, in_maps, core_ids=list(range(8)))
    return np.stack([np.asarray(r["out"], np.float32) for r in res.results], axis=0)
```

```python
import math
from contextlib import ExitStack

import numpy as np
import ml_dtypes
import concourse.bass as bass
import concourse.mybir as mybir
from concourse.bass_utils import run_bass_kernel_spmd

F32 = mybir.dt.float32
BF16 = mybir.dt.bfloat16
AF = mybir.ActivationFunctionType
ALU = mybir.AluOpType
AX = mybir.AxisListType

L = 2048
D = 1024
NT = 16
KC = 8
DFF = 2816
NFF = 22
ALPHA = 4.0 ** 0.25
EPS = 1e-5
PI = float(np.pi)


class Res:
    __slots__ = ("name", "w", "r", "excl")

    def __init__(self, name, excl=False):
        self.name = name
        self.w = None
        self.r = []
        self.excl = excl


class Sched:
    def __init__(self, nc, es):
        self.nc = nc
        self.es = es
        self.engs = {}
        for n in ("pe", "act", "dve", "pool", "sp"):
            sem = es.enter_context(nc.semaphore("s_" + n))
            self.engs[n] = dict(sem=sem, count=0, known={}, ops=[])
        self.dma_sems = {}
        self.n_dma_sems = 0

    def dma_sem(self, key):
        if key not in self.dma_sems:
            sem = self.es.enter_context(self.nc.semaphore("d%d" % self.n_dma_sems))
            self.n_dma_sems += 1
            self.dma_sems[key] = [sem, 0]
        return self.dma_sems[key]

    def op(self, eng, fns, reads=(), writes=(), dma_key=None):
        E = self.engs[eng]
        deps = {}

        def add(ev):
            if ev is None:
                return
            sem, val = ev
            if deps.get(sem, 0) < val:
                deps[sem] = val

        excl_reads = [r for r in reads if r.excl]
        writes = list(writes) + [r for r in excl_reads if r not in writes]
        reads = [r for r in reads if not r.excl]
        for r in reads:
            add(r.w)
        for w in writes:
            add(w.w)
            for ev in w.r:
                add(ev)
        waits = []
        for sem, val in deps.items():
            if sem is E["sem"] and eng == "pe" and dma_key is None:
                continue
            if E["known"].get(sem, 0) >= val:
                continue
            E["known"][sem] = val
            waits.append((sem, val))
        if dma_key is not None:
            ds = self.dma_sem(dma_key)
            ds[1] += 16
            ev = (ds[0], ds[1])
            inc = (ds[0], 16)
        else:
            E["count"] += 1
            ev = (E["sem"], E["count"])
            inc = (E["sem"], 1)
        for r in reads:
            r.r.append(ev)
        for w in writes:
            w.w = ev
            w.r = []
        if not isinstance(fns, (list, tuple)):
            fns = [fns]
        E["ops"].append((waits, list(fns), inc))
        return ev

    def dma(self, queue, out, in_, reads=(), writes=(), key=None):
        assert key is not None
        return self.op(queue, [lambda e: e.dma_start(out=out, in_=in_)], reads, writes, dma_key=key)

    def barrier(self):
        evs = [(E["sem"], E["count"]) for E in self.engs.values() if E["count"] > 0]
        evs += [(s, c) for (s, c) in self.dma_sems.values() if c > 0]
        for n, E in self.engs.items():
            waits = []
            for sem, val in evs:
                if E["known"].get(sem, 0) >= val:
                    continue
                if sem is E["sem"] and n == "pe":
                    continue
                E["known"][sem] = val
                waits.append((sem, val))
            if waits:
                E["ops"].append((waits, [], None))

    def emit(self):
        nc = self.nc

        def run(name):
            def f(e):
                for waits, fns, inc in self.engs[name]["ops"]:
                    for sem, val in waits:
                        e.wait_ge(sem, val)
                    n = len(fns)
                    for i, fn in enumerate(fns):
                        ins = fn(e)
                        if i == n - 1:
                            ins.then_inc(inc[0], inc[1])
            return f

        with nc.Block() as block:
            block.tensor(run("pe"))
            block.scalar(run("act"))
            block.vector(run("dve"))
            block.gpsimd(run("pool"))
            block.sync(run("sp"))


def _bf(a):
    return np.ascontiguousarray(a.astype(ml_dtypes.bfloat16))


_CONST_CACHE = {}


def const_tables():
    if _CONST_CACHE:
        return _CONST_CACHE
    N = 4096
    f = np.arange(2048, dtype=np.float64)
    t = np.arange(2048, dtype=np.float64)
    m = np.mod(np.outer(2 * f + 1, t), 2 * N)
    ang = np.pi * m / N
    C = np.cos(ang)
    Sn = np.sin(ang)
    CT = C.T.reshape(16, 128, 16, 128)
    ST = (-Sn).T.reshape(16, 128, 16, 128)
    fwd = np.stack([CT, ST], axis=0)
    fwd = fwd.transpose(3, 2, 0, 1, 4)
    _CONST_CACHE["fwd_tab"] = _bf(fwd.reshape(16, 128, 2 * 16 * 128))
    Ci = (C / 2048.0).reshape(4, 4, 128, 4, 512)
    Si = (-Sn / 2048.0).reshape(4, 4, 128, 4, 512)
    inv = np.stack([Ci, Si], axis=0)
    inv = inv.transpose(4, 1, 3, 2, 0, 5)
    _CONST_CACHE["inv_tab"] = _bf(inv.reshape(4, 4, 128, 4 * 2 * 512))
    f32 = np.float32
    tl = np.linspace(0.0, 1.0, L, dtype=f32)[:, None]
    w = (f32(2.0 * math.pi) * np.arange(L, dtype=f32)[:, None] / f32(L)).astype(f32)
    fr = np.linspace(1e-4, 15, 16, dtype=f32)[None, :]
    z = np.concatenate([tl, np.cos(fr * w), -np.sin(fr * w)], axis=-1).astype(f32)
    _CONST_CACHE["zfT"] = np.ascontiguousarray(z.T)
    _CONST_CACHE["ntn"] = np.ascontiguousarray((-tl[:, 0]).reshape(16, 128).T.astype(f32))
    min_decay = math.log(1e-2) / 1.5
    max_decay = math.log(1e-2) / 0.3
    _CONST_CACHE["deltas"] = np.abs(np.linspace(min_decay, max_decay, 768, dtype=f32)).astype(f32)
    inv_f = (10000.0 ** (-np.arange(0, 64, 2, dtype=f32) / f32(64))).astype(f32)
    angr = (np.arange(L, dtype=f32)[:, None] * inv_f[None, :]).astype(f32)
    angr = np.concatenate([angr, angr], axis=-1)
    cosT = np.cos(angr).T.astype(f32)
    sinT = np.sin(angr).T.astype(f32)
    _CONST_CACHE["ropec"] = np.ascontiguousarray(np.concatenate([cosT, cosT], axis=0))
    _CONST_CACHE["ropes"] = np.ascontiguousarray(np.concatenate([sinT, sinT], axis=0))
    Pm = np.zeros((128, 128), np.float32)
    for po in range(128):
        d = po % 64
        if d < 32:
            Pm[po + 32, po] = -1.0
        else:
            Pm[po - 32, po] = 1.0
    _CONST_CACHE["pm"] = _bf(Pm)
    _CONST_CACHE["ident"] = _bf(np.eye(128, dtype=np.float32))
    k = np.arange(128)[:, None]
    q = np.arange(128)[None, :]
    NEG = -30000.0
    m_next = np.where(k <= q, 0.0, NEG)
    m_prev = np.where(k >= q, 0.0, NEG)
    _CONST_CACHE["maskb"] = _bf(np.concatenate([m_next, np.zeros((128, 128)), m_prev], axis=1))
    return _CONST_CACHE


def build_program(debug=None):
    nc = bass.Bass("TRN2", target_bir_lowering=False)
    dbg = {}

    def din(name, shape, dt=F32):
        return nc.dram_tensor(name, list(shape), dt, kind="ExternalInput").ap()

    x_d = din("x", [L, D])
    mem_d = din("mem", [256, D])
    wkv_d = din("w_mem_kv", [D, 512])
    w_in_d = [din("l0_w_in", [D, 2560]), din("l1_w_in", [D, 1536])]
    w_out_d = [din("l0_w_out", [D, D]), din("l1_w_out", [D, D])]
    w_up_d = [din("l%d_ffn_w_up" % i, [D, 2 * DFF]) for i in range(2)]
    w_dn_d = [din("l%d_ffn_w_down" % i, [DFF, D]) for i in range(2)]
    ln_d = [[din("l%d_%s" % (i, n), [D]) for n in ("ln1_g", "ln1_b", "ln2_g", "ln2_b")] for i in range(2)]
    fcwb_d = [din("l%d_fcwb" % i, [128, 44, 4]) for i in range(2)]
    cwb_d = din("l0_cwb", [128, 18, 4])
    fw1_d = din("l0_filt_w1", [33, 64])
    fw2_d = din("l0_filt_w2", [64, 64])
    fw3_d = din("l0_filt_w3", [64, 64])
    fwo_d = din("l0_filt_w_out", [64, 1536])
    fbf_d = din("l0_fbf", [64, 6])
    hd_d = din("l0_hyena_d", [768])
    sink_d = din("l1_sink", [12])
    fwd_tab_d = din("fwd_tab", [16, 128, 4096], BF16)
    inv_tab_d = din("inv_tab", [4, 4, 128, 4096], BF16)
    zfT_d = din("zfT", [33, L])
    ntn_d = din("ntn", [128, 16])
    deltas_d = din("deltas", [768])
    ropec_d = din("ropec", [128, L])
    ropes_d = din("ropes", [128, L])
    pm_d = din("pm", [128, 128], BF16)
    ident_d = din("ident", [128, 128], BF16)
    maskb_d = din("maskb", [128, 384], BF16)
    out_d = nc.dram_tensor("out", [L, D], F32, kind="ExternalOutput").ap()
    if debug:
        dbg_d = nc.dram_tensor("dbg", [L, D], F32, kind="ExternalOutput").ap()

    es = ExitStack()
    with es:
        S = Sched(nc, es)
        AR_BYTES = 194 * 1024
        arena = es.enter_context(nc.sbuf_tensor("arena", [128, AR_BYTES // 2], BF16))
        ps = es.enter_context(nc.psum_tensor("ps", [128, 4096], F32))
        PB = [Res("pb%d" % i, excl=True) for i in range(8)]

        def bank(b, n=512, p0=0, p1=128):
            return ps[p0:p1, b * 512:b * 512 + n]

        def bank_bf(b):
            return ps[:, b * 512:(b + 1) * 512].bitcast(BF16)

        def AV(off, shape, dt):
            n = int(np.prod(shape))
            if dt == F32:
                v = arena[:, off // 2: off // 2 + 2 * n].bitcast(F32)
            else:
                v = arena[:, off // 2: off // 2 + n]
            if len(shape) == 2:
                v = v.rearrange("p (a b) -> p a b", a=shape[0])
            elif len(shape) == 3:
                v = v.rearrange("p (a b c) -> p a b c", a=shape[0], b=shape[1])
            elif len(shape) == 4:
                v = v.rearrange("p (a b c d) -> p a b c d", a=shape[0], b=shape[1], c=shape[2])
            return v

        K_ = 1024
        R_X, R_XT, R_Y, R_YM, R_MQ, R_T = 0, 64 * K_, 96 * K_, 120 * K_, 128 * K_, 136 * K_

        def small(name, shape, dt):
            return es.enter_context(nc.sbuf_tensor("sb_" + name, list(shape), dt))

        ident = small("ident", [128, 128], BF16)
        pm = small("pm", [128, 128], BF16)
        maskb = small("maskb", [128, 384], BF16)
        ones64 = small("ones64", [128, 64], BF16)
        memKT = small("memKT", [128, 2, 256], BF16)
        memV = small("memV", [128, 2, 256], BF16)
        fcwb = small("fcwb", [128, 44, 4], F32)
        cwb = fcwb[:, 0:18, :]
        stat = small("stat", [128, 128], F32)
        GB = small("GB", [128, 2, 1024], F32)
        r_consts = Res("consts")
        r_memKV = Res("memKV")
        r_cwb = Res("cwb")
        r_fcwb = Res("fcwb")
        r_GB = Res("GB")

        X = AV(R_X, [16, 1024], F32)
        XT = AV(R_XT, [8, 2048], BF16)
        YT = AV(R_Y, [6, 2048], BF16)
        YM = AV(R_YM, [2, 2048], BF16)
        MQ = AV(R_MQ, [2, 2048], BF16)
        r_X = [Res("X%d" % i) for i in range(16)]
        r_XT = Res("XT")
        r_YT = Res("YT")
        r_YM = Res("YM")
        r_MQ = Res("MQ")

        _bk = [0]

        def nxt(lst):
            b = lst[_bk[0] % len(lst)]
            _bk[0] += 1
            return b

        def mm_group(out, pairs, reads, writes):
            n = len(pairs)
            fns = []
            for i, (l, r) in enumerate(pairs):
                fns.append(lambda e, l=l, r=r, i=i: e.matmul(out, l, r, start=(i == 0), stop=(i == n - 1)))
            return S.op("pe", fns, reads, writes)

        def act_copy(out, in_, reads, writes):
            return S.op("act", [lambda e: e.activation(out=out, in_=in_, func=AF.Identity)], reads, writes)

        S.op("dve", [lambda e: e.memset(ones64[:], 1.0)], writes=[r_consts])
        epst = small("epst", [128, 1], F32)
        identf = small("identf", [128, 128], F32)
        aident = small("aident", [128, 128], F32)
        S.op("dve", [lambda e: e.memset(epst[:], EPS)], writes=[r_consts])

        HS = AV(R_X, [2, 16, 768], BF16)
        r_HS = Res("HS")
        hA = AV(R_X + 48 * K_, [2048], F32)
        hB = AV(R_X + 56 * K_, [2048], F32)
        r_hA, r_hB = Res("hA"), Res("hB")
        zf = AV(R_Y, [2048], F32)
        fw = AV(R_Y + 8 * K_, [3, 64], F32)
        fwo = AV(R_Y + 9 * K_, [1536], F32)
        fbf = AV(R_Y + 15 * K_, [8], F32)
        dbc = AV(R_Y + 16 * K_, [768], F32)
        dlt = AV(R_YM, [768], F32)
        dec = AV(R_YM + 3 * K_, [768], F32)
        ntn = AV(R_YM + 6 * K_, [16], F32)
        fsb = AV(R_MQ, [768], F32)
        wtmp = AV(R_MQ + 3 * K_, [512], F32)
        wtm2 = AV(R_MQ + 5 * K_, [512], F32)
        r_f = Res("filt_in")
        r_dec, r_fsb, r_wtmp, r_wtm2, r_dbc = Res("dec"), Res("fsb"), Res("wtmp"), Res("wtm2"), Res("dbc")
        S.dma("sp", zf[0:33, :], zfT_d[:, :], writes=[r_f], key="f0")
        S.dma("sp", fw[0:33, 0, :], fw1_d[:, :], writes=[r_f], key="f1")
        S.dma("sp", fw[0:64, 1, :], fw2_d[:, :], writes=[r_f], key="f2")
        S.dma("sp", fw[0:64, 2, :], fw3_d[:, :], writes=[r_f], key="f3")
        S.dma("sp", fwo[0:64, :], fwo_d[:, :], writes=[r_f], key="f4")
        S.dma("sp", fbf[0:64, 0:6], fbf_d[:, :], writes=[r_f], key="f5")
        S.dma("sp", dbc, hd_d.partition_broadcast(128), writes=[r_dbc], key="f6")
        S.dma("sp", dlt, deltas_d.partition_broadcast(128), writes=[r_f], key="f7")
        S.dma("sp", ntn, ntn_d[:, :], writes=[r_f], key="f8")
        S.dma("sp", ident[:], ident_d[:, :], writes=[r_consts], key="c0")
        S.dma("sp", pm[:], pm_d[:, :], writes=[r_consts], key="c1")
        S.dma("sp", maskb[:], maskb_d[:, :], writes=[r_consts], key="c2")
        S.dma("sp", cwb, cwb_d[:, :, :], writes=[r_cwb], key="c3")
        S.op("act", [lambda e: e.activation(out=identf[:], in_=ident[:], func=AF.Identity)], reads=[r_consts], writes=[r_consts])
        S.op("act", [lambda e: e.activation(out=aident[:], in_=ident[:], func=AF.Identity, scale=ALPHA)], reads=[r_consts], writes=[r_consts])

        def transpose_to_XT(src_bf, tile, r_src):
            b = nxt([6, 7])
            bb = bank_bf(b)
            fns = [lambda e, kc=kc: e.transpose(bb[:, kc * 128:(kc + 1) * 128], src_bf[:, kc * 128:(kc + 1) * 128], ident[:])
                   for kc in range(8)]
            S.op("pe", fns, reads=[r_src, r_consts], writes=[PB[b]])
            S.op("act", [lambda e: e.activation(out=XT[:, :, tile * 128:(tile + 1) * 128],
                                                in_=bb.rearrange("p (k m) -> p k m", k=8), func=AF.Identity)],
                 reads=[PB[b]], writes=[r_XT])

        def load_ln(layer, which):
            g_d, b_d = ln_d[layer][2 * which], ln_d[layer][2 * which + 1]
            S.dma("sp", GB[:, 0, :], g_d.partition_broadcast(128), writes=[r_GB], key="gb0")
            S.dma("sp", GB[:, 1, :], b_d.partition_broadcast(128), writes=[r_GB], key="gb1")

        NRB = 6
        LN_LAG = 3
        _rboff = [46 * K_, 50 * K_, 28672, 28672 + 4096, 36880, 36880 + 4096]
        rbuf = [AV(R_T + _rboff[i], [1024], F32) for i in range(NRB)]
        _xboff = [54 * K_, 56 * K_, 24 * K_, 26 * K_]
        xbuf = [AV(R_T + _xboff[i], [1024], BF16) for i in range(4)]
        r_rbuf = [Res("rbuf%d" % i) for i in range(NRB)]
        r_xbuf = [Res("xbuf%d" % i) for i in range(4)]
        r_stat = Res("stat")
        r_stats = [Res("stat%d" % i) for i in range(8)]
        _ln = [0]

        class LNPipe:
            def __init__(self, final_out):
                self.final_out = final_out
                self.q = []

            def push(self, tile, pin, r_pin, r_in, r_r):
                ri = tile % NRB
                so = ri * 16
                r_stat = r_stats[ri]
                st6 = stat[:, so:so + 12]
                mv = stat[:, so + 12:so + 14]
                rstd = stat[:, so + 14:so + 15]
                nmr = stat[:, so + 15:so + 16]
                final_out = self.final_out

                def s1():
                    S.op("dve", [lambda e: e.bn_stats(out=st6[:, 0:6], in_=pin[:, 0:512])], reads=[r_pin[0]], writes=[r_stat])
                    S.op("dve", [lambda e: e.bn_stats(out=st6[:, 6:12], in_=pin[:, 512:1024])], reads=[r_pin[1]], writes=[r_stat])
                    S.op("dve", [lambda e: e.bn_aggr(out=mv, in_=st6)], reads=[r_stat], writes=[r_stat])

                def s2():
                    S.op("act", [lambda e: e.activation(out=rstd, in_=mv[:, 1:2], func=AF.Sqrt, bias=epst[:, 0:1])],
                         reads=[r_stat, r_consts], writes=[r_stat])

                def s3():
                    S.op("dve", [lambda e: e.reciprocal(out=rstd, in_=rstd)], reads=[r_stat], writes=[r_stat])
                    S.op("dve", [lambda e: e.scalar_tensor_tensor(out=nmr, in0=mv[:, 0:1], scalar=-1.0, in1=rstd,
                                                                  op0=ALU.mult, op1=ALU.mult)], reads=[r_stat], writes=[r_stat])

                def s4():
                    S.op("act", [lambda e: e.activation(out=r_in, in_=pin, func=AF.Identity, scale=rstd, bias=nmr)],
                         reads=list(r_pin) + [r_stat], writes=[r_r])

                def s5():
                    S.op("dve", [lambda e: e.tensor_tensor(out=r_in, in0=r_in, in1=GB[:, 0, :], op=ALU.mult)],
                         reads=[r_r, r_GB], writes=[r_r])
                    S.op("pool", [lambda e: e.tensor_tensor(out=X[:, tile, :], in0=r_in, in1=GB[:, 1, :], op=ALU.add)],
                         reads=[r_r, r_GB], writes=[r_X[tile]])

                def s6():
                    if final_out:
                        S.dma("sp", out_d[tile * 128:(tile + 1) * 128, :], X[:, tile, :], reads=[r_X[tile]], key="out%d" % (tile % 4))
                    else:
                        i4 = tile % 4
                        S.op("act", [lambda e: e.activation(out=xbuf[i4], in_=X[:, tile, :], func=AF.Identity)],
                             reads=[r_X[tile]], writes=[r_xbuf[i4]])

                def s7():
                    if not final_out:
                        i4 = tile % 4
                        transpose_to_XT(xbuf[i4], tile, r_xbuf[i4])

                s1()
                for d_, f_ in ((1, s2), (1, s3), (2, s4), (3, s5), (4, s6), (5, s7)):
                    self.q.append([d_, f_])
                self._tick()

            def _tick(self):
                keep = []
                for item in self.q:
                    item[0] -= 0
                ready = [it for it in self.q if it[0] <= 0]
                for it in ready:
                    it[1]()
                self.q = [it for it in self.q if it[0] > 0]
                for it in self.q:
                    it[0] -= 1

            def flush(self):
                while self.q:
                    self._tick()

        def mem_attention(toff):
            PT = [AV(toff + i * K_, [512], BF16) for i in range(8)]
            r_PT = [Res("mpt%d" % i) for i in range(8)]
            rdens = [AV(toff + 8 * K_ + i * 2 * K_, [512], F32) for i in range(2)]
            r_rdens = [Res("mrden%d" % i) for i in range(2)]
            its = [(hp, qt, hh) for hp in range(2) for qt in range(4) for hh in range(2)]

            def stage1(n):
                hp, qt, hh = its[n]
                qs = slice(qt * 512, (qt + 1) * 512)
                prow = slice(hh * 64, (hh + 1) * 64)
                for mt in range(2):
                    b = nxt([0, 1, 2, 3])
                    mm_group(bank(b), [(memKT[prow, hp, mt * 128:(mt + 1) * 128], MQ[prow, hp, qs])],
                             reads=[r_memKV, r_MQ], writes=[PB[b]])
                    k = (n % 2) * 4 + mt
                    S.op("act", [lambda e, k=k, b=b: e.activation(out=PT[k], in_=bank(b), func=AF.Exp, scale=0.125)],
                         reads=[PB[b]], writes=[r_PT[k]])

            def stage2(n):
                hp, qt, hh = its[n]
                h = 2 * hp + hh
                qs = slice(qt * 512, (qt + 1) * 512)
                prow = slice(hh * 64, (hh + 1) * 64)
                pts = [(n % 2) * 4 + mt for mt in range(2)]
                bo, bd = [4, 6][n % 2], [5, 7][n % 2]
                mm_group(bank(bo, 512, 0, 64), [(memV[:, mt, h * 64:(h + 1) * 64], PT[pts[mt]]) for mt in range(2)],
                         reads=[r_memKV, r_PT[pts[0]], r_PT[pts[1]]], writes=[PB[bo]])
                mm_group(bank(bd, 512, 0, 64), [(ones64[:], PT[pts[mt]]) for mt in range(2)],
                         reads=[r_consts, r_PT[pts[0]], r_PT[pts[1]]], writes=[PB[bd]])
                rd = rdens[n % 2]
                r_rd = r_rdens[n % 2]
                S.op("dve", [lambda e: e.reciprocal(out=rd[0:64, :], in_=bank(bd, 512, 0, 64))],
                     reads=[PB[bd]], writes=[r_rd])
                S.op("dve", [lambda e: e.tensor_tensor(
                    out=YM[prow, hp, qs], in0=bank(bo, 512, 0, 64), in1=rd[0:64, :], op=ALU.mult)],
                     reads=[PB[bo], r_rd], writes=[r_YM])

            for n in range(len(its) + 1):
                if n < len(its):
                    stage1(n)
                if n >= 1:
                    stage2(n - 1)

        def out_proj_ln1(layer):
            wout = AV(R_T, [8, 1024], BF16)
            r_wout = Res("wout")
            xs = [AV(R_T + 16 * K_ + i * 4 * K_, [1024], F32) for i in range(2)]
            r_xs = [Res("xs%d" % i) for i in range(2)]
            S.dma("pool", wout, w_out_d[layer].rearrange("(kc p) n -> p kc n", p=128),
                  writes=[r_wout] + ([r_KS] if layer == 0 else []), key="wout")
            load_ln(layer, 0)
            lnp = LNPipe(False)
            for tile in range(16):
                ts_ = slice(tile * 128, (tile + 1) * 128)
                bp = [0, 2, 4][tile % 3]
                i = tile % 2
                ri = tile % NRB
                if layer == 0:
                    S.dma("sp", xs[i], x_d[ts_, :], writes=[r_xs[i]] + ([r_KS] if tile < 2 else []), key="xs%d" % i)
                    xin, rxin = xs[i], r_xs[i]
                else:
                    xin, rxin = X[:, tile, :], r_X[tile]
                for half in range(2):
                    b = bp + half
                    pairs = []
                    for kc in range(8):
                        lhs = YT[:, kc, ts_] if kc < 6 else YM[:, kc - 6, ts_]
                        pairs.append((lhs, wout[:, kc, half * 512:(half + 1) * 512]))
                    mm_group(bank(b), pairs, reads=[r_YT, r_YM, r_wout], writes=[PB[b]])
                rb = rbuf[ri]
                S.op("dve", [lambda e, xin=xin, rb=rb, bp=bp: e.scalar_tensor_tensor(
                    out=rb, in0=xin, scalar=ALPHA, in1=ps[:, bp * 512:bp * 512 + 1024], op0=ALU.mult, op1=ALU.add)],
                     reads=[rxin, PB[bp], PB[bp + 1]], writes=[r_rbuf[ri]])
                lnp.push(tile, rb, [r_rbuf[ri], r_rbuf[ri]], rb, r_rbuf[ri])
            lnp.flush()

        def ffn_ln2(layer, final):
            blocks = [4, 4, 4, 4, 3, 3]
            S.dma("sp", fcwb[:], fcwb_d[layer][:, :, :], writes=[r_fcwb], key="fcwb")
            load_ln(layer, 1)
            GT = AV(R_Y, [4, 2048], BF16)
            r_GT = Res("GT")
            wdn = [AV(R_T + i * 8 * K_, [4, 1024], BF16) for i in range(2)]
            r_wdn = [Res("wdn%d" % i) for i in range(2)]
            wup = [AV(R_T + 16 * K_ + i * 4 * K_, [8, 2, 128], BF16) for i in range(3)]
            r_wup = [Res("wup%d" % i) for i in range(3)]
            hsb = [AV(R_T + 28 * K_ + i * 8208, [2052], F32) for i in range(2)]
            r_hsb = [Res("hsb%d" % i) for i in range(2)]
            for i in range(2):
                S.op("pool", [lambda e, i=i: e.memset(hsb[i][:, 0:1], 0.0)], writes=[r_hsb[i]])
                S.op("pool", [lambda e, i=i: e.memset(hsb[i][:, 2049:2050], 0.0)], writes=[r_hsb[i]])
            acc_as = [AV(R_YM, [2048], F32), AV(R_Y + 16 * K_, [2048], F32)]
            acc_g = AV(R_MQ, [2048], F32)
            sgb = AV(R_T + 50 * K_, [2048], F32)
            r_sgb = Res("sgb")
            r_accas, r_accg = [Res("acca0"), Res("acca1")], Res("accg")
            wd_v = w_dn_d[layer]
            wu_v = w_up_d[layer].rearrange("(kc p) n -> p kc n", p=128)
            def down_proj(bi, nb, wd):
                last = bi == len(blocks) - 1
                if last:
                    S.barrier()
                lnp = LNPipe(final)
                for tile in range(16):
                    ts_ = slice(tile * 128, (tile + 1) * 128)
                    bp = [0, 2, 4][tile % 3] if last else [4, 6][tile % 2]
                    pin = ps[:, bp * 512:bp * 512 + 1024]
                    if not last:
                        for half in range(2):
                            mm_group(bank(bp + half), [(GT[:, jj, ts_], wd[:, jj, half * 512:(half + 1) * 512]) for jj in range(nb)],
                                     reads=[r_GT, r_wdn[bi % 2]], writes=[PB[bp + half]])
                        if bi == 0:
                            S.op("dve", [lambda e, tile=tile, pin=pin: e.scalar_tensor_tensor(
                                out=X[:, tile, :], in0=X[:, tile, :], scalar=ALPHA, in1=pin, op0=ALU.mult, op1=ALU.add)],
                                 reads=[PB[bp], PB[bp + 1]], writes=[r_X[tile]])
                        else:
                            S.op("dve", [lambda e, tile=tile, pin=pin: e.tensor_tensor(
                                out=X[:, tile, :], in0=X[:, tile, :], in1=pin, op=ALU.add)],
                                 reads=[PB[bp], PB[bp + 1]], writes=[r_X[tile]])
                    else:
                        for half in range(2):
                            b = bp + half
                            hsl = slice(half * 512, (half + 1) * 512)
                            fns = []
                            for jj in range(nb):
                                fns.append(lambda e, jj=jj, b=b, hsl=hsl, ts_=ts_: e.matmul(bank(b), GT[:, jj, ts_], wd[:, jj, hsl], start=(jj == 0), stop=False))
                            fns.append(lambda e, b=b, hsl=hsl, tile=tile: e.matmul(bank(b), identf[:], X[:, tile, hsl], start=False, stop=True))
                            S.op("pe", fns, reads=[r_GT, r_wdn[bi % 2], r_X[tile], r_consts], writes=[PB[b]])
                        ri = tile % NRB
                        lnp.push(tile, pin, [PB[bp], PB[bp + 1]], rbuf[ri], r_rbuf[ri])
                lnp.flush()

            j0 = 0
            deferred = None
            deferred_endblk = False
            for bi, nb in enumerate(blocks):
                wd = wdn[bi % 2]
                S.dma("pool", wd[:, 0:nb, :], wd_v[j0 * 128:(j0 + nb) * 128, :].rearrange("(j p) n -> p j n", p=128),
                      writes=[r_wdn[bi % 2]], key="wdn%d" % (bi % 2))
                for jj in range(nb):
                    j = j0 + jj
                    wi = j % 3
                    wu = wup[wi]
                    acc_a, r_acca = acc_as[j % 2], r_accas[j % 2]
                    S.dma("pool", wu[:, :, 0, :], wu_v[:, :, j * 128:(j + 1) * 128], writes=[r_wup[wi]], key="wup%da" % wi)
                    S.dma("pool", wu[:, :, 1, :], wu_v[:, :, DFF + j * 128:DFF + (j + 1) * 128], writes=[r_wup[wi]], key="wup%db" % wi)
                    for ag in range(2):
                        hs_ = hsb[ag]
                        cj = ag * NFF + j
                        for tq in range(4):
                            b = nxt([0, 1, 2, 3])
                            mm_group(bank(b), [(wu[:, kc, ag, :], XT[:, kc, tq * 512:(tq + 1) * 512]) for kc in range(8)],
                                     reads=[r_wup[wi], r_XT], writes=[PB[b]])
                            S.op("act", [lambda e, hs_=hs_, tq=tq, b=b: e.activation(
                                out=hs_[:, 1 + tq * 512:1 + (tq + 1) * 512], in_=bank(b), func=AF.Identity)],
                                 reads=[PB[b]], writes=[r_hsb[ag]])
                        if ag == 0 and deferred is not None and deferred_endblk:
                            deferred()
                            deferred = None
                        acc, r_acc = (acc_a, r_acca) if ag == 0 else (acc_g, r_accg)
                        S.op("act", [lambda e, acc=acc, hs_=hs_, cj=cj: e.activation(
                            out=acc, in_=hs_[:, 1:2049], func=AF.Identity, scale=fcwb[:, cj, 1:2], bias=fcwb[:, cj, 3:4])],
                             reads=[r_hsb[ag], r_fcwb], writes=[r_acc])
                        S.op("dve", [lambda e, acc=acc, hs_=hs_, cj=cj: e.scalar_tensor_tensor(
                            out=acc, in0=hs_[:, 0:2048], scalar=fcwb[:, cj, 0:1], in1=acc, op0=ALU.mult, op1=ALU.add)],
                             reads=[r_hsb[ag], r_fcwb, r_acc], writes=[r_acc])
                        S.op("dve", [lambda e, acc=acc, hs_=hs_, cj=cj: e.scalar_tensor_tensor(
                            out=acc, in0=hs_[:, 2:2050], scalar=fcwb[:, cj, 2:3], in1=acc, op0=ALU.mult, op1=ALU.add)],
                             reads=[r_hsb[ag], r_fcwb, r_acc], writes=[r_acc])
                        if ag == 0 and deferred is not None:
                            deferred()
                            deferred = None

                    def _fin(jj=jj, acc_a=acc_a, r_acca=r_acca, endblk=(jj == nb - 1), bi=bi, nb=nb, wd=wd):
                        S.op("act", [lambda e: e.activation(out=sgb, in_=acc_g, func=AF.Silu)], reads=[r_accg], writes=[r_sgb])
                        S.op("dve", [lambda e: e.tensor_tensor(out=GT[:, jj, :], in0=sgb, in1=acc_a, op=ALU.mult)],
                             reads=[r_sgb, r_acca], writes=[r_GT])
                        if endblk:
                            down_proj(bi, nb, wd)
                    deferred = _fin
                    deferred_endblk = (jj == nb - 1)
                j0 += nb
            if deferred is not None:
                deferred()
                deferred = None

        memf = AV(R_T + 8 * K_, [2, 1024], F32)
        memb = AV(R_T + 28 * K_, [2, 1024], BF16)
        memT = AV(R_T + 32 * K_, [8, 256], BF16)
        wkv = AV(R_T, [8, 512], BF16)
        r_memf, r_memb, r_memT, r_wkv = Res("memf"), Res("memb"), Res("memT"), Res("wkv")
        S.dma("sp", memf, mem_d.rearrange("(mt p) d -> p mt d", p=128), writes=[r_memf], key="memf")
        S.dma("pool", wkv, wkv_d.rearrange("(kc p) n -> p kc n", p=128), writes=[r_wkv], key="wkv")
        act_copy(memb, memf, [r_memf], [r_memb])
        for mt in range(2):
            b = nxt([6, 7])
            bb = bank_bf(b)
            fns = [lambda e, kc=kc, mt=mt, bb=bb: e.transpose(bb[:, kc * 128:(kc + 1) * 128], memb[:, mt, kc * 128:(kc + 1) * 128], ident[:])
                   for kc in range(8)]
            S.op("pe", fns, reads=[r_memb, r_consts], writes=[PB[b]])
            S.op("act", [lambda e, mt=mt, bb=bb: e.activation(out=memT[:, :, mt * 128:(mt + 1) * 128],
                                                             in_=bb.rearrange("p (k m) -> p k m", k=8), func=AF.Identity)],
                 reads=[PB[b]], writes=[r_memT])
        for hp in range(2):
            b = nxt([0, 1])
            mm_group(bank(b, 256), [(wkv[:, kc, hp * 128:(hp + 1) * 128], memT[:, kc, :]) for kc in range(8)],
                     reads=[r_wkv, r_memT], writes=[PB[b]])
            act_copy(memKT[:, hp, :], bank(b, 256), [PB[b]], [r_memKV])
        for mt in range(2):
            b = nxt([0, 1])
            mm_group(bank(b, 256), [(memT[:, kc, mt * 128:(mt + 1) * 128], wkv[:, kc, 256:512]) for kc in range(8)],
                     reads=[r_wkv, r_memT], writes=[PB[b]])
            act_copy(memV[:, mt, :], bank(b, 256), [PB[b]], [r_memKV])

        if debug == "s_m":
            S.barrier()
            S.emit()
            return nc
        xs0 = [AV(R_T + 16 * K_ + i * 4 * K_, [1024], F32) for i in range(2)]
        r_xs0 = [Res("xs0_%d" % i) for i in range(2)]

        def x_step(tile):
            i = tile % 2
            i4 = tile % 4
            S.dma("sp", xs0[i], x_d[tile * 128:(tile + 1) * 128, :], writes=[r_xs0[i]], key="xs%d" % i)
            S.op("act", [lambda e: e.activation(out=xbuf[i4], in_=xs0[i], func=AF.Identity)],
                 reads=[r_xs0[i]], writes=[r_xbuf[i4]])
            transpose_to_XT(xbuf[i4], tile, r_xbuf[i4])
        x_steps = [(lambda t=t: x_step(t)) for t in range(16)]
        if debug == "s_x":
            S.barrier()
            S.emit()
            return nc
        fbs = stat[0:64, 120:123]
        for l in range(3):
            S.op("dve", [lambda e, l=l: e.tensor_tensor(out=fbs[:, l:l + 1], in0=fbf[0:64, 2 * l:2 * l + 1],
                                                        in1=fbf[0:64, 2 * l + 1:2 * l + 2], op=ALU.mult)],
                 reads=[r_f], writes=[r_stat])
        srcs = [(zf, 33, r_f), (hA, 64, r_hA), (hB, 64, r_hB)]
        dsts = [(hA, r_hA), (hB, r_hB), (hA, r_hA)]
        wtW = AV(R_T + 36 * K_, [2048], F32)
        wt2W = AV(R_T + 44 * K_, [2048], F32)
        r_wtW, r_wt2W = Res("wtW"), Res("wt2W")
        dec2 = [dec, AV(R_MQ, [768], F32)]
        r_dec2 = [Res("dec0"), Res("dec1")]
        f_steps = []

        def mlp_layer(l):
            src, kk, r_src = srcs[l]
            dst, r_dst = dsts[l]
            for tq in range(4):
                cs = slice(tq * 512, (tq + 1) * 512)
                mm_group(bank(tq, 512, 0, 64), [(fw[0:kk, l, :], src[0:kk, cs])], reads=[r_f, r_src], writes=[PB[tq]])
            pall = ps[0:64, 0:2048]
            S.op("dve", [lambda e: e.tensor_scalar(out=wtW[0:64, :], in0=pall, scalar1=fbf[0:64, 2 * l + 1:2 * l + 2],
                                                   scalar2=fbs[:, l:l + 1], op0=ALU.mult, op1=ALU.add)],
                 reads=[PB[0], PB[1], PB[2], PB[3], r_f, r_stat], writes=[r_wtW])
            S.op("dve", [lambda e: e.tensor_scalar(out=wt2W[0:64, :], in0=wtW[0:64, :], scalar1=-PI, scalar2=2 * PI,
                                                   op0=ALU.is_lt, op1=ALU.mult)], reads=[r_wtW], writes=[r_wt2W])
            S.op("dve", [lambda e: e.tensor_tensor(out=wtW[0:64, :], in0=wtW[0:64, :], in1=wt2W[0:64, :], op=ALU.add)],
                 reads=[r_wtW, r_wt2W], writes=[r_wtW])
            S.op("dve", [lambda e: e.tensor_scalar(out=wt2W[0:64, :], in0=wtW[0:64, :], scalar1=PI, scalar2=-2 * PI,
                                                   op0=ALU.is_gt, op1=ALU.mult)], reads=[r_wtW], writes=[r_wt2W])
            S.op("dve", [lambda e: e.tensor_tensor(out=wtW[0:64, :], in0=wtW[0:64, :], in1=wt2W[0:64, :], op=ALU.add)],
                 reads=[r_wtW, r_wt2W], writes=[r_wtW])
            S.op("act", [lambda e: e.activation(out=dst[0:64, :], in_=wtW[0:64, :], func=AF.Sin)],
                 reads=[r_wtW], writes=[r_dst])

        def wcomb():
            S.op("dve", [lambda e: e.tensor_tensor(out=fsb[0:64, :], in0=fwo[0:64, 0:768], in1=fwo[0:64, 768:1536], op=ALU.add)],
                 reads=[r_f], writes=[r_fsb])
            S.op("dve", [lambda e: e.tensor_tensor(out=fwo[0:64, 768:1536], in0=fwo[0:64, 0:768], in1=fwo[0:64, 768:1536], op=ALU.subtract)],
                 reads=[r_f], writes=[r_f])
            S.op("dve", [lambda e: e.tensor_copy(out=fwo[0:64, 0:768], in_=fsb[0:64, :])], reads=[r_fsb], writes=[r_f])

        def hfull(tile):
            ts_ = slice(tile * 128, (tile + 1) * 128)
            b0 = 4 if tile % 2 else 0
            dc, r_dc = dec2[tile % 2], r_dec2[tile % 2]
            for q3 in range(3):
                mm_group(bank(b0 + q3), [(hA[0:64, ts_], fwo[0:64, q3 * 512:(q3 + 1) * 512])], reads=[r_hA, r_f], writes=[PB[b0 + q3]])
            S.op("act", [lambda e: e.activation(out=dc, in_=dlt, func=AF.Exp, scale=ntn[:, tile:tile + 1])],
                 reads=[r_f], writes=[r_dc])
            S.op("dve", [lambda e: e.tensor_tensor(out=HS[:, 0, tile, :], in0=ps[:, b0 * 512:b0 * 512 + 768], in1=dc, op=ALU.mult)],
                 reads=[PB[b0], PB[b0 + 1], r_dc], writes=[r_HS])
            S.op("dve", [lambda e: e.tensor_tensor(out=HS[:, 1, tile, :], in0=ps[:, b0 * 512 + 768:b0 * 512 + 1536], in1=dc, op=ALU.mult)],
                 reads=[PB[b0 + 1], PB[b0 + 2], r_dc], writes=[r_HS])

        f_steps.append(wcomb)
        for l in range(3):
            f_steps.append(lambda l=l: mlp_layer(l))
        for tile in range(16):
            f_steps.append(lambda tile=tile: hfull(tile))
        xi = 0
        for k_, fs in enumerate(f_steps):
            fs()
            if k_ >= 1 and xi < 16:
                x_steps[xi]()
                xi += 1
        while xi < 16:
            x_steps[xi]()
            xi += 1
        S.barrier()

        if debug == "s_f":
            S.barrier()
            S.emit()
            return nc
        KS = AV(R_T, [2, 16, 768], BF16)
        r_KS = Res("KS")

        def fwd_pass(rhs_re, rhs_im, r_rhs, ftab, r_ftab, epilogue, extra_w=((), ())):
            for fc in range(16):
                si = fc % 2
                ft = ftab[si]
                S.dma("sp", ft.rearrange("p a b c -> p (a b c)"), fwd_tab_d[fc],
                      writes=[r_ftab[si]] + (list(extra_w[si]) if fc < 2 else []), key="ftab%d" % si)
                bs = [0, 1, 2, 3] if fc % 2 == 0 else [4, 5, 6, 7]
                for ri in range(2):
                    rhs = rhs_re if ri == 0 else rhs_im
                    o0 = bs[0] * 512 + ri * 1024
                    fns = []
                    for tc in range(16):
                        lhs = ft[:, ri, tc, :]
                        fns.append(lambda e, lhs=lhs, tc=tc, rhs=rhs, o0=o0: e.matmul(
                            ps[:, o0:o0 + 512], lhs, rhs[:, tc, 0:512], start=(tc == 0), stop=(tc == 15)))
                        fns.append(lambda e, lhs=lhs, tc=tc, rhs=rhs, o0=o0: e.matmul(
                            ps[:, o0 + 512:o0 + 768], lhs, rhs[:, tc, 512:768], start=(tc == 0), stop=(tc == 15)))
                    S.op("pe", fns, reads=[r_ftab[si], r_rhs], writes=[PB[bs[2 * ri]], PB[bs[2 * ri + 1]]])
                pre = ps[:, bs[0] * 512:bs[0] * 512 + 768]
                pim = ps[:, bs[2] * 512:bs[2] * 512 + 768]
                epilogue(fc, pre, pim, [PB[b] for b in bs])

        ftab = [AV(R_YM, [2, 16, 128], BF16), AV(R_MQ, [2, 16, 128], BF16)]
        r_ftab = [Res("ftab0"), Res("ftab1")]

        def k_epilogue(fc, pre, pim, rbs):
            S.op("dve", [lambda e: e.tensor_tensor(out=KS[:, 0, fc, :], in0=pre, in1=dbc, op=ALU.add)],
                 reads=rbs[0:2] + [r_dbc], writes=[r_KS])
            act_copy(KS[:, 1, fc, :], pim, rbs[2:4], [r_KS])

        fwd_pass(HS[:, 0], HS[:, 1], r_HS, ftab, r_ftab, k_epilogue)

        if debug == "s_k":
            S.barrier()
            S.emit()
            return nc
        Z = AV(R_X, [16, 768], BF16)
        r_Z = Res("Z")
        hsb0 = [AV(R_X + 24 * K_ + i * 8208, [2052], F32) for i in range(2)]
        r_hsb0 = [Res("hsb0_%d" % i) for i in range(2)]
        accx = AV(R_X + 24 * K_ + 16416, [2048], F32)
        accv = AV(R_X + 24 * K_ + 16416 + 8192, [2048], F32)
        zT = AV(R_X + 24 * K_ + 16416 + 16384, [2048], BF16)
        r_accx, r_accv, r_zT = Res("accx"), Res("accv"), Res("zT")
        wch = [AV(R_T + 48 * K_ + i * 2 * K_, [8, 128], BF16) for i in range(3)]
        r_wch = [Res("wch%d" % i) for i in range(3)]
        for i in range(2):
            S.op("pool", [lambda e, i=i: e.memset(hsb0[i][:, 0:1], 0.0)], writes=[r_hsb0[i], r_HS])
            S.op("pool", [lambda e, i=i: e.memset(hsb0[i][:, 2049:2050], 0.0)], writes=[r_hsb0[i], r_HS])
        w0v = w_in_d[0].rearrange("(kc p) n -> p kc n", p=128)
        order = [18, 19] + [0, 1, 2, 3, 4, 5]
        for i in range(6):
            order += [6 + i, 12 + i]
        _wc = [0]

        def proj_chunk(wv, col0, sink_fn, extra_reads=()):
            wi = _wc[0] % 3
            _wc[0] += 1
            S.dma("pool", wch[wi], wv[:, :, col0:col0 + 128], writes=[r_wch[wi]], key="wch%d" % wi)
            for tq in range(4):
                b = nxt([0, 1, 2, 3])
                mm_group(bank(b), [(wch[wi][:, kc, :], XT[:, kc, tq * 512:(tq + 1) * 512]) for kc in range(8)],
                         reads=[r_wch[wi], r_XT], writes=[PB[b]])
                sink_fn(tq, b)

        def conv_chunk(c, hs_, r_hs, acc_out, r_acc_list, out_final):
            S.op("act", [lambda e: e.activation(out=acc_out, in_=hs_[:, 1:2049], func=AF.Identity,
                                                scale=cwb[:, c, 1:2], bias=cwb[:, c, 3:4])],
                 reads=[r_hs, r_cwb], writes=r_acc_list)
            S.op("dve", [lambda e: e.scalar_tensor_tensor(out=acc_out, in0=hs_[:, 0:2048], scalar=cwb[:, c, 0:1],
                                                          in1=acc_out, op0=ALU.mult, op1=ALU.add)],
                 reads=[r_hs, r_cwb] + r_acc_list, writes=r_acc_list)
            S.op("dve", [lambda e: e.scalar_tensor_tensor(out=out_final[0], in0=hs_[:, 2:2050], scalar=cwb[:, c, 2:3],
                                                          in1=acc_out, op0=ALU.mult, op1=ALU.add)],
                 reads=[r_hs, r_cwb] + r_acc_list, writes=out_final[1])

        zdef = []
        for ci_, c in enumerate(order):
            if debug and debug.startswith('s_p') and ci_ == int(debug[3:]):
                S.barrier()
                S.emit()
                return nc
            if c >= 18:
                hp = c - 18

                def sink_mq(tq, b, hp=hp):
                    act_copy(MQ[:, hp, tq * 512:(tq + 1) * 512], bank(b), [PB[b]], [r_MQ])
                proj_chunk(w0v, c * 128, sink_mq)
                continue
            si = (c % 2) if c < 6 else (0 if c < 12 else 1)
            hs_ = hsb0[si]

            def sink_h(tq, b, hs_=hs_, si=si):
                act_copy(hs_[:, 1 + tq * 512:1 + (tq + 1) * 512], bank(b), [PB[b]], [r_hsb0[si]])
            proj_chunk(w0v, c * 128, sink_h)
            if c < 6:
                conv_chunk(c, hs_, r_hsb0[si], accv, [r_accv], (YT[:, c, :], [r_YT]))
            elif c < 12:
                conv_chunk(c, hs_, r_hsb0[si], accx, [r_accx], (accx, [r_accx]))
                while zdef:
                    zdef.pop(0)()
            else:
                i6 = c - 12
                conv_chunk(c, hs_, r_hsb0[si], accv, [r_accv], (accv, [r_accv]))
                S.op("dve", [lambda e: e.tensor_tensor(out=zT, in0=accv, in1=accx, op=ALU.mult)],
                     reads=[r_accv, r_accx], writes=[r_zT])
                def _ztr(i6=i6):
                    for g8 in range(2):
                        b = nxt([6, 7])
                        bb = bank_bf(b)
                        fns = [lambda e, t8=t8, bb=bb, g8=g8: e.transpose(bb[:, t8 * 128:(t8 + 1) * 128],
                                                                         zT[:, (g8 * 8 + t8) * 128:(g8 * 8 + t8 + 1) * 128], ident[:])
                               for t8 in range(8)]
                        S.op("pe", fns, reads=[r_zT, r_consts], writes=[PB[b]])
                        S.op("act", [lambda e, g8=g8, bb=bb, i6=i6: e.activation(
                            out=Z[:, g8 * 8:(g8 + 1) * 8, i6 * 128:(i6 + 1) * 128],
                            in_=bb.rearrange("p (k m) -> p k m", k=8), func=AF.Identity)],
                             reads=[PB[b]], writes=[r_Z])
                zdef.append(_ztr)
        while zdef:
            zdef.pop(0)()
        if debug == "z":
            for tile in range(16):
                S.op("act", [lambda e, tile=tile: e.activation(out=X[:, tile, 0:768] if False else hsb0[0][:, 0:768], in_=Z[:, tile, :], func=AF.Identity)],
                     reads=[r_Z], writes=[r_hsb0[0]])
                S.dma("sp", dbg_d[tile * 128:(tile + 1) * 128, 0:768], hsb0[0][:, 0:768], reads=[r_hsb0[0]], key="dbg")
            S.barrier()
            S.emit()
            return nc
        mem_attention(R_X + 24 * K_)

        YRE = AV(R_XT, [16, 768], BF16)
        YIM = AV(R_X + 24 * K_, [16, 768], BF16)
        r_Y = Res("Yspec")
        ct = [AV(R_X + 48 * K_ + i * 3 * K_, [768], F32) for i in range(4)]
        r_ct = [Res("ct%d" % i) for i in range(4)]
        ftabU = [AV(R_T + 48 * K_, [2, 16, 128], BF16), AV(R_MQ, [2, 16, 128], BF16)]
        r_ftabU = [Res("ftabU0"), Res("ftabU1")]

        def u_epilogue(fc, pre, pim, rbs):
            kre, kim = KS[:, 0, fc, :], KS[:, 1, fc, :]
            S.op("dve", [lambda e: e.tensor_tensor(out=ct[0], in0=pre, in1=kre, op=ALU.mult)], reads=rbs[0:2] + [r_KS], writes=[r_ct[0]])
            S.op("dve", [lambda e: e.tensor_tensor(out=ct[1], in0=pim, in1=kim, op=ALU.mult)], reads=rbs[2:4] + [r_KS], writes=[r_ct[1]])
            S.op("dve", [lambda e: e.tensor_tensor(out=ct[2], in0=pre, in1=kim, op=ALU.mult)], reads=rbs[0:2] + [r_KS], writes=[r_ct[2]])
            S.op("dve", [lambda e: e.tensor_tensor(out=ct[3], in0=pim, in1=kre, op=ALU.mult)], reads=rbs[2:4] + [r_KS], writes=[r_ct[3]])
            S.op("pool", [lambda e: e.tensor_tensor(out=YRE[:, fc, :], in0=ct[0], in1=ct[1], op=ALU.subtract)],
                 reads=[r_ct[0], r_ct[1]], writes=[r_Y])
            S.op("pool", [lambda e: e.tensor_tensor(out=YIM[:, fc, :], in0=ct[2], in1=ct[3], op=ALU.add)],
                 reads=[r_ct[2], r_ct[3]], writes=[r_Y])

        fwd_pass(Z, Z, r_Z, ftabU, r_ftabU, u_epilogue, extra_w=(r_wch, [r_MQ]))

        itab = [AV(R_T + 48 * K_, [4, 2, 512], BF16), AV(R_MQ, [4, 2, 512], BF16)]
        r_itab = r_ftabU
        _it = 0
        for tt in range(4):
            fn_all = []
            for fg in range(4):
                si = _it % 2
                _it += 1
                S.dma("sp", itab[si].rearrange("p a b c -> p (a b c)"), inv_tab_d[tt, fg], writes=[r_itab[si]], key="itab%d" % si)
                fns = []
                for fi in range(4):
                    fc = fg * 4 + fi
                    for ri in range(2):
                        Ysrc = YRE if ri == 0 else YIM
                        for cc in range(6):
                            first = (fc == 0 and ri == 0)
                            lastm = (fc == 15 and ri == 1)
                            fns.append(lambda e, cc=cc, Ysrc=Ysrc, fc=fc, si=si, fi=fi, ri=ri, first=first, lastm=lastm: e.matmul(
                                bank(cc), Ysrc[:, fc, cc * 128:(cc + 1) * 128], itab[si][:, fi, ri, :], start=first, stop=lastm))
                S.op("pe", fns, reads=[r_itab[si], r_Y], writes=[PB[cc] for cc in range(6)])
            for cc in range(6):
                S.op("dve", [lambda e, cc=cc, tt=tt: e.tensor_tensor(out=YT[:, cc, tt * 512:(tt + 1) * 512], in0=bank(cc),
                                                                    in1=YT[:, cc, tt * 512:(tt + 1) * 512], op=ALU.mult)],
                     reads=[PB[cc], r_YT], writes=[r_YT])

        if debug == "mix0":
            for c in range(8):
                src = YT[:, c, 0:1024] if c < 6 else YM[:, c - 6, 0:1024]
                S.op("act", [lambda e, src=src: e.activation(out=rbuf[0], in_=src, func=AF.Identity)], reads=[r_YT, r_YM], writes=[r_rbuf[0]])
                S.dma("sp", dbg_d[c * 128:(c + 1) * 128, :], rbuf[0], reads=[r_rbuf[0]], key="dbg")
            S.barrier()
            S.emit()
            return nc

        out_proj_ln1(0)
        S.barrier()
        if debug == "ln1_0":
            for tile in range(16):
                S.dma("sp", dbg_d[tile * 128:(tile + 1) * 128, :], X[:, tile, :], reads=[r_X[tile]], key="dbg")
            S.barrier()
            S.emit()
            return nc
        ffn_ln2(0, final=(debug == "l0"))
        S.barrier()

        if debug is None or debug.startswith("l1"):
            if debug == "l1_s":
                S.barrier()
                S.emit()
                return nc
            w1v = w_in_d[1].rearrange("(kc p) n -> p kc n", p=128)
            QT = YT
            r_QT = [Res("QT%d" % i) for i in range(12)]
            KT = AV(R_T, [4, 2048], BF16)
            r_KT = Res("KT")
            VT = AV(R_T + 16 * K_, [16, 256], BF16)
            r_VT = Res("VT")
            ropec = AV(R_T + 24 * K_, [2048], F32)
            ropes = AV(R_T + 32 * K_, [2048], F32)
            r_rope = Res("rope")
            wch1 = [AV(R_T + 40 * K_ + i * 2 * K_, [8, 128], BF16) for i in range(3)]
            r_wch1 = [Res("wch1_%d" % i) for i in range(3)]
            qsb = [AV(R_T + 46 * K_ + i * K_, [512], BF16) for i in range(2)]
            r_qsb = [Res("qsb%d" % i) for i in range(2)]
            rt1 = AV(R_T + 48 * K_, [512], F32)
            rt2 = AV(R_T + 50 * K_, [512], F32)
            r_rt1, r_rt2 = Res("rt1"), Res("rt2")
            wvt = AV(R_T + 52 * K_, [8, 256], BF16)
            r_wvt = Res("wvt")
            esk = small("esk", [64, 12], F32)
            r_esk = Res("esk")
            S.dma("sp", ropec, ropec_d[:, :], writes=[r_rope], key="rope0")
            S.dma("sp", ropes, ropes_d[:, :], writes=[r_rope], key="rope1")
            S.dma("sp", esk[:], sink_d.partition_broadcast(64), writes=[r_esk], key="esk")
            S.op("act", [lambda e: e.activation(out=esk[:], in_=esk[:], func=AF.Exp)], reads=[r_esk], writes=[r_esk])
            S.dma("pool", wvt, w1v[:, :, 1024:1280], writes=[r_wvt], key="wvt")
            if debug == "l1_p0":
                S.barrier()
                S.emit()
                return nc
            _w1 = [0]
            _rq = [0]

            def proj1(loads, sink_fn):
                wi = _w1[0] % 3
                _w1[0] += 1
                for k_, (dst0, dst1, c0, c1) in enumerate(loads):
                    S.dma("pool", wch1[wi][:, :, dst0:dst1], w1v[:, :, c0:c1], writes=[r_wch1[wi]], key="wch1_%d_%d" % (wi, k_))
                for tq in range(4):
                    b = nxt([0, 1, 2, 3])
                    mm_group(bank(b), [(wch1[wi][:, kc, :], XT[:, kc, tq * 512:(tq + 1) * 512]) for kc in range(8)],
                             reads=[r_wch1[wi], r_XT], writes=[PB[b]])
                    sink_fn(tq, b)

            def rope_sink(dest, r_dest):
                def sink(tq, b):
                    cs = slice(tq * 512, (tq + 1) * 512)
                    i = _rq[0] % 2
                    _rq[0] += 1
                    S.op("act", [lambda e: e.activation(out=qsb[i], in_=bank(b), func=AF.Identity)],
                         reads=[PB[b]], writes=[r_qsb[i]])
                    b2 = [4, 5][i]
                    mm_group(bank(b2), [(pm[:], qsb[i])], reads=[r_consts, r_qsb[i]], writes=[PB[b2]])
                    S.op("dve", [lambda e: e.tensor_tensor(out=rt1, in0=bank(b), in1=ropec[:, cs], op=ALU.mult)],
                         reads=[PB[b], r_rope], writes=[r_rt1])
                    S.op("dve", [lambda e: e.tensor_tensor(out=rt2, in0=bank(b2), in1=ropes[:, cs], op=ALU.mult)],
                         reads=[PB[b2], r_rope], writes=[r_rt2])
                    S.op("dve", [lambda e: e.tensor_tensor(out=dest[:, cs], in0=rt1, in1=rt2, op=ALU.add)],
                         reads=[r_rt1, r_rt2], writes=r_dest)
                return sink

            for c in range(6):
                proj1([(0, 128, c * 128, (c + 1) * 128)], rope_sink(QT[:, c, :], [r_QT[2 * c], r_QT[2 * c + 1]]))
            if debug == "l1_p1":
                S.barrier()
                S.emit()
                return nc
            for g in range(4):
                c0 = 768 + g * 64
                proj1([(0, 64, c0, c0 + 64), (64, 128, c0, c0 + 64)], rope_sink(KT[:, g, :], [r_KT]))
            if debug == "l1_p2":
                S.barrier()
                S.emit()
                return nc
            for hp in range(2):
                def sink_mq1(tq, b, hp=hp):
                    act_copy(MQ[:, hp, tq * 512:(tq + 1) * 512], bank(b), [PB[b]], [r_MQ])
                proj1([(0, 128, 1280 + hp * 128, 1280 + (hp + 1) * 128)], sink_mq1)
            for tile in range(16):
                b = nxt([0, 1, 2, 3])
                mm_group(bank(b, 256), [(XT[:, kc, tile * 128:(tile + 1) * 128], wvt[:, kc, :]) for kc in range(8)],
                         reads=[r_XT, r_wvt], writes=[PB[b]])
                act_copy(VT[:, tile, :], bank(b, 256), [PB[b]], [r_VT])
            S.barrier()

            if debug == "l1_p":
                S.barrier()
                S.emit()
                return nc
            PTs = [AV(R_XT + i * 12 * K_, [16, 384], BF16) for i in range(2)]
            r_PTs = [Res("PTs%d" % i) for i in range(2)]
            rden1s = [AV(R_XT + 24 * K_ + i * 2 * K_, [512], F32) for i in range(2)]
            r_rden1s = [Res("rden1_%d" % i) for i in range(2)]
            def st_exp(h, j):
                g = h // 3
                hh = h % 2
                c = h // 2
                prow = slice(hh * 64, (hh + 1) * 64)
                pt = PTs[h % 2]
                r_pt = r_PTs[h % 2]
                qlo = max(0, j - 1) * 128
                qhi = min(16, j + 2) * 128
                n = qhi - qlo
                moff = qlo - (j - 1) * 128
                b = nxt([0, 1, 2, 3])
                fns = [
                    lambda e: e.matmul(bank(b, n), KT[prow, g, j * 128:(j + 1) * 128], QT[prow, c, qlo:qhi], start=True, stop=False),
                    lambda e: e.matmul(bank(b, n), ident[:], maskb[:, moff:moff + n], start=False, stop=True),
                ]
                S.op("pe", fns, reads=[r_KT, r_QT[h], r_consts], writes=[PB[b]])
                S.op("act", [lambda e: e.activation(out=pt[:, j, 0:n], in_=bank(b, n), func=AF.Exp, scale=0.125)],
                     reads=[PB[b]], writes=[r_pt])

            def pv_norm(h, qt):
                g = h // 3
                hh = h % 2
                c = h // 2
                prow = slice(hh * 64, (hh + 1) * 64)
                pt = PTs[h % 2]
                r_pt = r_PTs[h % 2]
                bo, bd = [4, 6][qt % 2], [5, 7][qt % 2]
                fo, fd = [], []
                for i4 in range(4):
                    qb = 4 * qt + i4
                    js = [j for j in (qb - 1, qb, qb + 1) if 0 <= j < 16]
                    for k_, j in enumerate(js):
                        lc = (qb - max(0, j - 1)) * 128
                        st_, sp_ = (k_ == 0), (k_ == len(js) - 1)
                        fo.append(lambda e, i4=i4, j=j, lc=lc, st_=st_, sp_=sp_: e.matmul(
                            ps[0:64, bo * 512 + i4 * 128:bo * 512 + (i4 + 1) * 128], VT[:, j, g * 64:(g + 1) * 64],
                            pt[:, j, lc:lc + 128], start=st_, stop=sp_))
                        fd.append(lambda e, i4=i4, j=j, lc=lc, st_=st_, sp_=sp_: e.matmul(
                            ps[0:64, bd * 512 + i4 * 128:bd * 512 + (i4 + 1) * 128], ones64[:],
                            pt[:, j, lc:lc + 128], start=st_, stop=sp_))
                S.op("pe", fo, reads=[r_pt, r_VT], writes=[PB[bo]])
                S.op("pe", fd, reads=[r_pt, r_consts], writes=[PB[bd]])
                rd = rden1s[qt % 2]
                r_rd = r_rden1s[qt % 2]
                S.op("act", [lambda e: e.activation(out=rd[0:64, :], in_=bank(bd, 512, 0, 64), func=AF.Identity,
                                                    bias=esk[:, h:h + 1])],
                     reads=[PB[bd], r_esk], writes=[r_rd])
                S.op("dve", [lambda e: e.reciprocal(out=rd[0:64, :], in_=rd[0:64, :])], reads=[r_rd], writes=[r_rd])
                S.op("dve", [lambda e: e.tensor_tensor(
                    out=QT[prow, c, qt * 512:(qt + 1) * 512], in0=bank(bo, 512, 0, 64), in1=rd[0:64, :], op=ALU.mult)],
                     reads=[PB[bo], r_rd], writes=[r_QT[h]])

            for h in range(13):
                for qt in range(4):
                    if h < 12:
                        for j in range(4 * qt, 4 * qt + 4):
                            st_exp(h, j)
                    if h >= 1:
                        pv_norm(h - 1, qt)
            S.barrier()
            if debug == "l1_a":
                S.barrier()
                S.emit()
                return nc
            mem_attention(R_XT)
            S.barrier()
            if debug == "l1mix":
                for c in range(8):
                    src = YT[:, c, 0:1024] if c < 6 else YM[:, c - 6, 0:1024]
                    S.op("act", [lambda e, src=src: e.activation(out=rbuf[0], in_=src, func=AF.Identity)], reads=[r_YM], writes=[r_rbuf[0]])
                    S.dma("sp", dbg_d[c * 128:(c + 1) * 128, :], rbuf[0], reads=[r_rbuf[0]], key="dbg")
                S.barrier()
                S.emit()
                return nc
            out_proj_ln1(1)
            S.barrier()
            ffn_ln2(1, final=True)
            S.barrier()

        S.barrier()
        S.emit()
    return nc


def prep_shared(inputs):
    f32 = np.float32
    sh = {}
    for k in ("w_mem_kv", "l0_w_in", "l1_w_in", "l0_w_out", "l1_w_out", "l0_ffn_w_up", "l1_ffn_w_up",
              "l0_ffn_w_down", "l1_ffn_w_down", "l0_filt_w1", "l0_filt_w2", "l0_filt_w3", "l0_filt_w_out",
              "l0_hyena_d", "l1_sink"):
        sh[k] = np.ascontiguousarray(np.asarray(inputs[k], dtype=f32))
    for i in range(2):
        for n in ("ln1_g", "ln1_b", "ln2_g", "ln2_b"):
            sh["l%d_%s" % (i, n)] = np.ascontiguousarray(np.asarray(inputs["l%d_%s" % (i, n)], dtype=f32))
        cw = np.asarray(inputs["l%d_ffn_conv_w" % i], f32)
        cb = np.asarray(inputs["l%d_ffn_conv_b" % i], f32)
        a = np.concatenate([cw, cb[None, :]], axis=0)
        sh["l%d_fcwb" % i] = np.ascontiguousarray(a.reshape(4, 44, 128).transpose(2, 1, 0))
    cw = np.asarray(inputs["l0_conv_w"], f32)
    cb = np.asarray(inputs["l0_conv_b"], f32)
    a = np.concatenate([cw, cb[None, :]], axis=0)
    sh["l0_cwb"] = np.ascontiguousarray(a.reshape(4, 18, 128).transpose(2, 1, 0))
    sh["l0_fbf"] = np.ascontiguousarray(np.stack(
        [np.asarray(inputs["l0_filt_%s%d" % (n, l)], f32) for l in (1, 2, 3) for n in ("b", "f")], axis=1))
    sh.update(const_tables())
    return sh


_NC_CACHE = {}


def kernel(**inputs):
    sh = prep_shared(inputs)
    x = np.asarray(inputs["x"], np.float32)
    mem = np.asarray(inputs["mem"], np.float32)
    if "nc" not in _NC_CACHE:
        _NC_CACHE["nc"] = build_program()
    nc = _NC_CACHE["nc"]
    in_maps = []
    for b in range(8):
        m = dict(sh)
        m["x"] = np.ascontiguousarray(x[b])
        m["mem"] = np.ascontiguousarray(mem[b])
        in_maps.append(m)
    res = run_bass_kernel_spmd(nc, in_maps, core_ids=list(range(8)))
    return np.stack([np.asarray(r["out"], np.float32) for r in res.results], axis=0)
```

```python
import math
from contextlib import ExitStack

import numpy as np
import ml_dtypes
import concourse.bass as bass
import concourse.mybir as mybir
from concourse.bass_utils import run_bass_kernel_spmd

F32 = mybir.dt.float32
BF16 = mybir.dt.bfloat16
AF = mybir.ActivationFunctionType
ALU = mybir.AluOpType
AX = mybir.AxisListType

L = 2048
D = 1024
NT = 16
KC = 8
DFF = 2816
NFF = 22
ALPHA = 4.0 ** 0.25
EPS = 1e-5
PI = float(np.pi)


class Res:
    __slots__ = ("name", "w", "r", "excl")

    def __init__(self, name, excl=False):
        self.name = name
        self.w = None
        self.r = []
        self.excl = excl


class Sched:
    def __init__(self, nc, es):
        self.nc = nc
        self.es = es
        self.engs = {}
        for n in ("pe", "act", "dve", "pool", "sp"):
            sem = es.enter_context(nc.semaphore("s_" + n))
            self.engs[n] = dict(sem=sem, count=0, known={}, ops=[])
        self.dma_sems = {}
        self.n_dma_sems = 0

    def dma_sem(self, key):
        if key not in self.dma_sems:
            sem = self.es.enter_context(self.nc.semaphore("d%d" % self.n_dma_sems))
            self.n_dma_sems += 1
            self.dma_sems[key] = [sem, 0]
        return self.dma_sems[key]

    def op(self, eng, fns, reads=(), writes=(), dma_key=None):
        E = self.engs[eng]
        deps = {}

        def add(ev):
            if ev is None:
                return
            sem, val = ev
            if deps.get(sem, 0) < val:
                deps[sem] = val

        excl_reads = [r for r in reads if r.excl]
        writes = list(writes) + [r for r in excl_reads if r not in writes]
        reads = [r for r in reads if not r.excl]
        for r in reads:
            add(r.w)
        for w in writes:
            add(w.w)
            for ev in w.r:
                add(ev)
        waits = []
        for sem, val in deps.items():
            if sem is E["sem"] and eng == "pe" and dma_key is None:
                continue
            if E["known"].get(sem, 0) >= val:
                continue
            E["known"][sem] = val
            waits.append((sem, val))
        if dma_key is not None:
            ds = self.dma_sem(dma_key)
            ds[1] += 16
            ev = (ds[0], ds[1])
            inc = (ds[0], 16)
        else:
            E["count"] += 1
            ev = (E["sem"], E["count"])
            inc = (E["sem"], 1)
        for r in reads:
            r.r.append(ev)
        for w in writes:
            w.w = ev
            w.r = []
        if not isinstance(fns, (list, tuple)):
            fns = [fns]
        E["ops"].append((waits, list(fns), inc))
        return ev

    def dma(self, queue, out, in_, reads=(), writes=(), key=None):
        assert key is not None
        return self.op(queue, [lambda e: e.dma_start(out=out, in_=in_)], reads, writes, dma_key=key)

    def barrier(self):
        evs = [(E["sem"], E["count"]) for E in self.engs.values() if E["count"] > 0]
        evs += [(s, c) for (s, c) in self.dma_sems.values() if c > 0]
        for n, E in self.engs.items():
            waits = []
            for sem, val in evs:
                if E["known"].get(sem, 0) >= val:
                    continue
                if sem is E["sem"] and n == "pe":
                    continue
                E["known"][sem] = val
                waits.append((sem, val))
            if waits:
                E["ops"].append((waits, [], None))

    def emit(self):
        nc = self.nc

        def run(name):
            def f(e):
                for waits, fns, inc in self.engs[name]["ops"]:
                    for sem, val in waits:
                        e.wait_ge(sem, val)
                    n = len(fns)
                    for i, fn in enumerate(fns):
                        ins = fn(e)
                        if i == n - 1:
                            ins.then_inc(inc[0], inc[1])
            return f

        with nc.Block() as block:
            block.tensor(run("pe"))
            block.scalar(run("act"))
            block.vector(run("dve"))
            block.gpsimd(run("pool"))
            block.sync(run("sp"))


def _bf(a):
    return np.ascontiguousarray(a.astype(ml_dtypes.bfloat16))


_CONST_CACHE = {}


def const_tables():
    if _CONST_CACHE:
        return _CONST_CACHE
    N = 4096
    f = np.arange(2048, dtype=np.float64)
    t = np.arange(2048, dtype=np.float64)
    m = np.mod(np.outer(2 * f + 1, t), 2 * N)
    ang = np.pi * m / N
    C = np.cos(ang)
    Sn = np.sin(ang)
    CT = C.T.reshape(16, 128, 16, 128)
    ST = (-Sn).T.reshape(16, 128, 16, 128)
    fwd = np.stack([CT, ST], axis=0)
    fwd = fwd.transpose(3, 2, 0, 1, 4)
    _CONST_CACHE["fwd_tab"] = _bf(fwd.reshape(16, 128, 2 * 16 * 128))
    Ci = (C / 2048.0).reshape(4, 4, 128, 4, 512)
    Si = (-Sn / 2048.0).reshape(4, 4, 128, 4, 512)
    inv = np.stack([Ci, Si], axis=0)
    inv = inv.transpose(4, 1, 3, 2, 0, 5)
    _CONST_CACHE["inv_tab"] = _bf(inv.reshape(4, 4, 128, 4 * 2 * 512))
    f32 = np.float32
    tl = np.linspace(0.0, 1.0, L, dtype=f32)[:, None]
    w = (f32(2.0 * math.pi) * np.arange(L, dtype=f32)[:, None] / f32(L)).astype(f32)
    fr = np.linspace(1e-4, 15, 16, dtype=f32)[None, :]
    z = np.concatenate([tl, np.cos(fr * w), -np.sin(fr * w)], axis=-1).astype(f32)
    _CONST_CACHE["zfT"] = np.ascontiguousarray(z.T)
    _CONST_CACHE["ntn"] = np.ascontiguousarray((-tl[:, 0]).reshape(16, 128).T.astype(f32))
    min_decay = math.log(1e-2) / 1.5
    max_decay = math.log(1e-2) / 0.3
    _CONST_CACHE["deltas"] = np.abs(np.linspace(min_decay, max_decay, 768, dtype=f32)).astype(f32)
    inv_f = (10000.0 ** (-np.arange(0, 64, 2, dtype=f32) / f32(64))).astype(f32)
    angr = (np.arange(L, dtype=f32)[:, None] * inv_f[None, :]).astype(f32)
    angr = np.concatenate([angr, angr], axis=-1)
    cosT = np.cos(angr).T.astype(f32)
    sinT = np.sin(angr).T.astype(f32)
    _CONST_CACHE["ropec"] = np.ascontiguousarray(np.concatenate([cosT, cosT], axis=0))
    _CONST_CACHE["ropes"] = np.ascontiguousarray(np.concatenate([sinT, sinT], axis=0))
    Pm = np.zeros((128, 128), np.float32)
    for po in range(128):
        d = po % 64
        if d < 32:
            Pm[po + 32, po] = -1.0
        else:
            Pm[po - 32, po] = 1.0
    _CONST_CACHE["pm"] = _bf(Pm)
    _CONST_CACHE["ident"] = _bf(np.eye(128, dtype=np.float32))
    k = np.arange(128)[:, None]
    q = np.arange(128)[None, :]
    NEG = -30000.0
    m_next = np.where(k <= q, 0.0, NEG)
    m_prev = np.where(k >= q, 0.0, NEG)
    _CONST_CACHE["maskb"] = _bf(np.concatenate([m_next, np.zeros((128, 128)), m_prev], axis=1))
    return _CONST_CACHE


def build_program(debug=None):
    nc = bass.Bass("TRN2", target_bir_lowering=False)
    dbg = {}

    def din(name, shape, dt=F32):
        return nc.dram_tensor(name, list(shape), dt, kind="ExternalInput").ap()

    x_d = din("x", [L, D])
    mem_d = din("mem", [256, D])
    wkv_d = din("w_mem_kv", [D, 512])
    w_in_d = [din("l0_w_in", [D, 2560]), din("l1_w_in", [D, 1536])]
    w_out_d = [din("l0_w_out", [D, D]), din("l1_w_out", [D, D])]
    w_up_d = [din("l%d_ffn_w_up" % i, [D, 2 * DFF]) for i in range(2)]
    w_dn_d = [din("l%d_ffn_w_down" % i, [DFF, D]) for i in range(2)]
    ln_d = [[din("l%d_%s" % (i, n), [D]) for n in ("ln1_g", "ln1_b", "ln2_g", "ln2_b")] for i in range(2)]
    fcwb_d = [din("l%d_fcwb" % i, [128, 44, 4]) for i in range(2)]
    cwb_d = din("l0_cwb", [128, 18, 4])
    fw1_d = din("l0_filt_w1", [33, 64])
    fw2_d = din("l0_filt_w2", [64, 64])
    fw3_d = din("l0_filt_w3", [64, 64])
    fwo_d = din("l0_filt_w_out", [64, 1536])
    fbf_d = din("l0_fbf", [64, 6])
    hd_d = din("l0_hyena_d", [768])
    sink_d = din("l1_sink", [12])
    fwd_tab_d = din("fwd_tab", [16, 128, 4096], BF16)
    inv_tab_d = din("inv_tab", [4, 4, 128, 4096], BF16)
    zfT_d = din("zfT", [33, L])
    ntn_d = din("ntn", [128, 16])
    deltas_d = din("deltas", [768])
    ropec_d = din("ropec", [128, L])
    ropes_d = din("ropes", [128, L])
    pm_d = din("pm", [128, 128], BF16)
    ident_d = din("ident", [128, 128], BF16)
    maskb_d = din("maskb", [128, 384], BF16)
    out_d = nc.dram_tensor("out", [L, D], F32, kind="ExternalOutput").ap()
    if debug:
        dbg_d = nc.dram_tensor("dbg", [L, D], F32, kind="ExternalOutput").ap()

    es = ExitStack()
    with es:
        S = Sched(nc, es)
        AR_BYTES = 194 * 1024
        arena = es.enter_context(nc.sbuf_tensor("arena", [128, AR_BYTES // 2], BF16))
        ps = es.enter_context(nc.psum_tensor("ps", [128, 4096], F32))
        PB = [Res("pb%d" % i, excl=True) for i in range(8)]

        def bank(b, n=512, p0=0, p1=128):
            return ps[p0:p1, b * 512:b * 512 + n]

        def bank_bf(b):
            return ps[:, b * 512:(b + 1) * 512].bitcast(BF16)

        def AV(off, shape, dt):
            n = int(np.prod(shape))
            if dt == F32:
                v = arena[:, off // 2: off // 2 + 2 * n].bitcast(F32)
            else:
                v = arena[:, off // 2: off // 2 + n]
            if len(shape) == 2:
                v = v.rearrange("p (a b) -> p a b", a=shape[0])
            elif len(shape) == 3:
                v = v.rearrange("p (a b c) -> p a b c", a=shape[0], b=shape[1])
            elif len(shape) == 4:
                v = v.rearrange("p (a b c d) -> p a b c d", a=shape[0], b=shape[1], c=shape[2])
            return v

        K_ = 1024
        R_X, R_XT, R_Y, R_YM, R_MQ, R_T = 0, 64 * K_, 96 * K_, 120 * K_, 128 * K_, 136 * K_

        def small(name, shape, dt):
            return es.enter_context(nc.sbuf_tensor("sb_" + name, list(shape), dt))

        ident = small("ident", [128, 128], BF16)
        pm = small("pm", [128, 128], BF16)
        maskb = small("maskb", [128, 384], BF16)
        ones64 = small("ones64", [128, 64], BF16)
        memKT = small("memKT", [128, 2, 256], BF16)
        memV = small("memV", [128, 2, 256], BF16)
        fcwb = small("fcwb", [128, 44, 4], F32)
        cwb = fcwb[:, 0:18, :]
        stat = small("stat", [128, 128], F32)
        GB = small("GB", [128, 2, 1024], F32)
        r_consts = Res("consts")
        r_memKV = Res("memKV")
        r_cwb = Res("cwb")
        r_fcwb = Res("fcwb")
        r_GB = Res("GB")

        X = AV(R_X, [16, 1024], F32)
        XT = AV(R_XT, [8, 2048], BF16)
        YT = AV(R_Y, [6, 2048], BF16)
        YM = AV(R_YM, [2, 2048], BF16)
        MQ = AV(R_MQ, [2, 2048], BF16)
        r_X = [Res("X%d" % i) for i in range(16)]
        r_XT = Res("XT")
        r_YT = Res("YT")
        r_YM = Res("YM")
        r_MQ = Res("MQ")

        _bk = [0]

        def nxt(lst):
            b = lst[_bk[0] % len(lst)]
            _bk[0] += 1
            return b

        def mm_group(out, pairs, reads, writes):
            n = len(pairs)
            fns = []
            for i, (l, r) in enumerate(pairs):
                fns.append(lambda e, l=l, r=r, i=i: e.matmul(out, l, r, start=(i == 0), stop=(i == n - 1)))
            return S.op("pe", fns, reads, writes)

        def act_copy(out, in_, reads, writes):
            return S.op("act", [lambda e: e.activation(out=out, in_=in_, func=AF.Identity)], reads, writes)

        S.op("dve", [lambda e: e.memset(ones64[:], 1.0)], writes=[r_consts])
        epst = small("epst", [128, 1], F32)
        identf = small("identf", [128, 128], F32)
        aident = small("aident", [128, 128], F32)
        S.op("dve", [lambda e: e.memset(epst[:], EPS)], writes=[r_consts])

        HS = AV(R_X, [2, 16, 768], BF16)
        r_HS = Res("HS")
        hA = AV(R_X + 48 * K_, [2048], F32)
        hB = AV(R_X + 56 * K_, [2048], F32)
        r_hA, r_hB = Res("hA"), Res("hB")
        zf = AV(R_Y, [2048], F32)
        fw = AV(R_Y + 8 * K_, [3, 64], F32)
        fwo = AV(R_Y + 9 * K_, [1536], F32)
        fbf = AV(R_Y + 15 * K_, [8], F32)
        dbc = AV(R_Y + 16 * K_, [768], F32)
        dlt = AV(R_YM, [768], F32)
        dec = AV(R_YM + 3 * K_, [768], F32)
        ntn = AV(R_YM + 6 * K_, [16], F32)
        fsb = AV(R_MQ, [768], F32)
        wtmp = AV(R_MQ + 3 * K_, [512], F32)
        wtm2 = AV(R_MQ + 5 * K_, [512], F32)
        r_f = Res("filt_in")
        r_dec, r_fsb, r_wtmp, r_wtm2, r_dbc = Res("dec"), Res("fsb"), Res("wtmp"), Res("wtm2"), Res("dbc")
        S.dma("sp", zf[0:33, :], zfT_d[:, :], writes=[r_f], key="f0")
        S.dma("sp", fw[0:33, 0, :], fw1_d[:, :], writes=[r_f], key="f1")
        S.dma("sp", fw[0:64, 1, :], fw2_d[:, :], writes=[r_f], key="f2")
        S.dma("sp", fw[0:64, 2, :], fw3_d[:, :], writes=[r_f], key="f3")
        S.dma("sp", fwo[0:64, :], fwo_d[:, :], writes=[r_f], key="f4")
        S.dma("sp", fbf[0:64, 0:6], fbf_d[:, :], writes=[r_f], key="f5")
        S.dma("sp", dbc, hd_d.partition_broadcast(128), writes=[r_dbc], key="f6")
        S.dma("sp", dlt, deltas_d.partition_broadcast(128), writes=[r_f], key="f7")
        S.dma("sp", ntn, ntn_d[:, :], writes=[r_f], key="f8")
        S.dma("sp", ident[:], ident_d[:, :], writes=[r_consts], key="c0")
        S.dma("sp", pm[:], pm_d[:, :], writes=[r_consts], key="c1")
        S.dma("sp", maskb[:], maskb_d[:, :], writes=[r_consts], key="c2")
        S.dma("sp", cwb, cwb_d[:, :, :], writes=[r_cwb], key="c3")
        S.op("act", [lambda e: e.activation(out=identf[:], in_=ident[:], func=AF.Identity)], reads=[r_consts], writes=[r_consts])
        S.op("act", [lambda e: e.activation(out=aident[:], in_=ident[:], func=AF.Identity, scale=ALPHA)], reads=[r_consts], writes=[r_consts])

        def transpose_to_XT(src_bf, tile, r_src):
            b = nxt([6, 7])
            bb = bank_bf(b)
            fns = [lambda e, kc=kc: e.transpose(bb[:, kc * 128:(kc + 1) * 128], src_bf[:, kc * 128:(kc + 1) * 128], ident[:])
                   for kc in range(8)]
            S.op("pe", fns, reads=[r_src, r_consts], writes=[PB[b]])
            S.op("act", [lambda e: e.activation(out=XT[:, :, tile * 128:(tile + 1) * 128],
                                                in_=bb.rearrange("p (k m) -> p k m", k=8), func=AF.Identity)],
                 reads=[PB[b]], writes=[r_XT])

        def load_ln(layer, which):
            g_d, b_d = ln_d[layer][2 * which], ln_d[layer][2 * which + 1]
            S.dma("sp", GB[:, 0, :], g_d.partition_broadcast(128), writes=[r_GB], key="gb0")
            S.dma("sp", GB[:, 1, :], b_d.partition_broadcast(128), writes=[r_GB], key="gb1")

        NRB = 6
        LN_LAG = 3
        _rboff = [46 * K_, 50 * K_, 28672, 28672 + 4096, 36880, 36880 + 4096]
        rbuf = [AV(R_T + _rboff[i], [1024], F32) for i in range(NRB)]
        _xboff = [54 * K_, 56 * K_, 24 * K_, 26 * K_]
        xbuf = [AV(R_T + _xboff[i], [1024], BF16) for i in range(4)]
        r_rbuf = [Res("rbuf%d" % i) for i in range(NRB)]
        r_xbuf = [Res("xbuf%d" % i) for i in range(4)]
        r_stat = Res("stat")
        r_stats = [Res("stat%d" % i) for i in range(8)]
        _ln = [0]

        class LNPipe:
            def __init__(self, final_out):
                self.final_out = final_out
                self.q = []

            def push(self, tile, pin, r_pin, r_in, r_r):
                ri = tile % NRB
                so = ri * 16
                r_stat = r_stats[ri]
                st6 = stat[:, so:so + 12]
                mv = stat[:, so + 12:so + 14]
                rstd = stat[:, so + 14:so + 15]
                nmr = stat[:, so + 15:so + 16]
                final_out = self.final_out

                def s1():
                    S.op("dve", [lambda e: e.bn_stats(out=st6[:, 0:6], in_=pin[:, 0:512])], reads=[r_pin[0]], writes=[r_stat])
                    S.op("dve", [lambda e: e.bn_stats(out=st6[:, 6:12], in_=pin[:, 512:1024])], reads=[r_pin[1]], writes=[r_stat])
                    S.op("dve", [lambda e: e.bn_aggr(out=mv, in_=st6)], reads=[r_stat], writes=[r_stat])

                def s2():
                    S.op("act", [lambda e: e.activation(out=rstd, in_=mv[:, 1:2], func=AF.Sqrt, bias=epst[:, 0:1])],
                         reads=[r_stat, r_consts], writes=[r_stat])

                def s3():
                    S.op("dve", [lambda e: e.reciprocal(out=rstd, in_=rstd)], reads=[r_stat], writes=[r_stat])
                    S.op("dve", [lambda e: e.scalar_tensor_tensor(out=nmr, in0=mv[:, 0:1], scalar=-1.0, in1=rstd,
                                                                  op0=ALU.mult, op1=ALU.mult)], reads=[r_stat], writes=[r_stat])

                def s4():
                    S.op("act", [lambda e: e.activation(out=r_in, in_=pin, func=AF.Identity, scale=rstd, bias=nmr)],
                         reads=list(r_pin) + [r_stat], writes=[r_r])

                def s5():
                    S.op("dve", [lambda e: e.tensor_tensor(out=r_in, in0=r_in, in1=GB[:, 0, :], op=ALU.mult)],
                         reads=[r_r, r_GB], writes=[r_r])
                    S.op("pool", [lambda e: e.tensor_tensor(out=X[:, tile, :], in0=r_in, in1=GB[:, 1, :], op=ALU.add)],
                         reads=[r_r, r_GB], writes=[r_X[tile]])

                def s6():
                    if final_out:
                        S.dma("sp", out_d[tile * 128:(tile + 1) * 128, :], X[:, tile, :], reads=[r_X[tile]], key="out%d" % (tile % 4))
                    else:
                        i4 = tile % 4
                        S.op("act", [lambda e: e.activation(out=xbuf[i4], in_=X[:, tile, :], func=AF.Identity)],
                             reads=[r_X[tile]], writes=[r_xbuf[i4]])

                def s7():
                    if not final_out:
                        i4 = tile % 4
                        transpose_to_XT(xbuf[i4], tile, r_xbuf[i4])

                s1()
                for d_, f_ in ((1, s2), (1, s3), (2, s4), (3, s5), (4, s6), (5, s7)):
                    self.q.append([d_, f_])
                self._tick()

            def _tick(self):
                keep = []
                for item in self.q:
                    item[0] -= 0
                ready = [it for it in self.q if it[0] <= 0]
                for it in ready:
                    it[1]()
                self.q = [it for it in self.q if it[0] > 0]
                for it in self.q:
                    it[0] -= 1

            def flush(self):
                while self.q:
                    self._tick()

        def mem_attention(toff):
            PT = [AV(toff + i * K_, [512], BF16) for i in range(8)]
            r_PT = [Res("mpt%d" % i) for i in range(8)]
            rdens = [AV(toff + 8 * K_ + i * 2 * K_, [512], F32) for i in range(2)]
            r_rdens = [Res("mrden%d" % i) for i in range(2)]
            memVp = AV(toff + 12 * K_, [2, 2, 2, 128], BF16)
            onesp = AV(toff + 14 * K_, [2, 128], BF16)
            r_pad = Res("mpad")
            S.op("dve", [lambda e: e.memset(memVp.rearrange("p a b c d -> p (a b c d)"), 0.0)], writes=[r_pad])
            S.op("dve", [lambda e: e.memset(onesp.rearrange("p a b -> p (a b)"), 0.0)], writes=[r_pad])
            for hh in range(2):
                S.op("dve", [lambda e, hh=hh: e.memset(onesp[:, hh, hh * 64:(hh + 1) * 64], 1.0)], writes=[r_pad])
                for mt in range(2):
                    for hp in range(2):
                        h = 2 * hp + hh
                        S.op("dve", [lambda e, mt=mt, hp=hp, hh=hh, h=h: e.tensor_copy(
                            out=memVp[:, mt, hp, hh, hh * 64:(hh + 1) * 64], in_=memV[:, mt, h * 64:(h + 1) * 64])],
                             reads=[r_memKV], writes=[r_pad])
            its = [(hp, qt) for hp in range(2) for qt in range(4)]

            def stage1(n):
                hp, qt = its[n]
                qs = slice(qt * 512, (qt + 1) * 512)
                for hh in range(2):
                    prow = slice(hh * 64, (hh + 1) * 64)
                    for mt in range(2):
                        b = nxt([0, 1, 2, 3])
                        mm_group(bank(b), [(memKT[prow, hp, mt * 128:(mt + 1) * 128], MQ[prow, hp, qs])],
                                 reads=[r_memKV, r_MQ], writes=[PB[b]])
                        k = (n % 2) * 4 + hh * 2 + mt
                        S.op("act", [lambda e, k=k, b=b: e.activation(out=PT[k], in_=bank(b), func=AF.Exp, scale=0.125)],
                             reads=[PB[b]], writes=[r_PT[k]])

            def stage2(n):
                hp, qt = its[n]
                qs = slice(qt * 512, (qt + 1) * 512)
                ks = [((n % 2) * 4 + hh * 2 + mt, hh, mt) for hh in range(2) for mt in range(2)]
                bo, bd = [4, 6][n % 2], [5, 7][n % 2]
                mm_group(bank(bo), [(memVp[:, mt, hp, hh, :], PT[k]) for (k, hh, mt) in ks],
                         reads=[r_pad] + [r_PT[k] for (k, _, _) in ks], writes=[PB[bo]])
                mm_group(bank(bd), [(onesp[:, hh, :], PT[k]) for (k, hh, mt) in ks],
                         reads=[r_pad] + [r_PT[k] for (k, _, _) in ks], writes=[PB[bd]])
                rd = rdens[n % 2]
                r_rd = r_rdens[n % 2]
                S.op("dve", [lambda e: e.reciprocal(out=rd, in_=bank(bd))], reads=[PB[bd]], writes=[r_rd])
                S.op("dve", [lambda e: e.tensor_tensor(out=YM[:, hp, qs], in0=bank(bo), in1=rd, op=ALU.mult)],
                     reads=[PB[bo], r_rd], writes=[r_YM])

            for n in range(len(its) + 1):
                if n < len(its):
                    stage1(n)
                if n >= 1:
                    stage2(n - 1)

        def out_proj_ln1(layer):
            wout = AV(R_T, [8, 1024], BF16)
            r_wout = Res("wout")
            xs = [AV(R_T + 16 * K_ + i * 4 * K_, [1024], F32) for i in range(2)]
            r_xs = [Res("xs%d" % i) for i in range(2)]
            S.dma("pool", wout, w_out_d[layer].rearrange("(kc p) n -> p kc n", p=128),
                  writes=[r_wout] + ([r_KS] if layer == 0 else []), key="wout")
            load_ln(layer, 0)
            lnp = LNPipe(False)
            for tile in range(16):
                ts_ = slice(tile * 128, (tile + 1) * 128)
                bp = [0, 2, 4][tile % 3]
                i = tile % 2
                ri = tile % NRB
                if layer == 0:
                    S.dma("sp", xs[i], x_d[ts_, :], writes=[r_xs[i]] + ([r_KS] if tile < 2 else []), key="xs%d" % i)
                    xin, rxin = xs[i], r_xs[i]
                else:
                    xin, rxin = X[:, tile, :], r_X[tile]
                for half in range(2):
                    b = bp + half
                    pairs = []
                    for kc in range(8):
                        lhs = YT[:, kc, ts_] if kc < 6 else YM[:, kc - 6, ts_]
                        pairs.append((lhs, wout[:, kc, half * 512:(half + 1) * 512]))
                    mm_group(bank(b), pairs, reads=[r_YT, r_YM, r_wout], writes=[PB[b]])
                rb = rbuf[ri]
                S.op("dve", [lambda e, xin=xin, rb=rb, bp=bp: e.scalar_tensor_tensor(
                    out=rb, in0=xin, scalar=ALPHA, in1=ps[:, bp * 512:bp * 512 + 1024], op0=ALU.mult, op1=ALU.add)],
                     reads=[rxin, PB[bp], PB[bp + 1]], writes=[r_rbuf[ri]])
                lnp.push(tile, rb, [r_rbuf[ri], r_rbuf[ri]], rb, r_rbuf[ri])
            lnp.flush()

        def ffn_ln2(layer, final):
            blocks = [4, 4, 4, 4, 3, 3]
            S.dma("sp", fcwb[:], fcwb_d[layer][:, :, :], writes=[r_fcwb], key="fcwb")
            load_ln(layer, 1)
            GT = AV(R_Y, [4, 2048], BF16)
            r_GT = Res("GT")
            wdn = [AV(R_T + i * 8 * K_, [4, 1024], BF16) for i in range(2)]
            r_wdn = [Res("wdn%d" % i) for i in range(2)]
            wup = [AV(R_T + 16 * K_ + i * 4 * K_, [8, 2, 128], BF16) for i in range(3)]
            r_wup = [Res("wup%d" % i) for i in range(3)]
            hsb = [AV(R_T + 28 * K_ + i * 8208, [2052], F32) for i in range(2)]
            r_hsb = [Res("hsb%d" % i) for i in range(2)]
            for i in range(2):
                S.op("pool", [lambda e, i=i: e.memset(hsb[i][:, 0:1], 0.0)], writes=[r_hsb[i]])
                S.op("pool", [lambda e, i=i: e.memset(hsb[i][:, 2049:2050], 0.0)], writes=[r_hsb[i]])
            acc_as = [AV(R_YM, [2048], F32), AV(R_Y + 16 * K_, [2048], F32)]
            acc_g = AV(R_MQ, [2048], F32)
            sgb = AV(R_T + 50 * K_, [2048], F32)
            r_sgb = Res("sgb")
            r_accas, r_accg = [Res("acca0"), Res("acca1")], Res("accg")
            wd_v = w_dn_d[layer]
            wu_v = w_up_d[layer].rearrange("(kc p) n -> p kc n", p=128)
            def down_proj(bi, nb, wd):
                last = bi == len(blocks) - 1
                if last:
                    S.barrier()
                lnp = LNPipe(final)
                for tile in range(16):
                    ts_ = slice(tile * 128, (tile + 1) * 128)
                    bp = [0, 2, 4][tile % 3] if last else [4, 6][tile % 2]
                    pin = ps[:, bp * 512:bp * 512 + 1024]
                    if not last:
                        for half in range(2):
                            mm_group(bank(bp + half), [(GT[:, jj, ts_], wd[:, jj, half * 512:(half + 1) * 512]) for jj in range(nb)],
                                     reads=[r_GT, r_wdn[bi % 2]], writes=[PB[bp + half]])
                        if bi == 0:
                            S.op("dve", [lambda e, tile=tile, pin=pin: e.scalar_tensor_tensor(
                                out=X[:, tile, :], in0=X[:, tile, :], scalar=ALPHA, in1=pin, op0=ALU.mult, op1=ALU.add)],
                                 reads=[PB[bp], PB[bp + 1]], writes=[r_X[tile]])
                        else:
                            S.op("dve", [lambda e, tile=tile, pin=pin: e.tensor_tensor(
                                out=X[:, tile, :], in0=X[:, tile, :], in1=pin, op=ALU.add)],
                                 reads=[PB[bp], PB[bp + 1]], writes=[r_X[tile]])
                    else:
                        for half in range(2):
                            b = bp + half
                            hsl = slice(half * 512, (half + 1) * 512)
                            fns = []
                            for jj in range(nb):
                                fns.append(lambda e, jj=jj, b=b, hsl=hsl, ts_=ts_: e.matmul(bank(b), GT[:, jj, ts_], wd[:, jj, hsl], start=(jj == 0), stop=False))
                            fns.append(lambda e, b=b, hsl=hsl, tile=tile: e.matmul(bank(b), identf[:], X[:, tile, hsl], start=False, stop=True))
                            S.op("pe", fns, reads=[r_GT, r_wdn[bi % 2], r_X[tile], r_consts], writes=[PB[b]])
                        ri = tile % NRB
                        lnp.push(tile, pin, [PB[bp], PB[bp + 1]], rbuf[ri], r_rbuf[ri])
                lnp.flush()

            j0 = 0
            deferred = None
            deferred_endblk = False
            for bi, nb in enumerate(blocks):
                wd = wdn[bi % 2]
                S.dma("pool", wd[:, 0:nb, :], wd_v[j0 * 128:(j0 + nb) * 128, :].rearrange("(j p) n -> p j n", p=128),
                      writes=[r_wdn[bi % 2]], key="wdn%d" % (bi % 2))
                for jj in range(nb):
                    j = j0 + jj
                    wi = j % 3
                    wu = wup[wi]
                    acc_a, r_acca = acc_as[j % 2], r_accas[j % 2]
                    S.dma("pool", wu[:, :, 0, :], wu_v[:, :, j * 128:(j + 1) * 128], writes=[r_wup[wi]], key="wup%da" % wi)
                    S.dma("pool", wu[:, :, 1, :], wu_v[:, :, DFF + j * 128:DFF + (j + 1) * 128], writes=[r_wup[wi]], key="wup%db" % wi)
                    for ag in range(2):
                        hs_ = hsb[ag]
                        cj = ag * NFF + j
                        for tq in range(4):
                            b = nxt([0, 1, 2, 3])
                            mm_group(bank(b), [(wu[:, kc, ag, :], XT[:, kc, tq * 512:(tq + 1) * 512]) for kc in range(8)],
                                     reads=[r_wup[wi], r_XT], writes=[PB[b]])
                            S.op("act", [lambda e, hs_=hs_, tq=tq, b=b: e.activation(
                                out=hs_[:, 1 + tq * 512:1 + (tq + 1) * 512], in_=bank(b), func=AF.Identity)],
                                 reads=[PB[b]], writes=[r_hsb[ag]])
                        if ag == 0 and deferred is not None and deferred_endblk:
                            deferred()
                            deferred = None
                        acc, r_acc = (acc_a, r_acca) if ag == 0 else (acc_g, r_accg)
                        S.op("act", [lambda e, acc=acc, hs_=hs_, cj=cj: e.activation(
                            out=acc, in_=hs_[:, 1:2049], func=AF.Identity, scale=fcwb[:, cj, 1:2], bias=fcwb[:, cj, 3:4])],
                             reads=[r_hsb[ag], r_fcwb], writes=[r_acc])
                        S.op("dve", [lambda e, acc=acc, hs_=hs_, cj=cj: e.scalar_tensor_tensor(
                            out=acc, in0=hs_[:, 0:2048], scalar=fcwb[:, cj, 0:1], in1=acc, op0=ALU.mult, op1=ALU.add)],
                             reads=[r_hsb[ag], r_fcwb, r_acc], writes=[r_acc])
                        S.op("dve", [lambda e, acc=acc, hs_=hs_, cj=cj: e.scalar_tensor_tensor(
                            out=acc, in0=hs_[:, 2:2050], scalar=fcwb[:, cj, 2:3], in1=acc, op0=ALU.mult, op1=ALU.add)],
                             reads=[r_hsb[ag], r_fcwb, r_acc], writes=[r_acc])
                        if ag == 0 and deferred is not None:
                            deferred()
                            deferred = None

                    def _fin(jj=jj, acc_a=acc_a, r_acca=r_acca, endblk=(jj == nb - 1), bi=bi, nb=nb, wd=wd):
                        S.op("act", [lambda e: e.activation(out=sgb, in_=acc_g, func=AF.Silu)], reads=[r_accg], writes=[r_sgb])
                        S.op("dve", [lambda e: e.tensor_tensor(out=GT[:, jj, :], in0=sgb, in1=acc_a, op=ALU.mult)],
                             reads=[r_sgb, r_acca], writes=[r_GT])
                        if endblk:
                            down_proj(bi, nb, wd)
                    deferred = _fin
                    deferred_endblk = (jj == nb - 1)
                j0 += nb
            if deferred is not None:
                deferred()
                deferred = None

        memf = AV(R_T + 8 * K_, [2, 1024], F32)
        memb = AV(R_T + 28 * K_, [2, 1024], BF16)
        memT = AV(R_T + 32 * K_, [8, 256], BF16)
        wkv = AV(R_T, [8, 512], BF16)
        r_memf, r_memb, r_memT, r_wkv = Res("memf"), Res("memb"), Res("memT"), Res("wkv")
        S.dma("sp", memf, mem_d.rearrange("(mt p) d -> p mt d", p=128), writes=[r_memf], key="memf")
        S.dma("pool", wkv, wkv_d.rearrange("(kc p) n -> p kc n", p=128), writes=[r_wkv], key="wkv")
        act_copy(memb, memf, [r_memf], [r_memb])
        for mt in range(2):
            b = nxt([6, 7])
            bb = bank_bf(b)
            fns = [lambda e, kc=kc, mt=mt, bb=bb: e.transpose(bb[:, kc * 128:(kc + 1) * 128], memb[:, mt, kc * 128:(kc + 1) * 128], ident[:])
                   for kc in range(8)]
            S.op("pe", fns, reads=[r_memb, r_consts], writes=[PB[b]])
            S.op("act", [lambda e, mt=mt, bb=bb: e.activation(out=memT[:, :, mt * 128:(mt + 1) * 128],
                                                             in_=bb.rearrange("p (k m) -> p k m", k=8), func=AF.Identity)],
                 reads=[PB[b]], writes=[r_memT])
        for hp in range(2):
            b = nxt([0, 1])
            mm_group(bank(b, 256), [(wkv[:, kc, hp * 128:(hp + 1) * 128], memT[:, kc, :]) for kc in range(8)],
                     reads=[r_wkv, r_memT], writes=[PB[b]])
            act_copy(memKT[:, hp, :], bank(b, 256), [PB[b]], [r_memKV])
        for mt in range(2):
            b = nxt([0, 1])
            mm_group(bank(b, 256), [(memT[:, kc, mt * 128:(mt + 1) * 128], wkv[:, kc, 256:512]) for kc in range(8)],
                     reads=[r_wkv, r_memT], writes=[PB[b]])
            act_copy(memV[:, mt, :], bank(b, 256), [PB[b]], [r_memKV])

        if debug == "s_m":
            S.barrier()
            S.emit()
            return nc
        xs0 = [AV(R_T + 16 * K_ + i * 4 * K_, [1024], F32) for i in range(2)]
        r_xs0 = [Res("xs0_%d" % i) for i in range(2)]

        def x_step(tile):
            i = tile % 2
            i4 = tile % 4
            S.dma("sp", xs0[i], x_d[tile * 128:(tile + 1) * 128, :], writes=[r_xs0[i]], key="xs%d" % i)
            S.op("act", [lambda e: e.activation(out=xbuf[i4], in_=xs0[i], func=AF.Identity)],
                 reads=[r_xs0[i]], writes=[r_xbuf[i4]])
            transpose_to_XT(xbuf[i4], tile, r_xbuf[i4])
        x_steps = [(lambda t=t: x_step(t)) for t in range(16)]
        if debug == "s_x":
            S.barrier()
            S.emit()
            return nc
        fbs = stat[0:64, 120:123]
        for l in range(3):
            S.op("dve", [lambda e, l=l: e.tensor_tensor(out=fbs[:, l:l + 1], in0=fbf[0:64, 2 * l:2 * l + 1],
                                                        in1=fbf[0:64, 2 * l + 1:2 * l + 2], op=ALU.mult)],
                 reads=[r_f], writes=[r_stat])
        srcs = [(zf, 33, r_f), (hA, 64, r_hA), (hB, 64, r_hB)]
        dsts = [(hA, r_hA), (hB, r_hB), (hA, r_hA)]
        wtW = AV(R_T + 36 * K_, [2048], F32)
        wt2W = AV(R_T + 44 * K_, [2048], F32)
        r_wtW, r_wt2W = Res("wtW"), Res("wt2W")
        dec2 = [dec, AV(R_MQ, [768], F32)]
        r_dec2 = [Res("dec0"), Res("dec1")]
        f_steps = []

        def mlp_layer(l):
            src, kk, r_src = srcs[l]
            dst, r_dst = dsts[l]
            for tq in range(4):
                cs = slice(tq * 512, (tq + 1) * 512)
                mm_group(bank(tq, 512, 0, 64), [(fw[0:kk, l, :], src[0:kk, cs])], reads=[r_f, r_src], writes=[PB[tq]])
            pall = ps[0:64, 0:2048]
            S.op("dve", [lambda e: e.tensor_scalar(out=wtW[0:64, :], in0=pall, scalar1=fbf[0:64, 2 * l + 1:2 * l + 2],
                                                   scalar2=fbs[:, l:l + 1], op0=ALU.mult, op1=ALU.add)],
                 reads=[PB[0], PB[1], PB[2], PB[3], r_f, r_stat], writes=[r_wtW])
            S.op("dve", [lambda e: e.tensor_scalar(out=wt2W[0:64, :], in0=wtW[0:64, :], scalar1=-PI, scalar2=2 * PI,
                                                   op0=ALU.is_lt, op1=ALU.mult)], reads=[r_wtW], writes=[r_wt2W])
            S.op("dve", [lambda e: e.tensor_tensor(out=wtW[0:64, :], in0=wtW[0:64, :], in1=wt2W[0:64, :], op=ALU.add)],
                 reads=[r_wtW, r_wt2W], writes=[r_wtW])
            S.op("dve", [lambda e: e.tensor_scalar(out=wt2W[0:64, :], in0=wtW[0:64, :], scalar1=PI, scalar2=-2 * PI,
                                                   op0=ALU.is_gt, op1=ALU.mult)], reads=[r_wtW], writes=[r_wt2W])
            S.op("dve", [lambda e: e.tensor_tensor(out=wtW[0:64, :], in0=wtW[0:64, :], in1=wt2W[0:64, :], op=ALU.add)],
                 reads=[r_wtW, r_wt2W], writes=[r_wtW])
            S.op("act", [lambda e: e.activation(out=dst[0:64, :], in_=wtW[0:64, :], func=AF.Sin)],
                 reads=[r_wtW], writes=[r_dst])

        def wcomb():
            S.op("dve", [lambda e: e.tensor_tensor(out=fsb[0:64, :], in0=fwo[0:64, 0:768], in1=fwo[0:64, 768:1536], op=ALU.add)],
                 reads=[r_f], writes=[r_fsb])
            S.op("dve", [lambda e: e.tensor_tensor(out=fwo[0:64, 768:1536], in0=fwo[0:64, 0:768], in1=fwo[0:64, 768:1536], op=ALU.subtract)],
                 reads=[r_f], writes=[r_f])
            S.op("dve", [lambda e: e.tensor_copy(out=fwo[0:64, 0:768], in_=fsb[0:64, :])], reads=[r_fsb], writes=[r_f])

        def hfull(tile):
            ts_ = slice(tile * 128, (tile + 1) * 128)
            b0 = 4 if tile % 2 else 0
            dc, r_dc = dec2[tile % 2], r_dec2[tile % 2]
            for q3 in range(3):
                mm_group(bank(b0 + q3), [(hA[0:64, ts_], fwo[0:64, q3 * 512:(q3 + 1) * 512])], reads=[r_hA, r_f], writes=[PB[b0 + q3]])
            S.op("act", [lambda e: e.activation(out=dc, in_=dlt, func=AF.Exp, scale=ntn[:, tile:tile + 1])],
                 reads=[r_f], writes=[r_dc])
            S.op("dve", [lambda e: e.tensor_tensor(out=HS[:, 0, tile, :], in0=ps[:, b0 * 512:b0 * 512 + 768], in1=dc, op=ALU.mult)],
                 reads=[PB[b0], PB[b0 + 1], r_dc], writes=[r_HS])
            S.op("dve", [lambda e: e.tensor_tensor(out=HS[:, 1, tile, :], in0=ps[:, b0 * 512 + 768:b0 * 512 + 1536], in1=dc, op=ALU.mult)],
                 reads=[PB[b0 + 1], PB[b0 + 2], r_dc], writes=[r_HS])

        f_steps.append(wcomb)
        for l in range(3):
            f_steps.append(lambda l=l: mlp_layer(l))
        for tile in range(16):
            f_steps.append(lambda tile=tile: hfull(tile))
        xi = 0
        for k_, fs in enumerate(f_steps):
            fs()
            if k_ >= 1 and xi < 16:
                x_steps[xi]()
                xi += 1
        while xi < 16:
            x_steps[xi]()
            xi += 1
        S.barrier()

        if debug == "s_f":
            S.barrier()
            S.emit()
            return nc
        KS = AV(R_T, [2, 16, 768], BF16)
        r_KS = Res("KS")

        def fwd_pass(rhs_re, rhs_im, r_rhs, ftab, r_ftab, epilogue, extra_w=((), ())):
            for fc in range(16):
                si = fc % 2
                ft = ftab[si]
                S.dma("sp", ft.rearrange("p a b c -> p (a b c)"), fwd_tab_d[fc],
                      writes=[r_ftab[si]] + (list(extra_w[si]) if fc < 2 else []), key="ftab%d" % si)
                bs = [0, 1, 2, 3] if fc % 2 == 0 else [4, 5, 6, 7]
                for ri in range(2):
                    rhs = rhs_re if ri == 0 else rhs_im
                    o0 = bs[0] * 512 + ri * 1024
                    fns = []
                    for tc in range(16):
                        lhs = ft[:, ri, tc, :]
                        fns.append(lambda e, lhs=lhs, tc=tc, rhs=rhs, o0=o0: e.matmul(
                            ps[:, o0:o0 + 512], lhs, rhs[:, tc, 0:512], start=(tc == 0), stop=(tc == 15)))
                        fns.append(lambda e, lhs=lhs, tc=tc, rhs=rhs, o0=o0: e.matmul(
                            ps[:, o0 + 512:o0 + 768], lhs, rhs[:, tc, 512:768], start=(tc == 0), stop=(tc == 15)))
                    S.op("pe", fns, reads=[r_ftab[si], r_rhs], writes=[PB[bs[2 * ri]], PB[bs[2 * ri + 1]]])
                pre = ps[:, bs[0] * 512:bs[0] * 512 + 768]
                pim = ps[:, bs[2] * 512:bs[2] * 512 + 768]
                epilogue(fc, pre, pim, [PB[b] for b in bs])

        ftab = [AV(R_YM, [2, 16, 128], BF16), AV(R_MQ, [2, 16, 128], BF16)]
        r_ftab = [Res("ftab0"), Res("ftab1")]

        def k_epilogue(fc, pre, pim, rbs):
            S.op("dve", [lambda e: e.tensor_tensor(out=KS[:, 0, fc, :], in0=pre, in1=dbc, op=ALU.add)],
                 reads=rbs[0:2] + [r_dbc], writes=[r_KS])
            act_copy(KS[:, 1, fc, :], pim, rbs[2:4], [r_KS])

        fwd_pass(HS[:, 0], HS[:, 1], r_HS, ftab, r_ftab, k_epilogue)

        if debug == "s_k":
            S.barrier()
            S.emit()
            return nc
        Z = AV(R_X, [16, 768], BF16)
        r_Z = Res("Z")
        hsb0 = [AV(R_X + 24 * K_ + i * 8208, [2052], F32) for i in range(2)]
        r_hsb0 = [Res("hsb0_%d" % i) for i in range(2)]
        accx = AV(R_X + 24 * K_ + 16416, [2048], F32)
        accv = AV(R_X + 24 * K_ + 16416 + 8192, [2048], F32)
        zT = AV(R_X + 24 * K_ + 16416 + 16384, [2048], BF16)
        r_accx, r_accv, r_zT = Res("accx"), Res("accv"), Res("zT")
        wch = [AV(R_T + 48 * K_ + i * 2 * K_, [8, 128], BF16) for i in range(3)]
        r_wch = [Res("wch%d" % i) for i in range(3)]
        for i in range(2):
            S.op("pool", [lambda e, i=i: e.memset(hsb0[i][:, 0:1], 0.0)], writes=[r_hsb0[i], r_HS])
            S.op("pool", [lambda e, i=i: e.memset(hsb0[i][:, 2049:2050], 0.0)], writes=[r_hsb0[i], r_HS])
        w0v = w_in_d[0].rearrange("(kc p) n -> p kc n", p=128)
        order = [18, 19] + [0, 1, 2, 3, 4, 5]
        for i in range(6):
            order += [6 + i, 12 + i]
        _wc = [0]

        def proj_chunk(wv, col0, sink_fn, extra_reads=()):
            wi = _wc[0] % 3
            _wc[0] += 1
            S.dma("pool", wch[wi], wv[:, :, col0:col0 + 128], writes=[r_wch[wi]], key="wch%d" % wi)
            for tq in range(4):
                b = nxt([0, 1, 2, 3])
                mm_group(bank(b), [(wch[wi][:, kc, :], XT[:, kc, tq * 512:(tq + 1) * 512]) for kc in range(8)],
                         reads=[r_wch[wi], r_XT], writes=[PB[b]])
                sink_fn(tq, b)

        def conv_chunk(c, hs_, r_hs, acc_out, r_acc_list, out_final):
            S.op("act", [lambda e: e.activation(out=acc_out, in_=hs_[:, 1:2049], func=AF.Identity,
                                                scale=cwb[:, c, 1:2], bias=cwb[:, c, 3:4])],
                 reads=[r_hs, r_cwb], writes=r_acc_list)
            S.op("dve", [lambda e: e.scalar_tensor_tensor(out=acc_out, in0=hs_[:, 0:2048], scalar=cwb[:, c, 0:1],
                                                          in1=acc_out, op0=ALU.mult, op1=ALU.add)],
                 reads=[r_hs, r_cwb] + r_acc_list, writes=r_acc_list)
            S.op("dve", [lambda e: e.scalar_tensor_tensor(out=out_final[0], in0=hs_[:, 2:2050], scalar=cwb[:, c, 2:3],
                                                          in1=acc_out, op0=ALU.mult, op1=ALU.add)],
                 reads=[r_hs, r_cwb] + r_acc_list, writes=out_final[1])

        zdef = []
        for ci_, c in enumerate(order):
            if debug and debug.startswith('s_p') and ci_ == int(debug[3:]):
                S.barrier()
                S.emit()
                return nc
            if c >= 18:
                hp = c - 18

                def sink_mq(tq, b, hp=hp):
                    act_copy(MQ[:, hp, tq * 512:(tq + 1) * 512], bank(b), [PB[b]], [r_MQ])
                proj_chunk(w0v, c * 128, sink_mq)
                continue
            si = (c % 2) if c < 6 else (0 if c < 12 else 1)
            hs_ = hsb0[si]

            def sink_h(tq, b, hs_=hs_, si=si):
                act_copy(hs_[:, 1 + tq * 512:1 + (tq + 1) * 512], bank(b), [PB[b]], [r_hsb0[si]])
            proj_chunk(w0v, c * 128, sink_h)
            if c < 6:
                conv_chunk(c, hs_, r_hsb0[si], accv, [r_accv], (YT[:, c, :], [r_YT]))
            elif c < 12:
                conv_chunk(c, hs_, r_hsb0[si], accx, [r_accx], (accx, [r_accx]))
                while zdef:
                    zdef.pop(0)()
            else:
                i6 = c - 12
                conv_chunk(c, hs_, r_hsb0[si], accv, [r_accv], (accv, [r_accv]))
                S.op("dve", [lambda e: e.tensor_tensor(out=zT, in0=accv, in1=accx, op=ALU.mult)],
                     reads=[r_accv, r_accx], writes=[r_zT])
                def _ztr(i6=i6):
                    for g8 in range(2):
                        b = nxt([6, 7])
                        bb = bank_bf(b)
                        fns = [lambda e, t8=t8, bb=bb, g8=g8: e.transpose(bb[:, t8 * 128:(t8 + 1) * 128],
                                                                         zT[:, (g8 * 8 + t8) * 128:(g8 * 8 + t8 + 1) * 128], ident[:])
                               for t8 in range(8)]
                        S.op("pe", fns, reads=[r_zT, r_consts], writes=[PB[b]])
                        S.op("act", [lambda e, g8=g8, bb=bb, i6=i6: e.activation(
                            out=Z[:, g8 * 8:(g8 + 1) * 8, i6 * 128:(i6 + 1) * 128],
                            in_=bb.rearrange("p (k m) -> p k m", k=8), func=AF.Identity)],
                             reads=[PB[b]], writes=[r_Z])
                zdef.append(_ztr)
        while zdef:
            zdef.pop(0)()
        if debug == "z":
            for tile in range(16):
                S.op("act", [lambda e, tile=tile: e.activation(out=X[:, tile, 0:768] if False else hsb0[0][:, 0:768], in_=Z[:, tile, :], func=AF.Identity)],
                     reads=[r_Z], writes=[r_hsb0[0]])
                S.dma("sp", dbg_d[tile * 128:(tile + 1) * 128, 0:768], hsb0[0][:, 0:768], reads=[r_hsb0[0]], key="dbg")
            S.barrier()
            S.emit()
            return nc
        mem_attention(R_X + 24 * K_)

        YRE = AV(R_XT, [16, 768], BF16)
        YIM = AV(R_X + 24 * K_, [16, 768], BF16)
        r_Y = Res("Yspec")
        ct = [AV(R_X + 48 * K_ + i * 3 * K_, [768], F32) for i in range(4)]
        r_ct = [Res("ct%d" % i) for i in range(4)]
        ftabU = [AV(R_T + 48 * K_, [2, 16, 128], BF16), AV(R_MQ, [2, 16, 128], BF16)]
        r_ftabU = [Res("ftabU0"), Res("ftabU1")]

        def u_epilogue(fc, pre, pim, rbs):
            kre, kim = KS[:, 0, fc, :], KS[:, 1, fc, :]
            S.op("dve", [lambda e: e.tensor_tensor(out=ct[0], in0=pre, in1=kre, op=ALU.mult)], reads=rbs[0:2] + [r_KS], writes=[r_ct[0]])
            S.op("dve", [lambda e: e.tensor_tensor(out=ct[1], in0=pim, in1=kim, op=ALU.mult)], reads=rbs[2:4] + [r_KS], writes=[r_ct[1]])
            S.op("dve", [lambda e: e.tensor_tensor(out=ct[2], in0=pre, in1=kim, op=ALU.mult)], reads=rbs[0:2] + [r_KS], writes=[r_ct[2]])
            S.op("dve", [lambda e: e.tensor_tensor(out=ct[3], in0=pim, in1=kre, op=ALU.mult)], reads=rbs[2:4] + [r_KS], writes=[r_ct[3]])
            S.op("pool", [lambda e: e.tensor_tensor(out=YRE[:, fc, :], in0=ct[0], in1=ct[1], op=ALU.subtract)],
                 reads=[r_ct[0], r_ct[1]], writes=[r_Y])
            S.op("pool", [lambda e: e.tensor_tensor(out=YIM[:, fc, :], in0=ct[2], in1=ct[3], op=ALU.add)],
                 reads=[r_ct[2], r_ct[3]], writes=[r_Y])

        fwd_pass(Z, Z, r_Z, ftabU, r_ftabU, u_epilogue, extra_w=(r_wch, [r_MQ]))

        itab = [AV(R_T + 48 * K_, [4, 2, 512], BF16), AV(R_MQ, [4, 2, 512], BF16)]
        r_itab = r_ftabU
        _it = 0
        for tt in range(4):
            fn_all = []
            for fg in range(4):
                si = _it % 2
                _it += 1
                S.dma("sp", itab[si].rearrange("p a b c -> p (a b c)"), inv_tab_d[tt, fg], writes=[r_itab[si]], key="itab%d" % si)
                fns = []
                for fi in range(4):
                    fc = fg * 4 + fi
                    for ri in range(2):
                        Ysrc = YRE if ri == 0 else YIM
                        for cc in range(6):
                            first = (fc == 0 and ri == 0)
                            lastm = (fc == 15 and ri == 1)
                            fns.append(lambda e, cc=cc, Ysrc=Ysrc, fc=fc, si=si, fi=fi, ri=ri, first=first, lastm=lastm: e.matmul(
                                bank(cc), Ysrc[:, fc, cc * 128:(cc + 1) * 128], itab[si][:, fi, ri, :], start=first, stop=lastm))
                S.op("pe", fns, reads=[r_itab[si], r_Y], writes=[PB[cc] for cc in range(6)])
            for cc in range(6):
                S.op("dve", [lambda e, cc=cc, tt=tt: e.tensor_tensor(out=YT[:, cc, tt * 512:(tt + 1) * 512], in0=bank(cc),
                                                                    in1=YT[:, cc, tt * 512:(tt + 1) * 512], op=ALU.mult)],
                     reads=[PB[cc], r_YT], writes=[r_YT])

        if debug == "mix0":
            for c in range(8):
                src = YT[:, c, 0:1024] if c < 6 else YM[:, c - 6, 0:1024]
                S.op("act", [lambda e, src=src: e.activation(out=rbuf[0], in_=src, func=AF.Identity)], reads=[r_YT, r_YM], writes=[r_rbuf[0]])
                S.dma("sp", dbg_d[c * 128:(c + 1) * 128, :], rbuf[0], reads=[r_rbuf[0]], key="dbg")
            S.barrier()
            S.emit()
            return nc

        out_proj_ln1(0)
        S.barrier()
        if debug == "ln1_0":
            for tile in range(16):
                S.dma("sp", dbg_d[tile * 128:(tile + 1) * 128, :], X[:, tile, :], reads=[r_X[tile]], key="dbg")
            S.barrier()
            S.emit()
            return nc
        ffn_ln2(0, final=(debug == "l0"))
        S.barrier()

        if debug is None or debug.startswith("l1"):
            if debug == "l1_s":
                S.barrier()
                S.emit()
                return nc
            w1v = w_in_d[1].rearrange("(kc p) n -> p kc n", p=128)
            QT = YT
            r_QT = [Res("QT%d" % i) for i in range(12)]
            KT = AV(R_T, [4, 2048], BF16)
            r_KT = Res("KT")
            VT = AV(R_T + 16 * K_, [16, 256], BF16)
            r_VT = Res("VT")
            ropec = AV(R_T + 24 * K_, [2048], F32)
            ropes = AV(R_T + 32 * K_, [2048], F32)
            r_rope = Res("rope")
            wch1 = [AV(R_T + 40 * K_ + i * 2 * K_, [8, 128], BF16) for i in range(3)]
            r_wch1 = [Res("wch1_%d" % i) for i in range(3)]
            qsb = [AV(R_T + 46 * K_ + i * K_, [512], BF16) for i in range(2)]
            r_qsb = [Res("qsb%d" % i) for i in range(2)]
            rt1 = AV(R_T + 48 * K_, [512], F32)
            rt2 = AV(R_T + 50 * K_, [512], F32)
            r_rt1, r_rt2 = Res("rt1"), Res("rt2")
            wvt = AV(R_T + 52 * K_, [8, 256], BF16)
            r_wvt = Res("wvt")
            esk = small("esk", [64, 12], F32)
            r_esk = Res("esk")
            S.dma("sp", ropec, ropec_d[:, :], writes=[r_rope], key="rope0")
            S.dma("sp", ropes, ropes_d[:, :], writes=[r_rope], key="rope1")
            S.dma("sp", esk[:], sink_d.partition_broadcast(64), writes=[r_esk], key="esk")
            S.op("act", [lambda e: e.activation(out=esk[:], in_=esk[:], func=AF.Exp)], reads=[r_esk], writes=[r_esk])
            S.dma("pool", wvt, w1v[:, :, 1024:1280], writes=[r_wvt], key="wvt")
            if debug == "l1_p0":
                S.barrier()
                S.emit()
                return nc
            _w1 = [0]
            _rq = [0]

            def proj1(loads, sink_fn):
                wi = _w1[0] % 3
                _w1[0] += 1
                for k_, (dst0, dst1, c0, c1) in enumerate(loads):
                    S.dma("pool", wch1[wi][:, :, dst0:dst1], w1v[:, :, c0:c1], writes=[r_wch1[wi]], key="wch1_%d_%d" % (wi, k_))
                for tq in range(4):
                    b = nxt([0, 1, 2, 3])
                    mm_group(bank(b), [(wch1[wi][:, kc, :], XT[:, kc, tq * 512:(tq + 1) * 512]) for kc in range(8)],
                             reads=[r_wch1[wi], r_XT], writes=[PB[b]])
                    sink_fn(tq, b)

            def rope_sink(dest, r_dest):
                def sink(tq, b):
                    cs = slice(tq * 512, (tq + 1) * 512)
                    i = _rq[0] % 2
                    _rq[0] += 1
                    S.op("act", [lambda e: e.activation(out=qsb[i], in_=bank(b), func=AF.Identity)],
                         reads=[PB[b]], writes=[r_qsb[i]])
                    b2 = [4, 5][i]
                    mm_group(bank(b2), [(pm[:], qsb[i])], reads=[r_consts, r_qsb[i]], writes=[PB[b2]])
                    S.op("dve", [lambda e: e.tensor_tensor(out=rt1, in0=bank(b), in1=ropec[:, cs], op=ALU.mult)],
                         reads=[PB[b], r_rope], writes=[r_rt1])
                    S.op("dve", [lambda e: e.tensor_tensor(out=rt2, in0=bank(b2), in1=ropes[:, cs], op=ALU.mult)],
                         reads=[PB[b2], r_rope], writes=[r_rt2])
                    S.op("dve", [lambda e: e.tensor_tensor(out=dest[:, cs], in0=rt1, in1=rt2, op=ALU.add)],
                         reads=[r_rt1, r_rt2], writes=r_dest)
                return sink

            for c in range(6):
                proj1([(0, 128, c * 128, (c + 1) * 128)], rope_sink(QT[:, c, :], [r_QT[2 * c], r_QT[2 * c + 1]]))
            if debug == "l1_p1":
                S.barrier()
                S.emit()
                return nc
            for g in range(4):
                c0 = 768 + g * 64
                proj1([(0, 64, c0, c0 + 64), (64, 128, c0, c0 + 64)], rope_sink(KT[:, g, :], [r_KT]))
            if debug == "l1_p2":
                S.barrier()
                S.emit()
                return nc
            for hp in range(2):
                def sink_mq1(tq, b, hp=hp):
                    act_copy(MQ[:, hp, tq * 512:(tq + 1) * 512], bank(b), [PB[b]], [r_MQ])
                proj1([(0, 128, 1280 + hp * 128, 1280 + (hp + 1) * 128)], sink_mq1)
            for tile in range(16):
                b = nxt([0, 1, 2, 3])
                mm_group(bank(b, 256), [(XT[:, kc, tile * 128:(tile + 1) * 128], wvt[:, kc, :]) for kc in range(8)],
                         reads=[r_XT, r_wvt], writes=[PB[b]])
                act_copy(VT[:, tile, :], bank(b, 256), [PB[b]], [r_VT])
            S.barrier()

            if debug == "l1_p":
                S.barrier()
                S.emit()
                return nc
            PTs = [AV(R_XT + i * 12 * K_, [16, 384], BF16) for i in range(2)]
            r_PTs = [Res("PTs%d" % i) for i in range(2)]
            rden1s = [AV(R_XT + 24 * K_ + i * 2 * K_, [512], F32) for i in range(2)]
            r_rden1s = [Res("rden1_%d" % i) for i in range(2)]
            def st_exp(h, j):
                g = h // 3
                hh = h % 2
                c = h // 2
                prow = slice(hh * 64, (hh + 1) * 64)
                pt = PTs[h % 2]
                r_pt = r_PTs[h % 2]
                qlo = max(0, j - 1) * 128
                qhi = min(16, j + 2) * 128
                n = qhi - qlo
                moff = qlo - (j - 1) * 128
                b = nxt([0, 1, 2, 3])
                fns = [
                    lambda e: e.matmul(bank(b, n), KT[prow, g, j * 128:(j + 1) * 128], QT[prow, c, qlo:qhi], start=True, stop=False),
                    lambda e: e.matmul(bank(b, n), ident[:], maskb[:, moff:moff + n], start=False, stop=True),
                ]
                S.op("pe", fns, reads=[r_KT, r_QT[h], r_consts], writes=[PB[b]])
                S.op("act", [lambda e: e.activation(out=pt[:, j, 0:n], in_=bank(b, n), func=AF.Exp, scale=0.125)],
                     reads=[PB[b]], writes=[r_pt])

            def pv_norm(h, qt):
                g = h // 3
                hh = h % 2
                c = h // 2
                prow = slice(hh * 64, (hh + 1) * 64)
                pt = PTs[h % 2]
                r_pt = r_PTs[h % 2]
                bo, bd = [4, 6][qt % 2], [5, 7][qt % 2]
                fo, fd = [], []
                for i4 in range(4):
                    qb = 4 * qt + i4
                    js = [j for j in (qb - 1, qb, qb + 1) if 0 <= j < 16]
                    for k_, j in enumerate(js):
                        lc = (qb - max(0, j - 1)) * 128
                        st_, sp_ = (k_ == 0), (k_ == len(js) - 1)
                        fo.append(lambda e, i4=i4, j=j, lc=lc, st_=st_, sp_=sp_: e.matmul(
                            ps[0:64, bo * 512 + i4 * 128:bo * 512 + (i4 + 1) * 128], VT[:, j, g * 64:(g + 1) * 64],
                            pt[:, j, lc:lc + 128], start=st_, stop=sp_))
                        fd.append(lambda e, i4=i4, j=j, lc=lc, st_=st_, sp_=sp_: e.matmul(
                            ps[0:64, bd * 512 + i4 * 128:bd * 512 + (i4 + 1) * 128], ones64[:],
                            pt[:, j, lc:lc + 128], start=st_, stop=sp_))
                S.op("pe", fo, reads=[r_pt, r_VT], writes=[PB[bo]])
                S.op("pe", fd, reads=[r_pt, r_consts], writes=[PB[bd]])
                rd = rden1s[qt % 2]
                r_rd = r_rden1s[qt % 2]
                S.op("act", [lambda e: e.activation(out=rd[0:64, :], in_=bank(bd, 512, 0, 64), func=AF.Identity,
                                                    bias=esk[:, h:h + 1])],
                     reads=[PB[bd], r_esk], writes=[r_rd])
                S.op("dve", [lambda e: e.reciprocal(out=rd[0:64, :], in_=rd[0:64, :])], reads=[r_rd], writes=[r_rd])
                S.op("dve", [lambda e: e.tensor_tensor(
                    out=QT[prow, c, qt * 512:(qt + 1) * 512], in0=bank(bo, 512, 0, 64), in1=rd[0:64, :], op=ALU.mult)],
                     reads=[PB[bo], r_rd], writes=[r_QT[h]])

            for h in range(13):
                for qt in range(4):
                    if h < 12:
                        for j in range(4 * qt, 4 * qt + 4):
                            st_exp(h, j)
                    if h >= 1:
                        pv_norm(h - 1, qt)
            S.barrier()
            if debug == "l1_a":
                S.barrier()
                S.emit()
                return nc
            mem_attention(R_XT)
            S.barrier()
            if debug == "l1mix":
                for c in range(8):
                    src = YT[:, c, 0:1024] if c < 6 else YM[:, c - 6, 0:1024]
                    S.op("act", [lambda e, src=src: e.activation(out=rbuf[0], in_=src, func=AF.Identity)], reads=[r_YM], writes=[r_rbuf[0]])
                    S.dma("sp", dbg_d[c * 128:(c + 1) * 128, :], rbuf[0], reads=[r_rbuf[0]], key="dbg")
                S.barrier()
                S.emit()
                return nc
            out_proj_ln1(1)
            S.barrier()
            ffn_ln2(1, final=True)
            S.barrier()

        S.barrier()
        S.emit()
    return nc


def prep_shared(inputs):
    f32 = np.float32
    sh = {}
    for k in ("w_mem_kv", "l0_w_in", "l1_w_in", "l0_w_out", "l1_w_out", "l0_ffn_w_up", "l1_ffn_w_up",
              "l0_ffn_w_down", "l1_ffn_w_down", "l0_filt_w1", "l0_filt_w2", "l0_filt_w3", "l0_filt_w_out",
              "l0_hyena_d", "l1_sink"):
        sh[k] = np.ascontiguousarray(np.asarray(inputs[k], dtype=f32))
    for i in range(2):
        for n in ("ln1_g", "ln1_b", "ln2_g", "ln2_b"):
            sh["l%d_%s" % (i, n)] = np.ascontiguousarray(np.asarray(inputs["l%d_%s" % (i, n)], dtype=f32))
        cw = np.asarray(inputs["l%d_ffn_conv_w" % i], f32)
        cb = np.asarray(inputs["l%d_ffn_conv_b" % i], f32)
        a = np.concatenate([cw, cb[None, :]], axis=0)
        sh["l%d_fcwb" % i] = np.ascontiguousarray(a.reshape(4, 44, 128).transpose(2, 1, 0))
    cw = np.asarray(inputs["l0_conv_w"], f32)
    cb = np.asarray(inputs["l0_conv_b"], f32)
    a = np.concatenate([cw, cb[None, :]], axis=0)
    sh["l0_cwb"] = np.ascontiguousarray(a.reshape(4, 18, 128).transpose(2, 1, 0))
    sh["l0_fbf"] = np.ascontiguousarray(np.stack(
        [np.asarray(inputs["l0_filt_%s%d" % (n, l)], f32) for l in (1, 2, 3) for n in ("b", "f")], axis=1))
    sh.update(const_tables())
    return sh


_NC_CACHE = {}


def kernel(**inputs):
    sh = prep_shared(inputs)
    x = np.asarray(inputs["x"], np.float32)
    mem = np.asarray(inputs["mem"], np.float32)
    if "nc" not in _NC_CACHE:
        _NC_CACHE["nc"] = build_program()
    nc = _NC_CACHE["nc"]
    in_maps = []
    for b in range(8):
        m = dict(sh)
        m["x"] = np.ascontiguousarray(x[b])
        m["mem"] = np.ascontiguousarray(mem[b])
        in_maps.append(m)
    res = run_bass_kernel_spmd(nc, in_maps, core_ids=list(range(8)))
    return np.stack([np.asarray(r["out"], np.float32) for r in res.results], axis=0)
```

```python
import math
from contextlib import ExitStack

import numpy as np
import ml_dtypes
import concourse.bass as bass
import concourse.mybir as mybir
from concourse.bass_utils import run_bass_kernel_spmd

F32 = mybir.dt.float32
BF16 = mybir.dt.bfloat16
AF = mybir.ActivationFunctionType
ALU = mybir.AluOpType
AX = mybir.AxisListType

L = 2048
D = 1024
NT = 16
KC = 8
DFF = 2816
NFF = 22
ALPHA = 4.0 ** 0.25
EPS = 1e-5
PI = float(np.pi)


class Res:
    __slots__ = ("name", "w", "r", "excl")

    def __init__(self, name, excl=False):
        self.name = name
        self.w = None
        self.r = []
        self.excl = excl


class Sched:
    def __init__(self, nc, es):
        self.nc = nc
        self.es = es
        self.engs = {}
        for n in ("pe", "act", "dve", "pool", "sp"):
            sem = es.enter_context(nc.semaphore("s_" + n))
            self.engs[n] = dict(sem=sem, count=0, known={}, ops=[])
        self.dma_sems = {}
        self.n_dma_sems = 0

    def dma_sem(self, key):
        if key not in self.dma_sems:
            sem = self.es.enter_context(self.nc.semaphore("d%d" % self.n_dma_sems))
            self.n_dma_sems += 1
            self.dma_sems[key] = [sem, 0]
        return self.dma_sems[key]

    def op(self, eng, fns, reads=(), writes=(), dma_key=None):
        E = self.engs[eng]
        deps = {}

        def add(ev):
            if ev is None:
                return
            sem, val = ev
            if deps.get(sem, 0) < val:
                deps[sem] = val

        excl_reads = [r for r in reads if r.excl]
        writes = list(writes) + [r for r in excl_reads if r not in writes]
        reads = [r for r in reads if not r.excl]
        for r in reads:
            add(r.w)
        for w in writes:
            add(w.w)
            for ev in w.r:
                add(ev)
        waits = []
        for sem, val in deps.items():
            if sem is E["sem"] and eng == "pe" and dma_key is None:
                continue
            if E["known"].get(sem, 0) >= val:
                continue
            E["known"][sem] = val
            waits.append((sem, val))
        if dma_key is not None:
            ds = self.dma_sem(dma_key)
            ds[1] += 16
            ev = (ds[0], ds[1])
            inc = (ds[0], 16)
        else:
            E["count"] += 1
            ev = (E["sem"], E["count"])
            inc = (E["sem"], 1)
        for r in reads:
            r.r.append(ev)
        for w in writes:
            w.w = ev
            w.r = []
        if not isinstance(fns, (list, tuple)):
            fns = [fns]
        E["ops"].append((waits, list(fns), inc))
        return ev

    def dma(self, queue, out, in_, reads=(), writes=(), key=None):
        assert key is not None
        return self.op(queue, [lambda e: e.dma_start(out=out, in_=in_)], reads, writes, dma_key=key)

    def barrier(self):
        evs = [(E["sem"], E["count"]) for E in self.engs.values() if E["count"] > 0]
        evs += [(s, c) for (s, c) in self.dma_sems.values() if c > 0]
        for n, E in self.engs.items():
            waits = []
            for sem, val in evs:
                if E["known"].get(sem, 0) >= val:
                    continue
                if sem is E["sem"] and n == "pe":
                    continue
                E["known"][sem] = val
                waits.append((sem, val))
            if waits:
                E["ops"].append((waits, [], None))

    def emit(self):
        nc = self.nc

        def run(name):
            def f(e):
                for waits, fns, inc in self.engs[name]["ops"]:
                    for sem, val in waits:
                        e.wait_ge(sem, val)
                    n = len(fns)
                    for i, fn in enumerate(fns):
                        ins = fn(e)
                        if i == n - 1:
                            ins.then_inc(inc[0], inc[1])
            return f

        with nc.Block() as block:
            block.tensor(run("pe"))
            block.scalar(run("act"))
            block.vector(run("dve"))
            block.gpsimd(run("pool"))
            block.sync(run("sp"))


def _bf(a):
    return np.ascontiguousarray(a.astype(ml_dtypes.bfloat16))


_CONST_CACHE = {}


def const_tables():
    if _CONST_CACHE:
        return _CONST_CACHE
    N = 4096
    f = np.arange(2048, dtype=np.float64)
    t = np.arange(2048, dtype=np.float64)
    m = np.mod(np.outer(2 * f + 1, t), 2 * N)
    ang = np.pi * m / N
    C = np.cos(ang)
    Sn = np.sin(ang)
    CT = C.T.reshape(16, 128, 16, 128)
    ST = (-Sn).T.reshape(16, 128, 16, 128)
    fwd = np.stack([CT, ST], axis=0)
    fwd = fwd.transpose(3, 2, 0, 1, 4)
    _CONST_CACHE["fwd_tab"] = _bf(fwd.reshape(16, 128, 2 * 16 * 128))
    Ci = (C / 2048.0).reshape(4, 4, 128, 4, 512)
    Si = (-Sn / 2048.0).reshape(4, 4, 128, 4, 512)
    inv = np.stack([Ci, Si], axis=0)
    inv = inv.transpose(4, 1, 3, 2, 0, 5)
    _CONST_CACHE["inv_tab"] = _bf(inv.reshape(4, 4, 128, 4 * 2 * 512))
    f32 = np.float32
    tl = np.linspace(0.0, 1.0, L, dtype=f32)[:, None]
    w = (f32(2.0 * math.pi) * np.arange(L, dtype=f32)[:, None] / f32(L)).astype(f32)
    fr = np.linspace(1e-4, 15, 16, dtype=f32)[None, :]
    z = np.concatenate([tl, np.cos(fr * w), -np.sin(fr * w)], axis=-1).astype(f32)
    _CONST_CACHE["zfT"] = np.ascontiguousarray(z.T)
    _CONST_CACHE["ntn"] = np.ascontiguousarray((-tl[:, 0]).reshape(16, 128).T.astype(f32))
    min_decay = math.log(1e-2) / 1.5
    max_decay = math.log(1e-2) / 0.3
    _CONST_CACHE["deltas"] = np.abs(np.linspace(min_decay, max_decay, 768, dtype=f32)).astype(f32)
    inv_f = (10000.0 ** (-np.arange(0, 64, 2, dtype=f32) / f32(64))).astype(f32)
    angr = (np.arange(L, dtype=f32)[:, None] * inv_f[None, :]).astype(f32)
    angr = np.concatenate([angr, angr], axis=-1)
    cosT = np.cos(angr).T.astype(f32)
    sinT = np.sin(angr).T.astype(f32)
    _CONST_CACHE["ropec"] = np.ascontiguousarray(np.concatenate([cosT, cosT], axis=0))
    _CONST_CACHE["ropes"] = np.ascontiguousarray(np.concatenate([sinT, sinT], axis=0))
    Pm = np.zeros((128, 128), np.float32)
    for po in range(128):
        d = po % 64
        if d < 32:
            Pm[po + 32, po] = -1.0
        else:
            Pm[po - 32, po] = 1.0
    _CONST_CACHE["pm"] = _bf(Pm)
    _CONST_CACHE["ident"] = _bf(np.eye(128, dtype=np.float32))
    k = np.arange(128)[:, None]
    q = np.arange(128)[None, :]
    NEG = -30000.0
    m_next = np.where(k <= q, 0.0, NEG)
    m_prev = np.where(k >= q, 0.0, NEG)
    _CONST_CACHE["maskb"] = _bf(np.concatenate([m_next, np.zeros((128, 128)), m_prev], axis=1))
    return _CONST_CACHE


def build_program(debug=None):
    nc = bass.Bass("TRN2", target_bir_lowering=False)
    dbg = {}

    def din(name, shape, dt=F32):
        return nc.dram_tensor(name, list(shape), dt, kind="ExternalInput").ap()

    x_d = din("x", [L, D])
    mem_d = din("mem", [256, D])
    wkv_d = din("w_mem_kv", [D, 512])
    w_in_d = [din("l0_w_in", [D, 2560]), din("l1_w_in", [D, 1536])]
    w_out_d = [din("l0_w_out", [D, D]), din("l1_w_out", [D, D])]
    w_up_d = [din("l%d_ffn_w_up" % i, [D, 2 * DFF]) for i in range(2)]
    w_dn_d = [din("l%d_ffn_w_down" % i, [DFF, D]) for i in range(2)]
    ln_d = [[din("l%d_%s" % (i, n), [D]) for n in ("ln1_g", "ln1_b", "ln2_g", "ln2_b")] for i in range(2)]
    fcwb_d = [din("l%d_fcwb" % i, [128, 44, 4]) for i in range(2)]
    cwb_d = din("l0_cwb", [128, 18, 4])
    fw1_d = din("l0_filt_w1", [33, 64])
    fw2_d = din("l0_filt_w2", [64, 64])
    fw3_d = din("l0_filt_w3", [64, 64])
    fwo_d = din("l0_filt_w_out", [64, 1536])
    fbf_d = din("l0_fbf", [64, 6])
    hd_d = din("l0_hyena_d", [768])
    sink_d = din("l1_sink", [12])
    fwd_tab_d = din("fwd_tab", [16, 128, 4096], BF16)
    inv_tab_d = din("inv_tab", [4, 4, 128, 4096], BF16)
    zfT_d = din("zfT", [33, L])
    ntn_d = din("ntn", [128, 16])
    deltas_d = din("deltas", [768])
    ropec_d = din("ropec", [128, L])
    ropes_d = din("ropes", [128, L])
    pm_d = din("pm", [128, 128], BF16)
    ident_d = din("ident", [128, 128], BF16)
    maskb_d = din("maskb", [128, 384], BF16)
    out_d = nc.dram_tensor("out", [L, D], F32, kind="ExternalOutput").ap()
    if debug:
        dbg_d = nc.dram_tensor("dbg", [L, D], F32, kind="ExternalOutput").ap()

    es = ExitStack()
    with es:
        S = Sched(nc, es)
        AR_BYTES = 194 * 1024
        arena = es.enter_context(nc.sbuf_tensor("arena", [128, AR_BYTES // 2], BF16))
        ps = es.enter_context(nc.psum_tensor("ps", [128, 4096], F32))
        PB = [Res("pb%d" % i, excl=True) for i in range(8)]

        def bank(b, n=512, p0=0, p1=128):
            return ps[p0:p1, b * 512:b * 512 + n]

        def bank_bf(b):
            return ps[:, b * 512:(b + 1) * 512].bitcast(BF16)

        def AV(off, shape, dt):
            n = int(np.prod(shape))
            if dt == F32:
                v = arena[:, off // 2: off // 2 + 2 * n].bitcast(F32)
            else:
                v = arena[:, off // 2: off // 2 + n]
            if len(shape) == 2:
                v = v.rearrange("p (a b) -> p a b", a=shape[0])
            elif len(shape) == 3:
                v = v.rearrange("p (a b c) -> p a b c", a=shape[0], b=shape[1])
            elif len(shape) == 4:
                v = v.rearrange("p (a b c d) -> p a b c d", a=shape[0], b=shape[1], c=shape[2])
            return v

        K_ = 1024
        R_X, R_XT, R_Y, R_YM, R_MQ, R_T = 0, 64 * K_, 96 * K_, 120 * K_, 128 * K_, 136 * K_

        def small(name, shape, dt):
            return es.enter_context(nc.sbuf_tensor("sb_" + name, list(shape), dt))

        ident = small("ident", [128, 128], BF16)
        pm = small("pm", [128, 128], BF16)
        maskb = small("maskb", [128, 384], BF16)
        ones64 = small("ones64", [128, 64], BF16)
        memKT = small("memKT", [128, 2, 256], BF16)
        memV = small("memV", [128, 2, 256], BF16)
        fcwb = small("fcwb", [128, 44, 4], F32)
        cwb = fcwb[:, 0:18, :]
        stat = small("stat", [128, 128], F32)
        GB = small("GB", [128, 2, 1024], F32)
        r_consts = Res("consts")
        r_memKV = Res("memKV")
        r_cwb = Res("cwb")
        r_fcwb = Res("fcwb")
        r_GB = Res("GB")

        X = AV(R_X, [16, 1024], F32)
        XT = AV(R_XT, [8, 2048], BF16)
        YT = AV(R_Y, [6, 2048], BF16)
        YM = AV(R_YM, [2, 2048], BF16)
        MQ = AV(R_MQ, [2, 2048], BF16)
        r_X = [Res("X%d" % i) for i in range(16)]
        r_XT = Res("XT")
        r_YT = Res("YT")
        r_YM = Res("YM")
        r_MQ = Res("MQ")

        _bk = [0]

        def nxt(lst):
            b = lst[_bk[0] % len(lst)]
            _bk[0] += 1
            return b

        def mm_group(out, pairs, reads, writes):
            n = len(pairs)
            fns = []
            for i, (l, r) in enumerate(pairs):
                fns.append(lambda e, l=l, r=r, i=i: e.matmul(out, l, r, start=(i == 0), stop=(i == n - 1)))
            return S.op("pe", fns, reads, writes)

        def act_copy(out, in_, reads, writes):
            return S.op("act", [lambda e: e.activation(out=out, in_=in_, func=AF.Identity)], reads, writes)

        S.op("dve", [lambda e: e.memset(ones64[:], 1.0)], writes=[r_consts])
        epst = small("epst", [128, 1], F32)
        identf = small("identf", [128, 128], F32)
        aident = small("aident", [128, 128], F32)
        S.op("dve", [lambda e: e.memset(epst[:], EPS)], writes=[r_consts])

        HS = AV(R_X, [2, 16, 768], BF16)
        r_HS = Res("HS")
        hA = AV(R_X + 48 * K_, [2048], F32)
        hB = AV(R_X + 56 * K_, [2048], F32)
        r_hA, r_hB = Res("hA"), Res("hB")
        zf = AV(R_Y, [2048], F32)
        fw = AV(R_Y + 8 * K_, [3, 64], F32)
        fwo = AV(R_Y + 9 * K_, [1536], F32)
        fbf = AV(R_Y + 15 * K_, [8], F32)
        dbc = AV(R_Y + 16 * K_, [768], F32)
        dlt = AV(R_YM, [768], F32)
        dec = AV(R_YM + 3 * K_, [768], F32)
        ntn = AV(R_YM + 6 * K_, [16], F32)
        fsb = AV(R_MQ, [768], F32)
        wtmp = AV(R_MQ + 3 * K_, [512], F32)
        wtm2 = AV(R_MQ + 5 * K_, [512], F32)
        r_f = Res("filt_in")
        r_dec, r_fsb, r_wtmp, r_wtm2, r_dbc = Res("dec"), Res("fsb"), Res("wtmp"), Res("wtm2"), Res("dbc")
        S.dma("sp", zf[0:33, :], zfT_d[:, :], writes=[r_f], key="f0")
        S.dma("sp", fw[0:33, 0, :], fw1_d[:, :], writes=[r_f], key="f1")
        S.dma("sp", fw[0:64, 1, :], fw2_d[:, :], writes=[r_f], key="f2")
        S.dma("sp", fw[0:64, 2, :], fw3_d[:, :], writes=[r_f], key="f3")
        S.dma("sp", fwo[0:64, :], fwo_d[:, :], writes=[r_f], key="f4")
        S.dma("sp", fbf[0:64, 0:6], fbf_d[:, :], writes=[r_f], key="f5")
        S.dma("sp", dbc, hd_d.partition_broadcast(128), writes=[r_dbc], key="f6")
        S.dma("sp", dlt, deltas_d.partition_broadcast(128), writes=[r_f], key="f7")
        S.dma("sp", ntn, ntn_d[:, :], writes=[r_f], key="f8")
        S.dma("sp", ident[:], ident_d[:, :], writes=[r_consts], key="c0")
        S.dma("sp", pm[:], pm_d[:, :], writes=[r_consts], key="c1")
        S.dma("sp", maskb[:], maskb_d[:, :], writes=[r_consts], key="c2")
        S.dma("sp", cwb, cwb_d[:, :, :], writes=[r_cwb], key="c3")
        S.op("act", [lambda e: e.activation(out=identf[:], in_=ident[:], func=AF.Identity)], reads=[r_consts], writes=[r_consts])
        S.op("act", [lambda e: e.activation(out=aident[:], in_=ident[:], func=AF.Identity, scale=ALPHA)], reads=[r_consts], writes=[r_consts])

        def transpose_to_XT(src_bf, tile, r_src):
            b = nxt([6, 7])
            bb = bank_bf(b)
            fns = [lambda e, kc=kc: e.transpose(bb[:, kc * 128:(kc + 1) * 128], src_bf[:, kc * 128:(kc + 1) * 128], ident[:])
                   for kc in range(8)]
            S.op("pe", fns, reads=[r_src, r_consts], writes=[PB[b]])
            S.op("act", [lambda e: e.activation(out=XT[:, :, tile * 128:(tile + 1) * 128],
                                                in_=bb.rearrange("p (k m) -> p k m", k=8), func=AF.Identity)],
                 reads=[PB[b]], writes=[r_XT])

        def load_ln(layer, which):
            g_d, b_d = ln_d[layer][2 * which], ln_d[layer][2 * which + 1]
            S.dma("sp", GB[:, 0, :], g_d.partition_broadcast(128), writes=[r_GB], key="gb0")
            S.dma("sp", GB[:, 1, :], b_d.partition_broadcast(128), writes=[r_GB], key="gb1")

        NRB = 6
        LN_LAG = 3
        _rboff = [46 * K_, 50 * K_, 28672, 28672 + 4096, 36880, 36880 + 4096]
        rbuf = [AV(R_T + _rboff[i], [1024], F32) for i in range(NRB)]
        _xboff = [54 * K_, 56 * K_, 24 * K_, 26 * K_]
        xbuf = [AV(R_T + _xboff[i], [1024], BF16) for i in range(4)]
        r_rbuf = [Res("rbuf%d" % i) for i in range(NRB)]
        r_xbuf = [Res("xbuf%d" % i) for i in range(4)]
        r_stat = Res("stat")
        r_stats = [Res("stat%d" % i) for i in range(8)]
        _ln = [0]

        class LNPipe:
            def __init__(self, final_out):
                self.final_out = final_out
                self.q = []

            def push(self, tile, pin, r_pin, r_in, r_r):
                ri = tile % NRB
                so = ri * 16
                r_stat = r_stats[ri]
                st6 = stat[:, so:so + 12]
                mv = stat[:, so + 12:so + 14]
                rstd = stat[:, so + 14:so + 15]
                nmr = stat[:, so + 15:so + 16]
                final_out = self.final_out

                def s1():
                    S.op("dve", [lambda e: e.bn_stats(out=st6[:, 0:6], in_=pin[:, 0:512])], reads=[r_pin[0]], writes=[r_stat])
                    S.op("dve", [lambda e: e.bn_stats(out=st6[:, 6:12], in_=pin[:, 512:1024])], reads=[r_pin[1]], writes=[r_stat])
                    S.op("dve", [lambda e: e.bn_aggr(out=mv, in_=st6)], reads=[r_stat], writes=[r_stat])

                def s2():
                    S.op("act", [lambda e: e.activation(out=rstd, in_=mv[:, 1:2], func=AF.Sqrt, bias=epst[:, 0:1])],
                         reads=[r_stat, r_consts], writes=[r_stat])

                def s3():
                    S.op("dve", [lambda e: e.reciprocal(out=rstd, in_=rstd)], reads=[r_stat], writes=[r_stat])
                    S.op("dve", [lambda e: e.scalar_tensor_tensor(out=nmr, in0=mv[:, 0:1], scalar=-1.0, in1=rstd,
                                                                  op0=ALU.mult, op1=ALU.mult)], reads=[r_stat], writes=[r_stat])

                def s4():
                    S.op("act", [lambda e: e.activation(out=r_in, in_=pin, func=AF.Identity, scale=rstd, bias=nmr)],
                         reads=list(r_pin) + [r_stat], writes=[r_r])

                def s5():
                    S.op("dve", [lambda e: e.tensor_tensor(out=r_in, in0=r_in, in1=GB[:, 0, :], op=ALU.mult)],
                         reads=[r_r, r_GB], writes=[r_r])
                    S.op("pool", [lambda e: e.tensor_tensor(out=X[:, tile, :], in0=r_in, in1=GB[:, 1, :], op=ALU.add)],
                         reads=[r_r, r_GB], writes=[r_X[tile]])

                def s6():
                    if final_out:
                        S.dma("sp", out_d[tile * 128:(tile + 1) * 128, :], X[:, tile, :], reads=[r_X[tile]], key="out%d" % (tile % 4))
                    else:
                        i4 = tile % 4
                        S.op("act", [lambda e: e.activation(out=xbuf[i4], in_=X[:, tile, :], func=AF.Identity)],
                             reads=[r_X[tile]], writes=[r_xbuf[i4]])

                def s7():
                    if not final_out:
                        i4 = tile % 4
                        transpose_to_XT(xbuf[i4], tile, r_xbuf[i4])

                s1()
                for d_, f_ in ((1, s2), (1, s3), (2, s4), (3, s5), (4, s6), (5, s7)):
                    self.q.append([d_, f_])
                self._tick()

            def _tick(self):
                keep = []
                for item in self.q:
                    item[0] -= 0
                ready = [it for it in self.q if it[0] <= 0]
                for it in ready:
                    it[1]()
                self.q = [it for it in self.q if it[0] > 0]
                for it in self.q:
                    it[0] -= 1

            def flush(self):
                while self.q:
                    self._tick()

        def mem_attention(toff, extra_w=()):
            PT = [AV(toff + i * K_, [512], BF16) for i in range(8)]
            r_PT = [Res("mpt%d" % i) for i in range(8)]
            rdens = [AV(toff + 8 * K_ + i * 2 * K_, [512], F32) for i in range(2)]
            r_rdens = [Res("mrden%d" % i) for i in range(2)]
            memVp = AV(toff + 12 * K_, [2, 2, 2, 128], BF16)
            onesp = AV(toff + 14 * K_, [2, 128], BF16)
            r_pad = Res("mpad")
            S.op("dve", [lambda e: e.memset(memVp.rearrange("p a b c d -> p (a b c d)"), 0.0)], writes=[r_pad] + list(extra_w))
            S.op("dve", [lambda e: e.memset(onesp.rearrange("p a b -> p (a b)"), 0.0)], writes=[r_pad])
            for hh in range(2):
                S.op("dve", [lambda e, hh=hh: e.memset(onesp[:, hh, hh * 64:(hh + 1) * 64], 1.0)], writes=[r_pad])
                for mt in range(2):
                    for hp in range(2):
                        h = 2 * hp + hh
                        S.op("dve", [lambda e, mt=mt, hp=hp, hh=hh, h=h: e.tensor_copy(
                            out=memVp[:, mt, hp, hh, hh * 64:(hh + 1) * 64], in_=memV[:, mt, h * 64:(h + 1) * 64])],
                             reads=[r_memKV], writes=[r_pad])
            its = [(hp, qt) for hp in range(2) for qt in range(4)]

            def stage1(n):
                hp, qt = its[n]
                qs = slice(qt * 512, (qt + 1) * 512)
                for hh in range(2):
                    prow = slice(hh * 64, (hh + 1) * 64)
                    for mt in range(2):
                        b = nxt([0, 1, 2, 3])
                        mm_group(bank(b), [(memKT[prow, hp, mt * 128:(mt + 1) * 128], MQ[prow, hp, qs])],
                                 reads=[r_memKV, r_MQ], writes=[PB[b]])
                        k = (n % 2) * 4 + hh * 2 + mt
                        S.op("act", [lambda e, k=k, b=b: e.activation(out=PT[k], in_=bank(b), func=AF.Exp, scale=0.125)],
                             reads=[PB[b]], writes=[r_PT[k]])

            def stage2(n):
                hp, qt = its[n]
                qs = slice(qt * 512, (qt + 1) * 512)
                ks = [((n % 2) * 4 + hh * 2 + mt, hh, mt) for hh in range(2) for mt in range(2)]
                bo, bd = [4, 6][n % 2], [5, 7][n % 2]
                mm_group(bank(bo), [(memVp[:, mt, hp, hh, :], PT[k]) for (k, hh, mt) in ks],
                         reads=[r_pad] + [r_PT[k] for (k, _, _) in ks], writes=[PB[bo]])
                mm_group(bank(bd), [(onesp[:, hh, :], PT[k]) for (k, hh, mt) in ks],
                         reads=[r_pad] + [r_PT[k] for (k, _, _) in ks], writes=[PB[bd]])
                rd = rdens[n % 2]
                r_rd = r_rdens[n % 2]
                S.op("dve", [lambda e: e.reciprocal(out=rd, in_=bank(bd))], reads=[PB[bd]], writes=[r_rd])
                S.op("dve", [lambda e: e.tensor_tensor(out=YM[:, hp, qs], in0=bank(bo), in1=rd, op=ALU.mult)],
                     reads=[PB[bo], r_rd], writes=[r_YM])

            for n in range(len(its) + 1):
                if n < len(its):
                    stage1(n)
                if n >= 1:
                    stage2(n - 1)

        def out_proj_ln1(layer):
            wout = AV(R_T, [8, 1024], BF16)
            r_wout = Res("wout")
            xs = [AV(R_T + 16 * K_ + i * 4 * K_, [1024], F32) for i in range(2)]
            r_xs = [Res("xs%d" % i) for i in range(2)]
            S.dma("pool", wout, w_out_d[layer].rearrange("(kc p) n -> p kc n", p=128),
                  writes=[r_wout] + ([r_KS] if layer == 0 else [r_KT]), key="wout")
            load_ln(layer, 0)
            lnp = LNPipe(False)
            for tile in range(16):
                ts_ = slice(tile * 128, (tile + 1) * 128)
                bp = [0, 2, 4][tile % 3]
                i = tile % 2
                ri = tile % NRB
                if layer == 0:
                    S.dma("sp", xs[i], x_d[ts_, :], writes=[r_xs[i]] + ([r_KS] if tile < 2 else []), key="xs%d" % i)
                    xin, rxin = xs[i], r_xs[i]
                else:
                    xin, rxin = X[:, tile, :], r_X[tile]
                for half in range(2):
                    b = bp + half
                    pairs = []
                    for kc in range(8):
                        lhs = YT[:, kc, ts_] if kc < 6 else YM[:, kc - 6, ts_]
                        pairs.append((lhs, wout[:, kc, half * 512:(half + 1) * 512]))
                    mm_group(bank(b), pairs, reads=[r_YT, r_YM, r_wout], writes=[PB[b]])
                rb = rbuf[ri]
                S.op("dve", [lambda e, xin=xin, rb=rb, bp=bp: e.scalar_tensor_tensor(
                    out=rb, in0=xin, scalar=ALPHA, in1=ps[:, bp * 512:bp * 512 + 1024], op0=ALU.mult, op1=ALU.add)],
                     reads=[rxin, PB[bp], PB[bp + 1]], writes=[r_rbuf[ri]])
                lnp.push(tile, rb, [r_rbuf[ri], r_rbuf[ri]], rb, r_rbuf[ri])
            lnp.flush()

        def ffn_ln2(layer, final):
            blocks = [4, 4, 4, 4, 3, 3]
            S.dma("sp", fcwb[:], fcwb_d[layer][:, :, :], writes=[r_fcwb], key="fcwb")
            load_ln(layer, 1)
            GT = AV(R_Y, [4, 2048], BF16)
            r_GT = Res("GT")
            wdn = [AV(R_T + i * 8 * K_, [4, 1024], BF16) for i in range(2)]
            r_wdn = [Res("wdn%d" % i) for i in range(2)]
            wup = [AV(R_T + 16 * K_ + i * 4 * K_, [8, 2, 128], BF16) for i in range(3)]
            r_wup = [Res("wup%d" % i) for i in range(3)]
            hsb = [AV(R_T + 28 * K_ + i * 8208, [2052], F32) for i in range(2)]
            r_hsb = [Res("hsb%d" % i) for i in range(2)]
            for i in range(2):
                S.op("pool", [lambda e, i=i: e.memset(hsb[i][:, 0:1], 0.0)], writes=[r_hsb[i]])
                S.op("pool", [lambda e, i=i: e.memset(hsb[i][:, 2049:2050], 0.0)], writes=[r_hsb[i]])
            acc_as = [AV(R_YM, [2048], F32), AV(R_Y + 16 * K_, [2048], F32)]
            acc_g = AV(R_MQ, [2048], F32)
            sgb = AV(R_T + 50 * K_, [2048], F32)
            r_sgb = Res("sgb")
            r_accas, r_accg = [Res("acca0"), Res("acca1")], Res("accg")
            wd_v = w_dn_d[layer]
            wu_v = w_up_d[layer].rearrange("(kc p) n -> p kc n", p=128)
            def down_proj(bi, nb, wd):
                last = bi == len(blocks) - 1
                if last:
                    S.barrier()
                lnp = LNPipe(final)
                for tile in range(16):
                    ts_ = slice(tile * 128, (tile + 1) * 128)
                    bp = [0, 2, 4][tile % 3] if last else [4, 6][tile % 2]
                    pin = ps[:, bp * 512:bp * 512 + 1024]
                    if not last:
                        for half in range(2):
                            mm_group(bank(bp + half), [(GT[:, jj, ts_], wd[:, jj, half * 512:(half + 1) * 512]) for jj in range(nb)],
                                     reads=[r_GT, r_wdn[bi % 2]], writes=[PB[bp + half]])
                        if bi == 0:
                            S.op("dve", [lambda e, tile=tile, pin=pin: e.scalar_tensor_tensor(
                                out=X[:, tile, :], in0=X[:, tile, :], scalar=ALPHA, in1=pin, op0=ALU.mult, op1=ALU.add)],
                                 reads=[PB[bp], PB[bp + 1]], writes=[r_X[tile]])
                        else:
                            S.op("dve", [lambda e, tile=tile, pin=pin: e.tensor_tensor(
                                out=X[:, tile, :], in0=X[:, tile, :], in1=pin, op=ALU.add)],
                                 reads=[PB[bp], PB[bp + 1]], writes=[r_X[tile]])
                    else:
                        for half in range(2):
                            b = bp + half
                            hsl = slice(half * 512, (half + 1) * 512)
                            fns = []
                            for jj in range(nb):
                                fns.append(lambda e, jj=jj, b=b, hsl=hsl, ts_=ts_: e.matmul(bank(b), GT[:, jj, ts_], wd[:, jj, hsl], start=(jj == 0), stop=False))
                            fns.append(lambda e, b=b, hsl=hsl, tile=tile: e.matmul(bank(b), identf[:], X[:, tile, hsl], start=False, stop=True))
                            S.op("pe", fns, reads=[r_GT, r_wdn[bi % 2], r_X[tile], r_consts], writes=[PB[b]])
                        ri = tile % NRB
                        lnp.push(tile, pin, [PB[bp], PB[bp + 1]], rbuf[ri], r_rbuf[ri])
                lnp.flush()

            j0 = 0
            deferred = None
            deferred_endblk = False
            for bi, nb in enumerate(blocks):
                wd = wdn[bi % 2]
                S.dma("pool", wd[:, 0:nb, :], wd_v[j0 * 128:(j0 + nb) * 128, :].rearrange("(j p) n -> p j n", p=128),
                      writes=[r_wdn[bi % 2]], key="wdn%d" % (bi % 2))
                for jj in range(nb):
                    j = j0 + jj
                    wi = j % 3
                    wu = wup[wi]
                    acc_a, r_acca = acc_as[j % 2], r_accas[j % 2]
                    S.dma("pool", wu[:, :, 0, :], wu_v[:, :, j * 128:(j + 1) * 128], writes=[r_wup[wi]], key="wup%da" % wi)
                    S.dma("pool", wu[:, :, 1, :], wu_v[:, :, DFF + j * 128:DFF + (j + 1) * 128], writes=[r_wup[wi]], key="wup%db" % wi)
                    for ag in range(2):
                        hs_ = hsb[ag]
                        cj = ag * NFF + j
                        for tq in range(4):
                            b = nxt([0, 1, 2, 3])
                            mm_group(bank(b), [(wu[:, kc, ag, :], XT[:, kc, tq * 512:(tq + 1) * 512]) for kc in range(8)],
                                     reads=[r_wup[wi], r_XT], writes=[PB[b]])
                            S.op("act", [lambda e, hs_=hs_, tq=tq, b=b: e.activation(
                                out=hs_[:, 1 + tq * 512:1 + (tq + 1) * 512], in_=bank(b), func=AF.Identity)],
                                 reads=[PB[b]], writes=[r_hsb[ag]])
                        if ag == 0 and deferred is not None and deferred_endblk:
                            deferred()
                            deferred = None
                        acc, r_acc = (acc_a, r_acca) if ag == 0 else (acc_g, r_accg)
                        S.op("act", [lambda e, acc=acc, hs_=hs_, cj=cj: e.activation(
                            out=acc, in_=hs_[:, 1:2049], func=AF.Identity, scale=fcwb[:, cj, 1:2], bias=fcwb[:, cj, 3:4])],
                             reads=[r_hsb[ag], r_fcwb], writes=[r_acc])
                        S.op("dve", [lambda e, acc=acc, hs_=hs_, cj=cj: e.scalar_tensor_tensor(
                            out=acc, in0=hs_[:, 0:2048], scalar=fcwb[:, cj, 0:1], in1=acc, op0=ALU.mult, op1=ALU.add)],
                             reads=[r_hsb[ag], r_fcwb, r_acc], writes=[r_acc])
                        S.op("dve", [lambda e, acc=acc, hs_=hs_, cj=cj: e.scalar_tensor_tensor(
                            out=acc, in0=hs_[:, 2:2050], scalar=fcwb[:, cj, 2:3], in1=acc, op0=ALU.mult, op1=ALU.add)],
                             reads=[r_hsb[ag], r_fcwb, r_acc], writes=[r_acc])
                        if ag == 0 and deferred is not None:
                            deferred()
                            deferred = None

                    def _fin(jj=jj, acc_a=acc_a, r_acca=r_acca, endblk=(jj == nb - 1), bi=bi, nb=nb, wd=wd):
                        S.op("act", [lambda e: e.activation(out=sgb, in_=acc_g, func=AF.Silu)], reads=[r_accg], writes=[r_sgb])
                        S.op("dve", [lambda e: e.tensor_tensor(out=GT[:, jj, :], in0=sgb, in1=acc_a, op=ALU.mult)],
                             reads=[r_sgb, r_acca], writes=[r_GT])
                        if endblk:
                            down_proj(bi, nb, wd)
                    deferred = _fin
                    deferred_endblk = (jj == nb - 1)
                j0 += nb
            if deferred is not None:
                deferred()
                deferred = None

        memf = AV(R_T + 8 * K_, [2, 1024], F32)
        memb = AV(R_T + 28 * K_, [2, 1024], BF16)
        memT = AV(R_T + 32 * K_, [8, 256], BF16)
        wkv = AV(R_T, [8, 512], BF16)
        r_memf, r_memb, r_memT, r_wkv = Res("memf"), Res("memb"), Res("memT"), Res("wkv")
        S.dma("sp", memf, mem_d.rearrange("(mt p) d -> p mt d", p=128), writes=[r_memf], key="memf")
        S.dma("pool", wkv, wkv_d.rearrange("(kc p) n -> p kc n", p=128), writes=[r_wkv], key="wkv")
        act_copy(memb, memf, [r_memf], [r_memb])
        for mt in range(2):
            b = nxt([6, 7])
            bb = bank_bf(b)
            fns = [lambda e, kc=kc, mt=mt, bb=bb: e.transpose(bb[:, kc * 128:(kc + 1) * 128], memb[:, mt, kc * 128:(kc + 1) * 128], ident[:])
                   for kc in range(8)]
            S.op("pe", fns, reads=[r_memb, r_consts], writes=[PB[b]])
            S.op("act", [lambda e, mt=mt, bb=bb: e.activation(out=memT[:, :, mt * 128:(mt + 1) * 128],
                                                             in_=bb.rearrange("p (k m) -> p k m", k=8), func=AF.Identity)],
                 reads=[PB[b]], writes=[r_memT])
        for hp in range(2):
            b = nxt([0, 1])
            mm_group(bank(b, 256), [(wkv[:, kc, hp * 128:(hp + 1) * 128], memT[:, kc, :]) for kc in range(8)],
                     reads=[r_wkv, r_memT], writes=[PB[b]])
            act_copy(memKT[:, hp, :], bank(b, 256), [PB[b]], [r_memKV])
        for mt in range(2):
            b = nxt([0, 1])
            mm_group(bank(b, 256), [(memT[:, kc, mt * 128:(mt + 1) * 128], wkv[:, kc, 256:512]) for kc in range(8)],
                     reads=[r_wkv, r_memT], writes=[PB[b]])
            act_copy(memV[:, mt, :], bank(b, 256), [PB[b]], [r_memKV])

        if debug == "s_m":
            S.barrier()
            S.emit()
            return nc
        xs0 = [AV(R_T + 16 * K_ + i * 4 * K_, [1024], F32) for i in range(2)]
        r_xs0 = [Res("xs0_%d" % i) for i in range(2)]

        def x_step(tile):
            i = tile % 2
            i4 = tile % 4
            S.dma("sp", xs0[i], x_d[tile * 128:(tile + 1) * 128, :], writes=[r_xs0[i]], key="xs%d" % i)
            S.op("act", [lambda e: e.activation(out=xbuf[i4], in_=xs0[i], func=AF.Identity)],
                 reads=[r_xs0[i]], writes=[r_xbuf[i4]])
            transpose_to_XT(xbuf[i4], tile, r_xbuf[i4])
        x_steps = [(lambda t=t: x_step(t)) for t in range(16)]
        if debug == "s_x":
            S.barrier()
            S.emit()
            return nc
        fbs = stat[0:64, 120:123]
        for l in range(3):
            S.op("dve", [lambda e, l=l: e.tensor_tensor(out=fbs[:, l:l + 1], in0=fbf[0:64, 2 * l:2 * l + 1],
                                                        in1=fbf[0:64, 2 * l + 1:2 * l + 2], op=ALU.mult)],
                 reads=[r_f], writes=[r_stat])
        srcs = [(zf, 33, r_f), (hA, 64, r_hA), (hB, 64, r_hB)]
        dsts = [(hA, r_hA), (hB, r_hB), (hA, r_hA)]
        wtW = AV(R_T + 36 * K_, [2048], F32)
        wt2W = AV(R_T + 44 * K_, [2048], F32)
        r_wtW, r_wt2W = Res("wtW"), Res("wt2W")
        dec2 = [dec, AV(R_MQ, [768], F32)]
        r_dec2 = [Res("dec0"), Res("dec1")]
        f_steps = []

        def mlp_layer(l):
            src, kk, r_src = srcs[l]
            dst, r_dst = dsts[l]
            for tq in range(4):
                cs = slice(tq * 512, (tq + 1) * 512)
                mm_group(bank(tq, 512, 0, 64), [(fw[0:kk, l, :], src[0:kk, cs])], reads=[r_f, r_src], writes=[PB[tq]])
            pall = ps[0:64, 0:2048]
            S.op("dve", [lambda e: e.tensor_scalar(out=wtW[0:64, :], in0=pall, scalar1=fbf[0:64, 2 * l + 1:2 * l + 2],
                                                   scalar2=fbs[:, l:l + 1], op0=ALU.mult, op1=ALU.add)],
                 reads=[PB[0], PB[1], PB[2], PB[3], r_f, r_stat], writes=[r_wtW])
            S.op("dve", [lambda e: e.tensor_scalar(out=wt2W[0:64, :], in0=wtW[0:64, :], scalar1=-PI, scalar2=2 * PI,
                                                   op0=ALU.is_lt, op1=ALU.mult)], reads=[r_wtW], writes=[r_wt2W])
            S.op("dve", [lambda e: e.tensor_tensor(out=wtW[0:64, :], in0=wtW[0:64, :], in1=wt2W[0:64, :], op=ALU.add)],
                 reads=[r_wtW, r_wt2W], writes=[r_wtW])
            S.op("dve", [lambda e: e.tensor_scalar(out=wt2W[0:64, :], in0=wtW[0:64, :], scalar1=PI, scalar2=-2 * PI,
                                                   op0=ALU.is_gt, op1=ALU.mult)], reads=[r_wtW], writes=[r_wt2W])
            S.op("dve", [lambda e: e.tensor_tensor(out=wtW[0:64, :], in0=wtW[0:64, :], in1=wt2W[0:64, :], op=ALU.add)],
                 reads=[r_wtW, r_wt2W], writes=[r_wtW])
            S.op("act", [lambda e: e.activation(out=dst[0:64, :], in_=wtW[0:64, :], func=AF.Sin)],
                 reads=[r_wtW], writes=[r_dst])

        def wcomb():
            S.op("dve", [lambda e: e.tensor_tensor(out=fsb[0:64, :], in0=fwo[0:64, 0:768], in1=fwo[0:64, 768:1536], op=ALU.add)],
                 reads=[r_f], writes=[r_fsb])
            S.op("dve", [lambda e: e.tensor_tensor(out=fwo[0:64, 768:1536], in0=fwo[0:64, 0:768], in1=fwo[0:64, 768:1536], op=ALU.subtract)],
                 reads=[r_f], writes=[r_f])
            S.op("dve", [lambda e: e.tensor_copy(out=fwo[0:64, 0:768], in_=fsb[0:64, :])], reads=[r_fsb], writes=[r_f])

        def hfull(tile):
            ts_ = slice(tile * 128, (tile + 1) * 128)
            b0 = 4 if tile % 2 else 0
            dc, r_dc = dec2[tile % 2], r_dec2[tile % 2]
            for q3 in range(3):
                mm_group(bank(b0 + q3), [(hA[0:64, ts_], fwo[0:64, q3 * 512:(q3 + 1) * 512])], reads=[r_hA, r_f], writes=[PB[b0 + q3]])
            S.op("act", [lambda e: e.activation(out=dc, in_=dlt, func=AF.Exp, scale=ntn[:, tile:tile + 1])],
                 reads=[r_f], writes=[r_dc])
            S.op("dve", [lambda e: e.tensor_tensor(out=HS[:, 0, tile, :], in0=ps[:, b0 * 512:b0 * 512 + 768], in1=dc, op=ALU.mult)],
                 reads=[PB[b0], PB[b0 + 1], r_dc], writes=[r_HS])
            S.op("dve", [lambda e: e.tensor_tensor(out=HS[:, 1, tile, :], in0=ps[:, b0 * 512 + 768:b0 * 512 + 1536], in1=dc, op=ALU.mult)],
                 reads=[PB[b0 + 1], PB[b0 + 2], r_dc], writes=[r_HS])

        f_steps.append(wcomb)
        for l in range(3):
            f_steps.append(lambda l=l: mlp_layer(l))
        for tile in range(16):
            f_steps.append(lambda tile=tile: hfull(tile))
        xi = 0
        for k_, fs in enumerate(f_steps):
            fs()
            if k_ >= 1 and xi < 16:
                x_steps[xi]()
                xi += 1
        while xi < 16:
            x_steps[xi]()
            xi += 1
        S.barrier()

        if debug == "s_f":
            S.barrier()
            S.emit()
            return nc
        KS = AV(R_T, [2, 16, 768], BF16)
        r_KS = Res("KS")

        def fwd_pass(rhs_re, rhs_im, r_rhs, ftab, r_ftab, epilogue, extra_w=((), ())):
            for fc in range(16):
                si = fc % 2
                ft = ftab[si]
                S.dma("sp", ft.rearrange("p a b c -> p (a b c)"), fwd_tab_d[fc],
                      writes=[r_ftab[si]] + (list(extra_w[si]) if fc < 2 else []), key="ftab%d" % si)
                bs = [0, 1, 2, 3] if fc % 2 == 0 else [4, 5, 6, 7]
                for ri in range(2):
                    rhs = rhs_re if ri == 0 else rhs_im
                    o0 = bs[0] * 512 + ri * 1024
                    fns = []
                    for tc in range(16):
                        lhs = ft[:, ri, tc, :]
                        fns.append(lambda e, lhs=lhs, tc=tc, rhs=rhs, o0=o0: e.matmul(
                            ps[:, o0:o0 + 512], lhs, rhs[:, tc, 0:512], start=(tc == 0), stop=(tc == 15)))
                        fns.append(lambda e, lhs=lhs, tc=tc, rhs=rhs, o0=o0: e.matmul(
                            ps[:, o0 + 512:o0 + 768], lhs, rhs[:, tc, 512:768], start=(tc == 0), stop=(tc == 15)))
                    S.op("pe", fns, reads=[r_ftab[si], r_rhs], writes=[PB[bs[2 * ri]], PB[bs[2 * ri + 1]]])
                pre = ps[:, bs[0] * 512:bs[0] * 512 + 768]
                pim = ps[:, bs[2] * 512:bs[2] * 512 + 768]
                epilogue(fc, pre, pim, [PB[b] for b in bs])

        ftab = [AV(R_YM, [2, 16, 128], BF16), AV(R_MQ, [2, 16, 128], BF16)]
        r_ftab = [Res("ftab0"), Res("ftab1")]

        def k_epilogue(fc, pre, pim, rbs):
            S.op("dve", [lambda e: e.tensor_tensor(out=KS[:, 0, fc, :], in0=pre, in1=dbc, op=ALU.add)],
                 reads=rbs[0:2] + [r_dbc], writes=[r_KS])
            act_copy(KS[:, 1, fc, :], pim, rbs[2:4], [r_KS])

        fwd_pass(HS[:, 0], HS[:, 1], r_HS, ftab, r_ftab, k_epilogue)

        if debug == "s_k":
            S.barrier()
            S.emit()
            return nc
        Z = AV(R_X, [16, 768], BF16)
        r_Z = Res("Z")
        hsb0 = [AV(R_X + 24 * K_ + i * 8208, [2052], F32) for i in range(2)]
        r_hsb0 = [Res("hsb0_%d" % i) for i in range(2)]
        accx = AV(R_X + 24 * K_ + 16416, [2048], F32)
        accv = AV(R_X + 24 * K_ + 16416 + 8192, [2048], F32)
        zT = AV(R_X + 24 * K_ + 16416 + 16384, [2048], BF16)
        r_accx, r_accv, r_zT = Res("accx"), Res("accv"), Res("zT")
        wch = [AV(R_T + 48 * K_ + i * 2 * K_, [8, 128], BF16) for i in range(3)]
        r_wch = [Res("wch%d" % i) for i in range(3)]
        for i in range(2):
            S.op("pool", [lambda e, i=i: e.memset(hsb0[i][:, 0:1], 0.0)], writes=[r_hsb0[i], r_HS])
            S.op("pool", [lambda e, i=i: e.memset(hsb0[i][:, 2049:2050], 0.0)], writes=[r_hsb0[i], r_HS])
        w0v = w_in_d[0].rearrange("(kc p) n -> p kc n", p=128)
        order = [18, 19] + [0, 1, 2, 3, 4, 5]
        for i in range(6):
            order += [6 + i, 12 + i]
        _wc = [0]

        def proj_chunk(wv, col0, sink_fn, extra_reads=()):
            wi = _wc[0] % 3
            _wc[0] += 1
            S.dma("pool", wch[wi], wv[:, :, col0:col0 + 128], writes=[r_wch[wi]], key="wch%d" % wi)
            for tq in range(4):
                b = nxt([0, 1, 2, 3])
                mm_group(bank(b), [(wch[wi][:, kc, :], XT[:, kc, tq * 512:(tq + 1) * 512]) for kc in range(8)],
                         reads=[r_wch[wi], r_XT], writes=[PB[b]])
                sink_fn(tq, b)

        def conv_chunk(c, hs_, r_hs, acc_out, r_acc_list, out_final):
            S.op("act", [lambda e: e.activation(out=acc_out, in_=hs_[:, 1:2049], func=AF.Identity,
                                                scale=cwb[:, c, 1:2], bias=cwb[:, c, 3:4])],
                 reads=[r_hs, r_cwb], writes=r_acc_list)
            S.op("dve", [lambda e: e.scalar_tensor_tensor(out=acc_out, in0=hs_[:, 0:2048], scalar=cwb[:, c, 0:1],
                                                          in1=acc_out, op0=ALU.mult, op1=ALU.add)],
                 reads=[r_hs, r_cwb] + r_acc_list, writes=r_acc_list)
            S.op("dve", [lambda e: e.scalar_tensor_tensor(out=out_final[0], in0=hs_[:, 2:2050], scalar=cwb[:, c, 2:3],
                                                          in1=acc_out, op0=ALU.mult, op1=ALU.add)],
                 reads=[r_hs, r_cwb] + r_acc_list, writes=out_final[1])

        zdef = []
        for ci_, c in enumerate(order):
            if debug and debug.startswith('s_p') and ci_ == int(debug[3:]):
                S.barrier()
                S.emit()
                return nc
            if c >= 18:
                hp = c - 18

                def sink_mq(tq, b, hp=hp):
                    act_copy(MQ[:, hp, tq * 512:(tq + 1) * 512], bank(b), [PB[b]], [r_MQ])
                proj_chunk(w0v, c * 128, sink_mq)
                continue
            si = (c % 2) if c < 6 else (0 if c < 12 else 1)
            hs_ = hsb0[si]

            def sink_h(tq, b, hs_=hs_, si=si):
                act_copy(hs_[:, 1 + tq * 512:1 + (tq + 1) * 512], bank(b), [PB[b]], [r_hsb0[si]])
            proj_chunk(w0v, c * 128, sink_h)
            if c < 6:
                conv_chunk(c, hs_, r_hsb0[si], accv, [r_accv], (YT[:, c, :], [r_YT]))
            elif c < 12:
                conv_chunk(c, hs_, r_hsb0[si], accx, [r_accx], (accx, [r_accx]))
                while zdef:
                    zdef.pop(0)()
            else:
                i6 = c - 12
                conv_chunk(c, hs_, r_hsb0[si], accv, [r_accv], (accv, [r_accv]))
                S.op("dve", [lambda e: e.tensor_tensor(out=zT, in0=accv, in1=accx, op=ALU.mult)],
                     reads=[r_accv, r_accx], writes=[r_zT])
                def _ztr(i6=i6):
                    for g8 in range(2):
                        b = nxt([6, 7])
                        bb = bank_bf(b)
                        fns = [lambda e, t8=t8, bb=bb, g8=g8: e.transpose(bb[:, t8 * 128:(t8 + 1) * 128],
                                                                         zT[:, (g8 * 8 + t8) * 128:(g8 * 8 + t8 + 1) * 128], ident[:])
                               for t8 in range(8)]
                        S.op("pe", fns, reads=[r_zT, r_consts], writes=[PB[b]])
                        S.op("act", [lambda e, g8=g8, bb=bb, i6=i6: e.activation(
                            out=Z[:, g8 * 8:(g8 + 1) * 8, i6 * 128:(i6 + 1) * 128],
                            in_=bb.rearrange("p (k m) -> p k m", k=8), func=AF.Identity)],
                             reads=[PB[b]], writes=[r_Z])
                zdef.append(_ztr)
        while zdef:
            zdef.pop(0)()
        if debug == "z":
            for tile in range(16):
                S.op("act", [lambda e, tile=tile: e.activation(out=X[:, tile, 0:768] if False else hsb0[0][:, 0:768], in_=Z[:, tile, :], func=AF.Identity)],
                     reads=[r_Z], writes=[r_hsb0[0]])
                S.dma("sp", dbg_d[tile * 128:(tile + 1) * 128, 0:768], hsb0[0][:, 0:768], reads=[r_hsb0[0]], key="dbg")
            S.barrier()
            S.emit()
            return nc
        mem_attention(R_X + 24 * K_)

        YRE = AV(R_XT, [16, 768], BF16)
        YIM = AV(R_X + 24 * K_, [16, 768], BF16)
        r_Y = Res("Yspec")
        ct = [AV(R_X + 48 * K_ + i * 3 * K_, [768], F32) for i in range(4)]
        r_ct = [Res("ct%d" % i) for i in range(4)]
        ftabU = [AV(R_T + 48 * K_, [2, 16, 128], BF16), AV(R_MQ, [2, 16, 128], BF16)]
        r_ftabU = [Res("ftabU0"), Res("ftabU1")]

        def u_epilogue(fc, pre, pim, rbs):
            kre, kim = KS[:, 0, fc, :], KS[:, 1, fc, :]
            S.op("dve", [lambda e: e.tensor_tensor(out=ct[0], in0=pre, in1=kre, op=ALU.mult)], reads=rbs[0:2] + [r_KS], writes=[r_ct[0]])
            S.op("dve", [lambda e: e.tensor_tensor(out=ct[1], in0=pim, in1=kim, op=ALU.mult)], reads=rbs[2:4] + [r_KS], writes=[r_ct[1]])
            S.op("dve", [lambda e: e.tensor_tensor(out=ct[2], in0=pre, in1=kim, op=ALU.mult)], reads=rbs[0:2] + [r_KS], writes=[r_ct[2]])
            S.op("dve", [lambda e: e.tensor_tensor(out=ct[3], in0=pim, in1=kre, op=ALU.mult)], reads=rbs[2:4] + [r_KS], writes=[r_ct[3]])
            S.op("pool", [lambda e: e.tensor_tensor(out=YRE[:, fc, :], in0=ct[0], in1=ct[1], op=ALU.subtract)],
                 reads=[r_ct[0], r_ct[1]], writes=[r_Y])
            S.op("pool", [lambda e: e.tensor_tensor(out=YIM[:, fc, :], in0=ct[2], in1=ct[3], op=ALU.add)],
                 reads=[r_ct[2], r_ct[3]], writes=[r_Y])

        fwd_pass(Z, Z, r_Z, ftabU, r_ftabU, u_epilogue, extra_w=(r_wch, [r_MQ]))

        itab = [AV(R_T + 48 * K_, [4, 2, 512], BF16), AV(R_MQ, [4, 2, 512], BF16)]
        r_itab = r_ftabU
        _it = 0
        for tt in range(4):
            fn_all = []
            for fg in range(4):
                si = _it % 2
                _it += 1
                S.dma("sp", itab[si].rearrange("p a b c -> p (a b c)"), inv_tab_d[tt, fg], writes=[r_itab[si]], key="itab%d" % si)
                fns = []
                for fi in range(4):
                    fc = fg * 4 + fi
                    for ri in range(2):
                        Ysrc = YRE if ri == 0 else YIM
                        for cc in range(6):
                            first = (fc == 0 and ri == 0)
                            lastm = (fc == 15 and ri == 1)
                            fns.append(lambda e, cc=cc, Ysrc=Ysrc, fc=fc, si=si, fi=fi, ri=ri, first=first, lastm=lastm: e.matmul(
                                bank(cc), Ysrc[:, fc, cc * 128:(cc + 1) * 128], itab[si][:, fi, ri, :], start=first, stop=lastm))
                S.op("pe", fns, reads=[r_itab[si], r_Y], writes=[PB[cc] for cc in range(6)])
            for cc in range(6):
                S.op("dve", [lambda e, cc=cc, tt=tt: e.tensor_tensor(out=YT[:, cc, tt * 512:(tt + 1) * 512], in0=bank(cc),
                                                                    in1=YT[:, cc, tt * 512:(tt + 1) * 512], op=ALU.mult)],
                     reads=[PB[cc], r_YT], writes=[r_YT])

        if debug == "mix0":
            for c in range(8):
                src = YT[:, c, 0:1024] if c < 6 else YM[:, c - 6, 0:1024]
                S.op("act", [lambda e, src=src: e.activation(out=rbuf[0], in_=src, func=AF.Identity)], reads=[r_YT, r_YM], writes=[r_rbuf[0]])
                S.dma("sp", dbg_d[c * 128:(c + 1) * 128, :], rbuf[0], reads=[r_rbuf[0]], key="dbg")
            S.barrier()
            S.emit()
            return nc

        out_proj_ln1(0)
        S.barrier()
        if debug == "ln1_0":
            for tile in range(16):
                S.dma("sp", dbg_d[tile * 128:(tile + 1) * 128, :], X[:, tile, :], reads=[r_X[tile]], key="dbg")
            S.barrier()
            S.emit()
            return nc
        ffn_ln2(0, final=(debug == "l0"))
        S.barrier()

        if debug is None or debug.startswith("l1"):
            if debug == "l1_s":
                S.barrier()
                S.emit()
                return nc
            w1v = w_in_d[1].rearrange("(kc p) n -> p kc n", p=128)
            QT = YT
            r_QT = [Res("QT%d" % i) for i in range(12)]
            KT = AV(R_T, [4, 2048], BF16)
            r_KT = Res("KT")
            VT = AV(R_T + 16 * K_, [16, 256], BF16)
            r_VT = Res("VT")
            ropec = AV(R_T + 24 * K_, [2048], F32)
            ropes = AV(R_T + 32 * K_, [2048], F32)
            r_rope = Res("rope")
            wch1 = [AV(R_T + 40 * K_ + i * 2 * K_, [8, 128], BF16) for i in range(3)]
            r_wch1 = [Res("wch1_%d" % i) for i in range(3)]
            qsb = [AV(R_T + 46 * K_ + i * K_, [512], BF16) for i in range(2)]
            r_qsb = [Res("qsb%d" % i) for i in range(2)]
            rt1 = AV(R_T + 48 * K_, [512], F32)
            rt2 = AV(R_T + 50 * K_, [512], F32)
            r_rt1, r_rt2 = Res("rt1"), Res("rt2")
            wvt = AV(R_T + 52 * K_, [8, 256], BF16)
            r_wvt = Res("wvt")
            esk = small("esk", [64, 12], F32)
            r_esk = Res("esk")
            S.dma("sp", ropec, ropec_d[:, :], writes=[r_rope], key="rope0")
            S.dma("sp", ropes, ropes_d[:, :], writes=[r_rope], key="rope1")
            S.dma("sp", esk[:], sink_d.partition_broadcast(64), writes=[r_esk], key="esk")
            S.op("act", [lambda e: e.activation(out=esk[:], in_=esk[:], func=AF.Exp)], reads=[r_esk], writes=[r_esk])
            S.dma("pool", wvt, w1v[:, :, 1024:1280], writes=[r_wvt], key="wvt")
            if debug == "l1_p0":
                S.barrier()
                S.emit()
                return nc
            _w1 = [0]
            _rq = [0]

            def proj1(loads, sink_fn):
                wi = _w1[0] % 3
                _w1[0] += 1
                for k_, (dst0, dst1, c0, c1) in enumerate(loads):
                    S.dma("pool", wch1[wi][:, :, dst0:dst1], w1v[:, :, c0:c1], writes=[r_wch1[wi]], key="wch1_%d_%d" % (wi, k_))
                for tq in range(4):
                    b = nxt([0, 1, 2, 3])
                    mm_group(bank(b), [(wch1[wi][:, kc, :], XT[:, kc, tq * 512:(tq + 1) * 512]) for kc in range(8)],
                             reads=[r_wch1[wi], r_XT], writes=[PB[b]])
                    sink_fn(tq, b)

            def rope_sink(dest, r_dest):
                def sink(tq, b):
                    cs = slice(tq * 512, (tq + 1) * 512)
                    i = _rq[0] % 2
                    _rq[0] += 1
                    S.op("act", [lambda e: e.activation(out=qsb[i], in_=bank(b), func=AF.Identity)],
                         reads=[PB[b]], writes=[r_qsb[i]])
                    b2 = [4, 5][i]
                    mm_group(bank(b2), [(pm[:], qsb[i])], reads=[r_consts, r_qsb[i]], writes=[PB[b2]])
                    S.op("dve", [lambda e: e.tensor_tensor(out=rt1, in0=bank(b), in1=ropec[:, cs], op=ALU.mult)],
                         reads=[PB[b], r_rope], writes=[r_rt1])
                    S.op("dve", [lambda e: e.tensor_tensor(out=rt2, in0=bank(b2), in1=ropes[:, cs], op=ALU.mult)],
                         reads=[PB[b2], r_rope], writes=[r_rt2])
                    S.op("dve", [lambda e: e.tensor_tensor(out=dest[:, cs], in0=rt1, in1=rt2, op=ALU.add)],
                         reads=[r_rt1, r_rt2], writes=r_dest)
                return sink

            for c in range(6):
                proj1([(0, 128, c * 128, (c + 1) * 128)], rope_sink(QT[:, c, :], [r_QT[2 * c], r_QT[2 * c + 1]]))
            if debug == "l1_p1":
                S.barrier()
                S.emit()
                return nc
            for g in range(4):
                c0 = 768 + g * 64
                proj1([(0, 64, c0, c0 + 64), (64, 128, c0, c0 + 64)], rope_sink(KT[:, g, :], [r_KT]))
            if debug == "l1_p2":
                S.barrier()
                S.emit()
                return nc
            for hp in range(2):
                def sink_mq1(tq, b, hp=hp):
                    act_copy(MQ[:, hp, tq * 512:(tq + 1) * 512], bank(b), [PB[b]], [r_MQ])
                proj1([(0, 128, 1280 + hp * 128, 1280 + (hp + 1) * 128)], sink_mq1)
            for tile in range(16):
                b = nxt([0, 1, 2, 3])
                mm_group(bank(b, 256), [(XT[:, kc, tile * 128:(tile + 1) * 128], wvt[:, kc, :]) for kc in range(8)],
                         reads=[r_XT, r_wvt], writes=[PB[b]])
                act_copy(VT[:, tile, :], bank(b, 256), [PB[b]], [r_VT])

            if debug == "l1_p":
                S.barrier()
                S.emit()
                return nc
            PTs = [AV(R_XT + i * 12 * K_, [16, 384], BF16) for i in range(2)]
            r_PTs = [Res("PTs%d" % i) for i in range(2)]
            rden1s = [AV(R_XT + 24 * K_ + i * 2 * K_, [512], F32) for i in range(2)]
            r_rden1s = [Res("rden1_%d" % i) for i in range(2)]
            def st_exp(h, j):
                g = h // 3
                hh = h % 2
                c = h // 2
                prow = slice(hh * 64, (hh + 1) * 64)
                pt = PTs[h % 2]
                r_pt = r_PTs[h % 2]
                qlo = max(0, j - 1) * 128
                qhi = min(16, j + 2) * 128
                n = qhi - qlo
                moff = qlo - (j - 1) * 128
                b = nxt([0, 1, 2, 3])
                fns = [
                    lambda e: e.matmul(bank(b, n), KT[prow, g, j * 128:(j + 1) * 128], QT[prow, c, qlo:qhi], start=True, stop=False),
                    lambda e: e.matmul(bank(b, n), ident[:], maskb[:, moff:moff + n], start=False, stop=True),
                ]
                S.op("pe", fns, reads=[r_KT, r_QT[h], r_consts], writes=[PB[b]])
                S.op("act", [lambda e: e.activation(out=pt[:, j, 0:n], in_=bank(b, n), func=AF.Exp, scale=0.125)],
                     reads=[PB[b]], writes=[r_pt])

            def pv_norm(h, qt):
                g = h // 3
                hh = h % 2
                c = h // 2
                prow = slice(hh * 64, (hh + 1) * 64)
                pt = PTs[h % 2]
                r_pt = r_PTs[h % 2]
                bo, bd = [4, 6][qt % 2], [5, 7][qt % 2]
                fo, fd = [], []
                for i4 in range(4):
                    qb = 4 * qt + i4
                    js = [j for j in (qb - 1, qb, qb + 1) if 0 <= j < 16]
                    for k_, j in enumerate(js):
                        lc = (qb - max(0, j - 1)) * 128
                        st_, sp_ = (k_ == 0), (k_ == len(js) - 1)
                        fo.append(lambda e, i4=i4, j=j, lc=lc, st_=st_, sp_=sp_: e.matmul(
                            ps[0:64, bo * 512 + i4 * 128:bo * 512 + (i4 + 1) * 128], VT[:, j, g * 64:(g + 1) * 64],
                            pt[:, j, lc:lc + 128], start=st_, stop=sp_))
                        fd.append(lambda e, i4=i4, j=j, lc=lc, st_=st_, sp_=sp_: e.matmul(
                            ps[0:64, bd * 512 + i4 * 128:bd * 512 + (i4 + 1) * 128], ones64[:],
                            pt[:, j, lc:lc + 128], start=st_, stop=sp_))
                S.op("pe", fo, reads=[r_pt, r_VT], writes=[PB[bo]])
                S.op("pe", fd, reads=[r_pt, r_consts], writes=[PB[bd]])
                rd = rden1s[qt % 2]
                r_rd = r_rden1s[qt % 2]
                S.op("act", [lambda e: e.activation(out=rd[0:64, :], in_=bank(bd, 512, 0, 64), func=AF.Identity,
                                                    bias=esk[:, h:h + 1])],
                     reads=[PB[bd], r_esk], writes=[r_rd])
                S.op("dve", [lambda e: e.reciprocal(out=rd[0:64, :], in_=rd[0:64, :])], reads=[r_rd], writes=[r_rd])
                S.op("dve", [lambda e: e.tensor_tensor(
                    out=QT[prow, c, qt * 512:(qt + 1) * 512], in0=bank(bo, 512, 0, 64), in1=rd[0:64, :], op=ALU.mult)],
                     reads=[PB[bo], r_rd], writes=[r_QT[h]])

            for h in range(13):
                for qt in range(4):
                    if h < 12:
                        for j in range(4 * qt, 4 * qt + 4):
                            st_exp(h, j)
                    if h >= 1:
                        pv_norm(h - 1, qt)
            if debug == "l1_a":
                S.barrier()
                S.emit()
                return nc
            mem_attention(R_XT, extra_w=r_PTs)
            if debug == "l1mix":
                for c in range(8):
                    src = YT[:, c, 0:1024] if c < 6 else YM[:, c - 6, 0:1024]
                    S.op("act", [lambda e, src=src: e.activation(out=rbuf[0], in_=src, func=AF.Identity)], reads=[r_YM], writes=[r_rbuf[0]])
                    S.dma("sp", dbg_d[c * 128:(c + 1) * 128, :], rbuf[0], reads=[r_rbuf[0]], key="dbg")
                S.barrier()
                S.emit()
                return nc
            out_proj_ln1(1)
            S.barrier()
            ffn_ln2(1, final=True)
            S.barrier()

        S.barrier()
        S.emit()
    return nc


def prep_shared(inputs):
    f32 = np.float32
    sh = {}
    for k in ("w_mem_kv", "l0_w_in", "l1_w_in", "l0_w_out", "l1_w_out", "l0_ffn_w_up", "l1_ffn_w_up",
              "l0_ffn_w_down", "l1_ffn_w_down", "l0_filt_w1", "l0_filt_w2", "l0_filt_w3", "l0_filt_w_out",
              "l0_hyena_d", "l1_sink"):
        sh[k] = np.ascontiguousarray(np.asarray(inputs[k], dtype=f32))
    for i in range(2):
        for n in ("ln1_g", "ln1_b", "ln2_g", "ln2_b"):
            sh["l%d_%s" % (i, n)] = np.ascontiguousarray(np.asarray(inputs["l%d_%s" % (i, n)], dtype=f32))
        cw = np.asarray(inputs["l%d_ffn_conv_w" % i], f32)
        cb = np.asarray(inputs["l%d_ffn_conv_b" % i], f32)
        a = np.concatenate([cw, cb[None, :]], axis=0)
        sh["l%d_fcwb" % i] = np.ascontiguousarray(a.reshape(4, 44, 128).transpose(2, 1, 0))
    cw = np.asarray(inputs["l0_conv_w"], f32)
    cb = np.asarray(inputs["l0_conv_b"], f32)
    a = np.concatenate([cw, cb[None, :]], axis=0)
    sh["l0_cwb"] = np.ascontiguousarray(a.reshape(4, 18, 128).transpose(2, 1, 0))
    sh["l0_fbf"] = np.ascontiguousarray(np.stack(
        [np.asarray(inputs["l0_filt_%s%d" % (n, l)], f32) for l in (1, 2, 3) for n in ("b", "f")], axis=1))
    sh.update(const_tables())
    return sh


_NC_CACHE = {}


def kernel(**inputs):
    sh = prep_shared(inputs)
    x = np.asarray(inputs["x"], np.float32)
    mem = np.asarray(inputs["mem"], np.float32)
    if "nc" not in _NC_CACHE:
        _NC_CACHE["nc"] = build_program()
    nc = _NC_CACHE["nc"]
    in_maps = []
    for b in range(8):
        m = dict(sh)
        m["x"] = np.ascontiguousarray(x[b])
        m["mem"] = np.ascontiguousarray(mem[b])
        in_maps.append(m)
    res = run_bass_kernel_spmd(nc, in_maps, core_ids=list(range(8)))
    return np.stack([np.asarray(r["out"], np.float32) for r in res.results], axis=0)
```

```python
import math
from contextlib import ExitStack

import numpy as np
import ml_dtypes
import concourse.bass as bass
import concourse.mybir as mybir
from concourse.bass_utils import run_bass_kernel_spmd

F32 = mybir.dt.float32
BF16 = mybir.dt.bfloat16
AF = mybir.ActivationFunctionType
ALU = mybir.AluOpType
AX = mybir.AxisListType

L = 2048
D = 1024
NT = 16
KC = 8
DFF = 2816
NFF = 22
ALPHA = 4.0 ** 0.25
EPS = 1e-5
PI = float(np.pi)


class Res:
    __slots__ = ("name", "w", "r", "excl")

    def __init__(self, name, excl=False):
        self.name = name
        self.w = None
        self.r = []
        self.excl = excl


class Sched:
    def __init__(self, nc, es):
        self.nc = nc
        self.es = es
        self.engs = {}
        for n in ("pe", "act", "dve", "pool", "sp"):
            sem = es.enter_context(nc.semaphore("s_" + n))
            self.engs[n] = dict(sem=sem, count=0, known={}, ops=[])
        self.dma_sems = {}
        self.n_dma_sems = 0

    def dma_sem(self, key):
        if key not in self.dma_sems:
            sem = self.es.enter_context(self.nc.semaphore("d%d" % self.n_dma_sems))
            self.n_dma_sems += 1
            self.dma_sems[key] = [sem, 0]
        return self.dma_sems[key]

    def op(self, eng, fns, reads=(), writes=(), dma_key=None):
        E = self.engs[eng]
        deps = {}

        def add(ev):
            if ev is None:
                return
            sem, val = ev
            if deps.get(sem, 0) < val:
                deps[sem] = val

        excl_reads = [r for r in reads if r.excl]
        writes = list(writes) + [r for r in excl_reads if r not in writes]
        reads = [r for r in reads if not r.excl]
        for r in reads:
            add(r.w)
        for w in writes:
            add(w.w)
            for ev in w.r:
                add(ev)
        waits = []
        for sem, val in deps.items():
            if sem is E["sem"] and eng == "pe" and dma_key is None:
                continue
            if E["known"].get(sem, 0) >= val:
                continue
            E["known"][sem] = val
            waits.append((sem, val))
        if dma_key is not None:
            ds = self.dma_sem(dma_key)
            ds[1] += 16
            ev = (ds[0], ds[1])
            inc = (ds[0], 16)
        else:
            E["count"] += 1
            ev = (E["sem"], E["count"])
            inc = (E["sem"], 1)
        for r in reads:
            r.r.append(ev)
        for w in writes:
            w.w = ev
            w.r = []
        if not isinstance(fns, (list, tuple)):
            fns = [fns]
        E["ops"].append((waits, list(fns), inc))
        return ev

    def dma(self, queue, out, in_, reads=(), writes=(), key=None):
        assert key is not None
        return self.op(queue, [lambda e: e.dma_start(out=out, in_=in_)], reads, writes, dma_key=key)

    def barrier(self):
        evs = [(E["sem"], E["count"]) for E in self.engs.values() if E["count"] > 0]
        evs += [(s, c) for (s, c) in self.dma_sems.values() if c > 0]
        for n, E in self.engs.items():
            waits = []
            for sem, val in evs:
                if E["known"].get(sem, 0) >= val:
                    continue
                if sem is E["sem"] and n == "pe":
                    continue
                E["known"][sem] = val
                waits.append((sem, val))
            if waits:
                E["ops"].append((waits, [], None))

    def emit(self):
        nc = self.nc

        def run(name):
            def f(e):
                for waits, fns, inc in self.engs[name]["ops"]:
                    for sem, val in waits:
                        e.wait_ge(sem, val)
                    n = len(fns)
                    for i, fn in enumerate(fns):
                        ins = fn(e)
                        if i == n - 1:
                            ins.then_inc(inc[0], inc[1])
            return f

        with nc.Block() as block:
            block.tensor(run("pe"))
            block.scalar(run("act"))
            block.vector(run("dve"))
            block.gpsimd(run("pool"))
            block.sync(run("sp"))


def _bf(a):
    return np.ascontiguousarray(a.astype(ml_dtypes.bfloat16))


_CONST_CACHE = {}


def const_tables():
    if _CONST_CACHE:
        return _CONST_CACHE
    N = 4096
    f = np.arange(2048, dtype=np.float64)
    t = np.arange(2048, dtype=np.float64)
    m = np.mod(np.outer(2 * f + 1, t), 2 * N)
    ang = np.pi * m / N
    C = np.cos(ang)
    Sn = np.sin(ang)
    CT = C.T.reshape(16, 128, 16, 128)
    ST = (-Sn).T.reshape(16, 128, 16, 128)
    fwd = np.stack([CT, ST], axis=0)
    fwd = fwd.transpose(3, 2, 0, 1, 4)
    _CONST_CACHE["fwd_tab"] = _bf(fwd.reshape(16, 128, 2 * 16 * 128))
    Ci = (C / 2048.0).reshape(4, 4, 128, 4, 512)
    Si = (-Sn / 2048.0).reshape(4, 4, 128, 4, 512)
    inv = np.stack([Ci, Si], axis=0)
    inv = inv.transpose(4, 1, 3, 2, 0, 5)
    _CONST_CACHE["inv_tab"] = _bf(inv.reshape(4, 4, 128, 4 * 2 * 512))
    f32 = np.float32
    tl = np.linspace(0.0, 1.0, L, dtype=f32)[:, None]
    w = (f32(2.0 * math.pi) * np.arange(L, dtype=f32)[:, None] / f32(L)).astype(f32)
    fr = np.linspace(1e-4, 15, 16, dtype=f32)[None, :]
    z = np.concatenate([tl, np.cos(fr * w), -np.sin(fr * w)], axis=-1).astype(f32)
    _CONST_CACHE["zfT"] = np.ascontiguousarray(z.T)
    _CONST_CACHE["ntn"] = np.ascontiguousarray((-tl[:, 0]).reshape(16, 128).T.astype(f32))
    min_decay = math.log(1e-2) / 1.5
    max_decay = math.log(1e-2) / 0.3
    _CONST_CACHE["deltas"] = np.abs(np.linspace(min_decay, max_decay, 768, dtype=f32)).astype(f32)
    inv_f = (10000.0 ** (-np.arange(0, 64, 2, dtype=f32) / f32(64))).astype(f32)
    angr = (np.arange(L, dtype=f32)[:, None] * inv_f[None, :]).astype(f32)
    angr = np.concatenate([angr, angr], axis=-1)
    cosT = np.cos(angr).T.astype(f32)
    sinT = np.sin(angr).T.astype(f32)
    _CONST_CACHE["ropec"] = np.ascontiguousarray(np.concatenate([cosT, cosT], axis=0))
    _CONST_CACHE["ropes"] = np.ascontiguousarray(np.concatenate([sinT, sinT], axis=0))
    Pm = np.zeros((128, 128), np.float32)
    for po in range(128):
        d = po % 64
        if d < 32:
            Pm[po + 32, po] = -1.0
        else:
            Pm[po - 32, po] = 1.0
    _CONST_CACHE["pm"] = _bf(Pm)
    _CONST_CACHE["ident"] = _bf(np.eye(128, dtype=np.float32))
    k = np.arange(128)[:, None]
    q = np.arange(128)[None, :]
    NEG = -30000.0
    m_next = np.where(k <= q, 0.0, NEG)
    m_prev = np.where(k >= q, 0.0, NEG)
    _CONST_CACHE["maskb"] = _bf(np.concatenate([m_next, np.zeros((128, 128)), m_prev], axis=1))
    return _CONST_CACHE


def build_program(debug=None):
    nc = bass.Bass("TRN2", target_bir_lowering=False)
    dbg = {}

    def din(name, shape, dt=F32):
        return nc.dram_tensor(name, list(shape), dt, kind="ExternalInput").ap()

    x_d = din("x", [L, D])
    mem_d = din("mem", [256, D])
    wkv_d = din("w_mem_kv", [D, 512])
    w_in_d = [din("l0_w_in", [D, 2560]), din("l1_w_in", [D, 1536])]
    w_out_d = [din("l0_w_out", [D, D]), din("l1_w_out", [D, D])]
    w_up_d = [din("l%d_ffn_w_up" % i, [D, 2 * DFF]) for i in range(2)]
    w_dn_d = [din("l%d_ffn_w_down" % i, [DFF, D]) for i in range(2)]
    ln_d = [[din("l%d_%s" % (i, n), [D]) for n in ("ln1_g", "ln1_b", "ln2_g", "ln2_b")] for i in range(2)]
    fcwb_d = [din("l%d_fcwb" % i, [128, 44, 4]) for i in range(2)]
    cwb_d = din("l0_cwb", [128, 18, 4])
    fw1_d = din("l0_filt_w1", [33, 64])
    fw2_d = din("l0_filt_w2", [64, 64])
    fw3_d = din("l0_filt_w3", [64, 64])
    fwo_d = din("l0_filt_w_out", [64, 1536])
    fbf_d = din("l0_fbf", [64, 6])
    hd_d = din("l0_hyena_d", [768])
    sink_d = din("l1_sink", [12])
    fwd_tab_d = din("fwd_tab", [16, 128, 4096], BF16)
    inv_tab_d = din("inv_tab", [4, 4, 128, 4096], BF16)
    zfT_d = din("zfT", [33, L])
    ntn_d = din("ntn", [128, 16])
    deltas_d = din("deltas", [768])
    ropec_d = din("ropec", [128, L])
    ropes_d = din("ropes", [128, L])
    pm_d = din("pm", [128, 128], BF16)
    ident_d = din("ident", [128, 128], BF16)
    maskb_d = din("maskb", [128, 384], BF16)
    out_d = nc.dram_tensor("out", [L, D], F32, kind="ExternalOutput").ap()
    if debug:
        dbg_d = nc.dram_tensor("dbg", [L, D], F32, kind="ExternalOutput").ap()

    es = ExitStack()
    with es:
        S = Sched(nc, es)
        AR_BYTES = 194 * 1024
        arena = es.enter_context(nc.sbuf_tensor("arena", [128, AR_BYTES // 2], BF16))
        ps = es.enter_context(nc.psum_tensor("ps", [128, 4096], F32))
        PB = [Res("pb%d" % i, excl=True) for i in range(8)]

        def bank(b, n=512, p0=0, p1=128):
            return ps[p0:p1, b * 512:b * 512 + n]

        def bank_bf(b):
            return ps[:, b * 512:(b + 1) * 512].bitcast(BF16)

        def AV(off, shape, dt):
            n = int(np.prod(shape))
            if dt == F32:
                v = arena[:, off // 2: off // 2 + 2 * n].bitcast(F32)
            else:
                v = arena[:, off // 2: off // 2 + n]
            if len(shape) == 2:
                v = v.rearrange("p (a b) -> p a b", a=shape[0])
            elif len(shape) == 3:
                v = v.rearrange("p (a b c) -> p a b c", a=shape[0], b=shape[1])
            elif len(shape) == 4:
                v = v.rearrange("p (a b c d) -> p a b c d", a=shape[0], b=shape[1], c=shape[2])
            return v

        K_ = 1024
        R_X, R_XT, R_Y, R_YM, R_MQ, R_T = 0, 64 * K_, 96 * K_, 120 * K_, 128 * K_, 136 * K_

        def small(name, shape, dt):
            return es.enter_context(nc.sbuf_tensor("sb_" + name, list(shape), dt))

        ident = small("ident", [128, 128], BF16)
        pm = small("pm", [128, 128], BF16)
        maskb = small("maskb", [128, 384], BF16)
        ones64 = small("ones64", [128, 64], BF16)
        memKT = small("memKT", [128, 2, 256], BF16)
        memV = small("memV", [128, 2, 256], BF16)
        fcwb = small("fcwb", [128, 44, 4], F32)
        cwb = fcwb[:, 0:18, :]
        stat = small("stat", [128, 128], F32)
        GB = small("GB", [128, 2, 1024], F32)
        r_consts = Res("consts")
        r_memKV = Res("memKV")
        r_cwb = Res("cwb")
        r_fcwb = Res("fcwb")
        r_GB = Res("GB")

        X = AV(R_X, [16, 1024], F32)
        XT = AV(R_XT, [8, 2048], BF16)
        YT = AV(R_Y, [6, 2048], BF16)
        YM = AV(R_YM, [2, 2048], BF16)
        MQ = AV(R_MQ, [2, 2048], BF16)
        r_X = [Res("X%d" % i) for i in range(16)]
        r_XT = Res("XT")
        r_YT = Res("YT")
        r_YM = Res("YM")
        r_MQ = Res("MQ")

        _bk = [0]

        def nxt(lst):
            b = lst[_bk[0] % len(lst)]
            _bk[0] += 1
            return b

        def mm_group(out, pairs, reads, writes):
            n = len(pairs)
            fns = []
            for i, (l, r) in enumerate(pairs):
                fns.append(lambda e, l=l, r=r, i=i: e.matmul(out, l, r, start=(i == 0), stop=(i == n - 1)))
            return S.op("pe", fns, reads, writes)

        def act_copy(out, in_, reads, writes):
            return S.op("act", [lambda e: e.activation(out=out, in_=in_, func=AF.Identity)], reads, writes)

        S.op("dve", [lambda e: e.memset(ones64[:], 1.0)], writes=[r_consts])
        epst = small("epst", [128, 1], F32)
        identf = small("identf", [128, 128], F32)
        aident = small("aident", [128, 128], F32)
        S.op("dve", [lambda e: e.memset(epst[:], EPS)], writes=[r_consts])

        HS = AV(R_X, [2, 16, 768], BF16)
        r_HS = Res("HS")
        hA = AV(R_X + 48 * K_, [2048], F32)
        hB = AV(R_X + 56 * K_, [2048], F32)
        r_hA, r_hB = Res("hA"), Res("hB")
        zf = AV(R_Y, [2048], F32)
        fw = AV(R_Y + 8 * K_, [3, 64], F32)
        fwo = AV(R_Y + 9 * K_, [1536], F32)
        fbf = AV(R_Y + 15 * K_, [8], F32)
        dbc = AV(R_Y + 16 * K_, [768], F32)
        dlt = AV(R_YM, [768], F32)
        dec = AV(R_YM + 3 * K_, [768], F32)
        ntn = AV(R_YM + 6 * K_, [16], F32)
        fsb = AV(R_MQ, [768], F32)
        wtmp = AV(R_MQ + 3 * K_, [512], F32)
        wtm2 = AV(R_MQ + 5 * K_, [512], F32)
        r_f = Res("filt_in")
        r_dec, r_fsb, r_wtmp, r_wtm2, r_dbc = Res("dec"), Res("fsb"), Res("wtmp"), Res("wtm2"), Res("dbc")
        S.dma("sp", zf[0:33, :], zfT_d[:, :], writes=[r_f], key="f0")
        S.dma("sp", fw[0:33, 0, :], fw1_d[:, :], writes=[r_f], key="f1")
        S.dma("sp", fw[0:64, 1, :], fw2_d[:, :], writes=[r_f], key="f2")
        S.dma("sp", fw[0:64, 2, :], fw3_d[:, :], writes=[r_f], key="f3")
        S.dma("sp", fwo[0:64, :], fwo_d[:, :], writes=[r_f], key="f4")
        S.dma("sp", fbf[0:64, 0:6], fbf_d[:, :], writes=[r_f], key="f5")
        S.dma("sp", dbc, hd_d.partition_broadcast(128), writes=[r_dbc], key="f6")
        S.dma("sp", dlt, deltas_d.partition_broadcast(128), writes=[r_f], key="f7")
        S.dma("sp", ntn, ntn_d[:, :], writes=[r_f], key="f8")
        S.dma("sp", ident[:], ident_d[:, :], writes=[r_consts], key="c0")
        S.dma("sp", pm[:], pm_d[:, :], writes=[r_consts], key="c1")
        S.dma("sp", maskb[:], maskb_d[:, :], writes=[r_consts], key="c2")
        S.dma("sp", cwb, cwb_d[:, :, :], writes=[r_cwb], key="c3")
        S.op("act", [lambda e: e.activation(out=identf[:], in_=ident[:], func=AF.Identity)], reads=[r_consts], writes=[r_consts])
        S.op("act", [lambda e: e.activation(out=aident[:], in_=ident[:], func=AF.Identity, scale=ALPHA)], reads=[r_consts], writes=[r_consts])

        def transpose_to_XT(src_bf, tile, r_src):
            b = nxt([6, 7])
            bb = bank_bf(b)
            fns = [lambda e, kc=kc: e.transpose(bb[:, kc * 128:(kc + 1) * 128], src_bf[:, kc * 128:(kc + 1) * 128], ident[:])
                   for kc in range(8)]
            S.op("pe", fns, reads=[r_src, r_consts], writes=[PB[b]])
            S.op("act", [lambda e: e.activation(out=XT[:, :, tile * 128:(tile + 1) * 128],
                                                in_=bb.rearrange("p (k m) -> p k m", k=8), func=AF.Identity)],
                 reads=[PB[b]], writes=[r_XT])

        def load_ln(layer, which):
            g_d, b_d = ln_d[layer][2 * which], ln_d[layer][2 * which + 1]
            S.dma("sp", GB[:, 0, :], g_d.partition_broadcast(128), writes=[r_GB], key="gb0")
            S.dma("sp", GB[:, 1, :], b_d.partition_broadcast(128), writes=[r_GB], key="gb1")

        NRB = 6
        LN_LAG = 3
        _rboff = [46 * K_, 50 * K_, 28672, 28672 + 4096, 36880, 36880 + 4096]
        rbuf = [AV(R_T + _rboff[i], [1024], F32) for i in range(NRB)]
        _xboff = [54 * K_, 56 * K_, 24 * K_, 26 * K_]
        xbuf = [AV(R_T + _xboff[i], [1024], BF16) for i in range(4)]
        r_rbuf = [Res("rbuf%d" % i) for i in range(NRB)]
        r_xbuf = [Res("xbuf%d" % i) for i in range(4)]
        r_stat = Res("stat")
        r_stats = [Res("stat%d" % i) for i in range(8)]
        _ln = [0]

        class LNPipe:
            def __init__(self, final_out):
                self.final_out = final_out
                self.q = []

            def push(self, tile, pin, r_pin, r_in, r_r):
                ri = tile % NRB
                so = ri * 16
                r_stat = r_stats[ri]
                st6 = stat[:, so:so + 12]
                mv = stat[:, so + 12:so + 14]
                rstd = stat[:, so + 14:so + 15]
                nmr = stat[:, so + 15:so + 16]
                final_out = self.final_out

                def s1():
                    S.op("dve", [lambda e: e.bn_stats(out=st6[:, 0:6], in_=pin[:, 0:512])], reads=[r_pin[0]], writes=[r_stat])
                    S.op("dve", [lambda e: e.bn_stats(out=st6[:, 6:12], in_=pin[:, 512:1024])], reads=[r_pin[1]], writes=[r_stat])
                    S.op("dve", [lambda e: e.bn_aggr(out=mv, in_=st6)], reads=[r_stat], writes=[r_stat])

                def s2():
                    S.op("act", [lambda e: e.activation(out=rstd, in_=mv[:, 1:2], func=AF.Sqrt, bias=epst[:, 0:1])],
                         reads=[r_stat, r_consts], writes=[r_stat])

                def s3():
                    S.op("dve", [lambda e: e.reciprocal(out=rstd, in_=rstd)], reads=[r_stat], writes=[r_stat])
                    S.op("dve", [lambda e: e.scalar_tensor_tensor(out=nmr, in0=mv[:, 0:1], scalar=-1.0, in1=rstd,
                                                                  op0=ALU.mult, op1=ALU.mult)], reads=[r_stat], writes=[r_stat])

                def s4():
                    S.op("act", [lambda e: e.activation(out=r_in, in_=pin, func=AF.Identity, scale=rstd, bias=nmr)],
                         reads=list(r_pin) + [r_stat], writes=[r_r])

                def s5():
                    S.op("dve", [lambda e: e.tensor_tensor(out=r_in, in0=r_in, in1=GB[:, 0, :], op=ALU.mult)],
                         reads=[r_r, r_GB], writes=[r_r])
                    S.op("pool", [lambda e: e.tensor_tensor(out=X[:, tile, :], in0=r_in, in1=GB[:, 1, :], op=ALU.add)],
                         reads=[r_r, r_GB], writes=[r_X[tile]])

                def s6():
                    if final_out:
                        S.dma("sp", out_d[tile * 128:(tile + 1) * 128, :], X[:, tile, :], reads=[r_X[tile]], key="out%d" % (tile % 4))
                    else:
                        i4 = tile % 4
                        S.op("act", [lambda e: e.activation(out=xbuf[i4], in_=X[:, tile, :], func=AF.Identity)],
                             reads=[r_X[tile]], writes=[r_xbuf[i4]])

                def s7():
                    if not final_out:
                        i4 = tile % 4
                        transpose_to_XT(xbuf[i4], tile, r_xbuf[i4])

                s1()
                for d_, f_, pr_ in ((1, s2, 1), (1, s3, 2), (2, s4, 0), (3, s5, 3), (4, s6, 4), (5, s7, 5)):
                    self.q.append([d_, f_, pr_])
                self._tick()

            def _tick(self):
                keep = []
                for item in self.q:
                    item[0] -= 0
                ready = [it for it in self.q if it[0] <= 0]
                ready.sort(key=lambda it: it[2])
                for it in ready:
                    it[1]()
                self.q = [it for it in self.q if it[0] > 0]
                for it in self.q:
                    it[0] -= 1

            def flush(self):
                while self.q:
                    self._tick()

        def mem_attention(toff, extra_w=()):
            PT = [AV(toff + i * K_, [512], BF16) for i in range(8)]
            r_PT = [Res("mpt%d" % i) for i in range(8)]
            rdens = [AV(toff + 8 * K_ + i * 2 * K_, [512], F32) for i in range(2)]
            r_rdens = [Res("mrden%d" % i) for i in range(2)]
            memVp = AV(toff + 12 * K_, [2, 2, 2, 128], BF16)
            onesp = AV(toff + 14 * K_, [2, 128], BF16)
            r_pad = Res("mpad")
            S.op("dve", [lambda e: e.memset(memVp.rearrange("p a b c d -> p (a b c d)"), 0.0)], writes=[r_pad] + list(extra_w))
            S.op("dve", [lambda e: e.memset(onesp.rearrange("p a b -> p (a b)"), 0.0)], writes=[r_pad])
            for hh in range(2):
                S.op("dve", [lambda e, hh=hh: e.memset(onesp[:, hh, hh * 64:(hh + 1) * 64], 1.0)], writes=[r_pad])
                for mt in range(2):
                    for hp in range(2):
                        h = 2 * hp + hh
                        S.op("dve", [lambda e, mt=mt, hp=hp, hh=hh, h=h: e.tensor_copy(
                            out=memVp[:, mt, hp, hh, hh * 64:(hh + 1) * 64], in_=memV[:, mt, h * 64:(h + 1) * 64])],
                             reads=[r_memKV], writes=[r_pad])
            its = [(hp, qt) for hp in range(2) for qt in range(4)]

            def stage1(n):
                hp, qt = its[n]
                qs = slice(qt * 512, (qt + 1) * 512)
                for hh in range(2):
                    prow = slice(hh * 64, (hh + 1) * 64)
                    for mt in range(2):
                        b = nxt([0, 1, 2, 3])
                        mm_group(bank(b), [(memKT[prow, hp, mt * 128:(mt + 1) * 128], MQ[prow, hp, qs])],
                                 reads=[r_memKV, r_MQ], writes=[PB[b]])
                        k = (n % 2) * 4 + hh * 2 + mt
                        S.op("act", [lambda e, k=k, b=b: e.activation(out=PT[k], in_=bank(b), func=AF.Exp, scale=0.125)],
                             reads=[PB[b]], writes=[r_PT[k]])

            def stage2(n):
                hp, qt = its[n]
                qs = slice(qt * 512, (qt + 1) * 512)
                ks = [((n % 2) * 4 + hh * 2 + mt, hh, mt) for hh in range(2) for mt in range(2)]
                bo, bd = [4, 6][n % 2], [5, 7][n % 2]
                mm_group(bank(bo), [(memVp[:, mt, hp, hh, :], PT[k]) for (k, hh, mt) in ks],
                         reads=[r_pad] + [r_PT[k] for (k, _, _) in ks], writes=[PB[bo]])
                mm_group(bank(bd), [(onesp[:, hh, :], PT[k]) for (k, hh, mt) in ks],
                         reads=[r_pad] + [r_PT[k] for (k, _, _) in ks], writes=[PB[bd]])
                rd = rdens[n % 2]
                r_rd = r_rdens[n % 2]
                S.op("dve", [lambda e: e.reciprocal(out=rd, in_=bank(bd))], reads=[PB[bd]], writes=[r_rd])
                S.op("dve", [lambda e: e.tensor_tensor(out=YM[:, hp, qs], in0=bank(bo), in1=rd, op=ALU.mult)],
                     reads=[PB[bo], r_rd], writes=[r_YM])

            for n in range(len(its) + 1):
                if n < len(its):
                    stage1(n)
                if n >= 1:
                    stage2(n - 1)

        def out_proj_ln1(layer):
            wout = AV(R_T, [8, 1024], BF16)
            r_wout = Res("wout")
            xs = [AV(R_T + 16 * K_ + i * 4 * K_, [1024], F32) for i in range(2)]
            r_xs = [Res("xs%d" % i) for i in range(2)]
            S.dma("pool", wout, w_out_d[layer].rearrange("(kc p) n -> p kc n", p=128),
                  writes=[r_wout] + ([r_KS] if layer == 0 else [r_KT]), key="wout")
            load_ln(layer, 0)
            lnp = LNPipe(False)
            for tile in range(16):
                ts_ = slice(tile * 128, (tile + 1) * 128)
                bp = [0, 2, 4][tile % 3]
                i = tile % 2
                ri = tile % NRB
                if layer == 0:
                    S.dma("sp", xs[i], x_d[ts_, :], writes=[r_xs[i]] + ([r_KS] if tile < 2 else []), key="xs%d" % i)
                    xin, rxin = xs[i], r_xs[i]
                else:
                    xin, rxin = X[:, tile, :], r_X[tile]
                for half in range(2):
                    b = bp + half
                    pairs = []
                    for kc in range(8):
                        lhs = YT[:, kc, ts_] if kc < 6 else YM[:, kc - 6, ts_]
                        pairs.append((lhs, wout[:, kc, half * 512:(half + 1) * 512]))
                    mm_group(bank(b), pairs, reads=[r_YT, r_YM, r_wout], writes=[PB[b]])
                rb = rbuf[ri]
                S.op("dve", [lambda e, xin=xin, rb=rb, bp=bp: e.scalar_tensor_tensor(
                    out=rb, in0=xin, scalar=ALPHA, in1=ps[:, bp * 512:bp * 512 + 1024], op0=ALU.mult, op1=ALU.add)],
                     reads=[rxin, PB[bp], PB[bp + 1]], writes=[r_rbuf[ri]])
                lnp.push(tile, rb, [r_rbuf[ri], r_rbuf[ri]], rb, r_rbuf[ri])
            lnp.flush()

        def ffn_ln2(layer, final):
            blocks = [4, 4, 4, 4, 3, 3]
            S.dma("sp", fcwb[:], fcwb_d[layer][:, :, :], writes=[r_fcwb], key="fcwb")
            load_ln(layer, 1)
            GT = AV(R_Y, [4, 2048], BF16)
            r_GT = Res("GT")
            wdn = [AV(R_T + i * 8 * K_, [4, 1024], BF16) for i in range(2)]
            r_wdn = [Res("wdn%d" % i) for i in range(2)]
            wup = [AV(R_T + 16 * K_ + i * 4 * K_, [8, 2, 128], BF16) for i in range(3)]
            r_wup = [Res("wup%d" % i) for i in range(3)]
            hsb = [AV(R_T + 28 * K_ + i * 8208, [2052], F32) for i in range(2)]
            r_hsb = [Res("hsb%d" % i) for i in range(2)]
            for i in range(2):
                S.op("pool", [lambda e, i=i: e.memset(hsb[i][:, 0:1], 0.0)], writes=[r_hsb[i]])
                S.op("pool", [lambda e, i=i: e.memset(hsb[i][:, 2049:2050], 0.0)], writes=[r_hsb[i]])
            acc_as = [AV(R_YM, [2048], F32), AV(R_Y + 16 * K_, [2048], F32)]
            acc_g = AV(R_MQ, [2048], F32)
            sgb = AV(R_T + 50 * K_, [2048], F32)
            r_sgb = Res("sgb")
            r_accas, r_accg = [Res("acca0"), Res("acca1")], Res("accg")
            wd_v = w_dn_d[layer]
            wu_v = w_up_d[layer].rearrange("(kc p) n -> p kc n", p=128)
            def down_proj(bi, nb, wd):
                last = bi == len(blocks) - 1
                if last:
                    S.barrier()
                lnp = LNPipe(final)
                for tile in range(16):
                    ts_ = slice(tile * 128, (tile + 1) * 128)
                    bp = [0, 2, 4][tile % 3] if last else [4, 6][tile % 2]
                    pin = ps[:, bp * 512:bp * 512 + 1024]
                    if not last:
                        for half in range(2):
                            mm_group(bank(bp + half), [(GT[:, jj, ts_], wd[:, jj, half * 512:(half + 1) * 512]) for jj in range(nb)],
                                     reads=[r_GT, r_wdn[bi % 2]], writes=[PB[bp + half]])
                        if bi == 0:
                            S.op("dve", [lambda e, tile=tile, pin=pin: e.scalar_tensor_tensor(
                                out=X[:, tile, :], in0=X[:, tile, :], scalar=ALPHA, in1=pin, op0=ALU.mult, op1=ALU.add)],
                                 reads=[PB[bp], PB[bp + 1]], writes=[r_X[tile]])
                        else:
                            S.op("dve", [lambda e, tile=tile, pin=pin: e.tensor_tensor(
                                out=X[:, tile, :], in0=X[:, tile, :], in1=pin, op=ALU.add)],
                                 reads=[PB[bp], PB[bp + 1]], writes=[r_X[tile]])
                    else:
                        for half in range(2):
                            b = bp + half
                            hsl = slice(half * 512, (half + 1) * 512)
                            fns = []
                            for jj in range(nb):
                                fns.append(lambda e, jj=jj, b=b, hsl=hsl, ts_=ts_: e.matmul(bank(b), GT[:, jj, ts_], wd[:, jj, hsl], start=(jj == 0), stop=False))
                            fns.append(lambda e, b=b, hsl=hsl, tile=tile: e.matmul(bank(b), identf[:], X[:, tile, hsl], start=False, stop=True))
                            S.op("pe", fns, reads=[r_GT, r_wdn[bi % 2], r_X[tile], r_consts], writes=[PB[b]])
                        ri = tile % NRB
                        lnp.push(tile, pin, [PB[bp], PB[bp + 1]], rbuf[ri], r_rbuf[ri])
                lnp.flush()

            j0 = 0
            deferred = None
            deferred_endblk = False
            for bi, nb in enumerate(blocks):
                wd = wdn[bi % 2]
                S.dma("pool", wd[:, 0:nb, :], wd_v[j0 * 128:(j0 + nb) * 128, :].rearrange("(j p) n -> p j n", p=128),
                      writes=[r_wdn[bi % 2]], key="wdn%d" % (bi % 2))
                for jj in range(nb):
                    j = j0 + jj
                    wi = j % 3
                    wu = wup[wi]
                    acc_a, r_acca = acc_as[j % 2], r_accas[j % 2]
                    S.dma("pool", wu[:, :, 0, :], wu_v[:, :, j * 128:(j + 1) * 128], writes=[r_wup[wi]], key="wup%da" % wi)
                    S.dma("pool", wu[:, :, 1, :], wu_v[:, :, DFF + j * 128:DFF + (j + 1) * 128], writes=[r_wup[wi]], key="wup%db" % wi)
                    for ag in range(2):
                        hs_ = hsb[ag]
                        cj = ag * NFF + j
                        for tq in range(4):
                            b = nxt([0, 1, 2, 3])
                            mm_group(bank(b), [(wu[:, kc, ag, :], XT[:, kc, tq * 512:(tq + 1) * 512]) for kc in range(8)],
                                     reads=[r_wup[wi], r_XT], writes=[PB[b]])
                            S.op("act", [lambda e, hs_=hs_, tq=tq, b=b: e.activation(
                                out=hs_[:, 1 + tq * 512:1 + (tq + 1) * 512], in_=bank(b), func=AF.Identity)],
                                 reads=[PB[b]], writes=[r_hsb[ag]])
                        if ag == 0 and deferred is not None and deferred_endblk:
                            deferred()
                            deferred = None
                        acc, r_acc = (acc_a, r_acca) if ag == 0 else (acc_g, r_accg)
                        S.op("act", [lambda e, acc=acc, hs_=hs_, cj=cj: e.activation(
                            out=acc, in_=hs_[:, 1:2049], func=AF.Identity, scale=fcwb[:, cj, 1:2], bias=fcwb[:, cj, 3:4])],
                             reads=[r_hsb[ag], r_fcwb], writes=[r_acc])
                        S.op("dve", [lambda e, acc=acc, hs_=hs_, cj=cj: e.scalar_tensor_tensor(
                            out=acc, in0=hs_[:, 0:2048], scalar=fcwb[:, cj, 0:1], in1=acc, op0=ALU.mult, op1=ALU.add)],
                             reads=[r_hsb[ag], r_fcwb, r_acc], writes=[r_acc])
                        S.op("dve", [lambda e, acc=acc, hs_=hs_, cj=cj: e.scalar_tensor_tensor(
                            out=acc, in0=hs_[:, 2:2050], scalar=fcwb[:, cj, 2:3], in1=acc, op0=ALU.mult, op1=ALU.add)],
                             reads=[r_hsb[ag], r_fcwb, r_acc], writes=[r_acc])
                        if ag == 0 and deferred is not None:
                            deferred()
                            deferred = None

                    def _fin(jj=jj, acc_a=acc_a, r_acca=r_acca, endblk=(jj == nb - 1), bi=bi, nb=nb, wd=wd):
                        S.op("act", [lambda e: e.activation(out=sgb, in_=acc_g, func=AF.Silu)], reads=[r_accg], writes=[r_sgb])
                        S.op("dve", [lambda e: e.tensor_tensor(out=GT[:, jj, :], in0=sgb, in1=acc_a, op=ALU.mult)],
                             reads=[r_sgb, r_acca], writes=[r_GT])
                        if endblk:
                            down_proj(bi, nb, wd)
                    deferred = _fin
                    deferred_endblk = (jj == nb - 1)
                j0 += nb
            if deferred is not None:
                deferred()
                deferred = None

        memf = AV(R_T + 8 * K_, [2, 1024], F32)
        memb = AV(R_T + 28 * K_, [2, 1024], BF16)
        memT = AV(R_T + 32 * K_, [8, 256], BF16)
        wkv = AV(R_T, [8, 512], BF16)
        r_memf, r_memb, r_memT, r_wkv = Res("memf"), Res("memb"), Res("memT"), Res("wkv")
        S.dma("sp", memf, mem_d.rearrange("(mt p) d -> p mt d", p=128), writes=[r_memf], key="memf")
        S.dma("pool", wkv, wkv_d.rearrange("(kc p) n -> p kc n", p=128), writes=[r_wkv], key="wkv")
        act_copy(memb, memf, [r_memf], [r_memb])
        for mt in range(2):
            b = nxt([6, 7])
            bb = bank_bf(b)
            fns = [lambda e, kc=kc, mt=mt, bb=bb: e.transpose(bb[:, kc * 128:(kc + 1) * 128], memb[:, mt, kc * 128:(kc + 1) * 128], ident[:])
                   for kc in range(8)]
            S.op("pe", fns, reads=[r_memb, r_consts], writes=[PB[b]])
            S.op("act", [lambda e, mt=mt, bb=bb: e.activation(out=memT[:, :, mt * 128:(mt + 1) * 128],
                                                             in_=bb.rearrange("p (k m) -> p k m", k=8), func=AF.Identity)],
                 reads=[PB[b]], writes=[r_memT])
        for hp in range(2):
            b = nxt([0, 1])
            mm_group(bank(b, 256), [(wkv[:, kc, hp * 128:(hp + 1) * 128], memT[:, kc, :]) for kc in range(8)],
                     reads=[r_wkv, r_memT], writes=[PB[b]])
            act_copy(memKT[:, hp, :], bank(b, 256), [PB[b]], [r_memKV])
        for mt in range(2):
            b = nxt([0, 1])
            mm_group(bank(b, 256), [(memT[:, kc, mt * 128:(mt + 1) * 128], wkv[:, kc, 256:512]) for kc in range(8)],
                     reads=[r_wkv, r_memT], writes=[PB[b]])
            act_copy(memV[:, mt, :], bank(b, 256), [PB[b]], [r_memKV])

        if debug == "s_m":
            S.barrier()
            S.emit()
            return nc
        xs0 = [AV(R_T + 16 * K_ + i * 4 * K_, [1024], F32) for i in range(2)]
        r_xs0 = [Res("xs0_%d" % i) for i in range(2)]

        def x_step(tile):
            i = tile % 2
            i4 = tile % 4
            S.dma("sp", xs0[i], x_d[tile * 128:(tile + 1) * 128, :], writes=[r_xs0[i]], key="xs%d" % i)
            S.op("act", [lambda e: e.activation(out=xbuf[i4], in_=xs0[i], func=AF.Identity)],
                 reads=[r_xs0[i]], writes=[r_xbuf[i4]])
            transpose_to_XT(xbuf[i4], tile, r_xbuf[i4])
        x_steps = [(lambda t=t: x_step(t)) for t in range(16)]
        if debug == "s_x":
            S.barrier()
            S.emit()
            return nc
        fbs = stat[0:64, 120:123]
        for l in range(3):
            S.op("dve", [lambda e, l=l: e.tensor_tensor(out=fbs[:, l:l + 1], in0=fbf[0:64, 2 * l:2 * l + 1],
                                                        in1=fbf[0:64, 2 * l + 1:2 * l + 2], op=ALU.mult)],
                 reads=[r_f], writes=[r_stat])
        srcs = [(zf, 33, r_f), (hA, 64, r_hA), (hB, 64, r_hB)]
        dsts = [(hA, r_hA), (hB, r_hB), (hA, r_hA)]
        wtW = AV(R_T + 36 * K_, [2048], F32)
        wt2W = AV(R_T + 44 * K_, [2048], F32)
        r_wtW, r_wt2W = Res("wtW"), Res("wt2W")
        dec2 = [dec, AV(R_MQ, [768], F32)]
        r_dec2 = [Res("dec0"), Res("dec1")]
        f_steps = []

        def mlp_layer(l):
            src, kk, r_src = srcs[l]
            dst, r_dst = dsts[l]
            for tq in range(4):
                cs = slice(tq * 512, (tq + 1) * 512)
                mm_group(bank(tq, 512, 0, 64), [(fw[0:kk, l, :], src[0:kk, cs])], reads=[r_f, r_src], writes=[PB[tq]])
            pall = ps[0:64, 0:2048]
            S.op("dve", [lambda e: e.tensor_scalar(out=wtW[0:64, :], in0=pall, scalar1=fbf[0:64, 2 * l + 1:2 * l + 2],
                                                   scalar2=fbs[:, l:l + 1], op0=ALU.mult, op1=ALU.add)],
                 reads=[PB[0], PB[1], PB[2], PB[3], r_f, r_stat], writes=[r_wtW])
            S.op("dve", [lambda e: e.tensor_scalar(out=wt2W[0:64, :], in0=wtW[0:64, :], scalar1=-PI, scalar2=2 * PI,
                                                   op0=ALU.is_lt, op1=ALU.mult)], reads=[r_wtW], writes=[r_wt2W])
            S.op("dve", [lambda e: e.tensor_tensor(out=wtW[0:64, :], in0=wtW[0:64, :], in1=wt2W[0:64, :], op=ALU.add)],
                 reads=[r_wtW, r_wt2W], writes=[r_wtW])
            S.op("dve", [lambda e: e.tensor_scalar(out=wt2W[0:64, :], in0=wtW[0:64, :], scalar1=PI, scalar2=-2 * PI,
                                                   op0=ALU.is_gt, op1=ALU.mult)], reads=[r_wtW], writes=[r_wt2W])
            S.op("dve", [lambda e: e.tensor_tensor(out=wtW[0:64, :], in0=wtW[0:64, :], in1=wt2W[0:64, :], op=ALU.add)],
                 reads=[r_wtW, r_wt2W], writes=[r_wtW])
            S.op("act", [lambda e: e.activation(out=dst[0:64, :], in_=wtW[0:64, :], func=AF.Sin)],
                 reads=[r_wtW], writes=[r_dst])

        def wcomb():
            S.op("dve", [lambda e: e.tensor_tensor(out=fsb[0:64, :], in0=fwo[0:64, 0:768], in1=fwo[0:64, 768:1536], op=ALU.add)],
                 reads=[r_f], writes=[r_fsb])
            S.op("dve", [lambda e: e.tensor_tensor(out=fwo[0:64, 768:1536], in0=fwo[0:64, 0:768], in1=fwo[0:64, 768:1536], op=ALU.subtract)],
                 reads=[r_f], writes=[r_f])
            S.op("dve", [lambda e: e.tensor_copy(out=fwo[0:64, 0:768], in_=fsb[0:64, :])], reads=[r_fsb], writes=[r_f])

        def hfull(tile):
            ts_ = slice(tile * 128, (tile + 1) * 128)
            b0 = 4 if tile % 2 else 0
            dc, r_dc = dec2[tile % 2], r_dec2[tile % 2]
            for q3 in range(3):
                mm_group(bank(b0 + q3), [(hA[0:64, ts_], fwo[0:64, q3 * 512:(q3 + 1) * 512])], reads=[r_hA, r_f], writes=[PB[b0 + q3]])
            S.op("act", [lambda e: e.activation(out=dc, in_=dlt, func=AF.Exp, scale=ntn[:, tile:tile + 1])],
                 reads=[r_f], writes=[r_dc])
            S.op("dve", [lambda e: e.tensor_tensor(out=HS[:, 0, tile, :], in0=ps[:, b0 * 512:b0 * 512 + 768], in1=dc, op=ALU.mult)],
                 reads=[PB[b0], PB[b0 + 1], r_dc], writes=[r_HS])
            S.op("dve", [lambda e: e.tensor_tensor(out=HS[:, 1, tile, :], in0=ps[:, b0 * 512 + 768:b0 * 512 + 1536], in1=dc, op=ALU.mult)],
                 reads=[PB[b0 + 1], PB[b0 + 2], r_dc], writes=[r_HS])

        f_steps.append(wcomb)
        for l in range(3):
            f_steps.append(lambda l=l: mlp_layer(l))
        for tile in range(16):
            f_steps.append(lambda tile=tile: hfull(tile))
        xi = 0
        for k_, fs in enumerate(f_steps):
            fs()
            if k_ >= 1 and xi < 16:
                x_steps[xi]()
                xi += 1
        while xi < 16:
            x_steps[xi]()
            xi += 1
        S.barrier()

        if debug == "s_f":
            S.barrier()
            S.emit()
            return nc
        KS = AV(R_T, [2, 16, 768], BF16)
        r_KS = Res("KS")

        def fwd_pass(rhs_re, rhs_im, r_rhs, ftab, r_ftab, epilogue, extra_w=((), ())):
            for fc in range(16):
                si = fc % 2
                ft = ftab[si]
                S.dma("sp", ft.rearrange("p a b c -> p (a b c)"), fwd_tab_d[fc],
                      writes=[r_ftab[si]] + (list(extra_w[si]) if fc < 2 else []), key="ftab%d" % si)
                bs = [0, 1, 2, 3] if fc % 2 == 0 else [4, 5, 6, 7]
                for ri in range(2):
                    rhs = rhs_re if ri == 0 else rhs_im
                    o0 = bs[0] * 512 + ri * 1024
                    fns = []
                    for tc in range(16):
                        lhs = ft[:, ri, tc, :]
                        fns.append(lambda e, lhs=lhs, tc=tc, rhs=rhs, o0=o0: e.matmul(
                            ps[:, o0:o0 + 512], lhs, rhs[:, tc, 0:512], start=(tc == 0), stop=(tc == 15)))
                        fns.append(lambda e, lhs=lhs, tc=tc, rhs=rhs, o0=o0: e.matmul(
                            ps[:, o0 + 512:o0 + 768], lhs, rhs[:, tc, 512:768], start=(tc == 0), stop=(tc == 15)))
                    S.op("pe", fns, reads=[r_ftab[si], r_rhs], writes=[PB[bs[2 * ri]], PB[bs[2 * ri + 1]]])
                pre = ps[:, bs[0] * 512:bs[0] * 512 + 768]
                pim = ps[:, bs[2] * 512:bs[2] * 512 + 768]
                epilogue(fc, pre, pim, [PB[b] for b in bs])

        ftab = [AV(R_YM, [2, 16, 128], BF16), AV(R_MQ, [2, 16, 128], BF16)]
        r_ftab = [Res("ftab0"), Res("ftab1")]

        def k_epilogue(fc, pre, pim, rbs):
            S.op("dve", [lambda e: e.tensor_tensor(out=KS[:, 0, fc, :], in0=pre, in1=dbc, op=ALU.add)],
                 reads=rbs[0:2] + [r_dbc], writes=[r_KS])
            act_copy(KS[:, 1, fc, :], pim, rbs[2:4], [r_KS])

        fwd_pass(HS[:, 0], HS[:, 1], r_HS, ftab, r_ftab, k_epilogue)

        if debug == "s_k":
            S.barrier()
            S.emit()
            return nc
        Z = AV(R_X, [16, 768], BF16)
        r_Z = Res("Z")
        hsb0 = [AV(R_X + 24 * K_ + i * 8208, [2052], F32) for i in range(2)]
        r_hsb0 = [Res("hsb0_%d" % i) for i in range(2)]
        accx = AV(R_X + 24 * K_ + 16416, [2048], F32)
        accv = AV(R_X + 24 * K_ + 16416 + 8192, [2048], F32)
        zT = AV(R_X + 24 * K_ + 16416 + 16384, [2048], BF16)
        r_accx, r_accv, r_zT = Res("accx"), Res("accv"), Res("zT")
        wch = [AV(R_T + 48 * K_ + i * 2 * K_, [8, 128], BF16) for i in range(3)]
        r_wch = [Res("wch%d" % i) for i in range(3)]
        for i in range(2):
            S.op("pool", [lambda e, i=i: e.memset(hsb0[i][:, 0:1], 0.0)], writes=[r_hsb0[i], r_HS])
            S.op("pool", [lambda e, i=i: e.memset(hsb0[i][:, 2049:2050], 0.0)], writes=[r_hsb0[i], r_HS])
        w0v = w_in_d[0].rearrange("(kc p) n -> p kc n", p=128)
        order = [18, 19] + [0, 1, 2, 3, 4, 5]
        for i in range(6):
            order += [6 + i, 12 + i]
        _wc = [0]

        def proj_chunk(wv, col0, sink_fn, extra_reads=()):
            wi = _wc[0] % 3
            _wc[0] += 1
            S.dma("pool", wch[wi], wv[:, :, col0:col0 + 128], writes=[r_wch[wi]], key="wch%d" % wi)
            for tq in range(4):
                b = nxt([0, 1, 2, 3])
                mm_group(bank(b), [(wch[wi][:, kc, :], XT[:, kc, tq * 512:(tq + 1) * 512]) for kc in range(8)],
                         reads=[r_wch[wi], r_XT], writes=[PB[b]])
                sink_fn(tq, b)

        def conv_chunk(c, hs_, r_hs, acc_out, r_acc_list, out_final):
            S.op("act", [lambda e: e.activation(out=acc_out, in_=hs_[:, 1:2049], func=AF.Identity,
                                                scale=cwb[:, c, 1:2], bias=cwb[:, c, 3:4])],
                 reads=[r_hs, r_cwb], writes=r_acc_list)
            S.op("dve", [lambda e: e.scalar_tensor_tensor(out=acc_out, in0=hs_[:, 0:2048], scalar=cwb[:, c, 0:1],
                                                          in1=acc_out, op0=ALU.mult, op1=ALU.add)],
                 reads=[r_hs, r_cwb] + r_acc_list, writes=r_acc_list)
            S.op("dve", [lambda e: e.scalar_tensor_tensor(out=out_final[0], in0=hs_[:, 2:2050], scalar=cwb[:, c, 2:3],
                                                          in1=acc_out, op0=ALU.mult, op1=ALU.add)],
                 reads=[r_hs, r_cwb] + r_acc_list, writes=out_final[1])

        zdef = []
        for ci_, c in enumerate(order):
            if debug and debug.startswith('s_p') and ci_ == int(debug[3:]):
                S.barrier()
                S.emit()
                return nc
            if c >= 18:
                hp = c - 18

                def sink_mq(tq, b, hp=hp):
                    act_copy(MQ[:, hp, tq * 512:(tq + 1) * 512], bank(b), [PB[b]], [r_MQ])
                proj_chunk(w0v, c * 128, sink_mq)
                continue
            si = (c % 2) if c < 6 else (0 if c < 12 else 1)
            hs_ = hsb0[si]

            def sink_h(tq, b, hs_=hs_, si=si):
                act_copy(hs_[:, 1 + tq * 512:1 + (tq + 1) * 512], bank(b), [PB[b]], [r_hsb0[si]])
            proj_chunk(w0v, c * 128, sink_h)
            if c < 6:
                conv_chunk(c, hs_, r_hsb0[si], accv, [r_accv], (YT[:, c, :], [r_YT]))
            elif c < 12:
                conv_chunk(c, hs_, r_hsb0[si], accx, [r_accx], (accx, [r_accx]))
                while zdef:
                    zdef.pop(0)()
            else:
                i6 = c - 12
                conv_chunk(c, hs_, r_hsb0[si], accv, [r_accv], (accv, [r_accv]))
                S.op("dve", [lambda e: e.tensor_tensor(out=zT, in0=accv, in1=accx, op=ALU.mult)],
                     reads=[r_accv, r_accx], writes=[r_zT])
                def _ztr(i6=i6):
                    for g8 in range(2):
                        b = nxt([6, 7])
                        bb = bank_bf(b)
                        fns = [lambda e, t8=t8, bb=bb, g8=g8: e.transpose(bb[:, t8 * 128:(t8 + 1) * 128],
                                                                         zT[:, (g8 * 8 + t8) * 128:(g8 * 8 + t8 + 1) * 128], ident[:])
                               for t8 in range(8)]
                        S.op("pe", fns, reads=[r_zT, r_consts], writes=[PB[b]])
                        S.op("act", [lambda e, g8=g8, bb=bb, i6=i6: e.activation(
                            out=Z[:, g8 * 8:(g8 + 1) * 8, i6 * 128:(i6 + 1) * 128],
                            in_=bb.rearrange("p (k m) -> p k m", k=8), func=AF.Identity)],
                             reads=[PB[b]], writes=[r_Z])
                zdef.append(_ztr)
        while zdef:
            zdef.pop(0)()
        if debug == "z":
            for tile in range(16):
                S.op("act", [lambda e, tile=tile: e.activation(out=X[:, tile, 0:768] if False else hsb0[0][:, 0:768], in_=Z[:, tile, :], func=AF.Identity)],
                     reads=[r_Z], writes=[r_hsb0[0]])
                S.dma("sp", dbg_d[tile * 128:(tile + 1) * 128, 0:768], hsb0[0][:, 0:768], reads=[r_hsb0[0]], key="dbg")
            S.barrier()
            S.emit()
            return nc
        mem_attention(R_X + 24 * K_)

        YRE = AV(R_XT, [16, 768], BF16)
        YIM = AV(R_X + 24 * K_, [16, 768], BF16)
        r_Y = Res("Yspec")
        ct = [AV(R_X + 48 * K_ + i * 3 * K_, [768], F32) for i in range(4)]
        r_ct = [Res("ct%d" % i) for i in range(4)]
        ftabU = [AV(R_T + 48 * K_, [2, 16, 128], BF16), AV(R_MQ, [2, 16, 128], BF16)]
        r_ftabU = [Res("ftabU0"), Res("ftabU1")]

        def u_epilogue(fc, pre, pim, rbs):
            kre, kim = KS[:, 0, fc, :], KS[:, 1, fc, :]
            S.op("dve", [lambda e: e.tensor_tensor(out=ct[0], in0=pre, in1=kre, op=ALU.mult)], reads=rbs[0:2] + [r_KS], writes=[r_ct[0]])
            S.op("dve", [lambda e: e.tensor_tensor(out=ct[1], in0=pim, in1=kim, op=ALU.mult)], reads=rbs[2:4] + [r_KS], writes=[r_ct[1]])
            S.op("dve", [lambda e: e.tensor_tensor(out=ct[2], in0=pre, in1=kim, op=ALU.mult)], reads=rbs[0:2] + [r_KS], writes=[r_ct[2]])
            S.op("dve", [lambda e: e.tensor_tensor(out=ct[3], in0=pim, in1=kre, op=ALU.mult)], reads=rbs[2:4] + [r_KS], writes=[r_ct[3]])
            S.op("pool", [lambda e: e.tensor_tensor(out=YRE[:, fc, :], in0=ct[0], in1=ct[1], op=ALU.subtract)],
                 reads=[r_ct[0], r_ct[1]], writes=[r_Y])
            S.op("pool", [lambda e: e.tensor_tensor(out=YIM[:, fc, :], in0=ct[2], in1=ct[3], op=ALU.add)],
                 reads=[r_ct[2], r_ct[3]], writes=[r_Y])

        fwd_pass(Z, Z, r_Z, ftabU, r_ftabU, u_epilogue, extra_w=(r_wch, [r_MQ]))

        itab = [AV(R_T + 48 * K_, [4, 2, 512], BF16), AV(R_MQ, [4, 2, 512], BF16)]
        r_itab = r_ftabU
        _it = 0
        for tt in range(4):
            fn_all = []
            for fg in range(4):
                si = _it % 2
                _it += 1
                S.dma("sp", itab[si].rearrange("p a b c -> p (a b c)"), inv_tab_d[tt, fg], writes=[r_itab[si]], key="itab%d" % si)
                fns = []
                for fi in range(4):
                    fc = fg * 4 + fi
                    for ri in range(2):
                        Ysrc = YRE if ri == 0 else YIM
                        for cc in range(6):
                            first = (fc == 0 and ri == 0)
                            lastm = (fc == 15 and ri == 1)
                            fns.append(lambda e, cc=cc, Ysrc=Ysrc, fc=fc, si=si, fi=fi, ri=ri, first=first, lastm=lastm: e.matmul(
                                bank(cc), Ysrc[:, fc, cc * 128:(cc + 1) * 128], itab[si][:, fi, ri, :], start=first, stop=lastm))
                S.op("pe", fns, reads=[r_itab[si], r_Y], writes=[PB[cc] for cc in range(6)])
            for cc in range(6):
                S.op("dve", [lambda e, cc=cc, tt=tt: e.tensor_tensor(out=YT[:, cc, tt * 512:(tt + 1) * 512], in0=bank(cc),
                                                                    in1=YT[:, cc, tt * 512:(tt + 1) * 512], op=ALU.mult)],
                     reads=[PB[cc], r_YT], writes=[r_YT])

        if debug == "mix0":
            for c in range(8):
                src = YT[:, c, 0:1024] if c < 6 else YM[:, c - 6, 0:1024]
                S.op("act", [lambda e, src=src: e.activation(out=rbuf[0], in_=src, func=AF.Identity)], reads=[r_YT, r_YM], writes=[r_rbuf[0]])
                S.dma("sp", dbg_d[c * 128:(c + 1) * 128, :], rbuf[0], reads=[r_rbuf[0]], key="dbg")
            S.barrier()
            S.emit()
            return nc

        out_proj_ln1(0)
        S.barrier()
        if debug == "ln1_0":
            for tile in range(16):
                S.dma("sp", dbg_d[tile * 128:(tile + 1) * 128, :], X[:, tile, :], reads=[r_X[tile]], key="dbg")
            S.barrier()
            S.emit()
            return nc
        ffn_ln2(0, final=(debug == "l0"))
        S.barrier()

        if debug is None or debug.startswith("l1"):
            if debug == "l1_s":
                S.barrier()
                S.emit()
                return nc
            w1v = w_in_d[1].rearrange("(kc p) n -> p kc n", p=128)
            QT = YT
            r_QT = [Res("QT%d" % i) for i in range(12)]
            KT = AV(R_T, [4, 2048], BF16)
            r_KT = Res("KT")
            VT = AV(R_T + 16 * K_, [16, 256], BF16)
            r_VT = Res("VT")
            ropec = AV(R_T + 24 * K_, [2048], F32)
            ropes = AV(R_T + 32 * K_, [2048], F32)
            r_rope = Res("rope")
            wch1 = [AV(R_T + 40 * K_ + i * 2 * K_, [8, 128], BF16) for i in range(3)]
            r_wch1 = [Res("wch1_%d" % i) for i in range(3)]
            qsb = [AV(R_T + 46 * K_ + i * K_, [512], BF16) for i in range(2)]
            r_qsb = [Res("qsb%d" % i) for i in range(2)]
            rt1 = AV(R_T + 48 * K_, [512], F32)
            rt2 = AV(R_T + 50 * K_, [512], F32)
            r_rt1, r_rt2 = Res("rt1"), Res("rt2")
            wvt = AV(R_T + 52 * K_, [8, 256], BF16)
            r_wvt = Res("wvt")
            esk = small("esk", [64, 12], F32)
            r_esk = Res("esk")
            S.dma("sp", ropec, ropec_d[:, :], writes=[r_rope], key="rope0")
            S.dma("sp", ropes, ropes_d[:, :], writes=[r_rope], key="rope1")
            S.dma("sp", esk[:], sink_d.partition_broadcast(64), writes=[r_esk], key="esk")
            S.op("act", [lambda e: e.activation(out=esk[:], in_=esk[:], func=AF.Exp)], reads=[r_esk], writes=[r_esk])
            S.dma("pool", wvt, w1v[:, :, 1024:1280], writes=[r_wvt], key="wvt")
            if debug == "l1_p0":
                S.barrier()
                S.emit()
                return nc
            _w1 = [0]
            _rq = [0]

            def proj1(loads, sink_fn):
                wi = _w1[0] % 3
                _w1[0] += 1
                for k_, (dst0, dst1, c0, c1) in enumerate(loads):
                    S.dma("pool", wch1[wi][:, :, dst0:dst1], w1v[:, :, c0:c1], writes=[r_wch1[wi]], key="wch1_%d_%d" % (wi, k_))
                for tq in range(4):
                    b = nxt([0, 1, 2, 3])
                    mm_group(bank(b), [(wch1[wi][:, kc, :], XT[:, kc, tq * 512:(tq + 1) * 512]) for kc in range(8)],
                             reads=[r_wch1[wi], r_XT], writes=[PB[b]])
                    sink_fn(tq, b)

            def rope_sink(dest, r_dest):
                def sink(tq, b):
                    cs = slice(tq * 512, (tq + 1) * 512)
                    i = _rq[0] % 2
                    _rq[0] += 1
                    S.op("act", [lambda e: e.activation(out=qsb[i], in_=bank(b), func=AF.Identity)],
                         reads=[PB[b]], writes=[r_qsb[i]])
                    b2 = [4, 5][i]
                    mm_group(bank(b2), [(pm[:], qsb[i])], reads=[r_consts, r_qsb[i]], writes=[PB[b2]])
                    S.op("dve", [lambda e: e.tensor_tensor(out=rt1, in0=bank(b), in1=ropec[:, cs], op=ALU.mult)],
                         reads=[PB[b], r_rope], writes=[r_rt1])
                    S.op("dve", [lambda e: e.tensor_tensor(out=rt2, in0=bank(b2), in1=ropes[:, cs], op=ALU.mult)],
                         reads=[PB[b2], r_rope], writes=[r_rt2])
                    S.op("dve", [lambda e: e.tensor_tensor(out=dest[:, cs], in0=rt1, in1=rt2, op=ALU.add)],
                         reads=[r_rt1, r_rt2], writes=r_dest)
                return sink

            for c in range(6):
                proj1([(0, 128, c * 128, (c + 1) * 128)], rope_sink(QT[:, c, :], [r_QT[2 * c], r_QT[2 * c + 1]]))
            if debug == "l1_p1":
                S.barrier()
                S.emit()
                return nc
            for g in range(4):
                c0 = 768 + g * 64
                proj1([(0, 64, c0, c0 + 64), (64, 128, c0, c0 + 64)], rope_sink(KT[:, g, :], [r_KT]))
            if debug == "l1_p2":
                S.barrier()
                S.emit()
                return nc
            for hp in range(2):
                def sink_mq1(tq, b, hp=hp):
                    act_copy(MQ[:, hp, tq * 512:(tq + 1) * 512], bank(b), [PB[b]], [r_MQ])
                proj1([(0, 128, 1280 + hp * 128, 1280 + (hp + 1) * 128)], sink_mq1)
            for tile in range(16):
                b = nxt([0, 1, 2, 3])
                mm_group(bank(b, 256), [(XT[:, kc, tile * 128:(tile + 1) * 128], wvt[:, kc, :]) for kc in range(8)],
                         reads=[r_XT, r_wvt], writes=[PB[b]])
                act_copy(VT[:, tile, :], bank(b, 256), [PB[b]], [r_VT])

            if debug == "l1_p":
                S.barrier()
                S.emit()
                return nc
            PTs = [AV(R_XT + i * 12 * K_, [16, 384], BF16) for i in range(2)]
            r_PTs = [Res("PTs%d" % i) for i in range(2)]
            rden1s = [AV(R_XT + 24 * K_ + i * 2 * K_, [512], F32) for i in range(2)]
            r_rden1s = [Res("rden1_%d" % i) for i in range(2)]
            def st_exp(h, j):
                g = h // 3
                hh = h % 2
                c = h // 2
                prow = slice(hh * 64, (hh + 1) * 64)
                pt = PTs[h % 2]
                r_pt = r_PTs[h % 2]
                qlo = max(0, j - 1) * 128
                qhi = min(16, j + 2) * 128
                n = qhi - qlo
                moff = qlo - (j - 1) * 128
                b = nxt([0, 1, 2, 3])
                fns = [
                    lambda e: e.matmul(bank(b, n), KT[prow, g, j * 128:(j + 1) * 128], QT[prow, c, qlo:qhi], start=True, stop=False),
                    lambda e: e.matmul(bank(b, n), ident[:], maskb[:, moff:moff + n], start=False, stop=True),
                ]
                S.op("pe", fns, reads=[r_KT, r_QT[h], r_consts], writes=[PB[b]])
                S.op("act", [lambda e: e.activation(out=pt[:, j, 0:n], in_=bank(b, n), func=AF.Exp, scale=0.125)],
                     reads=[PB[b]], writes=[r_pt])

            def pv_norm(h, qt):
                g = h // 3
                hh = h % 2
                c = h // 2
                prow = slice(hh * 64, (hh + 1) * 64)
                pt = PTs[h % 2]
                r_pt = r_PTs[h % 2]
                bo, bd = [4, 6][qt % 2], [5, 7][qt % 2]
                fo, fd = [], []
                for i4 in range(4):
                    qb = 4 * qt + i4
                    js = [j for j in (qb - 1, qb, qb + 1) if 0 <= j < 16]
                    for k_, j in enumerate(js):
                        lc = (qb - max(0, j - 1)) * 128
                        st_, sp_ = (k_ == 0), (k_ == len(js) - 1)
                        fo.append(lambda e, i4=i4, j=j, lc=lc, st_=st_, sp_=sp_: e.matmul(
                            ps[0:64, bo * 512 + i4 * 128:bo * 512 + (i4 + 1) * 128], VT[:, j, g * 64:(g + 1) * 64],
                            pt[:, j, lc:lc + 128], start=st_, stop=sp_))
                        fd.append(lambda e, i4=i4, j=j, lc=lc, st_=st_, sp_=sp_: e.matmul(
                            ps[0:64, bd * 512 + i4 * 128:bd * 512 + (i4 + 1) * 128], ones64[:],
                            pt[:, j, lc:lc + 128], start=st_, stop=sp_))
                S.op("pe", fo, reads=[r_pt, r_VT], writes=[PB[bo]])
                S.op("pe", fd, reads=[r_pt, r_consts], writes=[PB[bd]])
                rd = rden1s[qt % 2]
                r_rd = r_rden1s[qt % 2]
                S.op("act", [lambda e: e.activation(out=rd[0:64, :], in_=bank(bd, 512, 0, 64), func=AF.Identity,
                                                    bias=esk[:, h:h + 1])],
                     reads=[PB[bd], r_esk], writes=[r_rd])
                S.op("dve", [lambda e: e.reciprocal(out=rd[0:64, :], in_=rd[0:64, :])], reads=[r_rd], writes=[r_rd])
                S.op("dve", [lambda e: e.tensor_tensor(
                    out=QT[prow, c, qt * 512:(qt + 1) * 512], in0=bank(bo, 512, 0, 64), in1=rd[0:64, :], op=ALU.mult)],
                     reads=[PB[bo], r_rd], writes=[r_QT[h]])

            for h in range(13):
                for qt in range(4):
                    if h < 12:
                        for j in range(4 * qt, 4 * qt + 4):
                            st_exp(h, j)
                    if h >= 1:
                        pv_norm(h - 1, qt)
            if debug == "l1_a":
                S.barrier()
                S.emit()
                return nc
            mem_attention(R_XT, extra_w=r_PTs)
            if debug == "l1mix":
                for c in range(8):
                    src = YT[:, c, 0:1024] if c < 6 else YM[:, c - 6, 0:1024]
                    S.op("act", [lambda e, src=src: e.activation(out=rbuf[0], in_=src, func=AF.Identity)], reads=[r_YM], writes=[r_rbuf[0]])
                    S.dma("sp", dbg_d[c * 128:(c + 1) * 128, :], rbuf[0], reads=[r_rbuf[0]], key="dbg")
                S.barrier()
                S.emit()
                return nc
            out_proj_ln1(1)
            S.barrier()
            ffn_ln2(1, final=True)
            S.barrier()

        S.barrier()
        S.emit()
    return nc


def prep_shared(inputs):
    f32 = np.float32
    sh = {}
    for k in ("w_mem_kv", "l0_w_in", "l1_w_in", "l0_w_out", "l1_w_out", "l0_ffn_w_up", "l1_ffn_w_up",
              "l0_ffn_w_down", "l1_ffn_w_down", "l0_filt_w1", "l0_filt_w2", "l0_filt_w3", "l0_filt_w_out",
              "l0_hyena_d", "l1_sink"):
        sh[k] = np.ascontiguousarray(np.asarray(inputs[k], dtype=f32))
    for i in range(2):
        for n in ("ln1_g", "ln1_b", "ln2_g", "ln2_b"):
            sh["l%d_%s" % (i, n)] = np.ascontiguousarray(np.asarray(inputs["l%d_%s" % (i, n)], dtype=f32))
        cw = np.asarray(inputs["l%d_ffn_conv_w" % i], f32)
        cb = np.asarray(inputs["l%d_ffn_conv_b" % i], f32)
        a = np.concatenate([cw, cb[None, :]], axis=0)
        sh["l%d_fcwb" % i] = np.ascontiguousarray(a.reshape(4, 44, 128).transpose(2, 1, 0))
    cw = np.asarray(inputs["l0_conv_w"], f32)
    cb = np.asarray(inputs["l0_conv_b"], f32)
    a = np.concatenate([cw, cb[None, :]], axis=0)
    sh["l0_cwb"] = np.ascontiguousarray(a.reshape(4, 18, 128).transpose(2, 1, 0))
    sh["l0_fbf"] = np.ascontiguousarray(np.stack(
        [np.asarray(inputs["l0_filt_%s%d" % (n, l)], f32) for l in (1, 2, 3) for n in ("b", "f")], axis=1))
    sh.update(const_tables())
    return sh


_NC_CACHE = {}


def kernel(**inputs):
    sh = prep_shared(inputs)
    x = np.asarray(inputs["x"], np.float32)
    mem = np.asarray(inputs["mem"], np.float32)
    if "nc" not in _NC_CACHE:
        _NC_CACHE["nc"] = build_program()
    nc = _NC_CACHE["nc"]
    in_maps = []
    for b in range(8):
        m = dict(sh)
        m["x"] = np.ascontiguousarray(x[b])
        m["mem"] = np.ascontiguousarray(mem[b])
        in_maps.append(m)
    res = run_bass_kernel_spmd(nc, in_maps, core_ids=list(range(8)))
    return np.stack([np.asarray(r["out"], np.float32) for r in res.results], axis=0)
```

```python
import math
from contextlib import ExitStack

import numpy as np
import ml_dtypes
import concourse.bass as bass
import concourse.mybir as mybir
from concourse.bass_utils import run_bass_kernel_spmd

F32 = mybir.dt.float32
BF16 = mybir.dt.bfloat16
AF = mybir.ActivationFunctionType
ALU = mybir.AluOpType
AX = mybir.AxisListType

L = 2048
D = 1024
NT = 16
KC = 8
DFF = 2816
NFF = 22
ALPHA = 4.0 ** 0.25
EPS = 1e-5
PI = float(np.pi)


class Res:
    __slots__ = ("name", "w", "r", "excl")

    def __init__(self, name, excl=False):
        self.name = name
        self.w = None
        self.r = []
        self.excl = excl


class Sched:
    def __init__(self, nc, es):
        self.nc = nc
        self.es = es
        self.engs = {}
        for n in ("pe", "act", "dve", "pool", "sp"):
            sem = es.enter_context(nc.semaphore("s_" + n))
            self.engs[n] = dict(sem=sem, count=0, known={}, ops=[])
        self.dma_sems = {}
        self.n_dma_sems = 0

    def dma_sem(self, key):
        if key not in self.dma_sems:
            sem = self.es.enter_context(self.nc.semaphore("d%d" % self.n_dma_sems))
            self.n_dma_sems += 1
            self.dma_sems[key] = [sem, 0]
        return self.dma_sems[key]

    def op(self, eng, fns, reads=(), writes=(), dma_key=None):
        E = self.engs[eng]
        deps = {}

        def add(ev):
            if ev is None:
                return
            sem, val = ev
            if deps.get(sem, 0) < val:
                deps[sem] = val

        excl_reads = [r for r in reads if r.excl]
        writes = list(writes) + [r for r in excl_reads if r not in writes]
        reads = [r for r in reads if not r.excl]
        for r in reads:
            add(r.w)
        for w in writes:
            add(w.w)
            for ev in w.r:
                add(ev)
        waits = []
        for sem, val in deps.items():
            if sem is E["sem"] and eng == "pe" and dma_key is None:
                continue
            if E["known"].get(sem, 0) >= val:
                continue
            E["known"][sem] = val
            waits.append((sem, val))
        if dma_key is not None:
            ds = self.dma_sem(dma_key)
            ds[1] += 16
            ev = (ds[0], ds[1])
            inc = (ds[0], 16)
        else:
            E["count"] += 1
            ev = (E["sem"], E["count"])
            inc = (E["sem"], 1)
        for r in reads:
            r.r.append(ev)
        for w in writes:
            w.w = ev
            w.r = []
        if not isinstance(fns, (list, tuple)):
            fns = [fns]
        E["ops"].append((waits, list(fns), inc))
        return ev

    def dma(self, queue, out, in_, reads=(), writes=(), key=None):
        assert key is not None
        return self.op(queue, [lambda e: e.dma_start(out=out, in_=in_)], reads, writes, dma_key=key)

    def barrier(self):
        evs = [(E["sem"], E["count"]) for E in self.engs.values() if E["count"] > 0]
        evs += [(s, c) for (s, c) in self.dma_sems.values() if c > 0]
        for n, E in self.engs.items():
            waits = []
            for sem, val in evs:
                if E["known"].get(sem, 0) >= val:
                    continue
                if sem is E["sem"] and n == "pe":
                    continue
                E["known"][sem] = val
                waits.append((sem, val))
            if waits:
                E["ops"].append((waits, [], None))

    def emit(self):
        nc = self.nc

        def run(name):
            def f(e):
                for waits, fns, inc in self.engs[name]["ops"]:
                    for sem, val in waits:
                        e.wait_ge(sem, val)
                    n = len(fns)
                    for i, fn in enumerate(fns):
                        ins = fn(e)
                        if i == n - 1:
                            ins.then_inc(inc[0], inc[1])
            return f

        with nc.Block() as block:
            block.tensor(run("pe"))
            block.scalar(run("act"))
            block.vector(run("dve"))
            block.gpsimd(run("pool"))
            block.sync(run("sp"))


def _bf(a):
    return np.ascontiguousarray(a.astype(ml_dtypes.bfloat16))


_CONST_CACHE = {}


def const_tables():
    if _CONST_CACHE:
        return _CONST_CACHE
    N = 4096
    f = np.arange(2048, dtype=np.float64)
    t = np.arange(2048, dtype=np.float64)
    m = np.mod(np.outer(2 * f + 1, t), 2 * N)
    ang = np.pi * m / N
    C = np.cos(ang)
    Sn = np.sin(ang)
    CT = C.T.reshape(16, 128, 16, 128)
    ST = (-Sn).T.reshape(16, 128, 16, 128)
    fwd = np.stack([CT, ST], axis=0)
    fwd = fwd.transpose(3, 2, 0, 1, 4)
    _CONST_CACHE["fwd_tab"] = _bf(fwd.reshape(16, 128, 2 * 16 * 128))
    Ci = (C / 2048.0).reshape(4, 4, 128, 4, 512)
    Si = (-Sn / 2048.0).reshape(4, 4, 128, 4, 512)
    inv = np.stack([Ci, Si], axis=0)
    inv = inv.transpose(4, 1, 3, 2, 0, 5)
    _CONST_CACHE["inv_tab"] = _bf(inv.reshape(4, 4, 128, 4 * 2 * 512))
    f32 = np.float32
    tl = np.linspace(0.0, 1.0, L, dtype=f32)[:, None]
    w = (f32(2.0 * math.pi) * np.arange(L, dtype=f32)[:, None] / f32(L)).astype(f32)
    fr = np.linspace(1e-4, 15, 16, dtype=f32)[None, :]
    z = np.concatenate([tl, np.cos(fr * w), -np.sin(fr * w)], axis=-1).astype(f32)
    _CONST_CACHE["zfT"] = np.ascontiguousarray(z.T)
    _CONST_CACHE["ntn"] = np.ascontiguousarray((-tl[:, 0]).reshape(16, 128).T.astype(f32))
    min_decay = math.log(1e-2) / 1.5
    max_decay = math.log(1e-2) / 0.3
    _CONST_CACHE["deltas"] = np.abs(np.linspace(min_decay, max_decay, 768, dtype=f32)).astype(f32)
    inv_f = (10000.0 ** (-np.arange(0, 64, 2, dtype=f32) / f32(64))).astype(f32)
    angr = (np.arange(L, dtype=f32)[:, None] * inv_f[None, :]).astype(f32)
    angr = np.concatenate([angr, angr], axis=-1)
    cosT = np.cos(angr).T.astype(f32)
    sinT = np.sin(angr).T.astype(f32)
    _CONST_CACHE["ropec"] = np.ascontiguousarray(np.concatenate([cosT, cosT], axis=0))
    _CONST_CACHE["ropes"] = np.ascontiguousarray(np.concatenate([sinT, sinT], axis=0))
    Pm = np.zeros((128, 128), np.float32)
    for po in range(128):
        d = po % 64
        if d < 32:
            Pm[po + 32, po] = -1.0
        else:
            Pm[po - 32, po] = 1.0
    _CONST_CACHE["pm"] = _bf(Pm)
    _CONST_CACHE["ident"] = _bf(np.eye(128, dtype=np.float32))
    k = np.arange(128)[:, None]
    q = np.arange(128)[None, :]
    NEG = -30000.0
    m_next = np.where(k <= q, 0.0, NEG)
    m_prev = np.where(k >= q, 0.0, NEG)
    _CONST_CACHE["maskb"] = _bf(np.concatenate([m_next, np.zeros((128, 128)), m_prev], axis=1))
    return _CONST_CACHE


def build_program(debug=None):
    nc = bass.Bass("TRN2", target_bir_lowering=False)
    dbg = {}

    def din(name, shape, dt=F32):
        return nc.dram_tensor(name, list(shape), dt, kind="ExternalInput").ap()

    x_d = din("x", [L, D])
    mem_d = din("mem", [256, D])
    wkv_d = din("w_mem_kv", [D, 512])
    w_in_d = [din("l0_w_in", [D, 2560]), din("l1_w_in", [D, 1536])]
    w_out_d = [din("l0_w_out", [D, D]), din("l1_w_out", [D, D])]
    w_up_d = [din("l%d_ffn_w_up" % i, [D, 2 * DFF]) for i in range(2)]
    w_dn_d = [din("l%d_ffn_w_down" % i, [DFF, D]) for i in range(2)]
    ln_d = [[din("l%d_%s" % (i, n), [D]) for n in ("ln1_g", "ln1_b", "ln2_g", "ln2_b")] for i in range(2)]
    fcwb_d = [din("l%d_fcwb" % i, [128, 44, 4]) for i in range(2)]
    cwb_d = din("l0_cwb", [128, 18, 4])
    fw1_d = din("l0_filt_w1", [33, 64])
    fw2_d = din("l0_filt_w2", [64, 64])
    fw3_d = din("l0_filt_w3", [64, 64])
    fwo_d = din("l0_filt_w_out", [64, 1536])
    fbf_d = din("l0_fbf", [64, 6])
    hd_d = din("l0_hyena_d", [768])
    sink_d = din("l1_sink", [12])
    fwd_tab_d = din("fwd_tab", [16, 128, 4096], BF16)
    inv_tab_d = din("inv_tab", [4, 4, 128, 4096], BF16)
    zfT_d = din("zfT", [33, L])
    ntn_d = din("ntn", [128, 16])
    deltas_d = din("deltas", [768])
    ropec_d = din("ropec", [128, L])
    ropes_d = din("ropes", [128, L])
    pm_d = din("pm", [128, 128], BF16)
    ident_d = din("ident", [128, 128], BF16)
    maskb_d = din("maskb", [128, 384], BF16)
    out_d = nc.dram_tensor("out", [L, D], F32, kind="ExternalOutput").ap()
    if debug:
        dbg_d = nc.dram_tensor("dbg", [L, D], F32, kind="ExternalOutput").ap()

    es = ExitStack()
    with es:
        S = Sched(nc, es)
        AR_BYTES = 194 * 1024
        arena = es.enter_context(nc.sbuf_tensor("arena", [128, AR_BYTES // 2], BF16))
        ps = es.enter_context(nc.psum_tensor("ps", [128, 4096], F32))
        PB = [Res("pb%d" % i, excl=True) for i in range(8)]

        def bank(b, n=512, p0=0, p1=128):
            return ps[p0:p1, b * 512:b * 512 + n]

        def bank_bf(b):
            return ps[:, b * 512:(b + 1) * 512].bitcast(BF16)

        def AV(off, shape, dt):
            n = int(np.prod(shape))
            if dt == F32:
                v = arena[:, off // 2: off // 2 + 2 * n].bitcast(F32)
            else:
                v = arena[:, off // 2: off // 2 + n]
            if len(shape) == 2:
                v = v.rearrange("p (a b) -> p a b", a=shape[0])
            elif len(shape) == 3:
                v = v.rearrange("p (a b c) -> p a b c", a=shape[0], b=shape[1])
            elif len(shape) == 4:
                v = v.rearrange("p (a b c d) -> p a b c d", a=shape[0], b=shape[1], c=shape[2])
            return v

        K_ = 1024
        R_X, R_XT, R_Y, R_YM, R_MQ, R_T = 0, 64 * K_, 96 * K_, 120 * K_, 128 * K_, 136 * K_

        def small(name, shape, dt):
            return es.enter_context(nc.sbuf_tensor("sb_" + name, list(shape), dt))

        ident = small("ident", [128, 128], BF16)
        pm = small("pm", [128, 128], BF16)
        maskb = small("maskb", [128, 384], BF16)
        ones64 = small("ones64", [128, 64], BF16)
        memKT = small("memKT", [128, 2, 256], BF16)
        memV = small("memV", [128, 2, 256], BF16)
        fcwb = small("fcwb", [128, 44, 4], F32)
        cwb = fcwb[:, 0:18, :]
        stat = small("stat", [128, 128], F32)
        GB = small("GB", [128, 2, 1024], F32)
        r_consts = Res("consts")
        r_memKV = Res("memKV")
        r_cwb = Res("cwb")
        r_fcwb = Res("fcwb")
        r_GB = Res("GB")

        X = AV(R_X, [16, 1024], F32)
        XT = AV(R_XT, [8, 2048], BF16)
        YT = AV(R_Y, [6, 2048], BF16)
        YM = AV(R_YM, [2, 2048], BF16)
        MQ = AV(R_MQ, [2, 2048], BF16)
        r_X = [Res("X%d" % i) for i in range(16)]
        r_XT = Res("XT")
        r_YT = Res("YT")
        r_YM = Res("YM")
        r_MQ = Res("MQ")

        _bk = [0]

        def nxt(lst):
            b = lst[_bk[0] % len(lst)]
            _bk[0] += 1
            return b

        def mm_group(out, pairs, reads, writes):
            n = len(pairs)
            fns = []
            for i, (l, r) in enumerate(pairs):
                fns.append(lambda e, l=l, r=r, i=i: e.matmul(out, l, r, start=(i == 0), stop=(i == n - 1)))
            return S.op("pe", fns, reads, writes)

        def act_copy(out, in_, reads, writes):
            return S.op("act", [lambda e: e.activation(out=out, in_=in_, func=AF.Identity)], reads, writes)

        S.op("dve", [lambda e: e.memset(ones64[:], 1.0)], writes=[r_consts])
        epst = small("epst", [128, 1], F32)
        identf = small("identf", [128, 128], F32)
        aident = small("aident", [128, 128], F32)
        S.op("dve", [lambda e: e.memset(epst[:], EPS)], writes=[r_consts])

        HS = AV(R_X, [2, 16, 768], BF16)
        r_HS = Res("HS")
        hA = AV(R_X + 48 * K_, [2048], F32)
        hB = AV(R_X + 56 * K_, [2048], F32)
        r_hA, r_hB = Res("hA"), Res("hB")
        zf = AV(R_Y, [2048], F32)
        fw = AV(R_Y + 8 * K_, [3, 64], F32)
        fwo = AV(R_Y + 9 * K_, [1536], F32)
        fbf = AV(R_Y + 15 * K_, [8], F32)
        dbc = AV(R_Y + 16 * K_, [768], F32)
        dlt = AV(R_YM, [768], F32)
        dec = AV(R_YM + 3 * K_, [768], F32)
        ntn = AV(R_YM + 6 * K_, [16], F32)
        fsb = AV(R_MQ, [768], F32)
        wtmp = AV(R_MQ + 3 * K_, [512], F32)
        wtm2 = AV(R_MQ + 5 * K_, [512], F32)
        r_f = Res("filt_in")
        r_dec, r_fsb, r_wtmp, r_wtm2, r_dbc = Res("dec"), Res("fsb"), Res("wtmp"), Res("wtm2"), Res("dbc")
        S.dma("sp", zf[0:33, :], zfT_d[:, :], writes=[r_f], key="f0")
        S.dma("sp", fw[0:33, 0, :], fw1_d[:, :], writes=[r_f], key="f1")
        S.dma("sp", fw[0:64, 1, :], fw2_d[:, :], writes=[r_f], key="f2")
        S.dma("sp", fw[0:64, 2, :], fw3_d[:, :], writes=[r_f], key="f3")
        S.dma("sp", fwo[0:64, :], fwo_d[:, :], writes=[r_f], key="f4")
        S.dma("sp", fbf[0:64, 0:6], fbf_d[:, :], writes=[r_f], key="f5")
        S.dma("sp", dbc, hd_d.partition_broadcast(128), writes=[r_dbc], key="f6")
        S.dma("sp", dlt, deltas_d.partition_broadcast(128), writes=[r_f], key="f7")
        S.dma("sp", ntn, ntn_d[:, :], writes=[r_f], key="f8")
        S.dma("sp", ident[:], ident_d[:, :], writes=[r_consts], key="c0")
        S.dma("sp", pm[:], pm_d[:, :], writes=[r_consts], key="c1")
        S.dma("sp", maskb[:], maskb_d[:, :], writes=[r_consts], key="c2")
        S.dma("sp", cwb, cwb_d[:, :, :], writes=[r_cwb], key="c3")
        S.op("act", [lambda e: e.activation(out=identf[:], in_=ident[:], func=AF.Identity)], reads=[r_consts], writes=[r_consts])
        S.op("act", [lambda e: e.activation(out=aident[:], in_=ident[:], func=AF.Identity, scale=ALPHA)], reads=[r_consts], writes=[r_consts])

        def transpose_to_XT(src_bf, tile, r_src):
            b = nxt([6, 7])
            bb = bank_bf(b)
            fns = [lambda e, kc=kc: e.transpose(bb[:, kc * 128:(kc + 1) * 128], src_bf[:, kc * 128:(kc + 1) * 128], ident[:])
                   for kc in range(8)]
            S.op("pe", fns, reads=[r_src, r_consts], writes=[PB[b]])
            S.op("act", [lambda e: e.activation(out=XT[:, :, tile * 128:(tile + 1) * 128],
                                                in_=bb.rearrange("p (k m) -> p k m", k=8), func=AF.Identity)],
                 reads=[PB[b]], writes=[r_XT])

        def load_ln(layer, which):
            g_d, b_d = ln_d[layer][2 * which], ln_d[layer][2 * which + 1]
            S.dma("sp", GB[:, 0, :], g_d.partition_broadcast(128), writes=[r_GB], key="gb0")
            S.dma("sp", GB[:, 1, :], b_d.partition_broadcast(128), writes=[r_GB], key="gb1")

        NRB = 6
        LN_LAG = 3
        _rboff = [46 * K_, 50 * K_, 28672, 28672 + 4096, 36880, 36880 + 4096]
        rbuf = [AV(R_T + _rboff[i], [1024], F32) for i in range(NRB)]
        _xboff = [54 * K_, 56 * K_, 24 * K_, 26 * K_]
        xbuf = [AV(R_T + _xboff[i], [1024], BF16) for i in range(4)]
        r_rbuf = [Res("rbuf%d" % i) for i in range(NRB)]
        r_xbuf = [Res("xbuf%d" % i) for i in range(4)]
        r_stat = Res("stat")
        r_stats = [Res("stat%d" % i) for i in range(8)]
        _ln = [0]

        class LNPipe:
            def __init__(self, final_out):
                self.final_out = final_out
                self.q = []

            def push(self, tile, pin, r_pin, r_in, r_r):
                ri = tile % NRB
                so = ri * 16
                r_stat = r_stats[ri]
                st6 = stat[:, so:so + 12]
                mv = stat[:, so + 12:so + 14]
                rstd = stat[:, so + 14:so + 15]
                nmr = stat[:, so + 15:so + 16]
                final_out = self.final_out

                def s1():
                    S.op("dve", [lambda e: e.bn_stats(out=st6[:, 0:6], in_=pin[:, 0:512])], reads=[r_pin[0]], writes=[r_stat])
                    S.op("dve", [lambda e: e.bn_stats(out=st6[:, 6:12], in_=pin[:, 512:1024])], reads=[r_pin[1]], writes=[r_stat])
                    S.op("dve", [lambda e: e.bn_aggr(out=mv, in_=st6)], reads=[r_stat], writes=[r_stat])

                def s2():
                    S.op("act", [lambda e: e.activation(out=rstd, in_=mv[:, 1:2], func=AF.Sqrt, bias=epst[:, 0:1])],
                         reads=[r_stat, r_consts], writes=[r_stat])

                def s3():
                    S.op("dve", [lambda e: e.reciprocal(out=rstd, in_=rstd)], reads=[r_stat], writes=[r_stat])
                    S.op("dve", [lambda e: e.scalar_tensor_tensor(out=nmr, in0=mv[:, 0:1], scalar=-1.0, in1=rstd,
                                                                  op0=ALU.mult, op1=ALU.mult)], reads=[r_stat], writes=[r_stat])

                def s4():
                    S.op("act", [lambda e: e.activation(out=r_in, in_=pin, func=AF.Identity, scale=rstd, bias=nmr)],
                         reads=list(r_pin) + [r_stat], writes=[r_r])

                def s5():
                    S.op("dve", [lambda e: e.tensor_tensor(out=r_in, in0=r_in, in1=GB[:, 0, :], op=ALU.mult)],
                         reads=[r_r, r_GB], writes=[r_r])
                    S.op("pool", [lambda e: e.tensor_tensor(out=X[:, tile, :], in0=r_in, in1=GB[:, 1, :], op=ALU.add)],
                         reads=[r_r, r_GB], writes=[r_X[tile]])

                def s6():
                    if final_out:
                        S.dma("sp", out_d[tile * 128:(tile + 1) * 128, :], X[:, tile, :], reads=[r_X[tile]], key="out%d" % (tile % 4))
                    else:
                        i4 = tile % 4
                        S.op("act", [lambda e: e.activation(out=xbuf[i4], in_=X[:, tile, :], func=AF.Identity)],
                             reads=[r_X[tile]], writes=[r_xbuf[i4]])

                def s7():
                    if not final_out:
                        i4 = tile % 4
                        transpose_to_XT(xbuf[i4], tile, r_xbuf[i4])

                s1()
                for d_, f_, pr_ in ((1, s2, 1), (1, s3, 2), (2, s4, 0), (3, s5, 3), (4, s6, 4), (5, s7, 5)):
                    self.q.append([d_, f_, pr_])
                self._tick()

            def _tick(self):
                keep = []
                for item in self.q:
                    item[0] -= 0
                ready = [it for it in self.q if it[0] <= 0]
                ready.sort(key=lambda it: it[2])
                for it in ready:
                    it[1]()
                self.q = [it for it in self.q if it[0] > 0]
                for it in self.q:
                    it[0] -= 1

            def flush(self):
                while self.q:
                    self._tick()

        def mem_attention(toff, extra_w=()):
            PT = [AV(toff + i * K_, [512], BF16) for i in range(8)]
            r_PT = [Res("mpt%d" % i) for i in range(8)]
            rdens = [AV(toff + 8 * K_ + i * 2 * K_, [512], F32) for i in range(2)]
            r_rdens = [Res("mrden%d" % i) for i in range(2)]
            memVp = AV(toff + 12 * K_, [2, 2, 2, 128], BF16)
            onesp = AV(toff + 14 * K_, [2, 128], BF16)
            r_pad = Res("mpad")
            S.op("dve", [lambda e: e.memset(memVp.rearrange("p a b c d -> p (a b c d)"), 0.0)], writes=[r_pad] + list(extra_w))
            S.op("dve", [lambda e: e.memset(onesp.rearrange("p a b -> p (a b)"), 0.0)], writes=[r_pad])
            for hh in range(2):
                S.op("dve", [lambda e, hh=hh: e.memset(onesp[:, hh, hh * 64:(hh + 1) * 64], 1.0)], writes=[r_pad])
                for mt in range(2):
                    for hp in range(2):
                        h = 2 * hp + hh
                        S.op("dve", [lambda e, mt=mt, hp=hp, hh=hh, h=h: e.tensor_copy(
                            out=memVp[:, mt, hp, hh, hh * 64:(hh + 1) * 64], in_=memV[:, mt, h * 64:(h + 1) * 64])],
                             reads=[r_memKV], writes=[r_pad])
            its = [(hp, qt) for hp in range(2) for qt in range(4)]

            def stage1(n):
                hp, qt = its[n]
                qs = slice(qt * 512, (qt + 1) * 512)
                for hh in range(2):
                    prow = slice(hh * 64, (hh + 1) * 64)
                    for mt in range(2):
                        b = nxt([0, 1, 2, 3])
                        mm_group(bank(b), [(memKT[prow, hp, mt * 128:(mt + 1) * 128], MQ[prow, hp, qs])],
                                 reads=[r_memKV, r_MQ], writes=[PB[b]])
                        k = (n % 2) * 4 + hh * 2 + mt
                        S.op("act", [lambda e, k=k, b=b: e.activation(out=PT[k], in_=bank(b), func=AF.Exp, scale=0.125)],
                             reads=[PB[b]], writes=[r_PT[k]])

            def stage2(n):
                hp, qt = its[n]
                qs = slice(qt * 512, (qt + 1) * 512)
                ks = [((n % 2) * 4 + hh * 2 + mt, hh, mt) for hh in range(2) for mt in range(2)]
                bo, bd = [4, 6][n % 2], [5, 7][n % 2]
                mm_group(bank(bo), [(memVp[:, mt, hp, hh, :], PT[k]) for (k, hh, mt) in ks],
                         reads=[r_pad] + [r_PT[k] for (k, _, _) in ks], writes=[PB[bo]])
                mm_group(bank(bd), [(onesp[:, hh, :], PT[k]) for (k, hh, mt) in ks],
                         reads=[r_pad] + [r_PT[k] for (k, _, _) in ks], writes=[PB[bd]])
                rd = rdens[n % 2]
                r_rd = r_rdens[n % 2]
                S.op("dve", [lambda e: e.reciprocal(out=rd, in_=bank(bd))], reads=[PB[bd]], writes=[r_rd])
                S.op("dve", [lambda e: e.tensor_tensor(out=YM[:, hp, qs], in0=bank(bo), in1=rd, op=ALU.mult)],
                     reads=[PB[bo], r_rd], writes=[r_YM])

            for n in range(len(its) + 1):
                if n < len(its):
                    stage1(n)
                if n >= 1:
                    stage2(n - 1)

        def out_proj_ln1(layer):
            wout = AV(R_T, [8, 1024], BF16)
            r_wout = Res("wout")
            xs = [AV(R_T + 16 * K_ + i * 4 * K_, [1024], F32) for i in range(2)]
            r_xs = [Res("xs%d" % i) for i in range(2)]
            S.dma("pool", wout, w_out_d[layer].rearrange("(kc p) n -> p kc n", p=128),
                  writes=[r_wout] + ([r_KS] if layer == 0 else [r_KT]), key="wout")
            load_ln(layer, 0)
            lnp = LNPipe(False)
            for tile in range(16):
                ts_ = slice(tile * 128, (tile + 1) * 128)
                bp = [0, 2, 4][tile % 3]
                i = tile % 2
                ri = tile % NRB
                if layer == 0:
                    S.dma("sp", xs[i], x_d[ts_, :], writes=[r_xs[i]] + ([r_KS] if tile < 2 else []), key="xs%d" % i)
                    xin, rxin = xs[i], r_xs[i]
                else:
                    xin, rxin = X[:, tile, :], r_X[tile]
                for half in range(2):
                    b = bp + half
                    pairs = []
                    for kc in range(8):
                        lhs = YT[:, kc, ts_] if kc < 6 else YM[:, kc - 6, ts_]
                        pairs.append((lhs, wout[:, kc, half * 512:(half + 1) * 512]))
                    mm_group(bank(b), pairs, reads=[r_YT, r_YM, r_wout], writes=[PB[b]])
                rb = rbuf[ri]
                S.op("dve", [lambda e, xin=xin, rb=rb, bp=bp: e.scalar_tensor_tensor(
                    out=rb, in0=xin, scalar=ALPHA, in1=ps[:, bp * 512:bp * 512 + 1024], op0=ALU.mult, op1=ALU.add)],
                     reads=[rxin, PB[bp], PB[bp + 1]], writes=[r_rbuf[ri]])
                lnp.push(tile, rb, [r_rbuf[ri], r_rbuf[ri]], rb, r_rbuf[ri])
            lnp.flush()

        def ffn_ln2(layer, final):
            blocks = [4, 4, 4, 4, 3, 3]
            S.dma("sp", fcwb[:], fcwb_d[layer][:, :, :], writes=[r_fcwb], key="fcwb")
            load_ln(layer, 1)
            GT = AV(R_Y, [4, 2048], BF16)
            r_GT = Res("GT")
            wdn = [AV(R_T + i * 8 * K_, [4, 1024], BF16) for i in range(2)]
            r_wdn = [Res("wdn%d" % i) for i in range(2)]
            wup = [AV(R_T + 16 * K_ + i * 4 * K_, [8, 2, 128], BF16) for i in range(3)]
            r_wup = [Res("wup%d" % i) for i in range(3)]
            hsb = [AV(R_T + 28 * K_ + i * 8208, [2052], F32) for i in range(2)]
            r_hsb = [Res("hsb%d" % i) for i in range(2)]
            for i in range(2):
                S.op("pool", [lambda e, i=i: e.memset(hsb[i][:, 0:1], 0.0)], writes=[r_hsb[i]])
                S.op("pool", [lambda e, i=i: e.memset(hsb[i][:, 2049:2050], 0.0)], writes=[r_hsb[i]])
            acc_as = [AV(R_YM, [2048], F32), AV(R_Y + 16 * K_, [2048], F32)]
            acc_g = AV(R_MQ, [2048], F32)
            sgb = AV(R_T + 50 * K_, [2048], F32)
            r_sgb = Res("sgb")
            r_accas, r_accg = [Res("acca0"), Res("acca1")], Res("accg")
            wd_v = w_dn_d[layer]
            wu_v = w_up_d[layer].rearrange("(kc p) n -> p kc n", p=128)
            def down_proj(bi, nb, wd):
                last = bi == len(blocks) - 1
                if last:
                    S.barrier()
                lnp = LNPipe(final)
                for tile in range(16):
                    ts_ = slice(tile * 128, (tile + 1) * 128)
                    bp = [0, 2, 4][tile % 3] if last else [4, 6][tile % 2]
                    pin = ps[:, bp * 512:bp * 512 + 1024]
                    if not last:
                        for half in range(2):
                            mm_group(bank(bp + half), [(GT[:, jj, ts_], wd[:, jj, half * 512:(half + 1) * 512]) for jj in range(nb)],
                                     reads=[r_GT, r_wdn[bi % 2]], writes=[PB[bp + half]])
                        if bi == 0:
                            S.op("dve", [lambda e, tile=tile, pin=pin: e.scalar_tensor_tensor(
                                out=X[:, tile, :], in0=X[:, tile, :], scalar=ALPHA, in1=pin, op0=ALU.mult, op1=ALU.add)],
                                 reads=[PB[bp], PB[bp + 1]], writes=[r_X[tile]])
                        else:
                            S.op("dve", [lambda e, tile=tile, pin=pin: e.tensor_tensor(
                                out=X[:, tile, :], in0=X[:, tile, :], in1=pin, op=ALU.add)],
                                 reads=[PB[bp], PB[bp + 1]], writes=[r_X[tile]])
                    else:
                        for half in range(2):
                            b = bp + half
                            hsl = slice(half * 512, (half + 1) * 512)
                            fns = []
                            for jj in range(nb):
                                fns.append(lambda e, jj=jj, b=b, hsl=hsl, ts_=ts_: e.matmul(bank(b), GT[:, jj, ts_], wd[:, jj, hsl], start=(jj == 0), stop=False))
                            fns.append(lambda e, b=b, hsl=hsl, tile=tile: e.matmul(bank(b), identf[:], X[:, tile, hsl], start=False, stop=True))
                            S.op("pe", fns, reads=[r_GT, r_wdn[bi % 2], r_X[tile], r_consts], writes=[PB[b]])
                        ri = tile % NRB
                        lnp.push(tile, pin, [PB[bp], PB[bp + 1]], rbuf[ri], r_rbuf[ri])
                lnp.flush()

            j0 = 0
            deferred = None
            deferred_endblk = False
            for bi, nb in enumerate(blocks):
                wd = wdn[bi % 2]
                S.dma("pool", wd[:, 0:nb, :], wd_v[j0 * 128:(j0 + nb) * 128, :].rearrange("(j p) n -> p j n", p=128),
                      writes=[r_wdn[bi % 2]], key="wdn%d" % (bi % 2))
                for jj in range(nb):
                    j = j0 + jj
                    wi = j % 3
                    wu = wup[wi]
                    acc_a, r_acca = acc_as[j % 2], r_accas[j % 2]
                    S.dma("pool", wu[:, :, 0, :], wu_v[:, :, j * 128:(j + 1) * 128], writes=[r_wup[wi]], key="wup%da" % wi)
                    S.dma("pool", wu[:, :, 1, :], wu_v[:, :, DFF + j * 128:DFF + (j + 1) * 128], writes=[r_wup[wi]], key="wup%db" % wi)
                    for ag in range(2):
                        hs_ = hsb[ag]
                        cj = ag * NFF + j
                        for tq in range(4):
                            b = nxt([0, 1, 2, 3])
                            mm_group(bank(b), [(wu[:, kc, ag, :], XT[:, kc, tq * 512:(tq + 1) * 512]) for kc in range(8)],
                                     reads=[r_wup[wi], r_XT], writes=[PB[b]])
                            S.op("act", [lambda e, hs_=hs_, tq=tq, b=b: e.activation(
                                out=hs_[:, 1 + tq * 512:1 + (tq + 1) * 512], in_=bank(b), func=AF.Identity)],
                                 reads=[PB[b]], writes=[r_hsb[ag]])
                        if ag == 0 and deferred is not None and deferred_endblk:
                            deferred()
                            deferred = None
                        acc, r_acc = (acc_a, r_acca) if ag == 0 else (acc_g, r_accg)
                        S.op("act", [lambda e, acc=acc, hs_=hs_, cj=cj: e.activation(
                            out=acc, in_=hs_[:, 1:2049], func=AF.Identity, scale=fcwb[:, cj, 1:2], bias=fcwb[:, cj, 3:4])],
                             reads=[r_hsb[ag], r_fcwb], writes=[r_acc])
                        S.op("dve", [lambda e, acc=acc, hs_=hs_, cj=cj: e.scalar_tensor_tensor(
                            out=acc, in0=hs_[:, 0:2048], scalar=fcwb[:, cj, 0:1], in1=acc, op0=ALU.mult, op1=ALU.add)],
                             reads=[r_hsb[ag], r_fcwb, r_acc], writes=[r_acc])
                        S.op("dve", [lambda e, acc=acc, hs_=hs_, cj=cj: e.scalar_tensor_tensor(
                            out=acc, in0=hs_[:, 2:2050], scalar=fcwb[:, cj, 2:3], in1=acc, op0=ALU.mult, op1=ALU.add)],
                             reads=[r_hsb[ag], r_fcwb, r_acc], writes=[r_acc])
                        if ag == 0 and deferred is not None:
                            deferred()
                            deferred = None

                    def _fin(jj=jj, acc_a=acc_a, r_acca=r_acca, endblk=(jj == nb - 1), bi=bi, nb=nb, wd=wd):
                        S.op("act", [lambda e: e.activation(out=sgb, in_=acc_g, func=AF.Silu)], reads=[r_accg], writes=[r_sgb])
                        S.op("dve", [lambda e: e.tensor_tensor(out=GT[:, jj, :], in0=sgb, in1=acc_a, op=ALU.mult)],
                             reads=[r_sgb, r_acca], writes=[r_GT])
                        if endblk:
                            down_proj(bi, nb, wd)
                    deferred = _fin
                    deferred_endblk = (jj == nb - 1)
                j0 += nb
            if deferred is not None:
                deferred()
                deferred = None

        memf = AV(R_T + 8 * K_, [2, 1024], F32)
        memb = AV(R_T + 28 * K_, [2, 1024], BF16)
        memT = AV(R_T + 32 * K_, [8, 256], BF16)
        wkv = AV(R_T, [8, 512], BF16)
        r_memf, r_memb, r_memT, r_wkv = Res("memf"), Res("memb"), Res("memT"), Res("wkv")
        S.dma("sp", memf, mem_d.rearrange("(mt p) d -> p mt d", p=128), writes=[r_memf], key="memf")
        S.dma("pool", wkv, wkv_d.rearrange("(kc p) n -> p kc n", p=128), writes=[r_wkv], key="wkv")
        act_copy(memb, memf, [r_memf], [r_memb])
        for mt in range(2):
            b = nxt([6, 7])
            bb = bank_bf(b)
            fns = [lambda e, kc=kc, mt=mt, bb=bb: e.transpose(bb[:, kc * 128:(kc + 1) * 128], memb[:, mt, kc * 128:(kc + 1) * 128], ident[:])
                   for kc in range(8)]
            S.op("pe", fns, reads=[r_memb, r_consts], writes=[PB[b]])
            S.op("act", [lambda e, mt=mt, bb=bb: e.activation(out=memT[:, :, mt * 128:(mt + 1) * 128],
                                                             in_=bb.rearrange("p (k m) -> p k m", k=8), func=AF.Identity)],
                 reads=[PB[b]], writes=[r_memT])
        for hp in range(2):
            b = nxt([0, 1])
            mm_group(bank(b, 256), [(wkv[:, kc, hp * 128:(hp + 1) * 128], memT[:, kc, :]) for kc in range(8)],
                     reads=[r_wkv, r_memT], writes=[PB[b]])
            act_copy(memKT[:, hp, :], bank(b, 256), [PB[b]], [r_memKV])
        for mt in range(2):
            b = nxt([0, 1])
            mm_group(bank(b, 256), [(memT[:, kc, mt * 128:(mt + 1) * 128], wkv[:, kc, 256:512]) for kc in range(8)],
                     reads=[r_wkv, r_memT], writes=[PB[b]])
            act_copy(memV[:, mt, :], bank(b, 256), [PB[b]], [r_memKV])

        if debug == "s_m":
            S.barrier()
            S.emit()
            return nc
        xs0 = [AV(R_T + 16 * K_ + i * 4 * K_, [1024], F32) for i in range(2)]
        r_xs0 = [Res("xs0_%d" % i) for i in range(2)]

        def x_step(tile):
            i = tile % 2
            i4 = tile % 4
            S.dma("sp", xs0[i], x_d[tile * 128:(tile + 1) * 128, :], writes=[r_xs0[i]], key="xs%d" % i)
            S.op("act", [lambda e: e.activation(out=xbuf[i4], in_=xs0[i], func=AF.Identity)],
                 reads=[r_xs0[i]], writes=[r_xbuf[i4]])
            transpose_to_XT(xbuf[i4], tile, r_xbuf[i4])
        x_steps = [(lambda t=t: x_step(t)) for t in range(16)]
        if debug == "s_x":
            S.barrier()
            S.emit()
            return nc
        fbs = stat[0:64, 120:123]
        for l in range(3):
            S.op("dve", [lambda e, l=l: e.tensor_tensor(out=fbs[:, l:l + 1], in0=fbf[0:64, 2 * l:2 * l + 1],
                                                        in1=fbf[0:64, 2 * l + 1:2 * l + 2], op=ALU.mult)],
                 reads=[r_f], writes=[r_stat])
        srcs = [(zf, 33, r_f), (hA, 64, r_hA), (hB, 64, r_hB)]
        dsts = [(hA, r_hA), (hB, r_hB), (hA, r_hA)]
        wtW = AV(R_T + 36 * K_, [2048], F32)
        wt2W = AV(R_T + 44 * K_, [2048], F32)
        r_wtW, r_wt2W = Res("wtW"), Res("wt2W")
        dec2 = [dec, AV(R_MQ, [768], F32)]
        r_dec2 = [Res("dec0"), Res("dec1")]
        f_steps = []

        def mlp_layer(l):
            src, kk, r_src = srcs[l]
            dst, r_dst = dsts[l]
            for tq in range(4):
                cs = slice(tq * 512, (tq + 1) * 512)
                mm_group(bank(tq, 512, 0, 64), [(fw[0:kk, l, :], src[0:kk, cs])], reads=[r_f, r_src], writes=[PB[tq]])
            pall = ps[0:64, 0:2048]
            S.op("dve", [lambda e: e.tensor_scalar(out=wtW[0:64, :], in0=pall, scalar1=fbf[0:64, 2 * l + 1:2 * l + 2],
                                                   scalar2=fbs[:, l:l + 1], op0=ALU.mult, op1=ALU.add)],
                 reads=[PB[0], PB[1], PB[2], PB[3], r_f, r_stat], writes=[r_wtW])
            S.op("dve", [lambda e: e.tensor_scalar(out=wt2W[0:64, :], in0=wtW[0:64, :], scalar1=-PI, scalar2=2 * PI,
                                                   op0=ALU.is_lt, op1=ALU.mult)], reads=[r_wtW], writes=[r_wt2W])
            S.op("dve", [lambda e: e.tensor_tensor(out=wtW[0:64, :], in0=wtW[0:64, :], in1=wt2W[0:64, :], op=ALU.add)],
                 reads=[r_wtW, r_wt2W], writes=[r_wtW])
            S.op("dve", [lambda e: e.tensor_scalar(out=wt2W[0:64, :], in0=wtW[0:64, :], scalar1=PI, scalar2=-2 * PI,
                                                   op0=ALU.is_gt, op1=ALU.mult)], reads=[r_wtW], writes=[r_wt2W])
            S.op("dve", [lambda e: e.tensor_tensor(out=wtW[0:64, :], in0=wtW[0:64, :], in1=wt2W[0:64, :], op=ALU.add)],
                 reads=[r_wtW, r_wt2W], writes=[r_wtW])
            S.op("act", [lambda e: e.activation(out=dst[0:64, :], in_=wtW[0:64, :], func=AF.Sin)],
                 reads=[r_wtW], writes=[r_dst])

        def wcomb():
            S.op("dve", [lambda e: e.tensor_tensor(out=fsb[0:64, :], in0=fwo[0:64, 0:768], in1=fwo[0:64, 768:1536], op=ALU.add)],
                 reads=[r_f], writes=[r_fsb])
            S.op("dve", [lambda e: e.tensor_tensor(out=fwo[0:64, 768:1536], in0=fwo[0:64, 0:768], in1=fwo[0:64, 768:1536], op=ALU.subtract)],
                 reads=[r_f], writes=[r_f])
            S.op("dve", [lambda e: e.tensor_copy(out=fwo[0:64, 0:768], in_=fsb[0:64, :])], reads=[r_fsb], writes=[r_f])

        def hfull(tile):
            ts_ = slice(tile * 128, (tile + 1) * 128)
            b0 = 4 if tile % 2 else 0
            dc, r_dc = dec2[tile % 2], r_dec2[tile % 2]
            for q3 in range(3):
                mm_group(bank(b0 + q3), [(hA[0:64, ts_], fwo[0:64, q3 * 512:(q3 + 1) * 512])], reads=[r_hA, r_f], writes=[PB[b0 + q3]])
            S.op("act", [lambda e: e.activation(out=dc, in_=dlt, func=AF.Exp, scale=ntn[:, tile:tile + 1])],
                 reads=[r_f], writes=[r_dc])
            S.op("dve", [lambda e: e.tensor_tensor(out=HS[:, 0, tile, :], in0=ps[:, b0 * 512:b0 * 512 + 768], in1=dc, op=ALU.mult)],
                 reads=[PB[b0], PB[b0 + 1], r_dc], writes=[r_HS])
            S.op("dve", [lambda e: e.tensor_tensor(out=HS[:, 1, tile, :], in0=ps[:, b0 * 512 + 768:b0 * 512 + 1536], in1=dc, op=ALU.mult)],
                 reads=[PB[b0 + 1], PB[b0 + 2], r_dc], writes=[r_HS])

        f_steps.append(wcomb)
        for l in range(3):
            f_steps.append(lambda l=l: mlp_layer(l))
        for tile in range(16):
            f_steps.append(lambda tile=tile: hfull(tile))
        xi = 0
        for k_, fs in enumerate(f_steps):
            fs()
            if k_ >= 1 and xi < 16:
                x_steps[xi]()
                xi += 1
        while xi < 16:
            x_steps[xi]()
            xi += 1
        S.barrier()

        if debug == "s_f":
            S.barrier()
            S.emit()
            return nc
        KS = AV(R_T, [2, 16, 768], BF16)
        r_KS = Res("KS")

        def fwd_pass(rhs_re, rhs_im, r_rhs, ftab, r_ftab, epilogue, extra_w=((), ())):
            for fc in range(16):
                si = fc % 2
                ft = ftab[si]
                S.dma("sp", ft.rearrange("p a b c -> p (a b c)"), fwd_tab_d[fc],
                      writes=[r_ftab[si]] + (list(extra_w[si]) if fc < 2 else []), key="ftab%d" % si)
                bs = [0, 1, 2, 3] if fc % 2 == 0 else [4, 5, 6, 7]
                for ri in range(2):
                    rhs = rhs_re if ri == 0 else rhs_im
                    o0 = bs[0] * 512 + ri * 1024
                    fns = []
                    for tc in range(16):
                        lhs = ft[:, ri, tc, :]
                        fns.append(lambda e, lhs=lhs, tc=tc, rhs=rhs, o0=o0: e.matmul(
                            ps[:, o0:o0 + 512], lhs, rhs[:, tc, 0:512], start=(tc == 0), stop=(tc == 15)))
                        fns.append(lambda e, lhs=lhs, tc=tc, rhs=rhs, o0=o0: e.matmul(
                            ps[:, o0 + 512:o0 + 768], lhs, rhs[:, tc, 512:768], start=(tc == 0), stop=(tc == 15)))
                    S.op("pe", fns, reads=[r_ftab[si], r_rhs], writes=[PB[bs[2 * ri]], PB[bs[2 * ri + 1]]])
                pre = ps[:, bs[0] * 512:bs[0] * 512 + 768]
                pim = ps[:, bs[2] * 512:bs[2] * 512 + 768]
                epilogue(fc, pre, pim, [PB[b] for b in bs])

        ftab = [AV(R_YM, [2, 16, 128], BF16), AV(R_MQ, [2, 16, 128], BF16)]
        r_ftab = [Res("ftab0"), Res("ftab1")]

        def k_epilogue(fc, pre, pim, rbs):
            S.op("dve", [lambda e: e.tensor_tensor(out=KS[:, 0, fc, :], in0=pre, in1=dbc, op=ALU.add)],
                 reads=rbs[0:2] + [r_dbc], writes=[r_KS])
            act_copy(KS[:, 1, fc, :], pim, rbs[2:4], [r_KS])

        fwd_pass(HS[:, 0], HS[:, 1], r_HS, ftab, r_ftab, k_epilogue)

        if debug == "s_k":
            S.barrier()
            S.emit()
            return nc
        Z = AV(R_X, [16, 768], BF16)
        r_Z = Res("Z")
        hsb0 = [AV(R_X + 24 * K_ + i * 8208, [2052], F32) for i in range(2)]
        r_hsb0 = [Res("hsb0_%d" % i) for i in range(2)]
        accx = AV(R_X + 24 * K_ + 16416, [2048], F32)
        accv = AV(R_X + 24 * K_ + 16416 + 8192, [2048], F32)
        zT = AV(R_X + 24 * K_ + 16416 + 16384, [2048], BF16)
        r_accx, r_accv, r_zT = Res("accx"), Res("accv"), Res("zT")
        wch = [AV(R_T + 48 * K_ + i * 2 * K_, [8, 128], BF16) for i in range(3)]
        r_wch = [Res("wch%d" % i) for i in range(3)]
        for i in range(2):
            S.op("pool", [lambda e, i=i: e.memset(hsb0[i][:, 0:1], 0.0)], writes=[r_hsb0[i], r_HS])
            S.op("pool", [lambda e, i=i: e.memset(hsb0[i][:, 2049:2050], 0.0)], writes=[r_hsb0[i], r_HS])
        w0v = w_in_d[0].rearrange("(kc p) n -> p kc n", p=128)
        order = [18, 19] + [0, 1, 2, 3, 4, 5]
        for i in range(6):
            order += [6 + i, 12 + i]
        _wc = [0]

        def proj_chunk(wv, col0, sink_fn, extra_reads=()):
            wi = _wc[0] % 3
            _wc[0] += 1
            S.dma("pool", wch[wi], wv[:, :, col0:col0 + 128], writes=[r_wch[wi]], key="wch%d" % wi)
            for tq in range(4):
                b = nxt([0, 1, 2, 3])
                mm_group(bank(b), [(wch[wi][:, kc, :], XT[:, kc, tq * 512:(tq + 1) * 512]) for kc in range(8)],
                         reads=[r_wch[wi], r_XT], writes=[PB[b]])
                sink_fn(tq, b)

        def conv_chunk(c, hs_, r_hs, acc_out, r_acc_list, out_final):
            S.op("act", [lambda e: e.activation(out=acc_out, in_=hs_[:, 1:2049], func=AF.Identity,
                                                scale=cwb[:, c, 1:2], bias=cwb[:, c, 3:4])],
                 reads=[r_hs, r_cwb], writes=r_acc_list)
            S.op("dve", [lambda e: e.scalar_tensor_tensor(out=acc_out, in0=hs_[:, 0:2048], scalar=cwb[:, c, 0:1],
                                                          in1=acc_out, op0=ALU.mult, op1=ALU.add)],
                 reads=[r_hs, r_cwb] + r_acc_list, writes=r_acc_list)
            S.op("dve", [lambda e: e.scalar_tensor_tensor(out=out_final[0], in0=hs_[:, 2:2050], scalar=cwb[:, c, 2:3],
                                                          in1=acc_out, op0=ALU.mult, op1=ALU.add)],
                 reads=[r_hs, r_cwb] + r_acc_list, writes=out_final[1])

        zdef = []
        for ci_, c in enumerate(order):
            if debug and debug.startswith('s_p') and ci_ == int(debug[3:]):
                S.barrier()
                S.emit()
                return nc
            if c >= 18:
                hp = c - 18

                def sink_mq(tq, b, hp=hp):
                    act_copy(MQ[:, hp, tq * 512:(tq + 1) * 512], bank(b), [PB[b]], [r_MQ])
                proj_chunk(w0v, c * 128, sink_mq)
                continue
            si = (c % 2) if c < 6 else (0 if c < 12 else 1)
            hs_ = hsb0[si]

            def sink_h(tq, b, hs_=hs_, si=si):
                act_copy(hs_[:, 1 + tq * 512:1 + (tq + 1) * 512], bank(b), [PB[b]], [r_hsb0[si]])
            proj_chunk(w0v, c * 128, sink_h)
            if c < 6:
                conv_chunk(c, hs_, r_hsb0[si], accv, [r_accv], (YT[:, c, :], [r_YT]))
            elif c < 12:
                conv_chunk(c, hs_, r_hsb0[si], accx, [r_accx], (accx, [r_accx]))
                while zdef:
                    zdef.pop(0)()
            else:
                i6 = c - 12
                conv_chunk(c, hs_, r_hsb0[si], accv, [r_accv], (accv, [r_accv]))
                S.op("dve", [lambda e: e.tensor_tensor(out=zT, in0=accv, in1=accx, op=ALU.mult)],
                     reads=[r_accv, r_accx], writes=[r_zT])
                def _ztr(i6=i6):
                    for g8 in range(2):
                        b = nxt([6, 7])
                        bb = bank_bf(b)
                        fns = [lambda e, t8=t8, bb=bb, g8=g8: e.transpose(bb[:, t8 * 128:(t8 + 1) * 128],
                                                                         zT[:, (g8 * 8 + t8) * 128:(g8 * 8 + t8 + 1) * 128], ident[:])
                               for t8 in range(8)]
                        S.op("pe", fns, reads=[r_zT, r_consts], writes=[PB[b]])
                        S.op("act", [lambda e, g8=g8, bb=bb, i6=i6: e.activation(
                            out=Z[:, g8 * 8:(g8 + 1) * 8, i6 * 128:(i6 + 1) * 128],
                            in_=bb.rearrange("p (k m) -> p k m", k=8), func=AF.Identity)],
                             reads=[PB[b]], writes=[r_Z])
                zdef.append(_ztr)
        while zdef:
            zdef.pop(0)()
        if debug == "z":
            for tile in range(16):
                S.op("act", [lambda e, tile=tile: e.activation(out=X[:, tile, 0:768] if False else hsb0[0][:, 0:768], in_=Z[:, tile, :], func=AF.Identity)],
                     reads=[r_Z], writes=[r_hsb0[0]])
                S.dma("sp", dbg_d[tile * 128:(tile + 1) * 128, 0:768], hsb0[0][:, 0:768], reads=[r_hsb0[0]], key="dbg")
            S.barrier()
            S.emit()
            return nc
        mem_attention(R_X + 24 * K_)

        YRE = AV(R_XT, [16, 768], BF16)
        YIM = AV(R_X + 24 * K_, [16, 768], BF16)
        r_Y = Res("Yspec")
        ct = [AV(R_X + 48 * K_ + i * 3 * K_, [768], F32) for i in range(4)]
        r_ct = [Res("ct%d" % i) for i in range(4)]
        ftabU = [AV(R_T + 48 * K_, [2, 16, 128], BF16), AV(R_MQ, [2, 16, 128], BF16)]
        r_ftabU = [Res("ftabU0"), Res("ftabU1")]

        def u_epilogue(fc, pre, pim, rbs):
            kre, kim = KS[:, 0, fc, :], KS[:, 1, fc, :]
            S.op("dve", [lambda e: e.tensor_tensor(out=ct[0], in0=pre, in1=kre, op=ALU.mult)], reads=rbs[0:2] + [r_KS], writes=[r_ct[0]])
            S.op("dve", [lambda e: e.tensor_tensor(out=ct[1], in0=pim, in1=kim, op=ALU.mult)], reads=rbs[2:4] + [r_KS], writes=[r_ct[1]])
            S.op("dve", [lambda e: e.tensor_tensor(out=ct[2], in0=pre, in1=kim, op=ALU.mult)], reads=rbs[0:2] + [r_KS], writes=[r_ct[2]])
            S.op("dve", [lambda e: e.tensor_tensor(out=ct[3], in0=pim, in1=kre, op=ALU.mult)], reads=rbs[2:4] + [r_KS], writes=[r_ct[3]])
            S.op("pool", [lambda e: e.tensor_tensor(out=YRE[:, fc, :], in0=ct[0], in1=ct[1], op=ALU.subtract)],
                 reads=[r_ct[0], r_ct[1]], writes=[r_Y])
            S.op("pool", [lambda e: e.tensor_tensor(out=YIM[:, fc, :], in0=ct[2], in1=ct[3], op=ALU.add)],
                 reads=[r_ct[2], r_ct[3]], writes=[r_Y])

        fwd_pass(Z, Z, r_Z, ftabU, r_ftabU, u_epilogue, extra_w=(r_wch, [r_MQ]))

        itab = [AV(R_T + 48 * K_, [4, 2, 512], BF16), AV(R_MQ, [4, 2, 512], BF16)]
        r_itab = r_ftabU
        _it = 0
        for tt in range(4):
            fn_all = []
            for fg in range(4):
                si = _it % 2
                _it += 1
                S.dma("sp", itab[si].rearrange("p a b c -> p (a b c)"), inv_tab_d[tt, fg], writes=[r_itab[si]], key="itab%d" % si)
                fns = []
                for fi in range(4):
                    fc = fg * 4 + fi
                    for ri in range(2):
                        Ysrc = YRE if ri == 0 else YIM
                        for cc in range(6):
                            first = (fc == 0 and ri == 0)
                            lastm = (fc == 15 and ri == 1)
                            fns.append(lambda e, cc=cc, Ysrc=Ysrc, fc=fc, si=si, fi=fi, ri=ri, first=first, lastm=lastm: e.matmul(
                                bank(cc), Ysrc[:, fc, cc * 128:(cc + 1) * 128], itab[si][:, fi, ri, :], start=first, stop=lastm))
                if fg == 0:
                    for cc in range(6):
                        S.op("pe", [f_ for k_, f_ in enumerate(fns) if k_ % 6 == cc], reads=[r_itab[si], r_Y], writes=[PB[cc]])
                else:
                    S.op("pe", fns, reads=[r_itab[si], r_Y], writes=[PB[cc] for cc in range(6)])
            for cc in range(6):
                S.op("dve", [lambda e, cc=cc, tt=tt: e.tensor_tensor(out=YT[:, cc, tt * 512:(tt + 1) * 512], in0=bank(cc),
                                                                    in1=YT[:, cc, tt * 512:(tt + 1) * 512], op=ALU.mult)],
                     reads=[PB[cc], r_YT], writes=[r_YT])

        if debug == "mix0":
            for c in range(8):
                src = YT[:, c, 0:1024] if c < 6 else YM[:, c - 6, 0:1024]
                S.op("act", [lambda e, src=src: e.activation(out=rbuf[0], in_=src, func=AF.Identity)], reads=[r_YT, r_YM], writes=[r_rbuf[0]])
                S.dma("sp", dbg_d[c * 128:(c + 1) * 128, :], rbuf[0], reads=[r_rbuf[0]], key="dbg")
            S.barrier()
            S.emit()
            return nc

        out_proj_ln1(0)
        S.barrier()
        if debug == "ln1_0":
            for tile in range(16):
                S.dma("sp", dbg_d[tile * 128:(tile + 1) * 128, :], X[:, tile, :], reads=[r_X[tile]], key="dbg")
            S.barrier()
            S.emit()
            return nc
        ffn_ln2(0, final=(debug == "l0"))
        S.barrier()

        if debug is None or debug.startswith("l1"):
            if debug == "l1_s":
                S.barrier()
                S.emit()
                return nc
            w1v = w_in_d[1].rearrange("(kc p) n -> p kc n", p=128)
            QT = YT
            r_QT = [Res("QT%d" % i) for i in range(12)]
            KT = AV(R_T, [4, 2048], BF16)
            r_KT = Res("KT")
            VT = AV(R_T + 16 * K_, [16, 256], BF16)
            r_VT = Res("VT")
            ropec = AV(R_T + 24 * K_, [2048], F32)
            ropes = AV(R_T + 32 * K_, [2048], F32)
            r_rope = Res("rope")
            wch1 = [AV(R_T + 40 * K_ + i * 2 * K_, [8, 128], BF16) for i in range(3)]
            r_wch1 = [Res("wch1_%d" % i) for i in range(3)]
            qsb = [AV(R_T + 46 * K_ + i * K_, [512], BF16) for i in range(2)]
            r_qsb = [Res("qsb%d" % i) for i in range(2)]
            rt1 = AV(R_T + 48 * K_, [512], F32)
            rt2 = AV(R_T + 50 * K_, [512], F32)
            r_rt1, r_rt2 = Res("rt1"), Res("rt2")
            wvt = AV(R_T + 52 * K_, [8, 256], BF16)
            r_wvt = Res("wvt")
            esk = small("esk", [64, 12], F32)
            r_esk = Res("esk")
            S.dma("sp", ropec, ropec_d[:, :], writes=[r_rope], key="rope0")
            S.dma("sp", ropes, ropes_d[:, :], writes=[r_rope], key="rope1")
            S.dma("sp", esk[:], sink_d.partition_broadcast(64), writes=[r_esk], key="esk")
            S.op("act", [lambda e: e.activation(out=esk[:], in_=esk[:], func=AF.Exp)], reads=[r_esk], writes=[r_esk])
            S.dma("pool", wvt, w1v[:, :, 1024:1280], writes=[r_wvt], key="wvt")
            if debug == "l1_p0":
                S.barrier()
                S.emit()
                return nc
            _w1 = [0]
            _rq = [0]

            def proj1(loads, sink_fn):
                wi = _w1[0] % 3
                _w1[0] += 1
                for k_, (dst0, dst1, c0, c1) in enumerate(loads):
                    S.dma("pool", wch1[wi][:, :, dst0:dst1], w1v[:, :, c0:c1], writes=[r_wch1[wi]], key="wch1_%d_%d" % (wi, k_))
                for tq in range(4):
                    b = nxt([0, 1, 2, 3])
                    mm_group(bank(b), [(wch1[wi][:, kc, :], XT[:, kc, tq * 512:(tq + 1) * 512]) for kc in range(8)],
                             reads=[r_wch1[wi], r_XT], writes=[PB[b]])
                    sink_fn(tq, b)

            def rope_sink(dest, r_dest):
                def sink(tq, b):
                    cs = slice(tq * 512, (tq + 1) * 512)
                    i = _rq[0] % 2
                    _rq[0] += 1
                    S.op("act", [lambda e: e.activation(out=qsb[i], in_=bank(b), func=AF.Identity)],
                         reads=[PB[b]], writes=[r_qsb[i]])
                    b2 = [4, 5][i]
                    mm_group(bank(b2), [(pm[:], qsb[i])], reads=[r_consts, r_qsb[i]], writes=[PB[b2]])
                    S.op("dve", [lambda e: e.tensor_tensor(out=rt1, in0=bank(b), in1=ropec[:, cs], op=ALU.mult)],
                         reads=[PB[b], r_rope], writes=[r_rt1])
                    S.op("dve", [lambda e: e.tensor_tensor(out=rt2, in0=bank(b2), in1=ropes[:, cs], op=ALU.mult)],
                         reads=[PB[b2], r_rope], writes=[r_rt2])
                    S.op("dve", [lambda e: e.tensor_tensor(out=dest[:, cs], in0=rt1, in1=rt2, op=ALU.add)],
                         reads=[r_rt1, r_rt2], writes=r_dest)
                return sink

            for c in range(6):
                proj1([(0, 128, c * 128, (c + 1) * 128)], rope_sink(QT[:, c, :], [r_QT[2 * c], r_QT[2 * c + 1]]))
            if debug == "l1_p1":
                S.barrier()
                S.emit()
                return nc
            for g in range(4):
                c0 = 768 + g * 64
                proj1([(0, 64, c0, c0 + 64), (64, 128, c0, c0 + 64)], rope_sink(KT[:, g, :], [r_KT]))
            if debug == "l1_p2":
                S.barrier()
                S.emit()
                return nc
            for hp in range(2):
                def sink_mq1(tq, b, hp=hp):
                    act_copy(MQ[:, hp, tq * 512:(tq + 1) * 512], bank(b), [PB[b]], [r_MQ])
                proj1([(0, 128, 1280 + hp * 128, 1280 + (hp + 1) * 128)], sink_mq1)
            for tile in range(16):
                b = nxt([0, 1, 2, 3])
                mm_group(bank(b, 256), [(XT[:, kc, tile * 128:(tile + 1) * 128], wvt[:, kc, :]) for kc in range(8)],
                         reads=[r_XT, r_wvt], writes=[PB[b]])
                act_copy(VT[:, tile, :], bank(b, 256), [PB[b]], [r_VT])

            if debug == "l1_p":
                S.barrier()
                S.emit()
                return nc
            PTs = [AV(R_XT + i * 12 * K_, [16, 384], BF16) for i in range(2)]
            r_PTs = [Res("PTs%d" % i) for i in range(2)]
            rden1s = [AV(R_XT + 24 * K_ + i * 2 * K_, [512], F32) for i in range(2)]
            r_rden1s = [Res("rden1_%d" % i) for i in range(2)]
            def st_exp(h, j):
                g = h // 3
                hh = h % 2
                c = h // 2
                prow = slice(hh * 64, (hh + 1) * 64)
                pt = PTs[h % 2]
                r_pt = r_PTs[h % 2]
                qlo = max(0, j - 1) * 128
                qhi = min(16, j + 2) * 128
                n = qhi - qlo
                moff = qlo - (j - 1) * 128
                b = nxt([0, 1, 2, 3])
                fns = [
                    lambda e: e.matmul(bank(b, n), KT[prow, g, j * 128:(j + 1) * 128], QT[prow, c, qlo:qhi], start=True, stop=False),
                    lambda e: e.matmul(bank(b, n), ident[:], maskb[:, moff:moff + n], start=False, stop=True),
                ]
                S.op("pe", fns, reads=[r_KT, r_QT[h], r_consts], writes=[PB[b]])
                S.op("act", [lambda e: e.activation(out=pt[:, j, 0:n], in_=bank(b, n), func=AF.Exp, scale=0.125)],
                     reads=[PB[b]], writes=[r_pt])

            def pv_norm(h, qt):
                g = h // 3
                hh = h % 2
                c = h // 2
                prow = slice(hh * 64, (hh + 1) * 64)
                pt = PTs[h % 2]
                r_pt = r_PTs[h % 2]
                bo, bd = [4, 6][qt % 2], [5, 7][qt % 2]
                fo, fd = [], []
                for i4 in range(4):
                    qb = 4 * qt + i4
                    js = [j for j in (qb - 1, qb, qb + 1) if 0 <= j < 16]
                    for k_, j in enumerate(js):
                        lc = (qb - max(0, j - 1)) * 128
                        st_, sp_ = (k_ == 0), (k_ == len(js) - 1)
                        fo.append(lambda e, i4=i4, j=j, lc=lc, st_=st_, sp_=sp_: e.matmul(
                            ps[0:64, bo * 512 + i4 * 128:bo * 512 + (i4 + 1) * 128], VT[:, j, g * 64:(g + 1) * 64],
                            pt[:, j, lc:lc + 128], start=st_, stop=sp_))
                        fd.append(lambda e, i4=i4, j=j, lc=lc, st_=st_, sp_=sp_: e.matmul(
                            ps[0:64, bd * 512 + i4 * 128:bd * 512 + (i4 + 1) * 128], ones64[:],
                            pt[:, j, lc:lc + 128], start=st_, stop=sp_))
                S.op("pe", fo, reads=[r_pt, r_VT], writes=[PB[bo]])
                S.op("pe", fd, reads=[r_pt, r_consts], writes=[PB[bd]])
                rd = rden1s[qt % 2]
                r_rd = r_rden1s[qt % 2]
                S.op("act", [lambda e: e.activation(out=rd[0:64, :], in_=bank(bd, 512, 0, 64), func=AF.Identity,
                                                    bias=esk[:, h:h + 1])],
                     reads=[PB[bd], r_esk], writes=[r_rd])
                S.op("dve", [lambda e: e.reciprocal(out=rd[0:64, :], in_=rd[0:64, :])], reads=[r_rd], writes=[r_rd])
                S.op("dve", [lambda e: e.tensor_tensor(
                    out=QT[prow, c, qt * 512:(qt + 1) * 512], in0=bank(bo, 512, 0, 64), in1=rd[0:64, :], op=ALU.mult)],
                     reads=[PB[bo], r_rd], writes=[r_QT[h]])

            for h in range(13):
                for qt in range(4):
                    if h < 12:
                        for j in range(4 * qt, 4 * qt + 4):
                            st_exp(h, j)
                    if h >= 1:
                        pv_norm(h - 1, qt)
            if debug == "l1_a":
                S.barrier()
                S.emit()
                return nc
            mem_attention(R_XT, extra_w=r_PTs)
            if debug == "l1mix":
                for c in range(8):
                    src = YT[:, c, 0:1024] if c < 6 else YM[:, c - 6, 0:1024]
                    S.op("act", [lambda e, src=src: e.activation(out=rbuf[0], in_=src, func=AF.Identity)], reads=[r_YM], writes=[r_rbuf[0]])
                    S.dma("sp", dbg_d[c * 128:(c + 1) * 128, :], rbuf[0], reads=[r_rbuf[0]], key="dbg")
                S.barrier()
                S.emit()
                return nc
            out_proj_ln1(1)
            S.barrier()
            ffn_ln2(1, final=True)
            S.barrier()

        S.barrier()
        S.emit()
    return nc


def prep_shared(inputs):
    f32 = np.float32
    sh = {}
    for k in ("w_mem_kv", "l0_w_in", "l1_w_in", "l0_w_out", "l1_w_out", "l0_ffn_w_up", "l1_ffn_w_up",
              "l0_ffn_w_down", "l1_ffn_w_down", "l0_filt_w1", "l0_filt_w2", "l0_filt_w3", "l0_filt_w_out",
              "l0_hyena_d", "l1_sink"):
        sh[k] = np.ascontiguousarray(np.asarray(inputs[k], dtype=f32))
    for i in range(2):
        for n in ("ln1_g", "ln1_b", "ln2_g", "ln2_b"):
            sh["l%d_%s" % (i, n)] = np.ascontiguousarray(np.asarray(inputs["l%d_%s" % (i, n)], dtype=f32))
        cw = np.asarray(inputs["l%d_ffn_conv_w" % i], f32)
        cb = np.asarray(inputs["l%d_ffn_conv_b" % i], f32)
        a = np.concatenate([cw, cb[None, :]], axis=0)
        sh["l%d_fcwb" % i] = np.ascontiguousarray(a.reshape(4, 44, 128).transpose(2, 1, 0))
    cw = np.asarray(inputs["l0_conv_w"], f32)
    cb = np.asarray(inputs["l0_conv_b"], f32)
    a = np.concatenate([cw, cb[None, :]], axis=0)
    sh["l0_cwb"] = np.ascontiguousarray(a.reshape(4, 18, 128).transpose(2, 1, 0))
    sh["l0_fbf"] = np.ascontiguousarray(np.stack(
        [np.asarray(inputs["l0_filt_%s%d" % (n, l)], f32) for l in (1, 2, 3) for n in ("b", "f")], axis=1))
    sh.update(const_tables())
    return sh


_NC_CACHE = {}


def kernel(**inputs):
    sh = prep_shared(inputs)
    x = np.asarray(inputs["x"], np.float32)
    mem = np.asarray(inputs["mem"], np.float32)
    if "nc" not in _NC_CACHE:
        _NC_CACHE["nc"] = build_program()
    nc = _NC_CACHE["nc"]
    in_maps = []
    for b in range(8):
        m = dict(sh)
        m["x"] = np.ascontiguousarray(x[b])
        m["mem"] = np.ascontiguousarray(mem[b])
        in_maps.append(m)
    res = run_bass_kernel_spmd(nc, in_maps, core_ids=list(range(8)))
    return np.stack([np.asarray(r["out"], np.float32) for r in res.results], axis=0)
```
